# Optimizing a Trainium2 kernel written in Bass

```python
import math
import jax, jax.numpy as jnp
from jax import lax
import numpy as np

D_MODEL = 1024
BATCH = 16
SEQ = 2048
DEPTH = 1
DEC_BATCH = 16
DEC_SEQ = 4096
PAST_LEN = 128

HEAD_DIM = 64
N_HEADS = D_MODEL // HEAD_DIM
ATTN_HEADS = N_HEADS // 2
HGRN_HEADS = N_HEADS - ATTN_HEADS
ATTN_WIDTH = ATTN_HEADS * HEAD_DIM
HGRN_KEY_DIM = HEAD_DIM
HGRN_VAL_DIM = HEAD_DIM
HGRN_KEY_WIDTH = HGRN_HEADS * HGRN_KEY_DIM
HGRN_VAL_WIDTH = HGRN_HEADS * HGRN_VAL_DIM
MIX_WIDTH = ATTN_WIDTH + HGRN_VAL_WIDTH
IN_SIZES = (ATTN_WIDTH, ATTN_WIDTH, ATTN_WIDTH,
            HGRN_KEY_WIDTH, HGRN_KEY_WIDTH, HGRN_KEY_WIDTH,
            HGRN_VAL_WIDTH, HGRN_VAL_WIDTH)
IN_WIDTH = sum(IN_SIZES)
DILATED_BRANCHES = ((128, 1), (512, 4), (2048, 16))
ROPE_THETA = 500000.0
ROT_DIM = HEAD_DIM // 4
HGRN_CHUNK = 64
D_FF = ((8 * D_MODEL // 3 + 255) // 256) * 256
CONV_WIDTH = 3
NORM_EPS = 1e-6
NEG_INF = -1e30

kernel_name = "hybrid_dilated_attn_hgrn2_encoder"


def _rmsnorm(x, w):
    xf = x.astype(jnp.float32)
    y = xf * lax.rsqrt(jnp.mean(xf * xf, axis=-1, keepdims=True) + NORM_EPS)
    return (y * w.astype(jnp.float32)).astype(x.dtype)


def _partial_rotary(t, positions):
    half = ROT_DIM // 2
    inv_freq = ROPE_THETA ** (-jnp.arange(half, dtype=jnp.float32) * 2.0 / ROT_DIM)
    ang = positions.astype(jnp.float32)[:, None] * inv_freq[None, :]
    cos = jnp.cos(ang)[None, :, None, :]
    sin = jnp.sin(ang)[None, :, None, :]
    tr = t[..., :ROT_DIM].astype(jnp.float32)
    x1, x2 = tr[..., :half], tr[..., half:]
    rot = jnp.concatenate([x1 * cos - x2 * sin, x2 * cos + x1 * sin], axis=-1)
    return jnp.concatenate([rot.astype(t.dtype), t[..., ROT_DIM:]], axis=-1)


def _dilated_branch(q, k, v, window, dilation):
    B, S, H, Dh = q.shape
    n_side = window // (2 * dilation)
    blk = n_side
    L = S // dilation
    nb = -(-L // blk)
    Lp = nb * blk

    def to_sub(t):
        return t.reshape(B, L, dilation, H, Dh).transpose(0, 2, 3, 1, 4)

    qs, ks, vs = to_sub(q), to_sub(k), to_sub(v)
    qb = jnp.pad(qs, ((0, 0), (0, 0), (0, 0), (0, Lp - L), (0, 0))).reshape(B, dilation, H, nb, blk, Dh)

    def neighbours(t):
        tb = jnp.pad(t, ((0, 0), (0, 0), (0, 0), (blk, Lp - L + blk), (0, 0))).reshape(B, dilation, H, nb + 2, blk, Dh)
        return jnp.concatenate([tb[:, :, :, :-2], tb[:, :, :, 1:-1], tb[:, :, :, 2:]], axis=4)

    kb, vb = neighbours(ks), neighbours(vs)
    s = jnp.einsum('brhnqd,brhnkd->brhnqk', qb, kb, preferred_element_type=jnp.float32) * (1.0 / math.sqrt(Dh))
    qi = jnp.arange(blk)[:, None]
    ki = jnp.arange(3 * blk)[None, :]
    band = jnp.abs(qi + blk - ki) <= n_side
    kpos = jnp.arange(nb)[:, None] * blk + jnp.arange(3 * blk)[None, :] - blk
    inrange = (kpos >= 0) & (kpos < L)
    mask = band[None, :, :] & inrange[:, None, :]
    s = jnp.where(mask, s, NEG_INF)
    lse = jax.nn.logsumexp(s, axis=-1)
    p = jnp.exp(s - lse[..., None])
    o = jnp.einsum('brhnqk,brhnkd->brhnqd', p.astype(v.dtype), vb)
    o = o.reshape(B, dilation, H, Lp, Dh)[:, :, :, :L].transpose(0, 3, 1, 2, 4).reshape(B, S, H, Dh)
    lse = lse.reshape(B, dilation, H, Lp)[:, :, :, :L].transpose(0, 3, 1, 2).reshape(B, S, H)
    return o, lse


def _dilated_attention(q, k, v):
    outs, lses = [], []
    for window, dilation in DILATED_BRANCHES:
        o, lse = _dilated_branch(q, k, v, window, dilation)
        outs.append(o)
        lses.append(lse)
    w = jax.nn.softmax(jnp.stack(lses, axis=0), axis=0)
    return jnp.einsum('nbsh,nbshd->bshd', w.astype(q.dtype), jnp.stack(outs, axis=0))


def _hgrn2_scan(q, k, v, log_f):
    B, S, H, Dk = q.shape
    Dv = v.shape[-1]
    C = HGRN_CHUNK
    n = S // C

    def chunks(t):
        return t.reshape(B, n, C, H, t.shape[-1]).transpose(1, 0, 3, 2, 4)

    causal = jnp.tril(jnp.ones((C, C), dtype=bool))

    def step(state, inp):
        qc, kc, vc, gc = inp
        A = jnp.cumsum(gc, axis=-2)
        o_inter = jnp.einsum('bhck,bhkv->bhcv', qc * jnp.exp(A), state)
        diff = A[:, :, :, None, :] - A[:, :, None, :, :]
        decay = jnp.exp(jnp.where(causal[:, :, None], diff, -jnp.inf))
        att = jnp.einsum('bhtk,bhtsk,bhsk->bhts', qc, decay, kc)
        o_intra = jnp.einsum('bhts,bhsv->bhtv', att, vc)
        A_last = A[:, :, -1:, :]
        new_state = jnp.exp(A_last[:, :, 0, :])[..., None] * state + jnp.einsum(
            'bhsk,bhsv->bhkv', kc * jnp.exp(A_last - A), vc)
        return new_state, o_inter + o_intra

    init = jnp.zeros((B, H, Dk, Dv), jnp.float32)
    _, o = lax.scan(step, init, (chunks(q), chunks(k), chunks(v), chunks(log_f)))
    return o.transpose(1, 0, 3, 2, 4).reshape(B, S, H, Dv)


def _hgrn2_gate(z, lb):
    zf = z.astype(jnp.float32)
    log_f = jnp.log(lb + (1.0 - lb) * jax.nn.sigmoid(zf))
    k = (1.0 - lb) * jax.nn.sigmoid(-zf)
    return log_f, k


def _mixer(xn, w_in, lb_fwd_param, lb_bwd_param, out_norm_w, w_out, layer):
    B, S, _ = xn.shape
    proj = xn @ w_in
    offs = [int(o) for o in np.cumsum(IN_SIZES)[:-1]]
    aq, ak, av, hq, hf_f, hf_b, hi, hg = jnp.split(proj, offs, axis=-1)

    pos = jnp.arange(S)
    aq = _partial_rotary(aq.reshape(B, S, ATTN_HEADS, HEAD_DIM), pos)
    ak = _partial_rotary(ak.reshape(B, S, ATTN_HEADS, HEAD_DIM), pos)
    av = av.reshape(B, S, ATTN_HEADS, HEAD_DIM)
    attn_out = _dilated_attention(aq, ak, av).reshape(B, S, ATTN_WIDTH)

    lb_f = jnp.cumsum(jax.nn.softmax(lb_fwd_param.astype(jnp.float32), axis=0), axis=0)[layer]
    lb_b = jnp.cumsum(jax.nn.softmax(lb_bwd_param.astype(jnp.float32), axis=0), axis=0)[layer]
    lb_f = lb_f.reshape(HGRN_HEADS, HGRN_KEY_DIM)
    lb_b = lb_b.reshape(HGRN_HEADS, HGRN_KEY_DIM)
    q = jax.nn.silu(hq.astype(jnp.float32)).reshape(B, S, HGRN_HEADS, HGRN_KEY_DIM)
    v = hi.astype(jnp.float32).reshape(B, S, HGRN_HEADS, HGRN_VAL_DIM)
    logf_f, k_f = _hgrn2_gate(hf_f.reshape(B, S, HGRN_HEADS, HGRN_KEY_DIM), lb_f)
    logf_b, k_b = _hgrn2_gate(hf_b.reshape(B, S, HGRN_HEADS, HGRN_KEY_DIM), lb_b)
    o_fwd = _hgrn2_scan(q, k_f, v, logf_f)
    flip = lambda t: jnp.flip(t, axis=1)
    o_bwd = flip(_hgrn2_scan(flip(q), flip(k_b), flip(v), flip(logf_b)))
    o = _rmsnorm(o_fwd + o_bwd, out_norm_w)
    gate = jax.nn.silu(hg.astype(jnp.float32)).reshape(B, S, HGRN_HEADS, HGRN_VAL_DIM)
    hgrn_out = (o * gate).reshape(B, S, HGRN_VAL_WIDTH).astype(xn.dtype)

    return jnp.concatenate([attn_out, hgrn_out], axis=-1) @ w_out


def _conv_ffn(xn, w_gate, w_up, conv_w, conv_b, w_down):
    a = xn @ w_gate
    a = lax.conv_general_dilated(
        a, conv_w[:, None, :], window_strides=(1,),
        padding=((CONV_WIDTH // 2, CONV_WIDTH // 2),),
        dimension_numbers=('NWC', 'WIO', 'NWC'),
        feature_group_count=D_FF) + conv_b
    b = xn @ w_up
    return (jax.nn.gelu(a, approximate=True) * b) @ w_down


def _encode(x, norm_mix_pre, w_in, hgrn_lb_fwd, hgrn_lb_bwd, hgrn_out_norm, w_out, norm_mix_post,
            norm_ffn_pre, w_gate, w_up, conv_w, conv_b, w_down, norm_ffn_post):
    for l in range(DEPTH):
        mix = _mixer(_rmsnorm(x, norm_mix_pre[l]), w_in[l], hgrn_lb_fwd, hgrn_lb_bwd,
                     hgrn_out_norm[l], w_out[l], l)
        x = x + _rmsnorm(mix, norm_mix_post[l])
        ffn = _conv_ffn(_rmsnorm(x, norm_ffn_pre[l]), w_gate[l], w_up[l], conv_w[l], conv_b[l], w_down[l])
        x = x + _rmsnorm(ffn, norm_ffn_post[l])
    return x


def setup_inputs(seed: int = 0) -> dict:
    key = jax.random.key(seed)
    ks = jax.random.split(key, 16)
    nrm = lambda k, shape, scale: jax.random.normal(k, shape, jnp.float32) * scale
    gain = lambda k, shape: 1.0 + 0.05 * jax.random.normal(k, shape, jnp.float32)
    return {
        "x_prompt": nrm(ks[0], (BATCH, SEQ, D_MODEL), 1.0),
        "x_sample": nrm(ks[1], (DEC_BATCH, DEC_SEQ, D_MODEL), 1.0),
        "norm_mix_pre": gain(ks[2], (DEPTH, D_MODEL)),
        "w_in": nrm(ks[3], (DEPTH, D_MODEL, IN_WIDTH), D_MODEL ** -0.5),
        "hgrn_lb_fwd": nrm(ks[4], (DEPTH + 1, HGRN_KEY_WIDTH), 0.1),
        "hgrn_lb_bwd": nrm(ks[5], (DEPTH + 1, HGRN_KEY_WIDTH), 0.1),
        "hgrn_out_norm": gain(ks[6], (DEPTH, HGRN_VAL_DIM)),
        "w_out": nrm(ks[7], (DEPTH, MIX_WIDTH, D_MODEL), MIX_WIDTH ** -0.5),
        "norm_mix_post": gain(ks[8], (DEPTH, D_MODEL)),
        "norm_ffn_pre": gain(ks[9], (DEPTH, D_MODEL)),
        "w_gate": nrm(ks[10], (DEPTH, D_MODEL, D_FF), D_MODEL ** -0.5),
        "w_up": nrm(ks[11], (DEPTH, D_MODEL, D_FF), D_MODEL ** -0.5),
        "conv_w": nrm(ks[12], (DEPTH, CONV_WIDTH, D_FF), CONV_WIDTH ** -0.5),
        "conv_b": nrm(ks[13], (DEPTH, D_FF), 0.01),
        "w_down": nrm(ks[14], (DEPTH, D_FF, D_MODEL), D_FF ** -0.5),
        "norm_ffn_post": gain(ks[15], (DEPTH, D_MODEL)),
    }


def reference(x_prompt, x_sample, norm_mix_pre, w_in, hgrn_lb_fwd, hgrn_lb_bwd, hgrn_out_norm, w_out,
              norm_mix_post, norm_ffn_pre, w_gate, w_up, conv_w, conv_b, w_down, norm_ffn_post):
    y_prompt = _encode(x_prompt, norm_mix_pre, w_in, hgrn_lb_fwd, hgrn_lb_bwd, hgrn_out_norm, w_out,
                       norm_mix_post, norm_ffn_pre, w_gate, w_up, conv_w, conv_b, w_down, norm_ffn_post)
    y_sample = _encode(x_sample, norm_mix_pre, w_in, hgrn_lb_fwd, hgrn_lb_bwd, hgrn_out_norm, w_out,
                       norm_mix_post, norm_ffn_pre, w_gate, w_up, conv_w, conv_b, w_down, norm_ffn_post)
    return (y_prompt, y_sample)
```

```python
import os
import numpy as np
import ml_dtypes
import concourse.bass as bass
import concourse.mybir as mybir
from concourse.bass_utils import run_bass_kernel_spmd

F32 = mybir.dt.float32
BF16 = mybir.dt.bfloat16
AF = mybir.ActivationFunctionType
ALU = mybir.AluOpType

D = 1024
KC = 8
INW = 4096
DFF = 2816
NFC = 22
EPS = 1e-6
ROPE_THETA = 500000.0
BRANCHES = (1, 4, 16)
CH = 64
COMPUTE = ('pe', 'act', 'dve', 'pool')


class Prog:
    def __init__(self, ndma=12):
        self.ndma = ndma
        self.lists = {e: [] for e in COMPUTE + ('sp',)}
        self.bystream = {}
        self.tok = {}
        self.vc = {e: {} for e in COMPUTE + ('sp',)}
        self.dma_n = 0
        self.pending_barrier = {}

    def barrier(self):
        deps = set()
        for s, ops in self.bystream.items():
            if ops:
                deps.add((s, len(ops) - 1))
        for e in self.lists:
            self.pending_barrier[e] = set(deps)
        self.tok = {}

    def add(self, eng, fn, r=(), w=()):
        if eng == 'sp':
            stream = 'd%d' % (self.dma_n % self.ndma)
            self.dma_n += 1
        else:
            stream = eng
        slist = self.bystream.setdefault(stream, [])
        sidx = len(slist)
        deps = set()
        if eng in self.pending_barrier:
            deps |= self.pending_barrier.pop(eng)
        for k in r:
            st = self.tok.get(k)
            if st is not None and st[0] is not None:
                deps.add(st[0])
        for k in w:
            st = self.tok.get(k)
            if st is not None:
                if st[0] is not None:
                    deps.add(st[0])
                deps.update(st[1])
        if eng == 'sp' and sidx > 0:
            deps.add((stream, sidx - 1))
        vc = self.vc[eng]
        waits = {}
        for (s, i) in deps:
            if s == 'pe' and eng == 'pe':
                continue
            if vc.get(s, -1) < i:
                if waits.get(s, -1) < i:
                    waits[s] = i
        for s, i in waits.items():
            dop = self.bystream[s][i]
            dop['sig'] = True
            for s2, i2 in dop['vc'].items():
                if vc.get(s2, -1) < i2:
                    vc[s2] = i2
            if vc.get(s, -1) < i:
                vc[s] = i
        ovc = dict(vc)
        ovc[stream] = sidx
        op = dict(eng=eng, fn=fn, stream=stream, sidx=sidx, waits=waits, sig=False, vc=ovc)
        slist.append(op)
        self.lists[eng].append(op)
        me = (stream, sidx)
        for k in r:
            st = self.tok.setdefault(k, [None, []])
            st[1].append(me)
        for k in w:
            self.tok[k] = [me, []]
        return op

    def emit(self, nc, block, sems):
        counts = {}
        for s in COMPUTE:
            c = 0
            arr = []
            for op in self.bystream.get(s, []):
                if op['sig']:
                    c += 1
                arr.append(c)
            counts[s] = arr

        def val(s, i):
            if s in COMPUTE:
                return counts[s][i]
            return 16 * (i + 1)

        def run(e, ename):
            for op in self.lists[ename]:
                for s, i in op['waits'].items():
                    e.wait_ge(sems[s], val(s, i))
                ins = op['fn'](e)
                if ename == 'sp':
                    ins.then_inc(sems[op['stream']], 16)
                elif op['sig']:
                    ins.then_inc(sems[ename], 1)
            if ename == 'sp':
                for s, ops in self.bystream.items():
                    if s not in COMPUTE and ops:
                        e.wait_ge(sems[s], 16 * len(ops))

        @block.sync
        def _(e):
            run(e, 'sp')

        @block.tensor
        def _(e):
            run(e, 'pe')

        @block.scalar
        def _(e):
            run(e, 'act')

        @block.vector
        def _(e):
            run(e, 'dve')

        @block.gpsimd
        def _(e):
            run(e, 'pool')


def bcast_free(ap, n):
    return bass.AP(ap.tensor, ap.offset, [list(x) for x in ap.ap] + [[0, n]])


def bcast_mid(ap, n):
    l = [list(x) for x in ap.ap]
    return bass.AP(ap.tensor, ap.offset, [l[0], [0, n]] + l[1:])


def sst(lo, n, d):
    return slice(lo, lo + (n - 1) * d + 1, d)


def pstok(bank, lo=0, hi=0):
    return [('ps', bank)]


DEBUG_OFFS = {}


def build(seq_lens, branches=BRANCHES, stop_after=None):
    nc = bass.Bass("TRN2", target_bir_lowering=False)
    NT = sum(seq_lens)
    SM = max(seq_lens)
    dt = nc.dram_tensor
    xs = dt("xs", [NT, D], F32, kind="ExternalInput").ap()
    ys = dt("ys", [NT, D], F32, kind="ExternalOutput").ap()
    w_in = dt("w_in", [D, INW], F32, kind="ExternalInput").ap()
    w_out = dt("w_out", [D, D], F32, kind="ExternalInput").ap()
    w_gate = dt("w_gate", [D, DFF], F32, kind="ExternalInput").ap()
    w_up = dt("w_up", [D, DFF], F32, kind="ExternalInput").ap()
    w_down = dt("w_down", [DFF, D], F32, kind="ExternalInput").ap()
    vec = {}
    for nm, n in (("norm_mix_pre", D), ("norm_mix_post", D), ("norm_ffn_pre", D), ("norm_ffn_post", D),
                  ("conv_b", DFF), ("hgrn_out_norm", 64)):
        vec[nm] = dt(nm, [n], F32, kind="ExternalInput")
    conv_w = dt("conv_w", [3, DFF], F32, kind="ExternalInput")
    lbf = dt("hgrn_lb_fwd", [2, 512], F32, kind="ExternalInput")
    lbb = dt("hgrn_lb_bwd", [2, 512], F32, kind="ExternalInput")
    rotc_d = dt("rot_c", [128, SM], F32, kind="ExternalInput").ap()
    rots_d = dt("rot_s", [128, SM], F32, kind="ExternalInput").ap()
    win_b = dt("win_b", [D, INW], BF16, kind=("ExternalOutput" if os.environ.get("KDEBUG") else "Internal")).ap()
    winsw_b = dt("winsw_b", [D, 1024], BF16, kind="Internal").ap()
    wout_b = dt("wout_b", [D, D], BF16, kind="Internal").ap()
    wg_b = dt("wg_b", [NFC, 128, KC, 128], BF16, kind="Internal").ap()
    wu_b = dt("wu_b", [NFC, 128, KC, 128], BF16, kind="Internal").ap()
    wd_b = dt("wd_b", [DFF, D], BF16, kind="Internal").ap()
    mix_s = dt("mix_s", [KC, 128, SM], BF16, kind=("ExternalOutput" if os.environ.get("KDEBUG") else "Internal")).ap()

    dbg_xt = dt("dbg_xt", [128, KC, SM + 2], BF16, kind="ExternalOutput").ap() if os.environ.get("KDEBUG") else None
    P = Prog()
    from contextlib import ExitStack
    es = ExitStack()
    ARF = 53200
    arena = es.enter_context(nc.sbuf_tensor("arena", [128, ARF], F32))
    arena_b = arena.bitcast(BF16)
    psf = [es.enter_context(nc.psum_tensor("ps%d" % i, [128, 512], F32)) for i in range(8)]
    psb = [p.bitcast(BF16) for p in psf]
    sems = {}
    for s in list(COMPUTE) + ['d%d' % i for i in range(P.ndma)]:
        sems[s] = es.enter_context(nc.semaphore("sem_" + s))

    state = {'off': 0, 'uid': 0}

    def alloc(shape, dtype, name):
        n = 1
        for s_ in shape:
            n *= s_
        esz = 4 if dtype == F32 else 2
        off = (state['off'] + 31) // 32 * 32
        state['off'] = off + n * esz
        assert state['off'] <= ARF * 4, ("SBUF arena overflow", name, state['off'])
        DEBUG_OFFS[name] = (off, list(shape), 'f32' if dtype == F32 else 'bf16')
        base = arena if dtype == F32 else arena_b
        o = off // esz
        v = base[:, o:o + n]
        if len(shape) == 2:
            v = v.rearrange("p (a b) -> p a b", a=shape[0])
        elif len(shape) == 3:
            v = v.rearrange("p (a b c) -> p a b c", a=shape[0], b=shape[1])
        return v

    XT = alloc([KC, SM + 2], BF16, "xnT")
    ident = alloc([128], BF16, "ident")
    maskA = alloc([256], BF16, "maskA")
    maskMB = alloc([256], BF16, "maskMB")
    maskF = alloc([64], BF16, "maskF")
    maskB = alloc([64], BF16, "maskB")
    onesbd = alloc([128], F32, "onesbd")
    esel = alloc([64], F32, "esel")
    cneg = alloc([1], F32, "cneg")
    eps256 = alloc([1], F32, "eps256")
    wpre = alloc([KC], F32, "wpre")
    wfpre = alloc([KC], F32, "wfpre")
    cw = alloc([3, NFC], F32, "cw")
    cb = alloc([NFC], F32, "cb")
    onw4 = alloc([1], F32, "onw4")
    lbt = alloc([2, 2, 4], F32, "lbt")
    ga = alloc([2, 4], F32, "ga")
    gb = alloc([2, 4], F32, "gb")
    gna = alloc([2, 4], F32, "gna")
    gnb = alloc([2, 4], F32, "gnb")
    state['off'] += int(os.environ.get('KPAD', '0'))
    PERSIST = state['off']

    def setup():
        P.add('pool', lambda e: e.memset(ident, 0.0), w=['ident'])
        P.add('pool', lambda e: e.affine_select(out=ident, in_=ident, pattern=[[-1, 128]], compare_op=ALU.not_equal,
                                                 fill=1.0, base=0, channel_multiplier=1), r=['ident'], w=['ident'])
        P.add('pool', lambda e: e.memset(maskA, 1.0), w=['maskA'])
        P.add('pool', lambda e: e.affine_select(out=maskA, in_=maskA, pattern=[[1, 256]], compare_op=ALU.is_ge,
                                                 fill=0.0, base=0, channel_multiplier=-1), r=['maskA'], w=['maskA'])
        P.add('pool', lambda e: e.affine_select(out=maskA, in_=maskA, pattern=[[-1, 256]], compare_op=ALU.is_ge,
                                                 fill=0.0, base=128, channel_multiplier=1), r=['maskA'], w=['maskA'])
        P.add('dve', lambda e: e.tensor_scalar(out=maskMB, in0=maskA, scalar1=-1.0, scalar2=30000.0, op0=ALU.add, op1=ALU.mult),
              r=['maskA'], w=['maskMB'])
        P.add('pool', lambda e: e.memset(maskF[0:64, :], 1.0), w=['maskF'])
        P.add('pool', lambda e: e.affine_select(out=maskF[0:64, :], in_=maskF[0:64, :], pattern=[[1, 64]], compare_op=ALU.is_ge,
                                                 fill=0.0, base=0, channel_multiplier=-1), r=['maskF'], w=['maskF'])
        P.add('pool', lambda e: e.memset(maskB[0:64, :], 1.0), w=['maskB'])
        P.add('pool', lambda e: e.affine_select(out=maskB[0:64, :], in_=maskB[0:64, :], pattern=[[-1, 64]], compare_op=ALU.is_ge,
                                                 fill=0.0, base=0, channel_multiplier=1), r=['maskB'], w=['maskB'])
        P.add('pool', lambda e: e.memset(onesbd, 0.0), w=['onesbd'])
        P.add('pool', lambda e: e.memset(onesbd[0:64, 0:64], 1.0), r=['onesbd'], w=['onesbd'])
        P.add('pool', lambda e: e.memset(onesbd[64:128, 64:128], 1.0), r=['onesbd'], w=['onesbd'])
        P.add('pool', lambda e: e.memset(esel[0:65, :], 0.0), w=['esel'])
        P.add('pool', lambda e: e.memset(esel[64:65, :], 1.0), r=['esel'], w=['esel'])
        P.add('pool', lambda e: e.memset(cneg, -0.5), w=['cneg'])
        P.add('pool', lambda e: e.memset(eps256, 256.0 * EPS), w=['eps256'])
        P.add('pool', lambda e: e.memset(XT[:, :, 0:1], 0.0), w=['xhalo'])
        with nc.allow_non_contiguous_dma(reason="tiny per-feature vectors"):
            P.add('sp', lambda e: e.dma_start(allow_slow_non_contiguous=True, out=wpre, in_=vec["norm_mix_pre"].ap().rearrange("(k p) -> p k", p=128)), w=['wpre'])
            P.add('sp', lambda e: e.dma_start(allow_slow_non_contiguous=True, out=wfpre, in_=vec["norm_ffn_pre"].ap().rearrange("(k p) -> p k", p=128)), w=['wfpre'])
            P.add('sp', lambda e: e.dma_start(allow_slow_non_contiguous=True, out=cw, in_=conv_w.ap().rearrange("w (f p) -> p w f", p=128)), w=['cw'])
            P.add('sp', lambda e: e.dma_start(allow_slow_non_contiguous=True, out=cb, in_=vec["conv_b"].ap().rearrange("(f p) -> p f", p=128)), w=['cb'])
            P.add('sp', lambda e: e.dma_start(allow_slow_non_contiguous=True, out=onw4[0:64, :], in_=vec["hgrn_out_norm"].ap().rearrange("(p o) -> p o", o=1)), w=['onw4a'])
            P.add('sp', lambda e: e.dma_start(allow_slow_non_contiguous=True, out=onw4[64:128, :], in_=vec["hgrn_out_norm"].ap().rearrange("(p o) -> p o", o=1)), w=['onw4b'])
            P.add('sp', lambda e: e.dma_start(allow_slow_non_contiguous=True, out=lbt[:, 0, :, :], in_=lbf.ap().rearrange("s (c p) -> p s c", p=128)), w=['lbt0'])
            P.add('sp', lambda e: e.dma_start(allow_slow_non_contiguous=True, out=lbt[:, 1, :, :], in_=lbb.ap().rearrange("s (c p) -> p s c", p=128)), w=['lbt1'])
        P.add('dve', lambda e: e.tensor_scalar(out=onw4, in0=onw4, scalar1=4.0, scalar2=None, op0=ALU.mult),
              r=['onw4a', 'onw4b'], w=['onw4'])
        P.add('dve', lambda e: e.tensor_tensor(out=ga, in0=lbt[:, :, 0, :], in1=lbt[:, :, 1, :], op=ALU.subtract),
              r=['lbt0', 'lbt1'], w=['ga'])
        P.add('act', lambda e: e.activation(out=gb, in_=ga, func=AF.Tanh, scale=0.5), r=['ga'], w=['gb'])
        P.add('dve', lambda e: e.tensor_scalar(out=ga, in0=gb, scalar1=0.25, scalar2=0.75, op0=ALU.mult, op1=ALU.add),
              r=['gb'], w=['ga'])
        P.add('dve', lambda e: e.tensor_scalar(out=gna, in0=gb, scalar1=-0.25, scalar2=0.25, op0=ALU.mult, op1=ALU.add),
              r=['gb'], w=['gna'])
        P.add('dve', lambda e: e.tensor_scalar(out=gnb, in0=gb, scalar1=0.25, scalar2=-0.25, op0=ALU.mult, op1=ALU.add),
              r=['gb'], w=['gnb'])
        P.add('dve', lambda e: e.tensor_scalar(out=gb, in0=gb, scalar1=-0.25, scalar2=0.25, op0=ALU.mult, op1=ALU.add),
              r=['gb', 'gna', 'gnb'], w=['gb'])

    def weight_prep():
        base = state['off']
        st = [alloc([4096], F32, "wst%d" % i) for i in range(2)]
        bt = [alloc([4096], BF16, "wbt%d" % i) for i in range(2)]
        sw2 = alloc([1024], BF16, "wsw")
        sw = sw2.rearrange("p (h d) -> p h d", h=16)
        P.add('pool', lambda e: e.memset(sw2, 0.0), w=['wsw'])
        it = [0]

        def cast_rows(src, dst, ncols, sw_dst=None, dst_rearr=None):
            r_ = it[0] % 2
            it[0] += 1
            s_, b_ = st[r_], bt[r_]
            P.add('sp', lambda e: e.dma_start(allow_slow_non_contiguous=True, out=s_[:, 0:ncols], in_=src), w=[('wst', r_)])
            h1 = ncols // 2
            P.add('act', lambda e: e.activation(out=b_[:, 0:h1], in_=s_[:, 0:h1], func=AF.Copy), r=[('wst', r_)], w=[('wbt', r_, 0)])
            P.add('dve', lambda e: e.tensor_copy(out=b_[:, h1:ncols], in_=s_[:, h1:ncols]), r=[('wst', r_)], w=[('wbt', r_, 1)])
            if sw_dst is not None:
                sv = s_[:, 0:1024].rearrange("p (h d) -> p h d", h=16)
                P.add('pool', lambda e: e.tensor_copy(out=sw[:, :, 0:8], in_=sv[:, :, 8:16]), r=[('wst', r_)], w=['wsw'])
                P.add('pool', lambda e: e.tensor_copy(out=sw[:, :, 8:16], in_=sv[:, :, 0:8]), r=[('wst', r_)], w=['wsw'])
                P.add('sp', lambda e: e.dma_start(allow_slow_non_contiguous=True, out=sw_dst, in_=sw2), r=['wsw'], w=['winsw_b'])
            if dst_rearr is None:
                P.add('sp', lambda e: e.dma_start(allow_slow_non_contiguous=True, out=dst, in_=b_[:, 0:ncols]), r=[('wbt', r_, 0), ('wbt', r_, 1)], w=['wscr'])
            else:
                P.add('sp', lambda e: e.dma_start(allow_slow_non_contiguous=True, out=dst, in_=b_[:, 0:ncols].rearrange("p (f j) -> p f j", j=128)),
                      r=[('wbt', r_, 0), ('wbt', r_, 1)], w=['wscr'])

        for kc in range(KC):
            rs = slice(kc * 128, (kc + 1) * 128)
            cast_rows(w_in[rs, :], win_b[rs, :], INW, sw_dst=winsw_b[rs, :])
        for kc in range(KC):
            rs = slice(kc * 128, (kc + 1) * 128)
            cast_rows(w_out[rs, :], wout_b[rs, :], D)
        with nc.allow_non_contiguous_dma(reason="chunked weight scratch, 256B segments, one-time"):
            for kc in range(KC):
                rs = slice(kc * 128, (kc + 1) * 128)
                cast_rows(w_gate[rs, :], wg_b[:, :, kc, :].rearrange("f p j -> p f j"), DFF, dst_rearr=True)
                cast_rows(w_up[rs, :], wu_b[:, :, kc, :].rearrange("f p j -> p f j"), DFF, dst_rearr=True)
        for fc in range(NFC):
            rs = slice(fc * 128, (fc + 1) * 128)
            cast_rows(w_down[rs, :], wd_b[rs, :], D)
        state['off'] = base

    pr = {'i': 0}

    def rstd_from_ssq(ssq, out, n, eps, name):
        P.add('dve', lambda e: e.tensor_scalar(out=out, in0=ssq, scalar1=1.0 / n, scalar2=eps, op0=ALU.mult, op1=ALU.add),
              r=[name + 'ssq'], w=[name + 'rstd'])
        P.add('pool', lambda e: e.tensor_tensor(out=out, in0=out, in1=cneg, op=ALU.pow),
              r=[name + 'rstd', 'cneg'], w=[name + 'rstd'])

    def phase_A0(t_base, S, B):
        base = state['off']
        xt = [alloc([D], F32, "xt%d" % i) for i in range(4)]
        xb = [alloc([D], BF16, "xb%d" % i) for i in range(2)]
        junk = alloc([D], BF16, "junk")
        ss = [alloc([2], F32, "ss%d" % i) for i in range(4)]
        NJ = S // 128

        def a1(j):
            r_ = j % 4
            x_, s_ = xt[r_], ss[r_]
            nm = 'a0_%d' % r_
            P.add('sp', lambda e, j=j, x_=x_: e.dma_start(allow_slow_non_contiguous=True, out=x_, in_=xs[t_base + j * 128: t_base + (j + 1) * 128, :]), w=[('xt', r_)])
            P.add('act', lambda e, x_=x_, s_=s_: e.activation(out=junk, in_=x_, func=AF.Square, accum_out=s_[:, 0:1]),
                  r=[('xt', r_)], w=['junk', nm + 'ssq'])
            rstd_from_ssq(s_[:, 0:1], s_[:, 1:2], D, EPS, nm)

        def a2(j):
            r_ = j % 4
            x_, s_, b_ = xt[r_], ss[r_], xb[j % 2]
            nm = 'a0_%d' % r_
            P.add('act', lambda e, x_=x_, s_=s_, b_=b_: e.activation(out=b_, in_=x_, func=AF.Copy, scale=s_[:, 1:2]),
                  r=[('xt', r_), nm + 'rstd'], w=[('xb', j % 2)])

        def a3(j):
            b_ = xb[j % 2]
            bank = 6 + (j % 2)
            for kc in range(KC):
                P.add('pe', lambda e, kc=kc, b_=b_, bank=bank: e.transpose(out=psb[bank][:, kc * 128:(kc + 1) * 128],
                                                                          in_=b_[:, kc * 128:(kc + 1) * 128], identity=ident),
                      r=[('xb', j % 2), 'ident'], w=pstok(bank))
            P.add('dve', lambda e, j=j, bank=bank: e.tensor_tensor(
                out=XT[:, :, 1 + j * 128: 1 + (j + 1) * 128],
                in0=psb[bank][:, :].rearrange("p (k t) -> p k t", k=KC),
                in1=bcast_free(wpre, 128), op=ALU.mult),
                r=pstok(bank) + ['wpre'], w=[('XT', kc, j) for kc in range(KC)])

        for j in range(NJ + 2):
            if j < NJ:
                a1(j)
            if 0 <= j - 1 < NJ:
                a2(j - 1)
            if 0 <= j - 2 < NJ:
                a3(j - 2)
        state['off'] = base

    def load_wA(cols_main, cols_sw, wA):
        i = 0
        with nc.allow_non_contiguous_dma(reason="weight column chunk, 256B segments"):
            for c in cols_main:
                P.add('sp', lambda e, c=c, i=i: e.dma_start(allow_slow_non_contiguous=True, out=wA[i], in_=win_b[:, c:c + 128].rearrange("(k p) j -> p k j", p=128)),
                      r=['wscr'], w=[('wA', i)])
                i += 1
            for c in cols_sw:
                P.add('sp', lambda e, c=c, i=i: e.dma_start(allow_slow_non_contiguous=True, out=wA[i], in_=winsw_b[:, c:c + 128].rearrange("(k p) j -> p k j", p=128)),
                      r=['winsw_b'], w=[('wA', i)])
                i += 1

    def proj(wA_i, tb, bank):
        for kc in range(KC):
            P.add('pe', lambda e, kc=kc: e.matmul(psf[bank][:, :], lhsT=wA_i[1][:, kc, :], rhs=XT[:, kc, 1 + tb * 512: 1 + (tb + 1) * 512],
                                                   start=(kc == 0), stop=(kc == KC - 1)),
                  r=[('wA', wA_i[0])] + [('XT', kc, tb * 4 + q) for q in range(4)], w=pstok(bank, 0, 2048))

    def phase_attn(S, hp, B):
        base = state['off']
        wA = [alloc([KC, 128], BF16, "wA%d" % i) for i in range(5)]
        rotc = [alloc([512], F32, "rotc%d" % i) for i in range(2)]
        rots = [alloc([512], F32, "rots%d" % i) for i in range(2)]
        qT = alloc([S], BF16, "qT")
        kT = alloc([S], BF16, "kT")
        vT = alloc([S], BF16, "vT")
        NTL = S // 128
        vtok = [alloc([NTL, 2, 65], BF16, "vtok%d" % b) for b in range(len(B))]
        tA = [alloc([512], F32, "tA%d" % i) for i in range(2)]
        tB = [alloc([512], F32, "tB%d" % i) for i in range(2)]
        praw = [alloc([256], BF16, "praw%d" % i) for i in range(4)]
        pmk = [alloc([256], BF16, "pmk%d" % i) for i in range(4)]
        UT = alloc([S], F32, "UT")
        rrow = alloc([512], F32, "rrow")
        aout = [alloc([512], BF16, "aout%d" % i) for i in range(2)]
        load_wA([hp * 128, 512 + hp * 128, 1024 + hp * 128], [hp * 128, 512 + hp * 128], wA)
        for b in range(len(B)):
            P.add('pool', lambda e, b=b: e.memset(vtok[b][:, :, :, 64:65], 1.0), w=[('vones', b)])
        P.add('pool', lambda e: e.memset(rrow[0:65, :], 0.0), w=['rrow'])
        for tb in range(S // 512):
            sl = slice(tb * 512, (tb + 1) * 512)
            rr = tb % 2
            P.add('sp', lambda e, rr=rr, sl=sl: e.dma_start(out=rotc[rr], in_=rotc_d[:, sl]), w=[('rotc', rr)])
            P.add('sp', lambda e, rr=rr, sl=sl: e.dma_start(out=rots[rr], in_=rots_d[:, sl]), w=[('rots', rr)])
            for (dst, wi, swi, nm) in ((qT, 0, 3, 'qT'), (kT, 1, 4, 'kT')):
                b0 = pr['i'] % 4
                b1 = (pr['i'] + 1) % 4
                pr['i'] += 2
                r_ = (pr['i'] // 2) % 2
                proj((wi, wA[wi]), tb, b0)
                proj((swi, wA[swi]), tb, b1)
                P.add('dve', lambda e, b0=b0, r_=r_, rr=rr: e.tensor_tensor(out=tA[r_], in0=psf[b0][:, :], in1=rotc[rr], op=ALU.mult),
                      r=pstok(b0, 0, 2048) + [('rotc', rr)], w=[('tA', r_)])
                P.add('dve', lambda e, b1=b1, r_=r_, rr=rr: e.tensor_tensor(out=tB[r_], in0=psf[b1][:, :], in1=rots[rr], op=ALU.mult),
                      r=pstok(b1, 0, 2048) + [('rots', rr)], w=[('tB', r_)])
                P.add('dve', lambda e, dst=dst, r_=r_, sl=sl: e.tensor_tensor(out=dst[:, sl], in0=tA[r_], in1=tB[r_], op=ALU.add),
                      r=[('tA', r_), ('tB', r_)], w=[(nm, tb)])
            b0 = pr['i'] % 4
            pr['i'] += 1
            proj((2, wA[2]), tb, b0)
            P.add('act', lambda e, b0=b0, sl=sl: e.activation(out=vT[:, sl], in_=psf[b0][:, :], func=AF.Copy),
                  r=pstok(b0, 0, 2048), w=[('vT', tb)])
        for b, d in enumerate(B):
            L = S // d
            for r in range(d):
                for i0 in range(0, L // 128, 4):
                    n4 = min(4, L // 128 - i0)
                    bank = 6 + (pr['i'] % 2)
                    pr['i'] += 1
                    for ii in range(n4):
                        i = i0 + ii
                        lo = r + d * 128 * i
                        P.add('pe', lambda e, ii=ii, lo=lo, bank=bank, d=d: e.transpose(
                            out=psb[bank][:, ii * 128:(ii + 1) * 128], in_=vT[:, sst(lo, 128, d)], identity=ident),
                            r=[('vT', t) for t in range(lo // 512, (lo + 128 * d - d) // 512 + 1)] + ['ident'],
                            w=pstok(bank, ii * 256, ii * 256 + 256))
                    t0 = r * (L // 128) + i0
                    P.add('dve', lambda e, b=b, t0=t0, n4=n4, bank=bank: e.tensor_copy(
                        out=vtok[b][:, t0:t0 + n4, :, 0:64],
                        in_=psb[bank][:, 0:n4 * 128].rearrange("p (a h c) -> p a h c", a=n4, h=2)),
                        r=pstok(bank, 0, n4 * 256), w=[('vtok', b, t0 + q) for q in range(n4)])
        for h in range(2):
            hb = h * 64
            items = []
            for b, d in enumerate(B):
                L = S // d
                NCH = L // 128
                for r in range(d):
                    for i in range(NCH):
                        items.append((b, d, L, NCH, r, i))

            def stA(k, hb=hb):
                b, d, L, NCH, r, i = items[k]
                qlo = max(0, 128 * i - 64)
                qhi = min(L, 128 * i + 192)
                nq = qhi - qlo
                off = qlo - (128 * i - 64)
                slot = k % 4
                bank = (0, 1, 4, 5)[slot]
                klo = r + d * 128 * i
                qpl = r + d * qlo
                ktoks = [('kT', t) for t in range(klo // 512, (klo + 127 * d) // 512 + 1)]
                qtoks = [('qT', t) for t in range(qpl // 512, (qpl + (nq - 1) * d) // 512 + 1)]
                P.add('pe', lambda e: e.matmul(
                    psf[bank][:, 0:nq], lhsT=kT[hb:hb + 64, sst(klo, 128, d)],
                    rhs=qT[hb:hb + 64, sst(qpl, nq, d)], start=True, stop=False),
                    r=ktoks + qtoks, w=pstok(bank))
                P.add('pe', lambda e: e.matmul(
                    psf[bank][:, 0:nq], lhsT=ident, rhs=maskMB[:, off:off + nq], start=False, stop=True),
                    r=['ident', 'maskMB'], w=pstok(bank))
                P.add('act', lambda e: e.activation(
                    out=pmk[slot][:, 0:nq], in_=psf[bank][:, 0:nq], func=AF.Exp, scale=0.125),
                    r=pstok(bank), w=[('pmk', slot)])

            def stB(k, h=h):
                b, d, L, NCH, r, i = items[k]
                qlo = max(0, 128 * i - 64)
                slot = k % 4
                vt = vtok[b][:, r * NCH + i, h, :]
                for n in (i, i + 1):
                    jlo = max(0, 128 * n - 64)
                    jhi = min(L, 128 * n + 64)
                    nb = jhi - jlo
                    c0 = jlo - qlo
                    obk = (6, 7, 2, 3)[n % 4]
                    first = (n == i + 1) or (i == 0)
                    last = (n == i) or (i == NCH - 1)
                    P.add('pe', lambda e, nb=nb, c0=c0, first=first, last=last, obk=obk: e.matmul(
                        psf[obk][0:65, 0:nb], lhsT=vt, rhs=pmk[slot][:, c0:c0 + nb],
                        start=first, stop=last),
                        r=[('pmk', slot), ('vtok', b, r * NCH + i), ('vones', b)], w=pstok(obk))
                    if last:
                        plo = r + d * jlo
                        is_end = (k == len(items) - 1 or items[k + 1][0] != b) and n == i + 1 or \
                                 ((k == len(items) - 1 or items[k + 1][0] != b) and i == NCH - 1 and n == i and NCH - 1 == i and False)
                        wtok = [('UTop', h, b, k, n)]
                        if (k == len(items) - 1 or items[k + 1][0] != b) and n == i + 1:
                            wtok.append(('UTend', b))
                        if b == 0:
                            P.add('act', lambda e, nb=nb, plo=plo, obk=obk: e.activation(
                                out=UT[0:65, sst(plo, nb, d)], in_=psf[obk][0:65, 0:nb], func=AF.Copy),
                                r=pstok(obk) + ['UTnorm'], w=wtok)
                        else:
                            P.add('dve', lambda e, nb=nb, plo=plo, obk=obk: e.tensor_tensor(
                                out=UT[0:65, sst(plo, nb, d)], in0=psf[obk][0:65, 0:nb],
                                in1=UT[0:65, sst(plo, nb, d)], op=ALU.add),
                                r=pstok(obk) + [('UTend', b - 1)], w=wtok)

            LA = 2
            for k in range(min(LA, len(items))):
                stA(k)
            for k in range(len(items)):
                if k + LA < len(items):
                    stA(k + LA)
                stB(k)
            for tb in range(S // 512):
                sl = slice(tb * 512, (tb + 1) * 512)
                bank = pr['i'] % 4
                pr['i'] += 1
                P.add('dve', lambda e, sl=sl: e.reciprocal(out=rrow[64:65, :], in_=UT[64:65, sl]), r=[('UTend', len(B) - 1)], w=['rrow'])
                P.add('pe', lambda e, bank=bank: e.matmul(psf[bank][0:64, :], lhsT=esel[0:65, :], rhs=rrow[0:65, :], start=True, stop=True),
                      r=['rrow', 'esel'], w=pstok(bank, 0, 2048))
                ar_ = tb % 2
                P.add('dve', lambda e, sl=sl, bank=bank, ar_=ar_: e.tensor_tensor(out=aout[ar_][0:64, :], in0=psf[bank][0:64, :], in1=UT[0:64, sl], op=ALU.mult),
                      r=pstok(bank, 0, 2048) + [('UTend', len(B) - 1)], w=[('aout', ar_)] + (['UTnorm'] if tb == S // 512 - 1 else []))
                P.add('sp', lambda e, hb=hb, sl=sl, ar_=ar_: e.dma_start(allow_slow_non_contiguous=True, out=mix_s[hp, hb:hb + 64, sl], in_=aout[ar_][0:64, :]),
                      r=[('aout', ar_)], w=[('mix_s', hp, h, tb)])
        state['off'] = base

    def phase_hgrn(S, hp):
        base = state['off']
        NCk = S // CH
        wA = [alloc([KC, 128], BF16, "wA%d" % i) for i in range(5)]
        qf = [alloc([S], BF16, "qf%d" % dr) for dr in range(2)]
        kf = [alloc([S], BF16, "kf%d" % dr) for dr in range(2)]
        vtk = alloc([NCk, 128], BF16, "vtk")
        gate = alloc([S], BF16, "gate")
        dS = [alloc([NCk, 64], F32, "dS%d" % dr) for dr in range(2)]
        Dl = [alloc([NCk], F32, "Dl%d" % dr) for dr in range(2)]
        p1base = state['off']
        th = [alloc([512], F32, "th%d" % i) for i in range(2)]
        thd = [alloc([512], F32, "thd%d" % i) for i in range(2)]
        q2 = alloc([512], F32, "q2")
        Fms = [alloc([512], F32, "Fm%d" % i) for i in range(2)]
        D1s = [alloc([512], F32, "D1_%d" % i) for i in range(2)]
        Kks = [alloc([512], F32, "Kk%d" % i) for i in range(2)]
        Ics = [alloc([512], F32, "Ic%d" % i) for i in range(2)]
        RIs = [alloc([512], F32, "RI%d" % i) for i in range(2)]
        vTb = alloc([512], BF16, "vTb")
        ktk = [alloc([128], BF16, "ktk%d" % i) for i in range(4)]
        c0 = 1536 + hp * 128
        load_wA([c0, c0 + 512, c0 + 1024, c0 + 1536, c0 + 2048], [], wA)
        for i in range(2):
            P.add('pool', lambda e, i=i: e.memset(D1s[i], 0.0), w=[('D1', i)])
        pending = []
        for tb in range(S // 512):
            pending_new = []
            sl = slice(tb * 512, (tb + 1) * 512)
            b0 = pr['i'] % 4
            pr['i'] += 1
            proj((0, wA[0]), tb, b0)
            P.add('act', lambda e, b0=b0: e.activation(out=th[0], in_=psf[b0][:, :], func=AF.Tanh, scale=0.5),
                  r=pstok(b0, 0, 2048), w=[('th', 0)])
            P.add('dve', lambda e, b0=b0: e.scalar_tensor_tensor(out=q2, in0=th[0], scalar=1.0, in1=psf[b0][:, :], op0=ALU.add, op1=ALU.mult),
                  r=pstok(b0, 0, 2048) + [('th', 0)], w=['q2'])
            b0 = pr['i'] % 4
            pr['i'] += 1
            proj((3, wA[3]), tb, b0)
            P.add('act', lambda e, b0=b0: e.activation(out=vTb, in_=psf[b0][:, :], func=AF.Copy), r=pstok(b0, 0, 2048), w=['vTb'])
            for half in range(2):
                bank = 6 + (pr['i'] % 2)
                pr['i'] += 1
                for cc in range(4):
                    c = half * 4 + cc
                    P.add('pe', lambda e, c=c, cc=cc, bank=bank: e.transpose(out=psb[bank][0:64, cc * 128:(cc + 1) * 128],
                                                                               in_=vTb[:, c * 64:(c + 1) * 64], identity=ident),
                          r=['vTb', 'ident'], w=pstok(bank, cc * 256, cc * 256 + 256))
                cg = tb * 8 + half * 4
                P.add('dve', lambda e, cg=cg, bank=bank: e.tensor_copy(out=vtk[0:64, cg:cg + 4, :],
                                                                      in_=psb[bank][0:64, 0:512].rearrange("p (a c) -> p a c", a=4)),
                      r=pstok(bank, 0, 1024), w=[('vtk', cg + q) for q in range(4)])
            b0 = pr['i'] % 4
            pr['i'] += 1
            proj((4, wA[4]), tb, b0)
            P.add('act', lambda e, b0=b0: e.activation(out=th[1], in_=psf[b0][:, :], func=AF.Tanh, scale=0.5),
                  r=pstok(b0, 0, 2048), w=[('th', 1)])
            P.add('dve', lambda e, b0=b0, sl=sl: e.scalar_tensor_tensor(out=gate[:, sl], in0=th[1], scalar=1.0, in1=psf[b0][:, :],
                                                                        op0=ALU.add, op1=ALU.mult),
                  r=pstok(b0, 0, 2048) + [('th', 1)], w=[('gate', tb)])
            for dr in range(2):
                b0 = pr['i'] % 4
                pr['i'] += 1
                proj((1 + dr, wA[1 + dr]), tb, b0)
                Fm, Kk, Ic, RI, tdr = Fms[dr], Kks[dr], Ics[dr], RIs[dr], thd[dr]
                P.add('act', lambda e, b0=b0, tdr=tdr: e.activation(out=tdr, in_=psf[b0][:, :], func=AF.Tanh, scale=0.5),
                      r=pstok(b0, 0, 2048), w=[('thd', dr)])
                P.add('dve', lambda e, dr=dr, Fm=Fm, tdr=tdr: e.tensor_scalar(out=Fm, in0=tdr, scalar1=gb[:, dr, hp:hp + 1], scalar2=ga[:, dr, hp:hp + 1],
                                                             op0=ALU.mult, op1=ALU.add), r=[('thd', dr), 'ga', 'gb'], w=[('Fm', dr)])
                P.add('act', lambda e, dr=dr, Kk=Kk, tdr=tdr: e.activation(out=Kk, in_=tdr, func=AF.Identity, scale=gnb[:, dr, hp:hp + 1],
                                                           bias=gb[:, dr, hp:hp + 1]), r=[('thd', dr), 'gnb', 'gb'], w=[('Kk', dr)])
                D1 = D1s[dr]
                if dr == 0:
                    edge = slice(0, 512, 64)
                    Fv, D1v, Iv = Fm, D1, Ic
                else:
                    edge = slice(63, 512, 64)
                    Fv, D1v, Iv = Fm[:, ::-1], D1[:, ::-1], Ic[:, ::-1]
                P.add('dve', lambda e, edge=edge, D1=D1, Fm=Fm: e.tensor_copy(out=D1[:, edge], in_=Fm[:, edge]), r=[('Fm', dr)], w=[('D1', dr)])
                P.add('dve', lambda e, edge=edge, Fm=Fm: e.memset(Fm[:, edge], 0.0), r=[('D1', dr), ('Fm', dr)], w=[('Fm', dr)])
                P.add('dve', lambda e, Fv=Fv, D1v=D1v, Iv=Iv: e.tensor_tensor_scan(out=Iv, data0=Fv, data1=D1v, initial=0.0,
                                                                                   op0=ALU.mult, op1=ALU.add),
                      r=[('Fm', dr), ('D1', dr)], w=[('Ic', dr)])
                P.add('dve', lambda e, RI=RI, Ic=Ic: e.reciprocal(out=RI, in_=Ic), r=[('Ic', dr)], w=[('RI', dr)])
                P.add('dve', lambda e, dr=dr, sl=sl, Ic=Ic: e.tensor_tensor(out=qf[dr][:, sl], in0=q2, in1=Ic, op=ALU.mult),
                      r=['q2', ('Ic', dr)], w=[('qf', dr, tb)])
                P.add('dve', lambda e, dr=dr, sl=sl, Kk=Kk, RI=RI: e.tensor_tensor(out=kf[dr][:, sl], in0=Kk, in1=RI, op=ALU.mult),
                      r=[('Kk', dr), ('RI', dr)], w=[('kf', dr, tb)])
                ecol = slice(63, 512, 64) if dr == 0 else slice(0, 512, 64)
                P.add('pool', lambda e, dr=dr, ecol=ecol, tb=tb, Ic=Ic: e.tensor_copy(out=Dl[dr][:, tb * 8:(tb + 1) * 8], in_=Ic[:, ecol]),
                      r=[('Ic', dr)], w=[('Dl', dr, tb)])
                if os.environ.get('HSTOP') == 'p1a':
                    continue
                def dsA(c8, dr=dr, tb=tb):
                    c = tb * 8 + c8
                    kr = c8 % 4
                    bank = 6 + (c8 % 2)
                    P.add('pe', lambda e: e.transpose(out=psb[bank][0:64, 0:128], in_=kf[dr][:, c * 64:(c + 1) * 64], identity=ident),
                          r=[('kf', dr, tb), 'ident'], w=pstok(bank))
                    P.add('act', lambda e: e.activation(out=ktk[kr][0:64, :], in_=psb[bank][0:64, 0:128], func=AF.Copy),
                          r=pstok(bank), w=[('ktk', kr)])

                def dsB(c8, dr=dr, tb=tb):
                    c = tb * 8 + c8
                    kr = c8 % 4
                    bank = 4 + (c8 % 2)
                    P.add('pe', lambda e: e.matmul(psf[bank][:, 0:128], lhsT=ktk[kr][0:64, :], rhs=vtk[0:64, c, :], start=True, stop=True),
                          r=[('ktk', kr), ('vtk', c)], w=pstok(bank))
                    for hh in range(2):
                        ps_ = slice(hh * 64, hh * 64 + 64)
                        if hh == 0:
                            P.add('dve', lambda e, ps_=ps_, hh=hh: e.tensor_scalar(
                                out=dS[dr][ps_, c, :], in0=psf[bank][ps_, hh * 64: 64 + hh * 64],
                                scalar1=Dl[dr][ps_, c:c + 1], scalar2=None, op0=ALU.mult),
                                r=pstok(bank) + [('Dl', dr, tb)], w=[('dS', dr, c, hh)])
                        else:
                            P.add('act', lambda e, ps_=ps_, hh=hh: e.activation(
                                out=dS[dr][ps_, c, :], in_=psf[bank][ps_, hh * 64: 64 + hh * 64], func=AF.Copy,
                                scale=Dl[dr][ps_, c:c + 1]),
                                r=pstok(bank) + [('Dl', dr, tb)], w=[('dS', dr, c, hh)])
                    if dr == 0 and c > 0:
                        P.add('dve', lambda e: e.scalar_tensor_tensor(
                            out=dS[0][:, c, :], in0=dS[0][:, c - 1, :], scalar=Dl[0][:, c:c + 1], in1=dS[0][:, c, :],
                            op0=ALU.mult, op1=ALU.add),
                            r=[('dS', 0, c - 1, 0), ('dS', 0, c - 1, 1), ('dS', 0, c, 0), ('dS', 0, c, 1), ('Dl', 0, tb)],
                            w=[('dS', 0, c, 0), ('dS', 0, c, 1)])

                def run_ds(dsA=dsA, dsB=dsB):
                    dsA(0)
                    for c8 in range(8):
                        if c8 + 1 < 8:
                            dsA(c8 + 1)
                        dsB(c8)
                pending_new.append(run_ds)
            for f_ in pending:
                f_()
            pending = pending_new
        for f_ in pending:
            f_()
        if os.environ.get('HSTOP') in ('p1a', 'p1'):
            state['off'] = base
            return
        for n_ in range(1, NCk):
            for dr in (1,):
                c = n_ if dr == 0 else NCk - 1 - n_
                pc = c - 1 if dr == 0 else c + 1
                P.add('dve', lambda e, dr=dr, c=c, pc=pc: e.scalar_tensor_tensor(
                    out=dS[dr][:, c, :], in0=dS[dr][:, pc, :], scalar=Dl[dr][:, c:c + 1], in1=dS[dr][:, c, :],
                    op0=ALU.mult, op1=ALU.add),
                    r=[('dS', dr, pc, 0), ('dS', dr, pc, 1), ('dS', dr, c, 0), ('dS', dr, c, 1), ('Dl', dr, c // 8)],
                    w=[('dS', dr, c, 0), ('dS', dr, c, 1)])
        if os.environ.get('HSTOP') == 'chain':
            state['off'] = base
            return
        P.barrier()
        state['off'] = p1base
        att = [alloc([2, 64], BF16, "att%d" % i) for i in range(2)]
        Sbd4 = [alloc([8, 128], BF16, "Sbd%d" % i) for i in range(4)]
        osum = alloc([512], F32, "osum")
        osq = alloc([512], F32, "osq")
        rs8 = alloc([512], F32, "rs8")
        houts = [alloc([512], BF16, "hout%d" % i) for i in range(2)]
        for i in range(4):
            P.add('pool', lambda e, i=i: e.memset(Sbd4[i], 0.0), w=[('Sbd', i % 2, i // 2)])
        for tb in range(S // 512):
            sl = slice(tb * 512, (tb + 1) * 512)
            obA = 4 + 2 * (tb % 2)
            obB = 5 + 2 * (tb % 2)
            Sbd = [Sbd4[0 + 2 * (tb % 2)], Sbd4[1 + 2 * (tb % 2)]]
            sbp = tb % 2
            for dr in range(2):
                for hh in range(2):
                    ps_ = slice(hh * 64, hh * 64 + 64)
                    cs = [tb * 8 + c8 + (-1 if dr == 0 else 1) for c8 in range(8)]
                    valid = [c8 for c8 in range(8) if 0 <= cs[c8] < NCk]
                    lo, hi = valid[0], valid[-1] + 1
                    P.add('pool', lambda e, dr=dr, ps_=ps_, lo=lo, hi=hi, cs=cs, hh=hh, Sbd=Sbd: e.tensor_copy(
                        out=Sbd[dr][ps_, lo:hi, hh * 64:hh * 64 + 64], in_=dS[dr][ps_, cs[lo]:cs[hi - 1] + 1, :]),
                        r=[('dS', dr, cs[c8], hh) for c8 in valid], w=[('Sbd', dr, sbp)])
            if os.environ.get('HSTOP') == 'p2a1':
                continue
            items2 = [(c8, dr) for c8 in range(8) for dr in range(2)]

            def p2A(k, tb=tb):
                c8, dr = items2[k]
                c = tb * 8 + c8
                cs_ = slice(c * 64, (c + 1) * 64)
                ar = k % 2
                mk = maskF if dr == 0 else maskB
                for hh in range(2):
                    ps_ = slice(hh * 64, hh * 64 + 64)
                    abank = ar * 2 + hh
                    P.add('pe', lambda e, ps_=ps_, abank=abank: e.matmul(
                        psf[abank][0:64, 0:64], lhsT=kf[dr][ps_, cs_], rhs=qf[dr][ps_, cs_], start=True, stop=True),
                        r=[('kf', dr, tb), ('qf', dr, tb)], w=pstok(abank))
                    P.add('dve', lambda e, abank=abank, hh=hh: e.tensor_tensor(
                        out=att[ar][0:64, hh, :], in0=psf[abank][0:64, 0:64], in1=mk[0:64, :], op=ALU.mult),
                        r=pstok(abank) + ['maskF', 'maskB'], w=[('att', ar, hh)])

            def p2B(k, tb=tb, obA=obA, obB=obB, Sbd=Sbd, sbp=sbp):
                c8, dr = items2[k]
                c = tb * 8 + c8
                cs_ = slice(c * 64, (c + 1) * 64)
                ar = k % 2
                first = (dr == 0)
                skip_inter = (dr == 0 and c == 0) or (dr == 1 and c == NCk - 1)
                for hh, ob in ((0, obA), (1, obB)):
                    P.add('pe', lambda e, hh=hh, ob=ob: e.matmul(
                        psf[ob][:, c8 * 64:(c8 + 1) * 64], lhsT=vtk[0:64, c, :], rhs=att[ar][0:64, hh, :],
                        start=first, stop=(dr == 1 and skip_inter)),
                        r=[('att', ar, hh), ('vtk', c)], w=pstok(ob))
                    if not skip_inter:
                        P.add('pe', lambda e, ob=ob: e.matmul(
                            psf[ob][:, c8 * 64:(c8 + 1) * 64], lhsT=Sbd[dr][:, c8, :], rhs=qf[dr][:, cs_],
                            start=False, stop=(dr == 1)),
                            r=[('Sbd', dr, sbp), ('qf', dr, tb)], w=pstok(ob))

            p2A(0)
            for k in range(16):
                if k + 1 < 16:
                    p2A(k + 1)
                p2B(k)
            if os.environ.get('HSTOP') in ('p2a', 'p2b', 'p2a1', 'p2a2'):
                continue
            P.add('act', lambda e, obA=obA: e.activation(out=osum[0:64, :], in_=psf[obA][0:64, :], func=AF.Copy), r=pstok(obA, 0, 2048), w=['osumA'])
            P.add('act', lambda e, obB=obB: e.activation(out=osum[64:128, :], in_=psf[obB][64:128, :], func=AF.Copy), r=pstok(obB, 0, 2048), w=['osumB'])
            P.add('act', lambda e: e.activation(out=osq, in_=osum, func=AF.Square), r=['osumA', 'osumB'], w=['osq'])
            nb_ = pr['i'] % 4
            pr['i'] += 1
            P.add('pe', lambda e, nb_=nb_: e.matmul(psf[nb_][:, :], lhsT=onesbd, rhs=osq, start=True, stop=True),
                  r=['osq', 'onesbd'], w=pstok(nb_, 0, 2048))
            if os.environ.get('HSTOP') == 'p2c':
                continue
            P.add('act', lambda e, nb_=nb_: e.activation(out=rs8, in_=psf[nb_][:, :], func=AF.Ln, bias=eps256[:, 0:1]),
                  r=pstok(nb_, 0, 2048) + ['eps256'], w=['rs8a'])
            P.add('act', lambda e: e.activation(out=rs8, in_=rs8, func=AF.Exp, scale=-0.5), r=['rs8a'], w=['rs8'])
            P.add('dve', lambda e: e.tensor_tensor(out=osum, in0=osum, in1=rs8, op=ALU.mult), r=['osumA', 'osumB', 'rs8'], w=['osn'])
            hr = tb % 2
            P.add('dve', lambda e, sl=sl, hr=hr: e.scalar_tensor_tensor(out=houts[hr], in0=osum, scalar=onw4[:, 0:1], in1=gate[:, sl],
                                                                op0=ALU.mult, op1=ALU.mult),
                  r=['osn', 'onw4', ('gate', tb)], w=[('hout', hr)])
            P.add('sp', lambda e, sl=sl, hr=hr: e.dma_start(allow_slow_non_contiguous=True, out=mix_s[4 + hp, :, sl], in_=houts[hr]),
                  r=[('hout', hr)], w=[('mix_s', 4 + hp, tb)])
        state['off'] = base

    def phase_B1(t_base, S):
        base = state['off']
        wo = alloc([KC, D], BF16, "wo")
        wpost = alloc([D], F32, "wpost")
        P.add('sp', lambda e: e.dma_start(allow_slow_non_contiguous=True, out=wpost, in_=bass.AP(vec["norm_mix_post"], 0, [[0, 128], [1, D]])), w=['wpost'])
        mt = [alloc([KC, 512], BF16, "mt%d" % i) for i in range(2)]
        xt = [alloc([D], F32, "xt%d" % i) for i in range(4)]
        ht = [alloc([D], F32, "ht%d" % i) for i in range(4)]
        tm = [alloc([D], F32, "tm%d" % i) for i in range(4)]
        hb_ = [alloc([D], BF16, "hb%d" % i) for i in range(2)]
        junk = alloc([D], BF16, "junk")
        ss = [alloc([4], F32, "ss%d" % i) for i in range(4)]
        P.add('sp', lambda e: e.dma_start(allow_slow_non_contiguous=True, out=wo, in_=wout_b.rearrange("(k p) c -> p k c", p=128)), r=['wscr'], w=['wo'])
        P.add('pool', lambda e: e.memset(XT[:, :, S + 1:S + 2], 0.0), w=['xhalo2'])
        NJ = S // 128

        ld = {'x': 0, 'm': 0}

        def b1_loads(j_upto, g_upto):
            while ld['m'] < min(g_upto, S // 512):
                g_ = ld['m']
                P.add('sp', lambda e, g_=g_: e.dma_start(allow_slow_non_contiguous=True, out=mt[g_ % 2], in_=mix_s[:, :, g_ * 512:(g_ + 1) * 512].rearrange("k p t -> p k t")),
                      r=[], w=[('mt', g_ % 2)])
                ld['m'] += 1
            while ld['x'] < min(j_upto, NJ):
                j_ = ld['x']
                P.add('sp', lambda e, j_=j_: e.dma_start(allow_slow_non_contiguous=True, out=xt[j_ % 4], in_=xs[t_base + j_ * 128: t_base + (j_ + 1) * 128, :]),
                      w=[('xt', j_ % 4)])
                ld['x'] += 1

        def b1a(j):
            g, tt = j // 4, j % 4
            m_ = mt[g % 2]
            b1_loads(j + 3, g + 2 if tt >= 2 else g + 1)
            r_ = j % 4
            x_, h_, t_, s_ = xt[r_], ht[r_], tm[r_], ss[r_]
            nm = 'b1_%d' % r_
            for half in range(2):
                bank = (j % 2) * 2 + half
                hs = slice(half * 512, (half + 1) * 512)
                for kc in range(KC):
                    P.add('pe', lambda e, kc=kc, hs=hs, bank=bank: e.matmul(
                        psf[bank][:, :], lhsT=m_[:, kc, tt * 128:(tt + 1) * 128], rhs=wo[:, kc, hs], start=(kc == 0), stop=(kc == KC - 1)),
                        r=[('mt', g % 2), 'wo'], w=pstok(bank))
                P.add('act', lambda e, bank=bank, half=half: e.activation(out=junk[:, 0:512], in_=psf[bank][:, :], func=AF.Square,
                                                                           accum_out=s_[:, half:half + 1]),
                      r=pstok(bank), w=['junk', (nm, 'p', half)])
                P.add('act', lambda e, bank=bank, hs=hs: e.activation(out=t_[:, hs], in_=psf[bank][:, :], func=AF.Copy),
                      r=pstok(bank), w=[('tm', r_, half)])
            P.add('dve', lambda e: e.tensor_tensor(out=s_[:, 2:3], in0=s_[:, 0:1], in1=s_[:, 1:2], op=ALU.add),
                  r=[(nm, 'p', 0), (nm, 'p', 1)], w=[nm + 'ssq'])
            rstd_from_ssq(s_[:, 2:3], s_[:, 3:4], D, EPS, nm)
            P.add('dve', lambda e: e.scalar_tensor_tensor(out=t_, in0=t_, scalar=s_[:, 3:4], in1=wpost, op0=ALU.mult, op1=ALU.mult),
                  r=[('tm', r_, 0), ('tm', r_, 1), nm + 'rstd', 'wpost'], w=[('tm', r_, 0), ('tm', r_, 1)])
            P.add('dve', lambda e: e.tensor_tensor(out=h_, in0=t_, in1=x_, op=ALU.add),
                  r=[('tm', r_, 0), ('tm', r_, 1), ('xt', r_)], w=[('ht', r_)])
            P.add('sp', lambda e: e.dma_start(allow_slow_non_contiguous=True, out=ys[t_base + j * 128: t_base + (j + 1) * 128, :], in_=h_),
                  r=[('ht', r_)], w=[('ys', j)])
            nm2 = 'b1n_%d' % r_
            P.add('act', lambda e: e.activation(out=junk, in_=h_, func=AF.Square, accum_out=s_[:, 0:1]),
                  r=[('ht', r_)], w=['junk', nm2 + 'ssq'])
            rstd_from_ssq(s_[:, 0:1], s_[:, 1:2], D, EPS, nm2)

        def b1b(j):
            r_ = j % 4
            h_, s_, b_ = ht[r_], ss[r_], hb_[j % 2]
            nm2 = 'b1n_%d' % r_
            P.add('act', lambda e: e.activation(out=b_, in_=h_, func=AF.Copy, scale=s_[:, 1:2]),
                  r=[('ht', r_), nm2 + 'rstd'], w=[('hb', j % 2)])

        def b1c(j):
            b_ = hb_[j % 2]
            bank = 6 + (j % 2)
            for kc in range(KC):
                P.add('pe', lambda e, kc=kc: e.transpose(out=psb[bank][:, kc * 128:(kc + 1) * 128],
                                                       in_=b_[:, kc * 128:(kc + 1) * 128], identity=ident),
                      r=[('hb', j % 2), 'ident'], w=pstok(bank))
            P.add('dve', lambda e: e.tensor_tensor(
                out=XT[:, :, 1 + j * 128: 1 + (j + 1) * 128], in0=psb[bank][:, :].rearrange("p (k t) -> p k t", k=KC),
                in1=bcast_free(wfpre, 128), op=ALU.mult),
                r=pstok(bank) + ['wfpre'], w=[('XT', kc, j) for kc in range(KC)])

        for j in range(NJ + 2):
            if j < NJ:
                b1a(j)
            if 0 <= j - 1 < NJ:
                b1b(j - 1)
            if 0 <= j - 2 < NJ:
                b1c(j - 2)
        state['off'] = base

    def phase_B2(t_base, S):
        base = state['off']
        wfpost = alloc([D], F32, "wfpost")
        P.add('sp', lambda e: e.dma_start(allow_slow_non_contiguous=True, out=wfpost, in_=bass.AP(vec["norm_ffn_post"], 0, [[0, 128], [1, D]])), w=['wfpost'])
        wg = [alloc([KC, 128], BF16, "wg%d" % i) for i in range(6)]
        wu = [alloc([KC, 128], BF16, "wu%d" % i) for i in range(6)]
        wd = [alloc([512], BF16, "wd%d" % i) for i in range(16)]
        hid = alloc([NFC, 512], BF16, "hid")
        Asb = [alloc([514], F32, "Asb%d" % i) for i in range(3)]
        cc = [alloc([512], F32, "cc%d" % i) for i in range(3)]
        c2 = [alloc([512], F32, "c2%d" % i) for i in range(3)]
        c3 = [alloc([512], F32, "c3%d" % i) for i in range(3)]
        fsb = alloc([4, D], F32, "fsb")
        ht = [alloc([D], F32, "ht%d" % i) for i in range(4)]
        junk = alloc([512], BF16, "junk")
        ss = alloc([4, 4], F32, "ssB")
        wi = {'g': 0, 'd': 0}
        NB = S // 512
        pf = {'g': 0, 'd': 0}

        def prefetch(g_upto, d_upto):
            while pf['g'] < min(g_upto, NB * NFC):
                g = pf['g']
                fc_, r3_ = g % NFC, g % 6
                P.add('sp', lambda e, fc_=fc_, r3_=r3_: e.dma_start(allow_slow_non_contiguous=True, out=wg[r3_], in_=wg_b[fc_]), r=['wscr'], w=[('wg', r3_)])
                P.add('sp', lambda e, fc_=fc_, r3_=r3_: e.dma_start(allow_slow_non_contiguous=True, out=wu[r3_], in_=wu_b[fc_]), r=['wscr'], w=[('wu', r3_)])
                pf['g'] += 1
            while pf['d'] < min(d_upto, NB * 2 * NFC):
                dd = pf['d']
                fc_, half_, r4_ = dd % NFC, (dd // NFC) % 2, dd % 16
                P.add('sp', lambda e, fc_=fc_, half_=half_, r4_=r4_: e.dma_start(
                    allow_slow_non_contiguous=True, out=wd[r4_], in_=wd_b[fc_ * 128:(fc_ + 1) * 128, half_ * 512:(half_ + 1) * 512]),
                    r=['wscr'], w=[('wd', r4_)])
                pf['d'] += 1

        for blk in range(NB):
            t0 = blk * 512
            xtoks = lambda kc: [('XT', kc, blk * 4 + q) for q in range(4)]
            for fc in range(NFC):
                r3 = wi['g'] % 6
                wi['g'] += 1
                r2 = fc % 3
                prefetch(wi['g'] + 4, wi['d'] + (10 if fc >= NFC - 6 else 0))
                gb_ = (0, 1)[fc % 2]
                ub_ = (2, 3, 6, 7)[fc % 4]
                hq_ = 0
                hbk = (4, 5)[fc % 2]
                for kc in range(KC):
                    P.add('pe', lambda e, kc=kc, r3=r3, gb_=gb_, t0=t0: e.matmul(psf[gb_][:, :], lhsT=wg[r3][:, kc, :],
                                                                         rhs=XT[:, kc, 1 + t0: 1 + t0 + 512], start=(kc == 0), stop=(kc == KC - 1)),
                          r=[('wg', r3)] + xtoks(kc), w=pstok(gb_, 0, 2048))
                halo_r = ['xhalo', 'xhalo2'] + [('XT', kc, q) for kc in range(KC) for q in (max(blk * 4 - 1, 0), min(blk * 4 + 4, S // 128 - 1))]
                for kc in range(KC):
                    P.add('pe', lambda e, kc=kc, r3=r3, hq_=hq_, t0=t0, hbk=hbk: e.matmul(psf[hbk][:, hq_ * 128: hq_ * 128 + 2], lhsT=wg[r3][:, kc, :],
                                                                         rhs=XT[:, kc, t0: t0 + 514: 513], start=(kc == 0), stop=(kc == KC - 1)),
                          r=[('wg', r3)] + halo_r, w=pstok(hbk))
                for kc in range(KC):
                    P.add('pe', lambda e, kc=kc, r3=r3, ub_=ub_, t0=t0: e.matmul(psf[ub_][:, :], lhsT=wu[r3][:, kc, :],
                                                                         rhs=XT[:, kc, 1 + t0: 1 + t0 + 512], start=(kc == 0), stop=(kc == KC - 1)),
                          r=[('wu', r3)] + xtoks(kc), w=pstok(ub_, 0, 2048))
                A_, c_, c2_, c3_ = Asb[r2], cc[r2], c2[r2], c3[r2]
                P.add('act', lambda e, A_=A_, gb_=gb_: e.activation(out=A_[:, 1:513], in_=psf[gb_][:, :], func=AF.Copy),
                      r=pstok(gb_, 0, 2048), w=[('Asb', r2, 0)])
                P.add('act', lambda e, A_=A_, hq_=hq_, hbk=hbk: e.activation(out=A_[:, 0:514:513], in_=psf[hbk][:, hq_ * 128: hq_ * 128 + 2], func=AF.Copy),
                      r=pstok(hbk), w=[('Asb', r2, 1)])
                P.add('act', lambda e, A_=A_, c_=c_, fc=fc: e.activation(out=c_, in_=A_[:, 1:513], func=AF.Identity, scale=cw[:, 1, fc:fc + 1],
                                                                       bias=cb[:, fc:fc + 1]),
                      r=[('Asb', r2, 0), 'cw', 'cb'], w=[('cc', r2)])
                P.add('dve', lambda e, A_=A_, c_=c_, fc=fc: e.scalar_tensor_tensor(out=c_, in0=A_[:, 0:512], scalar=cw[:, 0, fc:fc + 1], in1=c_,
                                                                                 op0=ALU.mult, op1=ALU.add),
                      r=[('Asb', r2, 0), ('Asb', r2, 1), 'cw', ('cc', r2)], w=[('cc', r2)])
                P.add('dve', lambda e, A_=A_, c_=c_, fc=fc: e.scalar_tensor_tensor(out=c_, in0=A_[:, 2:514], scalar=cw[:, 2, fc:fc + 1], in1=c_,
                                                                                 op0=ALU.mult, op1=ALU.add),
                      r=[('Asb', r2, 0), ('Asb', r2, 1), 'cw', ('cc', r2)], w=[('cc', r2)])
                P.add('act', lambda e, c_=c_, c2_=c2_: e.activation(out=c2_, in_=c_, func=AF.Square, scale=0.21145921592590237),
                      r=[('cc', r2)], w=[('c2', r2)])
                P.add('dve', lambda e, c_=c_, c2_=c2_, c3_=c3_: e.scalar_tensor_tensor(out=c3_, in0=c2_, scalar=1.0, in1=c_, op0=ALU.add, op1=ALU.mult),
                      r=[('c2', r2), ('cc', r2)], w=[('c3', r2)])
                P.add('act', lambda e, c3_=c3_: e.activation(out=c3_, in_=c3_, func=AF.Tanh, scale=0.7978845608028654),
                      r=[('c3', r2)], w=[('c3', r2)])
                P.add('dve', lambda e, c_=c_, c3_=c3_, c2_=c2_: e.scalar_tensor_tensor(out=c2_, in0=c3_, scalar=1.0, in1=c_, op0=ALU.add, op1=ALU.mult),
                      r=[('c3', r2), ('cc', r2), ('c2', r2)], w=[('c2', r2)])
                P.add('dve', lambda e, c2_=c2_, ub_=ub_, fc=fc: e.tensor_tensor(out=hid[:, fc, :], in0=psf[ub_][:, :], in1=c2_, op=ALU.mult),
                      r=pstok(ub_, 0, 2048) + [('c2', r2)], w=[('hid', fc)])
            for tt in range(4):
                j = blk * 4 + tt
                P.add('sp', lambda e, j=j: e.dma_start(allow_slow_non_contiguous=True, out=ht[j % 4], in_=ys[t_base + j * 128: t_base + (j + 1) * 128, :]),
                      r=[('ys', j)], w=[('ht', j % 4)])
            for half in range(2):
                hs = slice(half * 512, (half + 1) * 512)
                for fc in range(NFC):
                    r4 = wi['d'] % 16
                    wi['d'] += 1
                    prefetch(wi['g'] + (5 if (half == 1 and fc >= NFC - 8) else 0), wi['d'] + 11)
                    for tt in range(4):
                        P.add('pe', lambda e, fc=fc, r4=r4, tt=tt: e.matmul(psf[4 + tt][:, :], lhsT=hid[:, fc, tt * 128:(tt + 1) * 128], rhs=wd[r4],
                                                                          start=(fc == 0), stop=(fc == NFC - 1)),
                              r=[('hid', fc), ('wd', r4)], w=pstok(4 + tt))
                for tt in range(4):
                    P.add('act', lambda e, tt=tt, half=half: e.activation(out=junk, in_=psf[4 + tt][:, :], func=AF.Square,
                                                                        accum_out=ss[:, tt, half:half + 1]),
                          r=pstok(4 + tt), w=['junk', ('ssB', tt, half)])
                    P.add('act', lambda e, tt=tt, hs=hs: e.activation(out=fsb[:, tt, hs], in_=psf[4 + tt][:, :], func=AF.Copy),
                          r=pstok(4 + tt), w=[('fsb', tt, half)])
            for tt in range(4):
                j = blk * 4 + tt
                r_ = j % 4
                h_ = ht[r_]
                nm = 'b2_%d' % tt
                P.add('dve', lambda e, tt=tt: e.tensor_tensor(out=ss[:, tt, 2:3], in0=ss[:, tt, 0:1], in1=ss[:, tt, 1:2], op=ALU.add),
                      r=[('ssB', tt, 0), ('ssB', tt, 1)], w=[nm + 'ssq'])
                rstd_from_ssq(ss[:, tt, 2:3], ss[:, tt, 3:4], D, 4.0 * EPS, nm)
                P.add('dve', lambda e, tt=tt: e.scalar_tensor_tensor(out=fsb[:, tt, :], in0=fsb[:, tt, :], scalar=ss[:, tt, 3:4], in1=wfpost,
                                                                   op0=ALU.mult, op1=ALU.mult),
                      r=[('fsb', tt, 0), ('fsb', tt, 1), nm + 'rstd', 'wfpost'], w=[('fsb', tt, 0), ('fsb', tt, 1)])
                P.add('dve', lambda e, tt=tt, h_=h_: e.tensor_tensor(out=h_, in0=h_, in1=fsb[:, tt, :], op=ALU.add),
                      r=[('fsb', tt, 0), ('fsb', tt, 1), ('ht', r_)], w=[('ht', r_)])
                P.add('sp', lambda e, j=j, h_=h_: e.dma_start(allow_slow_non_contiguous=True, out=ys[t_base + j * 128: t_base + (j + 1) * 128, :], in_=h_),
                      r=[('ht', r_)], w=[('ys', j)])
        state['off'] = base

    setup()
    weight_prep()
    P.barrier()
    t_base = 0
    for S in seq_lens:
        phase_A0(t_base, S, branches)
        P.barrier()
        if dbg_xt is not None and t_base == 0:
            P.add('sp', lambda e, S=S: e.dma_start(out=dbg_xt[:, :, 0:S + 1], in_=XT[:, :, 0:S + 1]), w=['dbgxt'])
        for hp in range(4):
            phase_attn(S, hp, branches)
            P.barrier()
        if stop_after == 'attn':
            break
        for hp in range(4):
            phase_hgrn(S, hp)
            P.barrier()
            if stop_after == 'hgrn0':
                break
        if stop_after == 'hgrn0':
            break
        phase_B1(t_base, S)
        P.barrier()
        phase_B2(t_base, S)
        P.barrier()
        t_base += S

    with nc.Block() as block:
        P.emit(nc, block, sems)
    es.close()
    return nc


def rot_tables(SM):
    half = 8
    inv = ROPE_THETA ** (-np.arange(half, dtype=np.float32) * 2.0 / 16.0)
    ang = np.arange(SM, dtype=np.float32)[:, None] * inv[None, :]
    cos = np.cos(ang).astype(np.float32).T
    sin = np.sin(ang).astype(np.float32).T
    c = np.ones((128, SM), np.float32)
    s = np.zeros((128, SM), np.float32)
    for hb in (0, 64):
        c[hb:hb + 8] = cos
        c[hb + 8:hb + 16] = cos
        s[hb:hb + 8] = -sin
        s[hb + 8:hb + 16] = sin
    return c, s


_CACHE = {}


def kernel(x_prompt, x_sample, norm_mix_pre, w_in, hgrn_lb_fwd, hgrn_lb_bwd, hgrn_out_norm, w_out,
           norm_mix_post, norm_ffn_pre, w_gate, w_up, conv_w, conv_b, w_down, norm_ffn_post):
    n = 8
    x_prompt = np.asarray(x_prompt)
    x_sample = np.asarray(x_sample)
    Bp, Sp, _ = x_prompt.shape
    Bs, Ss, _ = x_sample.shape
    pp, sp_ = Bp // n, Bs // n
    seq_lens = tuple([Sp] * pp + [Ss] * sp_)
    if seq_lens not in _CACHE:
        _CACHE[seq_lens] = build(seq_lens)
    nc = _CACHE[seq_lens]
    rc, rs = rot_tables(max(seq_lens))
    f = lambda a: np.ascontiguousarray(np.asarray(a, dtype=np.float32))
    common = {
        "w_in": f(w_in)[0], "w_out": f(w_out)[0], "w_gate": f(w_gate)[0], "w_up": f(w_up)[0], "w_down": f(w_down)[0],
        "norm_mix_pre": f(norm_mix_pre)[0], "norm_mix_post": f(norm_mix_post)[0], "norm_ffn_pre": f(norm_ffn_pre)[0],
        "norm_ffn_post": f(norm_ffn_post)[0], "conv_b": f(conv_b)[0], "hgrn_out_norm": f(hgrn_out_norm)[0],
        "conv_w": f(conv_w)[0], "hgrn_lb_fwd": f(hgrn_lb_fwd), "hgrn_lb_bwd": f(hgrn_lb_bwd),
        "rot_c": rc, "rot_s": rs,
    }
    in_maps = []
    for c in range(n):
        xs = np.concatenate([x_prompt[c * pp:(c + 1) * pp].reshape(-1, D), x_sample[c * sp_:(c + 1) * sp_].reshape(-1, D)], axis=0)
        m = dict(common)
        m["xs"] = np.ascontiguousarray(xs, dtype=np.float32)
        in_maps.append(m)
    res = run_bass_kernel_spmd(nc, in_maps, core_ids=list(range(n)))
    yp = np.empty((Bp, Sp, D), np.float32)
    ysm = np.empty((Bs, Ss, D), np.float32)
    for c in range(n):
        y = res.results[c]["ys"]
        yp[c * pp:(c + 1) * pp] = y[:pp * Sp].reshape(pp, Sp, D)
        ysm[c * sp_:(c + 1) * sp_] = y[pp * Sp:].reshape(sp_, Ss, D)
    return (yp, ysm)
```

```python
import os
import numpy as np
import ml_dtypes
import concourse.bass as bass
import concourse.mybir as mybir
from concourse.bass_utils import run_bass_kernel_spmd

F32 = mybir.dt.float32
BF16 = mybir.dt.bfloat16
AF = mybir.ActivationFunctionType
ALU = mybir.AluOpType

D = 1024
KC = 8
INW = 4096
DFF = 2816
NFC = 22
EPS = 1e-6
ROPE_THETA = 500000.0
BRANCHES = (1, 4, 16)
CH = 64
COMPUTE = ('pe', 'act', 'dve', 'pool')


class Prog:
    def __init__(self, ndma=12):
        self.ndma = ndma
        self.lists = {e: [] for e in COMPUTE + ('sp',)}
        self.bystream = {}
        self.tok = {}
        self.vc = {e: {} for e in COMPUTE + ('sp',)}
        self.dma_n = 0
        self.pending_barrier = {}

    def barrier(self):
        deps = set()
        for s, ops in self.bystream.items():
            if ops:
                deps.add((s, len(ops) - 1))
        for e in self.lists:
            self.pending_barrier[e] = set(deps)
        self.tok = {}

    def add(self, eng, fn, r=(), w=()):
        if eng == 'sp':
            stream = 'd%d' % (self.dma_n % self.ndma)
            self.dma_n += 1
        else:
            stream = eng
        slist = self.bystream.setdefault(stream, [])
        sidx = len(slist)
        deps = set()
        if eng in self.pending_barrier:
            deps |= self.pending_barrier.pop(eng)
        for k in r:
            st = self.tok.get(k)
            if st is not None and st[0] is not None:
                deps.add(st[0])
        for k in w:
            st = self.tok.get(k)
            if st is not None:
                if st[0] is not None:
                    deps.add(st[0])
                deps.update(st[1])
        if eng == 'sp' and sidx > 0:
            deps.add((stream, sidx - 1))
        vc = self.vc[eng]
        waits = {}
        for (s, i) in deps:
            if s == 'pe' and eng == 'pe':
                continue
            if vc.get(s, -1) < i:
                if waits.get(s, -1) < i:
                    waits[s] = i
        for s, i in waits.items():
            dop = self.bystream[s][i]
            dop['sig'] = True
            for s2, i2 in dop['vc'].items():
                if vc.get(s2, -1) < i2:
                    vc[s2] = i2
            if vc.get(s, -1) < i:
                vc[s] = i
        ovc = dict(vc)
        ovc[stream] = sidx
        op = dict(eng=eng, fn=fn, stream=stream, sidx=sidx, waits=waits, sig=False, vc=ovc)
        slist.append(op)
        self.lists[eng].append(op)
        me = (stream, sidx)
        for k in r:
            st = self.tok.setdefault(k, [None, []])
            st[1].append(me)
        for k in w:
            self.tok[k] = [me, []]
        return op

    def emit(self, nc, block, sems):
        counts = {}
        for s in COMPUTE:
            c = 0
            arr = []
            for op in self.bystream.get(s, []):
                if op['sig']:
                    c += 1
                arr.append(c)
            counts[s] = arr

        def val(s, i):
            if s in COMPUTE:
                return counts[s][i]
            return 16 * (i + 1)

        def run(e, ename):
            for op in self.lists[ename]:
                for s, i in op['waits'].items():
                    e.wait_ge(sems[s], val(s, i))
                ins = op['fn'](e)
                if ename == 'sp':
                    ins.then_inc(sems[op['stream']], 16)
                elif op['sig']:
                    ins.then_inc(sems[ename], 1)
            if ename == 'sp':
                for s, ops in self.bystream.items():
                    if s not in COMPUTE and ops:
                        e.wait_ge(sems[s], 16 * len(ops))

        @block.sync
        def _(e):
            run(e, 'sp')

        @block.tensor
        def _(e):
            run(e, 'pe')

        @block.scalar
        def _(e):
            run(e, 'act')

        @block.vector
        def _(e):
            run(e, 'dve')

        @block.gpsimd
        def _(e):
            run(e, 'pool')


def bcast_free(ap, n):
    return bass.AP(ap.tensor, ap.offset, [list(x) for x in ap.ap] + [[0, n]])


def bcast_mid(ap, n):
    l = [list(x) for x in ap.ap]
    return bass.AP(ap.tensor, ap.offset, [l[0], [0, n]] + l[1:])


def sst(lo, n, d):
    return slice(lo, lo + (n - 1) * d + 1, d)


def pstok(bank, lo=0, hi=0):
    return [('ps', bank)]


DEBUG_OFFS = {}


def build(seq_lens, branches=BRANCHES, stop_after=None):
    nc = bass.Bass("TRN2", target_bir_lowering=False)
    NT = sum(seq_lens)
    SM = max(seq_lens)
    dt = nc.dram_tensor
    xs = dt("xs", [NT, D], F32, kind="ExternalInput").ap()
    ys = dt("ys", [NT, D], F32, kind="ExternalOutput").ap()
    w_in = dt("w_in", [D, INW], F32, kind="ExternalInput").ap()
    w_out = dt("w_out", [D, D], F32, kind="ExternalInput").ap()
    w_gate = dt("w_gate", [D, DFF], F32, kind="ExternalInput").ap()
    w_up = dt("w_up", [D, DFF], F32, kind="ExternalInput").ap()
    w_down = dt("w_down", [DFF, D], F32, kind="ExternalInput").ap()
    vec = {}
    for nm, n in (("norm_mix_pre", D), ("norm_mix_post", D), ("norm_ffn_pre", D), ("norm_ffn_post", D),
                  ("conv_b", DFF), ("hgrn_out_norm", 64)):
        vec[nm] = dt(nm, [n], F32, kind="ExternalInput")
    conv_w = dt("conv_w", [3, DFF], F32, kind="ExternalInput")
    lbf = dt("hgrn_lb_fwd", [2, 512], F32, kind="ExternalInput")
    lbb = dt("hgrn_lb_bwd", [2, 512], F32, kind="ExternalInput")
    rotc_d = dt("rot_c", [128, SM], F32, kind="ExternalInput").ap()
    rots_d = dt("rot_s", [128, SM], F32, kind="ExternalInput").ap()
    win_b = dt("win_b", [D, INW], BF16, kind=("ExternalOutput" if os.environ.get("KDEBUG") else "Internal")).ap()
    winsw_b = dt("winsw_b", [D, 1024], BF16, kind="Internal").ap()
    wout_b = dt("wout_b", [D, D], BF16, kind="Internal").ap()
    wg_b = dt("wg_b", [NFC, 128, KC, 128], BF16, kind="Internal").ap()
    wu_b = dt("wu_b", [NFC, 128, KC, 128], BF16, kind="Internal").ap()
    wd_b = dt("wd_b", [DFF, D], BF16, kind="Internal").ap()
    mix_s = dt("mix_s", [KC, 128, SM], BF16, kind=("ExternalOutput" if os.environ.get("KDEBUG") else "Internal")).ap()

    dbg_xt = dt("dbg_xt", [128, KC, SM + 2], BF16, kind="ExternalOutput").ap() if os.environ.get("KDEBUG") else None
    P = Prog()
    from contextlib import ExitStack
    es = ExitStack()
    ARF = 53200
    arena = es.enter_context(nc.sbuf_tensor("arena", [128, ARF], F32))
    arena_b = arena.bitcast(BF16)
    psf = [es.enter_context(nc.psum_tensor("ps%d" % i, [128, 512], F32)) for i in range(8)]
    psb = [p.bitcast(BF16) for p in psf]
    sems = {}
    for s in list(COMPUTE) + ['d%d' % i for i in range(P.ndma)]:
        sems[s] = es.enter_context(nc.semaphore("sem_" + s))

    state = {'off': 0, 'uid': 0}

    def alloc(shape, dtype, name):
        n = 1
        for s_ in shape:
            n *= s_
        esz = 4 if dtype == F32 else 2
        off = (state['off'] + 31) // 32 * 32
        state['off'] = off + n * esz
        assert state['off'] <= ARF * 4, ("SBUF arena overflow", name, state['off'])
        DEBUG_OFFS[name] = (off, list(shape), 'f32' if dtype == F32 else 'bf16')
        base = arena if dtype == F32 else arena_b
        o = off // esz
        v = base[:, o:o + n]
        if len(shape) == 2:
            v = v.rearrange("p (a b) -> p a b", a=shape[0])
        elif len(shape) == 3:
            v = v.rearrange("p (a b c) -> p a b c", a=shape[0], b=shape[1])
        return v

    XT = alloc([KC, SM + 2], BF16, "xnT")
    ident = alloc([128], BF16, "ident")
    maskA = alloc([256], BF16, "maskA")
    maskMB = alloc([256], BF16, "maskMB")
    maskF = alloc([64], BF16, "maskF")
    maskB = alloc([64], BF16, "maskB")
    onesbd = alloc([128], F32, "onesbd")
    esel = alloc([64], F32, "esel")
    cneg = alloc([1], F32, "cneg")
    eps256 = alloc([1], F32, "eps256")
    wpre = alloc([KC], F32, "wpre")
    wfpre = alloc([KC], F32, "wfpre")
    cw = alloc([3, NFC], F32, "cw")
    cb = alloc([NFC], F32, "cb")
    onw4 = alloc([1], F32, "onw4")
    lbt = alloc([2, 2, 4], F32, "lbt")
    ga = alloc([2, 4], F32, "ga")
    gb = alloc([2, 4], F32, "gb")
    gna = alloc([2, 4], F32, "gna")
    gnb = alloc([2, 4], F32, "gnb")
    state['off'] += int(os.environ.get('KPAD', '0'))
    PERSIST = state['off']

    def setup():
        P.add('pool', lambda e: e.memset(ident, 0.0), w=['ident'])
        P.add('pool', lambda e: e.affine_select(out=ident, in_=ident, pattern=[[-1, 128]], compare_op=ALU.not_equal,
                                                 fill=1.0, base=0, channel_multiplier=1), r=['ident'], w=['ident'])
        P.add('pool', lambda e: e.memset(maskA, 1.0), w=['maskA'])
        P.add('pool', lambda e: e.affine_select(out=maskA, in_=maskA, pattern=[[1, 256]], compare_op=ALU.is_ge,
                                                 fill=0.0, base=0, channel_multiplier=-1), r=['maskA'], w=['maskA'])
        P.add('pool', lambda e: e.affine_select(out=maskA, in_=maskA, pattern=[[-1, 256]], compare_op=ALU.is_ge,
                                                 fill=0.0, base=128, channel_multiplier=1), r=['maskA'], w=['maskA'])
        P.add('dve', lambda e: e.tensor_scalar(out=maskMB, in0=maskA, scalar1=-1.0, scalar2=30000.0, op0=ALU.add, op1=ALU.mult),
              r=['maskA'], w=['maskMB'])
        P.add('pool', lambda e: e.memset(maskF[0:64, :], 1.0), w=['maskF'])
        P.add('pool', lambda e: e.affine_select(out=maskF[0:64, :], in_=maskF[0:64, :], pattern=[[1, 64]], compare_op=ALU.is_ge,
                                                 fill=0.0, base=0, channel_multiplier=-1), r=['maskF'], w=['maskF'])
        P.add('pool', lambda e: e.memset(maskB[0:64, :], 1.0), w=['maskB'])
        P.add('pool', lambda e: e.affine_select(out=maskB[0:64, :], in_=maskB[0:64, :], pattern=[[-1, 64]], compare_op=ALU.is_ge,
                                                 fill=0.0, base=0, channel_multiplier=1), r=['maskB'], w=['maskB'])
        P.add('pool', lambda e: e.memset(onesbd, 0.0), w=['onesbd'])
        P.add('pool', lambda e: e.memset(onesbd[0:64, 0:64], 1.0), r=['onesbd'], w=['onesbd'])
        P.add('pool', lambda e: e.memset(onesbd[64:128, 64:128], 1.0), r=['onesbd'], w=['onesbd'])
        P.add('pool', lambda e: e.memset(esel[0:65, :], 0.0), w=['esel'])
        P.add('pool', lambda e: e.memset(esel[64:65, :], 1.0), r=['esel'], w=['esel'])
        P.add('pool', lambda e: e.memset(cneg, -0.5), w=['cneg'])
        P.add('pool', lambda e: e.memset(eps256, 256.0 * EPS), w=['eps256'])
        P.add('pool', lambda e: e.memset(XT[:, :, 0:1], 0.0), w=['xhalo'])
        with nc.allow_non_contiguous_dma(reason="tiny per-feature vectors"):
            P.add('sp', lambda e: e.dma_start(allow_slow_non_contiguous=True, out=wpre, in_=vec["norm_mix_pre"].ap().rearrange("(k p) -> p k", p=128)), w=['wpre'])
            P.add('sp', lambda e: e.dma_start(allow_slow_non_contiguous=True, out=wfpre, in_=vec["norm_ffn_pre"].ap().rearrange("(k p) -> p k", p=128)), w=['wfpre'])
            P.add('sp', lambda e: e.dma_start(allow_slow_non_contiguous=True, out=cw, in_=conv_w.ap().rearrange("w (f p) -> p w f", p=128)), w=['cw'])
            P.add('sp', lambda e: e.dma_start(allow_slow_non_contiguous=True, out=cb, in_=vec["conv_b"].ap().rearrange("(f p) -> p f", p=128)), w=['cb'])
            P.add('sp', lambda e: e.dma_start(allow_slow_non_contiguous=True, out=onw4[0:64, :], in_=vec["hgrn_out_norm"].ap().rearrange("(p o) -> p o", o=1)), w=['onw4a'])
            P.add('sp', lambda e: e.dma_start(allow_slow_non_contiguous=True, out=onw4[64:128, :], in_=vec["hgrn_out_norm"].ap().rearrange("(p o) -> p o", o=1)), w=['onw4b'])
            P.add('sp', lambda e: e.dma_start(allow_slow_non_contiguous=True, out=lbt[:, 0, :, :], in_=lbf.ap().rearrange("s (c p) -> p s c", p=128)), w=['lbt0'])
            P.add('sp', lambda e: e.dma_start(allow_slow_non_contiguous=True, out=lbt[:, 1, :, :], in_=lbb.ap().rearrange("s (c p) -> p s c", p=128)), w=['lbt1'])
        P.add('dve', lambda e: e.tensor_scalar(out=onw4, in0=onw4, scalar1=4.0, scalar2=None, op0=ALU.mult),
              r=['onw4a', 'onw4b'], w=['onw4'])
        P.add('dve', lambda e: e.tensor_tensor(out=ga, in0=lbt[:, :, 0, :], in1=lbt[:, :, 1, :], op=ALU.subtract),
              r=['lbt0', 'lbt1'], w=['ga'])
        P.add('act', lambda e: e.activation(out=gb, in_=ga, func=AF.Tanh, scale=0.5), r=['ga'], w=['gb'])
        P.add('dve', lambda e: e.tensor_scalar(out=ga, in0=gb, scalar1=0.25, scalar2=0.75, op0=ALU.mult, op1=ALU.add),
              r=['gb'], w=['ga'])
        P.add('dve', lambda e: e.tensor_scalar(out=gna, in0=gb, scalar1=-0.25, scalar2=0.25, op0=ALU.mult, op1=ALU.add),
              r=['gb'], w=['gna'])
        P.add('dve', lambda e: e.tensor_scalar(out=gnb, in0=gb, scalar1=0.25, scalar2=-0.25, op0=ALU.mult, op1=ALU.add),
              r=['gb'], w=['gnb'])
        P.add('dve', lambda e: e.tensor_scalar(out=gb, in0=gb, scalar1=-0.25, scalar2=0.25, op0=ALU.mult, op1=ALU.add),
              r=['gb', 'gna', 'gnb'], w=['gb'])

    def weight_prep():
        base = state['off']
        st = [alloc([4096], F32, "wst%d" % i) for i in range(2)]
        bt = [alloc([4096], BF16, "wbt%d" % i) for i in range(2)]
        sw2 = alloc([1024], BF16, "wsw")
        sw = sw2.rearrange("p (h d) -> p h d", h=16)
        P.add('pool', lambda e: e.memset(sw2, 0.0), w=['wsw'])
        it = [0]

        def cast_rows(src, dst, ncols, sw_dst=None, dst_rearr=None):
            r_ = it[0] % 2
            it[0] += 1
            s_, b_ = st[r_], bt[r_]
            P.add('sp', lambda e: e.dma_start(allow_slow_non_contiguous=True, out=s_[:, 0:ncols], in_=src), w=[('wst', r_)])
            h1 = ncols // 2
            P.add('act', lambda e: e.activation(out=b_[:, 0:h1], in_=s_[:, 0:h1], func=AF.Copy), r=[('wst', r_)], w=[('wbt', r_, 0)])
            P.add('dve', lambda e: e.tensor_copy(out=b_[:, h1:ncols], in_=s_[:, h1:ncols]), r=[('wst', r_)], w=[('wbt', r_, 1)])
            if sw_dst is not None:
                sv = s_[:, 0:1024].rearrange("p (h d) -> p h d", h=16)
                P.add('pool', lambda e: e.tensor_copy(out=sw[:, :, 0:8], in_=sv[:, :, 8:16]), r=[('wst', r_)], w=['wsw'])
                P.add('pool', lambda e: e.tensor_copy(out=sw[:, :, 8:16], in_=sv[:, :, 0:8]), r=[('wst', r_)], w=['wsw'])
                P.add('sp', lambda e: e.dma_start(allow_slow_non_contiguous=True, out=sw_dst, in_=sw2), r=['wsw'], w=['winsw_b'])
            if dst_rearr is None:
                P.add('sp', lambda e: e.dma_start(allow_slow_non_contiguous=True, out=dst, in_=b_[:, 0:ncols]), r=[('wbt', r_, 0), ('wbt', r_, 1)], w=['wscr'])
            else:
                P.add('sp', lambda e: e.dma_start(allow_slow_non_contiguous=True, out=dst, in_=b_[:, 0:ncols].rearrange("p (f j) -> p f j", j=128)),
                      r=[('wbt', r_, 0), ('wbt', r_, 1)], w=['wscr'])

        for kc in range(KC):
            rs = slice(kc * 128, (kc + 1) * 128)
            cast_rows(w_in[rs, :], win_b[rs, :], INW, sw_dst=winsw_b[rs, :])
        for kc in range(KC):
            rs = slice(kc * 128, (kc + 1) * 128)
            cast_rows(w_out[rs, :], wout_b[rs, :], D)
        with nc.allow_non_contiguous_dma(reason="chunked weight scratch, 256B segments, one-time"):
            for kc in range(KC):
                rs = slice(kc * 128, (kc + 1) * 128)
                cast_rows(w_gate[rs, :], wg_b[:, :, kc, :].rearrange("f p j -> p f j"), DFF, dst_rearr=True)
                cast_rows(w_up[rs, :], wu_b[:, :, kc, :].rearrange("f p j -> p f j"), DFF, dst_rearr=True)
        for fc in range(NFC):
            rs = slice(fc * 128, (fc + 1) * 128)
            cast_rows(w_down[rs, :], wd_b[rs, :], D)
        state['off'] = base

    pr = {'i': 0}

    def rstd_from_ssq(ssq, out, n, eps, name):
        P.add('dve', lambda e: e.tensor_scalar(out=out, in0=ssq, scalar1=1.0 / n, scalar2=eps, op0=ALU.mult, op1=ALU.add),
              r=[name + 'ssq'], w=[name + 'rstd'])
        P.add('pool', lambda e: e.tensor_tensor(out=out, in0=out, in1=cneg, op=ALU.pow),
              r=[name + 'rstd', 'cneg'], w=[name + 'rstd'])

    def phase_A0(t_base, S, B):
        base = state['off']
        xt = [alloc([D], F32, "xt%d" % i) for i in range(4)]
        xb = [alloc([D], BF16, "xb%d" % i) for i in range(2)]
        junk = alloc([D], BF16, "junk")
        ss = [alloc([2], F32, "ss%d" % i) for i in range(4)]
        NJ = S // 128

        def a1(j):
            r_ = j % 4
            x_, s_ = xt[r_], ss[r_]
            nm = 'a0_%d' % r_
            P.add('sp', lambda e, j=j, x_=x_: e.dma_start(allow_slow_non_contiguous=True, out=x_, in_=xs[t_base + j * 128: t_base + (j + 1) * 128, :]), w=[('xt', r_)])
            P.add('act', lambda e, x_=x_, s_=s_: e.activation(out=junk, in_=x_, func=AF.Square, accum_out=s_[:, 0:1]),
                  r=[('xt', r_)], w=['junk', nm + 'ssq'])
            rstd_from_ssq(s_[:, 0:1], s_[:, 1:2], D, EPS, nm)

        def a2(j):
            r_ = j % 4
            x_, s_, b_ = xt[r_], ss[r_], xb[j % 2]
            nm = 'a0_%d' % r_
            P.add('act', lambda e, x_=x_, s_=s_, b_=b_: e.activation(out=b_, in_=x_, func=AF.Copy, scale=s_[:, 1:2]),
                  r=[('xt', r_), nm + 'rstd'], w=[('xb', j % 2)])

        def a3(j):
            b_ = xb[j % 2]
            bank = 6 + (j % 2)
            for kc in range(KC):
                P.add('pe', lambda e, kc=kc, b_=b_, bank=bank: e.transpose(out=psb[bank][:, kc * 128:(kc + 1) * 128],
                                                                          in_=b_[:, kc * 128:(kc + 1) * 128], identity=ident),
                      r=[('xb', j % 2), 'ident'], w=pstok(bank))
            P.add('dve', lambda e, j=j, bank=bank: e.tensor_tensor(
                out=XT[:, :, 1 + j * 128: 1 + (j + 1) * 128],
                in0=psb[bank][:, :].rearrange("p (k t) -> p k t", k=KC),
                in1=bcast_free(wpre, 128), op=ALU.mult),
                r=pstok(bank) + ['wpre'], w=[('XT', kc, j) for kc in range(KC)])

        for j in range(NJ + 2):
            if j < NJ:
                a1(j)
            if 0 <= j - 1 < NJ:
                a2(j - 1)
            if 0 <= j - 2 < NJ:
                a3(j - 2)
        state['off'] = base

    def load_wA(cols_main, cols_sw, wA):
        i = 0
        with nc.allow_non_contiguous_dma(reason="weight column chunk, 256B segments"):
            for c in cols_main:
                P.add('sp', lambda e, c=c, i=i: e.dma_start(allow_slow_non_contiguous=True, out=wA[i], in_=win_b[:, c:c + 128].rearrange("(k p) j -> p k j", p=128)),
                      r=['wscr'], w=[('wA', i)])
                i += 1
            for c in cols_sw:
                P.add('sp', lambda e, c=c, i=i: e.dma_start(allow_slow_non_contiguous=True, out=wA[i], in_=winsw_b[:, c:c + 128].rearrange("(k p) j -> p k j", p=128)),
                      r=['winsw_b'], w=[('wA', i)])
                i += 1

    def proj(wA_i, tb, bank):
        for kc in range(KC):
            P.add('pe', lambda e, kc=kc: e.matmul(psf[bank][:, :], lhsT=wA_i[1][:, kc, :], rhs=XT[:, kc, 1 + tb * 512: 1 + (tb + 1) * 512],
                                                   start=(kc == 0), stop=(kc == KC - 1)),
                  r=[('wA', wA_i[0])] + [('XT', kc, tb * 4 + q) for q in range(4)], w=pstok(bank, 0, 2048))

    def phase_attn(S, hp, B):
        base = state['off']
        wA = [alloc([KC, 128], BF16, "wA%d" % i) for i in range(5)]
        rotc = [alloc([512], F32, "rotc%d" % i) for i in range(2)]
        rots = [alloc([512], F32, "rots%d" % i) for i in range(2)]
        qT = alloc([S], BF16, "qT")
        kT = alloc([S], BF16, "kT")
        vT = alloc([S], BF16, "vT")
        NTL = S // 128
        vtok = [alloc([NTL, 2, 65], BF16, "vtok%d" % b) for b in range(len(B))]
        tA = [alloc([512], F32, "tA%d" % i) for i in range(2)]
        tB = [alloc([512], F32, "tB%d" % i) for i in range(2)]
        praw = [alloc([256], BF16, "praw%d" % i) for i in range(4)]
        pmk = [alloc([256], BF16, "pmk%d" % i) for i in range(4)]
        UT = alloc([S], F32, "UT")
        rrow = alloc([512], F32, "rrow")
        aout = [alloc([512], BF16, "aout%d" % i) for i in range(2)]
        load_wA([hp * 128, 512 + hp * 128, 1024 + hp * 128], [hp * 128, 512 + hp * 128], wA)
        for b in range(len(B)):
            P.add('pool', lambda e, b=b: e.memset(vtok[b][:, :, :, 64:65], 1.0), w=[('vones', b)])
        P.add('pool', lambda e: e.memset(rrow[0:65, :], 0.0), w=['rrow'])
        for tb in range(S // 512):
            sl = slice(tb * 512, (tb + 1) * 512)
            rr = tb % 2
            P.add('sp', lambda e, rr=rr, sl=sl: e.dma_start(out=rotc[rr], in_=rotc_d[:, sl]), w=[('rotc', rr)])
            P.add('sp', lambda e, rr=rr, sl=sl: e.dma_start(out=rots[rr], in_=rots_d[:, sl]), w=[('rots', rr)])
            for (dst, wi, swi, nm) in ((qT, 0, 3, 'qT'), (kT, 1, 4, 'kT')):
                b0 = pr['i'] % 4
                b1 = (pr['i'] + 1) % 4
                pr['i'] += 2
                r_ = (pr['i'] // 2) % 2
                proj((wi, wA[wi]), tb, b0)
                proj((swi, wA[swi]), tb, b1)
                P.add('dve', lambda e, b0=b0, r_=r_, rr=rr: e.tensor_tensor(out=tA[r_], in0=psf[b0][:, :], in1=rotc[rr], op=ALU.mult),
                      r=pstok(b0, 0, 2048) + [('rotc', rr)], w=[('tA', r_)])
                P.add('dve', lambda e, b1=b1, r_=r_, rr=rr: e.tensor_tensor(out=tB[r_], in0=psf[b1][:, :], in1=rots[rr], op=ALU.mult),
                      r=pstok(b1, 0, 2048) + [('rots', rr)], w=[('tB', r_)])
                P.add('dve', lambda e, dst=dst, r_=r_, sl=sl: e.tensor_tensor(out=dst[:, sl], in0=tA[r_], in1=tB[r_], op=ALU.add),
                      r=[('tA', r_), ('tB', r_)], w=[(nm, tb)])
            b0 = pr['i'] % 4
            pr['i'] += 1
            proj((2, wA[2]), tb, b0)
            P.add('act', lambda e, b0=b0, sl=sl: e.activation(out=vT[:, sl], in_=psf[b0][:, :], func=AF.Copy),
                  r=pstok(b0, 0, 2048), w=[('vT', tb)])
        for b, d in enumerate(B):
            L = S // d
            for r in range(d):
                for i0 in range(0, L // 128, 4):
                    n4 = min(4, L // 128 - i0)
                    bank = 6 + (pr['i'] % 2)
                    pr['i'] += 1
                    for ii in range(n4):
                        i = i0 + ii
                        lo = r + d * 128 * i
                        P.add('pe', lambda e, ii=ii, lo=lo, bank=bank, d=d: e.transpose(
                            out=psb[bank][:, ii * 128:(ii + 1) * 128], in_=vT[:, sst(lo, 128, d)], identity=ident),
                            r=[('vT', t) for t in range(lo // 512, (lo + 128 * d - d) // 512 + 1)] + ['ident'],
                            w=pstok(bank, ii * 256, ii * 256 + 256))
                    t0 = r * (L // 128) + i0
                    P.add('dve', lambda e, b=b, t0=t0, n4=n4, bank=bank: e.tensor_copy(
                        out=vtok[b][:, t0:t0 + n4, :, 0:64],
                        in_=psb[bank][:, 0:n4 * 128].rearrange("p (a h c) -> p a h c", a=n4, h=2)),
                        r=pstok(bank, 0, n4 * 256), w=[('vtok', b, t0 + q) for q in range(n4)])
        for h in range(2):
            hb = h * 64
            items = []
            for b, d in enumerate(B):
                L = S // d
                NCH = L // 128
                for r in range(d):
                    for i in range(NCH):
                        items.append((b, d, L, NCH, r, i))

            def stA(k, hb=hb):
                b, d, L, NCH, r, i = items[k]
                qlo = max(0, 128 * i - 64)
                qhi = min(L, 128 * i + 192)
                nq = qhi - qlo
                off = qlo - (128 * i - 64)
                slot = k % 4
                bank = (0, 1, 4, 5)[slot]
                klo = r + d * 128 * i
                qpl = r + d * qlo
                ktoks = [('kT', t) for t in range(klo // 512, (klo + 127 * d) // 512 + 1)]
                qtoks = [('qT', t) for t in range(qpl // 512, (qpl + (nq - 1) * d) // 512 + 1)]
                P.add('pe', lambda e: e.matmul(
                    psf[bank][:, 0:nq], lhsT=kT[hb:hb + 64, sst(klo, 128, d)],
                    rhs=qT[hb:hb + 64, sst(qpl, nq, d)], start=True, stop=False),
                    r=ktoks + qtoks, w=pstok(bank))
                P.add('pe', lambda e: e.matmul(
                    psf[bank][:, 0:nq], lhsT=ident, rhs=maskMB[:, off:off + nq], start=False, stop=True),
                    r=['ident', 'maskMB'], w=pstok(bank))
                P.add('act', lambda e: e.activation(
                    out=pmk[slot][:, 0:nq], in_=psf[bank][:, 0:nq], func=AF.Exp, scale=0.125),
                    r=pstok(bank), w=[('pmk', slot)])

            def stB(k, h=h):
                b, d, L, NCH, r, i = items[k]
                qlo = max(0, 128 * i - 64)
                slot = k % 4
                vt = vtok[b][:, r * NCH + i, h, :]
                for n in (i, i + 1):
                    jlo = max(0, 128 * n - 64)
                    jhi = min(L, 128 * n + 64)
                    nb = jhi - jlo
                    c0 = jlo - qlo
                    obk = (6, 7, 2, 3)[n % 4]
                    first = (n == i + 1) or (i == 0)
                    last = (n == i) or (i == NCH - 1)
                    P.add('pe', lambda e, nb=nb, c0=c0, first=first, last=last, obk=obk: e.matmul(
                        psf[obk][0:65, 0:nb], lhsT=vt, rhs=pmk[slot][:, c0:c0 + nb],
                        start=first, stop=last),
                        r=[('pmk', slot), ('vtok', b, r * NCH + i), ('vones', b)], w=pstok(obk))
                    if last:
                        plo = r + d * jlo
                        is_end = (k == len(items) - 1 or items[k + 1][0] != b) and n == i + 1 or \
                                 ((k == len(items) - 1 or items[k + 1][0] != b) and i == NCH - 1 and n == i and NCH - 1 == i and False)
                        wtok = [('UTop', h, b, k, n)]
                        if (k == len(items) - 1 or items[k + 1][0] != b) and n == i + 1:
                            wtok.append(('UTend', b))
                        if b == 0:
                            P.add('act', lambda e, nb=nb, plo=plo, obk=obk: e.activation(
                                out=UT[0:65, sst(plo, nb, d)], in_=psf[obk][0:65, 0:nb], func=AF.Copy),
                                r=pstok(obk) + ['UTnorm'], w=wtok)
                        else:
                            P.add('dve', lambda e, nb=nb, plo=plo, obk=obk: e.tensor_tensor(
                                out=UT[0:65, sst(plo, nb, d)], in0=psf[obk][0:65, 0:nb],
                                in1=UT[0:65, sst(plo, nb, d)], op=ALU.add),
                                r=pstok(obk) + [('UTend', b - 1)], w=wtok)

            LA = 2
            for k in range(min(LA, len(items))):
                stA(k)
            for k in range(len(items)):
                if k + LA < len(items):
                    stA(k + LA)
                stB(k)
            for tb in range(S // 512):
                sl = slice(tb * 512, (tb + 1) * 512)
                bank = pr['i'] % 4
                pr['i'] += 1
                P.add('dve', lambda e, sl=sl: e.reciprocal(out=rrow[64:65, :], in_=UT[64:65, sl]), r=[('UTend', len(B) - 1)], w=['rrow'])
                P.add('pe', lambda e, bank=bank: e.matmul(psf[bank][0:64, :], lhsT=esel[0:65, :], rhs=rrow[0:65, :], start=True, stop=True),
                      r=['rrow', 'esel'], w=pstok(bank, 0, 2048))
                ar_ = tb % 2
                P.add('dve', lambda e, sl=sl, bank=bank, ar_=ar_: e.tensor_tensor(out=aout[ar_][0:64, :], in0=psf[bank][0:64, :], in1=UT[0:64, sl], op=ALU.mult),
                      r=pstok(bank, 0, 2048) + [('UTend', len(B) - 1)], w=[('aout', ar_)] + (['UTnorm'] if tb == S // 512 - 1 else []))
                P.add('sp', lambda e, hb=hb, sl=sl, ar_=ar_: e.dma_start(allow_slow_non_contiguous=True, out=mix_s[hp, hb:hb + 64, sl], in_=aout[ar_][0:64, :]),
                      r=[('aout', ar_)], w=[('mix_s', hp, h, tb)])
        state['off'] = base

    def phase_hgrn(S, hp):
        base = state['off']
        NCk = S // CH
        wA = [alloc([KC, 128], BF16, "wA%d" % i) for i in range(5)]
        qf = [alloc([S], BF16, "qf%d" % dr) for dr in range(2)]
        kf = [alloc([S], BF16, "kf%d" % dr) for dr in range(2)]
        vtk = alloc([NCk, 128], BF16, "vtk")
        gate = alloc([S], BF16, "gate")
        dS = [alloc([NCk, 64], F32, "dS%d" % dr) for dr in range(2)]
        Dl = [alloc([NCk], F32, "Dl%d" % dr) for dr in range(2)]
        p1base = state['off']
        th = [alloc([512], F32, "th%d" % i) for i in range(2)]
        thd = [alloc([512], F32, "thd%d" % i) for i in range(2)]
        q2 = alloc([512], F32, "q2")
        Fms = [alloc([512], F32, "Fm%d" % i) for i in range(2)]
        D1s = [alloc([512], F32, "D1_%d" % i) for i in range(2)]
        Kks = [alloc([512], F32, "Kk%d" % i) for i in range(2)]
        Ics = [alloc([512], F32, "Ic%d" % i) for i in range(2)]
        RIs = [alloc([512], F32, "RI%d" % i) for i in range(2)]
        vTb = alloc([512], BF16, "vTb")
        ktk = [alloc([128], BF16, "ktk%d" % i) for i in range(4)]
        c0 = 1536 + hp * 128
        load_wA([c0, c0 + 512, c0 + 1024, c0 + 1536, c0 + 2048], [], wA)
        for i in range(2):
            P.add('pool', lambda e, i=i: e.memset(D1s[i], 0.0), w=[('D1', i)])
        pending = []
        for tb in range(S // 512):
            pending_new = []
            sl = slice(tb * 512, (tb + 1) * 512)
            b0 = pr['i'] % 4
            pr['i'] += 1
            proj((0, wA[0]), tb, b0)
            P.add('act', lambda e, b0=b0: e.activation(out=th[0], in_=psf[b0][:, :], func=AF.Tanh, scale=0.5),
                  r=pstok(b0, 0, 2048), w=[('th', 0)])
            P.add('dve', lambda e, b0=b0: e.scalar_tensor_tensor(out=q2, in0=th[0], scalar=1.0, in1=psf[b0][:, :], op0=ALU.add, op1=ALU.mult),
                  r=pstok(b0, 0, 2048) + [('th', 0)], w=['q2'])
            b0 = pr['i'] % 4
            pr['i'] += 1
            proj((3, wA[3]), tb, b0)
            P.add('act', lambda e, b0=b0: e.activation(out=vTb, in_=psf[b0][:, :], func=AF.Copy), r=pstok(b0, 0, 2048), w=['vTb'])
            for half in range(2):
                bank = 6 + (pr['i'] % 2)
                pr['i'] += 1
                for cc in range(4):
                    c = half * 4 + cc
                    P.add('pe', lambda e, c=c, cc=cc, bank=bank: e.transpose(out=psb[bank][0:64, cc * 128:(cc + 1) * 128],
                                                                               in_=vTb[:, c * 64:(c + 1) * 64], identity=ident),
                          r=['vTb', 'ident'], w=pstok(bank, cc * 256, cc * 256 + 256))
                cg = tb * 8 + half * 4
                P.add('dve', lambda e, cg=cg, bank=bank: e.tensor_copy(out=vtk[0:64, cg:cg + 4, :],
                                                                      in_=psb[bank][0:64, 0:512].rearrange("p (a c) -> p a c", a=4)),
                      r=pstok(bank, 0, 1024), w=[('vtk', cg + q) for q in range(4)])
            b0 = pr['i'] % 4
            pr['i'] += 1
            proj((4, wA[4]), tb, b0)
            P.add('act', lambda e, b0=b0: e.activation(out=th[1], in_=psf[b0][:, :], func=AF.Tanh, scale=0.5),
                  r=pstok(b0, 0, 2048), w=[('th', 1)])
            P.add('dve', lambda e, b0=b0, sl=sl: e.scalar_tensor_tensor(out=gate[:, sl], in0=th[1], scalar=1.0, in1=psf[b0][:, :],
                                                                        op0=ALU.add, op1=ALU.mult),
                  r=pstok(b0, 0, 2048) + [('th', 1)], w=[('gate', tb)])
            for dr in range(2):
                b0 = pr['i'] % 4
                pr['i'] += 1
                proj((1 + dr, wA[1 + dr]), tb, b0)
                Fm, Kk, Ic, RI, tdr = Fms[dr], Kks[dr], Ics[dr], RIs[dr], thd[dr]
                P.add('act', lambda e, b0=b0, tdr=tdr: e.activation(out=tdr, in_=psf[b0][:, :], func=AF.Tanh, scale=0.5),
                      r=pstok(b0, 0, 2048), w=[('thd', dr)])
                P.add('dve', lambda e, dr=dr, Fm=Fm, tdr=tdr: e.tensor_scalar(out=Fm, in0=tdr, scalar1=gb[:, dr, hp:hp + 1], scalar2=ga[:, dr, hp:hp + 1],
                                                             op0=ALU.mult, op1=ALU.add), r=[('thd', dr), 'ga', 'gb'], w=[('Fm', dr)])
                P.add('act', lambda e, dr=dr, Kk=Kk, tdr=tdr: e.activation(out=Kk, in_=tdr, func=AF.Identity, scale=gnb[:, dr, hp:hp + 1],
                                                           bias=gb[:, dr, hp:hp + 1]), r=[('thd', dr), 'gnb', 'gb'], w=[('Kk', dr)])
                D1 = D1s[dr]
                if dr == 0:
                    edge = slice(0, 512, 64)
                    Fv, D1v, Iv = Fm, D1, Ic
                else:
                    edge = slice(63, 512, 64)
                    Fv, D1v, Iv = Fm[:, ::-1], D1[:, ::-1], Ic[:, ::-1]
                P.add('dve', lambda e, edge=edge, D1=D1, Fm=Fm: e.tensor_copy(out=D1[:, edge], in_=Fm[:, edge]), r=[('Fm', dr)], w=[('D1', dr)])
                P.add('dve', lambda e, edge=edge, Fm=Fm: e.memset(Fm[:, edge], 0.0), r=[('D1', dr), ('Fm', dr)], w=[('Fm', dr)])
                P.add('dve', lambda e, Fv=Fv, D1v=D1v, Iv=Iv: e.tensor_tensor_scan(out=Iv, data0=Fv, data1=D1v, initial=0.0,
                                                                                   op0=ALU.mult, op1=ALU.add),
                      r=[('Fm', dr), ('D1', dr)], w=[('Ic', dr)])
                P.add('dve', lambda e, RI=RI, Ic=Ic: e.reciprocal(out=RI, in_=Ic), r=[('Ic', dr)], w=[('RI', dr)])
                P.add('dve', lambda e, dr=dr, sl=sl, Ic=Ic: e.tensor_tensor(out=qf[dr][:, sl], in0=q2, in1=Ic, op=ALU.mult),
                      r=['q2', ('Ic', dr)], w=[('qf', dr, tb)])
                P.add('dve', lambda e, dr=dr, sl=sl, Kk=Kk, RI=RI: e.tensor_tensor(out=kf[dr][:, sl], in0=Kk, in1=RI, op=ALU.mult),
                      r=[('Kk', dr), ('RI', dr)], w=[('kf', dr, tb)])
                ecol = slice(63, 512, 64) if dr == 0 else slice(0, 512, 64)
                P.add('pool', lambda e, dr=dr, ecol=ecol, tb=tb, Ic=Ic: e.tensor_copy(out=Dl[dr][:, tb * 8:(tb + 1) * 8], in_=Ic[:, ecol]),
                      r=[('Ic', dr)], w=[('Dl', dr, tb)])
                if os.environ.get('HSTOP') == 'p1a':
                    continue
                def dsA(c8, dr=dr, tb=tb):
                    c = tb * 8 + c8
                    kr = c8 % 4
                    bank = 6 + (c8 % 2)
                    P.add('pe', lambda e: e.transpose(out=psb[bank][0:64, 0:128], in_=kf[dr][:, c * 64:(c + 1) * 64], identity=ident),
                          r=[('kf', dr, tb), 'ident'], w=pstok(bank))
                    P.add('act', lambda e: e.activation(out=ktk[kr][0:64, :], in_=psb[bank][0:64, 0:128], func=AF.Copy),
                          r=pstok(bank), w=[('ktk', kr)])

                def dsB(c8, dr=dr, tb=tb):
                    c = tb * 8 + c8
                    kr = c8 % 4
                    bank = 4 + (c8 % 2)
                    P.add('pe', lambda e: e.matmul(psf[bank][:, 0:128], lhsT=ktk[kr][0:64, :], rhs=vtk[0:64, c, :], start=True, stop=True),
                          r=[('ktk', kr), ('vtk', c)], w=pstok(bank))
                    for hh in range(2):
                        ps_ = slice(hh * 64, hh * 64 + 64)
                        if hh == 0:
                            P.add('dve', lambda e, ps_=ps_, hh=hh: e.tensor_scalar(
                                out=dS[dr][ps_, c, :], in0=psf[bank][ps_, hh * 64: 64 + hh * 64],
                                scalar1=Dl[dr][ps_, c:c + 1], scalar2=None, op0=ALU.mult),
                                r=pstok(bank) + [('Dl', dr, tb)], w=[('dS', dr, c, hh)])
                        else:
                            P.add('act', lambda e, ps_=ps_, hh=hh: e.activation(
                                out=dS[dr][ps_, c, :], in_=psf[bank][ps_, hh * 64: 64 + hh * 64], func=AF.Copy,
                                scale=Dl[dr][ps_, c:c + 1]),
                                r=pstok(bank) + [('Dl', dr, tb)], w=[('dS', dr, c, hh)])
                    if dr == 0 and c > 0:
                        P.add('dve', lambda e: e.scalar_tensor_tensor(
                            out=dS[0][:, c, :], in0=dS[0][:, c - 1, :], scalar=Dl[0][:, c:c + 1], in1=dS[0][:, c, :],
                            op0=ALU.mult, op1=ALU.add),
                            r=[('dS', 0, c - 1, 0), ('dS', 0, c - 1, 1), ('dS', 0, c, 0), ('dS', 0, c, 1), ('Dl', 0, tb)],
                            w=[('dS', 0, c, 0), ('dS', 0, c, 1)])

                def run_ds(dsA=dsA, dsB=dsB):
                    dsA(0)
                    for c8 in range(8):
                        if c8 + 1 < 8:
                            dsA(c8 + 1)
                        dsB(c8)
                pending_new.append(run_ds)
            for f_ in pending:
                f_()
            pending = pending_new
        for f_ in pending:
            f_()
        if os.environ.get('HSTOP') in ('p1a', 'p1'):
            state['off'] = base
            return
        for n_ in range(1, NCk):
            for dr in (1,):
                c = n_ if dr == 0 else NCk - 1 - n_
                pc = c - 1 if dr == 0 else c + 1
                P.add('dve', lambda e, dr=dr, c=c, pc=pc: e.scalar_tensor_tensor(
                    out=dS[dr][:, c, :], in0=dS[dr][:, pc, :], scalar=Dl[dr][:, c:c + 1], in1=dS[dr][:, c, :],
                    op0=ALU.mult, op1=ALU.add),
                    r=[('dS', dr, pc, 0), ('dS', dr, pc, 1), ('dS', dr, c, 0), ('dS', dr, c, 1), ('Dl', dr, c // 8)],
                    w=[('dS', dr, c, 0), ('dS', dr, c, 1)])
        if os.environ.get('HSTOP') == 'chain':
            state['off'] = base
            return
        P.barrier()
        state['off'] = p1base
        att = [alloc([2, 64], BF16, "att%d" % i) for i in range(2)]
        Sbd4 = [alloc([8, 128], BF16, "Sbd%d" % i) for i in range(4)]
        osum = alloc([512], F32, "osum")
        osq = alloc([512], F32, "osq")
        rs8 = alloc([512], F32, "rs8")
        houts = [alloc([512], BF16, "hout%d" % i) for i in range(2)]
        for i in range(4):
            P.add('pool', lambda e, i=i: e.memset(Sbd4[i], 0.0), w=[('Sbd', i % 2, i // 2)])
        for tb in range(S // 512):
            sl = slice(tb * 512, (tb + 1) * 512)
            obA = 4 + 2 * (tb % 2)
            obB = 5 + 2 * (tb % 2)
            Sbd = [Sbd4[0 + 2 * (tb % 2)], Sbd4[1 + 2 * (tb % 2)]]
            sbp = tb % 2
            for dr in range(2):
                for hh in range(2):
                    ps_ = slice(hh * 64, hh * 64 + 64)
                    cs = [tb * 8 + c8 + (-1 if dr == 0 else 1) for c8 in range(8)]
                    valid = [c8 for c8 in range(8) if 0 <= cs[c8] < NCk]
                    lo, hi = valid[0], valid[-1] + 1
                    P.add('pool', lambda e, dr=dr, ps_=ps_, lo=lo, hi=hi, cs=cs, hh=hh, Sbd=Sbd: e.tensor_copy(
                        out=Sbd[dr][ps_, lo:hi, hh * 64:hh * 64 + 64], in_=dS[dr][ps_, cs[lo]:cs[hi - 1] + 1, :]),
                        r=[('dS', dr, cs[c8], hh) for c8 in valid], w=[('Sbd', dr, sbp)])
            if os.environ.get('HSTOP') == 'p2a1':
                continue
            items2 = [(c8, dr) for c8 in range(8) for dr in range(2)]

            def p2A(k, tb=tb):
                c8, dr = items2[k]
                c = tb * 8 + c8
                cs_ = slice(c * 64, (c + 1) * 64)
                ar = k % 2
                mk = maskF if dr == 0 else maskB
                for hh in range(2):
                    ps_ = slice(hh * 64, hh * 64 + 64)
                    abank = ar * 2 + hh
                    P.add('pe', lambda e, ps_=ps_, abank=abank: e.matmul(
                        psf[abank][0:64, 0:64], lhsT=kf[dr][ps_, cs_], rhs=qf[dr][ps_, cs_], start=True, stop=True),
                        r=[('kf', dr, tb), ('qf', dr, tb)], w=pstok(abank))
                    P.add('dve', lambda e, abank=abank, hh=hh: e.tensor_tensor(
                        out=att[ar][0:64, hh, :], in0=psf[abank][0:64, 0:64], in1=mk[0:64, :], op=ALU.mult),
                        r=pstok(abank) + ['maskF', 'maskB'], w=[('att', ar, hh)])

            def p2B(k, tb=tb, obA=obA, obB=obB, Sbd=Sbd, sbp=sbp):
                c8, dr = items2[k]
                c = tb * 8 + c8
                cs_ = slice(c * 64, (c + 1) * 64)
                ar = k % 2
                first = (dr == 0)
                skip_inter = (dr == 0 and c == 0) or (dr == 1 and c == NCk - 1)
                for hh, ob in ((0, obA), (1, obB)):
                    P.add('pe', lambda e, hh=hh, ob=ob: e.matmul(
                        psf[ob][:, c8 * 64:(c8 + 1) * 64], lhsT=vtk[0:64, c, :], rhs=att[ar][0:64, hh, :],
                        start=first, stop=(dr == 1 and skip_inter)),
                        r=[('att', ar, hh), ('vtk', c)], w=pstok(ob))
                    if not skip_inter:
                        P.add('pe', lambda e, ob=ob: e.matmul(
                            psf[ob][:, c8 * 64:(c8 + 1) * 64], lhsT=Sbd[dr][:, c8, :], rhs=qf[dr][:, cs_],
                            start=False, stop=(dr == 1)),
                            r=[('Sbd', dr, sbp), ('qf', dr, tb)], w=pstok(ob))

            p2A(0)
            for k in range(16):
                if k + 1 < 16:
                    p2A(k + 1)
                p2B(k)
            if os.environ.get('HSTOP') in ('p2a', 'p2b', 'p2a1', 'p2a2'):
                continue
            P.add('act', lambda e, obA=obA: e.activation(out=osum[0:64, :], in_=psf[obA][0:64, :], func=AF.Copy), r=pstok(obA, 0, 2048), w=['osumA'])
            P.add('act', lambda e, obB=obB: e.activation(out=osum[64:128, :], in_=psf[obB][64:128, :], func=AF.Copy), r=pstok(obB, 0, 2048), w=['osumB'])
            P.add('act', lambda e: e.activation(out=osq, in_=osum, func=AF.Square), r=['osumA', 'osumB'], w=['osq'])
            nb_ = pr['i'] % 4
            pr['i'] += 1
            P.add('pe', lambda e, nb_=nb_: e.matmul(psf[nb_][:, :], lhsT=onesbd, rhs=osq, start=True, stop=True),
                  r=['osq', 'onesbd'], w=pstok(nb_, 0, 2048))
            if os.environ.get('HSTOP') == 'p2c':
                continue
            P.add('act', lambda e, nb_=nb_: e.activation(out=rs8, in_=psf[nb_][:, :], func=AF.Ln, bias=eps256[:, 0:1]),
                  r=pstok(nb_, 0, 2048) + ['eps256'], w=['rs8a'])
            P.add('act', lambda e: e.activation(out=rs8, in_=rs8, func=AF.Exp, scale=-0.5), r=['rs8a'], w=['rs8'])
            P.add('dve', lambda e: e.tensor_tensor(out=osum, in0=osum, in1=rs8, op=ALU.mult), r=['osumA', 'osumB', 'rs8'], w=['osn'])
            hr = tb % 2
            P.add('dve', lambda e, sl=sl, hr=hr: e.scalar_tensor_tensor(out=houts[hr], in0=osum, scalar=onw4[:, 0:1], in1=gate[:, sl],
                                                                op0=ALU.mult, op1=ALU.mult),
                  r=['osn', 'onw4', ('gate', tb)], w=[('hout', hr)])
            P.add('sp', lambda e, sl=sl, hr=hr: e.dma_start(allow_slow_non_contiguous=True, out=mix_s[4 + hp, :, sl], in_=houts[hr]),
                  r=[('hout', hr)], w=[('mix_s', 4 + hp, tb)])
        state['off'] = base

    def phase_B1(t_base, S):
        base = state['off']
        wo = alloc([KC, D], BF16, "wo")
        wpost = alloc([D], F32, "wpost")
        P.add('sp', lambda e: e.dma_start(allow_slow_non_contiguous=True, out=wpost, in_=bass.AP(vec["norm_mix_post"], 0, [[0, 128], [1, D]])), w=['wpost'])
        mt = [alloc([KC, 512], BF16, "mt%d" % i) for i in range(2)]
        xt = [alloc([D], F32, "xt%d" % i) for i in range(4)]
        ht = [alloc([D], F32, "ht%d" % i) for i in range(4)]
        tm = [alloc([D], F32, "tm%d" % i) for i in range(4)]
        hb_ = [alloc([D], BF16, "hb%d" % i) for i in range(2)]
        junk = alloc([D], BF16, "junk")
        ss = [alloc([4], F32, "ss%d" % i) for i in range(4)]
        P.add('sp', lambda e: e.dma_start(allow_slow_non_contiguous=True, out=wo, in_=wout_b.rearrange("(k p) c -> p k c", p=128)), r=['wscr'], w=['wo'])
        P.add('pool', lambda e: e.memset(XT[:, :, S + 1:S + 2], 0.0), w=['xhalo2'])
        NJ = S // 128

        ld = {'x': 0, 'm': 0}

        def b1_loads(j_upto, g_upto):
            while ld['m'] < min(g_upto, S // 512):
                g_ = ld['m']
                P.add('sp', lambda e, g_=g_: e.dma_start(allow_slow_non_contiguous=True, out=mt[g_ % 2], in_=mix_s[:, :, g_ * 512:(g_ + 1) * 512].rearrange("k p t -> p k t")),
                      r=[], w=[('mt', g_ % 2)])
                ld['m'] += 1
            while ld['x'] < min(j_upto, NJ):
                j_ = ld['x']
                P.add('sp', lambda e, j_=j_: e.dma_start(allow_slow_non_contiguous=True, out=xt[j_ % 4], in_=xs[t_base + j_ * 128: t_base + (j_ + 1) * 128, :]),
                      w=[('xt', j_ % 4)])
                ld['x'] += 1

        def b1a(j):
            g, tt = j // 4, j % 4
            m_ = mt[g % 2]
            b1_loads(j + 3, g + 2 if tt >= 2 else g + 1)
            r_ = j % 4
            x_, h_, t_, s_ = xt[r_], ht[r_], tm[r_], ss[r_]
            nm = 'b1_%d' % r_
            for half in range(2):
                bank = (j % 2) * 2 + half
                hs = slice(half * 512, (half + 1) * 512)
                for kc in range(KC):
                    P.add('pe', lambda e, kc=kc, hs=hs, bank=bank: e.matmul(
                        psf[bank][:, :], lhsT=m_[:, kc, tt * 128:(tt + 1) * 128], rhs=wo[:, kc, hs], start=(kc == 0), stop=(kc == KC - 1)),
                        r=[('mt', g % 2), 'wo'], w=pstok(bank))
                P.add('act', lambda e, bank=bank, half=half: e.activation(out=junk[:, 0:512], in_=psf[bank][:, :], func=AF.Square,
                                                                           accum_out=s_[:, half:half + 1]),
                      r=pstok(bank), w=['junk', (nm, 'p', half)])
                P.add('act', lambda e, bank=bank, hs=hs: e.activation(out=t_[:, hs], in_=psf[bank][:, :], func=AF.Copy),
                      r=pstok(bank), w=[('tm', r_, half)])
            P.add('dve', lambda e: e.tensor_tensor(out=s_[:, 2:3], in0=s_[:, 0:1], in1=s_[:, 1:2], op=ALU.add),
                  r=[(nm, 'p', 0), (nm, 'p', 1)], w=[nm + 'ssq'])
            rstd_from_ssq(s_[:, 2:3], s_[:, 3:4], D, EPS, nm)
            P.add('dve', lambda e: e.scalar_tensor_tensor(out=t_, in0=t_, scalar=s_[:, 3:4], in1=wpost, op0=ALU.mult, op1=ALU.mult),
                  r=[('tm', r_, 0), ('tm', r_, 1), nm + 'rstd', 'wpost'], w=[('tm', r_, 0), ('tm', r_, 1)])
            P.add('dve', lambda e: e.tensor_tensor(out=h_, in0=t_, in1=x_, op=ALU.add),
                  r=[('tm', r_, 0), ('tm', r_, 1), ('xt', r_)], w=[('ht', r_)])
            P.add('sp', lambda e: e.dma_start(allow_slow_non_contiguous=True, out=ys[t_base + j * 128: t_base + (j + 1) * 128, :], in_=h_),
                  r=[('ht', r_)], w=[('ys', j)])
            nm2 = 'b1n_%d' % r_
            P.add('act', lambda e: e.activation(out=junk, in_=h_, func=AF.Square, accum_out=s_[:, 0:1]),
                  r=[('ht', r_)], w=['junk', nm2 + 'ssq'])
            rstd_from_ssq(s_[:, 0:1], s_[:, 1:2], D, EPS, nm2)

        def b1b(j):
            r_ = j % 4
            h_, s_, b_ = ht[r_], ss[r_], hb_[j % 2]
            nm2 = 'b1n_%d' % r_
            P.add('act', lambda e: e.activation(out=b_, in_=h_, func=AF.Copy, scale=s_[:, 1:2]),
                  r=[('ht', r_), nm2 + 'rstd'], w=[('hb', j % 2)])

        def b1c(j):
            b_ = hb_[j % 2]
            bank = 6 + (j % 2)
            for kc in range(KC):
                P.add('pe', lambda e, kc=kc: e.transpose(out=psb[bank][:, kc * 128:(kc + 1) * 128],
                                                       in_=b_[:, kc * 128:(kc + 1) * 128], identity=ident),
                      r=[('hb', j % 2), 'ident'], w=pstok(bank))
            P.add('dve', lambda e: e.tensor_tensor(
                out=XT[:, :, 1 + j * 128: 1 + (j + 1) * 128], in0=psb[bank][:, :].rearrange("p (k t) -> p k t", k=KC),
                in1=bcast_free(wfpre, 128), op=ALU.mult),
                r=pstok(bank) + ['wfpre'], w=[('XT', kc, j) for kc in range(KC)])

        for j in range(NJ + 2):
            if j < NJ:
                b1a(j)
            if 0 <= j - 1 < NJ:
                b1b(j - 1)
            if 0 <= j - 2 < NJ:
                b1c(j - 2)
        state['off'] = base

    def phase_B2(t_base, S):
        base = state['off']
        wfpost = alloc([D], F32, "wfpost")
        P.add('sp', lambda e: e.dma_start(allow_slow_non_contiguous=True, out=wfpost, in_=bass.AP(vec["norm_ffn_post"], 0, [[0, 128], [1, D]])), w=['wfpost'])
        wg = [alloc([KC, 128], BF16, "wg%d" % i) for i in range(6)]
        wu = [alloc([KC, 128], BF16, "wu%d" % i) for i in range(6)]
        wd = [alloc([512], BF16, "wd%d" % i) for i in range(16)]
        hid = alloc([NFC, 512], BF16, "hid")
        Asb = [alloc([514], F32, "Asb%d" % i) for i in range(5)]
        cc = [alloc([512], F32, "cc%d" % i) for i in range(5)]
        c2 = [alloc([512], F32, "c2%d" % i) for i in range(5)]
        c3 = [alloc([512], F32, "c3%d" % i) for i in range(5)]
        fsb = alloc([4, D], F32, "fsb")
        ht = [alloc([D], F32, "ht%d" % i) for i in range(4)]
        junk = alloc([512], BF16, "junk")
        ss = alloc([4, 4], F32, "ssB")
        wi = {'g': 0, 'd': 0}
        NB = S // 512
        pf = {'g': 0, 'd': 0}

        def prefetch(g_upto, d_upto):
            while pf['g'] < min(g_upto, NB * NFC):
                g = pf['g']
                fc_, r3_ = g % NFC, g % 6
                P.add('sp', lambda e, fc_=fc_, r3_=r3_: e.dma_start(allow_slow_non_contiguous=True, out=wg[r3_], in_=wg_b[fc_]), r=['wscr'], w=[('wg', r3_)])
                P.add('sp', lambda e, fc_=fc_, r3_=r3_: e.dma_start(allow_slow_non_contiguous=True, out=wu[r3_], in_=wu_b[fc_]), r=['wscr'], w=[('wu', r3_)])
                pf['g'] += 1
            while pf['d'] < min(d_upto, NB * 2 * NFC):
                dd = pf['d']
                fc_, half_, r4_ = dd % NFC, (dd // NFC) % 2, dd % 16
                P.add('sp', lambda e, fc_=fc_, half_=half_, r4_=r4_: e.dma_start(
                    allow_slow_non_contiguous=True, out=wd[r4_], in_=wd_b[fc_ * 128:(fc_ + 1) * 128, half_ * 512:(half_ + 1) * 512]),
                    r=['wscr'], w=[('wd', r4_)])
                pf['d'] += 1

        for blk in range(NB):
            t0 = blk * 512
            xtoks = lambda kc: [('XT', kc, blk * 4 + q) for q in range(4)]
            def st1(fc, blk=blk, t0=t0):
                r3 = wi['g'] % 6
                wi['g'] += 1
                r2 = fc % 5
                prefetch(wi['g'] + 4, wi['d'] + (10 if fc >= NFC - 6 else 0))
                gb_ = (0, 1)[fc % 2]
                ub_ = (2, 3, 6, 7)[fc % 4]
                hbk = (4, 5)[fc % 2]
                xtoks = lambda kc: [('XT', kc, blk * 4 + q) for q in range(4)]
                for kc in range(KC):
                    P.add('pe', lambda e, kc=kc: e.matmul(psf[gb_][:, :], lhsT=wg[r3][:, kc, :],
                                                          rhs=XT[:, kc, 1 + t0: 1 + t0 + 512], start=(kc == 0), stop=(kc == KC - 1)),
                          r=[('wg', r3)] + xtoks(kc), w=pstok(gb_))
                halo_r = ['xhalo', 'xhalo2'] + [('XT', kc, q) for kc in range(KC) for q in (max(blk * 4 - 1, 0), min(blk * 4 + 4, S // 128 - 1))]
                for kc in range(KC):
                    P.add('pe', lambda e, kc=kc: e.matmul(psf[hbk][:, 0:2], lhsT=wg[r3][:, kc, :],
                                                          rhs=XT[:, kc, t0: t0 + 514: 513], start=(kc == 0), stop=(kc == KC - 1)),
                          r=[('wg', r3)] + halo_r, w=pstok(hbk))
                for kc in range(KC):
                    P.add('pe', lambda e, kc=kc: e.matmul(psf[ub_][:, :], lhsT=wu[r3][:, kc, :],
                                                          rhs=XT[:, kc, 1 + t0: 1 + t0 + 512], start=(kc == 0), stop=(kc == KC - 1)),
                          r=[('wu', r3)] + xtoks(kc), w=pstok(ub_))
                A_, c_ = Asb[r2], cc[r2]
                P.add('act', lambda e: e.activation(out=A_[:, 1:513], in_=psf[gb_][:, :], func=AF.Copy),
                      r=pstok(gb_), w=[('Asb', r2, 0)])
                P.add('act', lambda e: e.activation(out=A_[:, 0:514:513], in_=psf[hbk][:, 0:2], func=AF.Copy),
                      r=pstok(hbk), w=[('Asb', r2, 1)])
                P.add('act', lambda e: e.activation(out=c_, in_=A_[:, 1:513], func=AF.Identity, scale=cw[:, 1, fc:fc + 1],
                                                    bias=cb[:, fc:fc + 1]),
                      r=[('Asb', r2, 0), 'cw', 'cb'], w=[('cc', r2)])

            def st2(fc):
                r2 = fc % 5
                A_, c_, c2_ = Asb[r2], cc[r2], c2[r2]
                P.add('dve', lambda e: e.scalar_tensor_tensor(out=c_, in0=A_[:, 0:512], scalar=cw[:, 0, fc:fc + 1], in1=c_,
                                                              op0=ALU.mult, op1=ALU.add),
                      r=[('Asb', r2, 0), ('Asb', r2, 1), 'cw', ('cc', r2)], w=[('cc', r2)])
                P.add('dve', lambda e: e.scalar_tensor_tensor(out=c_, in0=A_[:, 2:514], scalar=cw[:, 2, fc:fc + 1], in1=c_,
                                                              op0=ALU.mult, op1=ALU.add),
                      r=[('Asb', r2, 0), ('Asb', r2, 1), 'cw', ('cc', r2)], w=[('cc', r2)])
                P.add('act', lambda e: e.activation(out=c2_, in_=c_, func=AF.Square, scale=0.21145921592590237),
                      r=[('cc', r2)], w=[('c2', r2)])

            def st3(fc):
                r2 = fc % 5
                c_, c2_, c3_ = cc[r2], c2[r2], c3[r2]
                P.add('dve', lambda e: e.scalar_tensor_tensor(out=c3_, in0=c2_, scalar=1.0, in1=c_, op0=ALU.add, op1=ALU.mult),
                      r=[('c2', r2), ('cc', r2)], w=[('c3', r2)])
                P.add('act', lambda e: e.activation(out=c3_, in_=c3_, func=AF.Tanh, scale=0.7978845608028654),
                      r=[('c3', r2)], w=[('c3', r2)])

            def st4(fc):
                r2 = fc % 5
                ub_ = (2, 3, 6, 7)[fc % 4]
                c_, c2_, c3_ = cc[r2], c2[r2], c3[r2]
                P.add('dve', lambda e: e.scalar_tensor_tensor(out=c2_, in0=c3_, scalar=1.0, in1=c_, op0=ALU.add, op1=ALU.mult),
                      r=[('c3', r2), ('cc', r2), ('c2', r2)], w=[('c2', r2)])
                P.add('dve', lambda e: e.tensor_tensor(out=hid[:, fc, :], in0=psf[ub_][:, :], in1=c2_, op=ALU.mult),
                      r=pstok(ub_) + [('c2', r2)], w=[('hid', fc)])

            for it in range(NFC + 3):
                if it < NFC:
                    st1(it)
                if 0 <= it - 1 < NFC:
                    st2(it - 1)
                if 0 <= it - 2 < NFC:
                    st3(it - 2)
                if 0 <= it - 3 < NFC:
                    st4(it - 3)
            for tt in range(4):
                j = blk * 4 + tt
                P.add('sp', lambda e, j=j: e.dma_start(allow_slow_non_contiguous=True, out=ht[j % 4], in_=ys[t_base + j * 128: t_base + (j + 1) * 128, :]),
                      r=[('ys', j)], w=[('ht', j % 4)])
            for half in range(2):
                hs = slice(half * 512, (half + 1) * 512)
                for fc in range(NFC):
                    r4 = wi['d'] % 16
                    wi['d'] += 1
                    prefetch(wi['g'] + (5 if (half == 1 and fc >= NFC - 8) else 0), wi['d'] + 11)
                    for tt in range(4):
                        P.add('pe', lambda e, fc=fc, r4=r4, tt=tt: e.matmul(psf[4 + tt][:, :], lhsT=hid[:, fc, tt * 128:(tt + 1) * 128], rhs=wd[r4],
                                                                          start=(fc == 0), stop=(fc == NFC - 1)),
                              r=[('hid', fc), ('wd', r4)], w=pstok(4 + tt))
                for tt in range(4):
                    P.add('act', lambda e, tt=tt, half=half: e.activation(out=junk, in_=psf[4 + tt][:, :], func=AF.Square,
                                                                        accum_out=ss[:, tt, half:half + 1]),
                          r=pstok(4 + tt), w=['junk', ('ssB', tt, half)])
                    P.add('act', lambda e, tt=tt, hs=hs: e.activation(out=fsb[:, tt, hs], in_=psf[4 + tt][:, :], func=AF.Copy),
                          r=pstok(4 + tt), w=[('fsb', tt, half)])
            for tt in range(4):
                j = blk * 4 + tt
                r_ = j % 4
                h_ = ht[r_]
                nm = 'b2_%d' % tt
                P.add('dve', lambda e, tt=tt: e.tensor_tensor(out=ss[:, tt, 2:3], in0=ss[:, tt, 0:1], in1=ss[:, tt, 1:2], op=ALU.add),
                      r=[('ssB', tt, 0), ('ssB', tt, 1)], w=[nm + 'ssq'])
                rstd_from_ssq(ss[:, tt, 2:3], ss[:, tt, 3:4], D, 4.0 * EPS, nm)
                P.add('dve', lambda e, tt=tt: e.scalar_tensor_tensor(out=fsb[:, tt, :], in0=fsb[:, tt, :], scalar=ss[:, tt, 3:4], in1=wfpost,
                                                                   op0=ALU.mult, op1=ALU.mult),
                      r=[('fsb', tt, 0), ('fsb', tt, 1), nm + 'rstd', 'wfpost'], w=[('fsb', tt, 0), ('fsb', tt, 1)])
                P.add('dve', lambda e, tt=tt, h_=h_: e.tensor_tensor(out=h_, in0=h_, in1=fsb[:, tt, :], op=ALU.add),
                      r=[('fsb', tt, 0), ('fsb', tt, 1), ('ht', r_)], w=[('ht', r_)])
                P.add('sp', lambda e, j=j, h_=h_: e.dma_start(allow_slow_non_contiguous=True, out=ys[t_base + j * 128: t_base + (j + 1) * 128, :], in_=h_),
                      r=[('ht', r_)], w=[('ys', j)])
        state['off'] = base

    setup()
    weight_prep()
    P.barrier()
    t_base = 0
    for S in seq_lens:
        phase_A0(t_base, S, branches)
        P.barrier()
        if dbg_xt is not None and t_base == 0:
            P.add('sp', lambda e, S=S: e.dma_start(out=dbg_xt[:, :, 0:S + 1], in_=XT[:, :, 0:S + 1]), w=['dbgxt'])
        for hp in range(4):
            phase_attn(S, hp, branches)
            P.barrier()
        if stop_after == 'attn':
            break
        for hp in range(4):
            phase_hgrn(S, hp)
            P.barrier()
            if stop_after == 'hgrn0':
                break
        if stop_after == 'hgrn0':
            break
        phase_B1(t_base, S)
        P.barrier()
        phase_B2(t_base, S)
        P.barrier()
        t_base += S

    with nc.Block() as block:
        P.emit(nc, block, sems)
    es.close()
    return nc


def rot_tables(SM):
    half = 8
    inv = ROPE_THETA ** (-np.arange(half, dtype=np.float32) * 2.0 / 16.0)
    ang = np.arange(SM, dtype=np.float32)[:, None] * inv[None, :]
    cos = np.cos(ang).astype(np.float32).T
    sin = np.sin(ang).astype(np.float32).T
    c = np.ones((128, SM), np.float32)
    s = np.zeros((128, SM), np.float32)
    for hb in (0, 64):
        c[hb:hb + 8] = cos
        c[hb + 8:hb + 16] = cos
        s[hb:hb + 8] = -sin
        s[hb + 8:hb + 16] = sin
    return c, s


_CACHE = {}


def kernel(x_prompt, x_sample, norm_mix_pre, w_in, hgrn_lb_fwd, hgrn_lb_bwd, hgrn_out_norm, w_out,
           norm_mix_post, norm_ffn_pre, w_gate, w_up, conv_w, conv_b, w_down, norm_ffn_post):
    n = 8
    x_prompt = np.asarray(x_prompt)
    x_sample = np.asarray(x_sample)
    Bp, Sp, _ = x_prompt.shape
    Bs, Ss, _ = x_sample.shape
    pp, sp_ = Bp // n, Bs // n
    seq_lens = tuple([Sp] * pp + [Ss] * sp_)
    if seq_lens not in _CACHE:
        _CACHE[seq_lens] = build(seq_lens)
    nc = _CACHE[seq_lens]
    rc, rs = rot_tables(max(seq_lens))
    f = lambda a: np.ascontiguousarray(np.asarray(a, dtype=np.float32))
    common = {
        "w_in": f(w_in)[0], "w_out": f(w_out)[0], "w_gate": f(w_gate)[0], "w_up": f(w_up)[0], "w_down": f(w_down)[0],
        "norm_mix_pre": f(norm_mix_pre)[0], "norm_mix_post": f(norm_mix_post)[0], "norm_ffn_pre": f(norm_ffn_pre)[0],
        "norm_ffn_post": f(norm_ffn_post)[0], "conv_b": f(conv_b)[0], "hgrn_out_norm": f(hgrn_out_norm)[0],
        "conv_w": f(conv_w)[0], "hgrn_lb_fwd": f(hgrn_lb_fwd), "hgrn_lb_bwd": f(hgrn_lb_bwd),
        "rot_c": rc, "rot_s": rs,
    }
    in_maps = []
    for c in range(n):
        xs = np.concatenate([x_prompt[c * pp:(c + 1) * pp].reshape(-1, D), x_sample[c * sp_:(c + 1) * sp_].reshape(-1, D)], axis=0)
        m = dict(common)
        m["xs"] = np.ascontiguousarray(xs, dtype=np.float32)
        in_maps.append(m)
    res = run_bass_kernel_spmd(nc, in_maps, core_ids=list(range(n)))
    yp = np.empty((Bp, Sp, D), np.float32)
    ysm = np.empty((Bs, Ss, D), np.float32)
    for c in range(n):
        y = res.results[c]["ys"]
        yp[c * pp:(c + 1) * pp] = y[:pp * Sp].reshape(pp, Sp, D)
        ysm[c * sp_:(c + 1) * sp_] = y[pp * Sp:].reshape(sp_, Ss, D)
    return (yp, ysm)
```

```python
import os
import numpy as np
import ml_dtypes
import concourse.bass as bass
import concourse.mybir as mybir
from concourse.bass_utils import run_bass_kernel_spmd

F32 = mybir.dt.float32
BF16 = mybir.dt.bfloat16
AF = mybir.ActivationFunctionType
ALU = mybir.AluOpType

D = 1024
KC = 8
INW = 4096
DFF = 2816
NFC = 22
EPS = 1e-6
ROPE_THETA = 500000.0
BRANCHES = (1, 4, 16)
CH = 64
COMPUTE = ('pe', 'act', 'dve', 'pool')


class Prog:
    def __init__(self, ndma=12):
        self.ndma = ndma
        self.lists = {e: [] for e in COMPUTE + ('sp',)}
        self.bystream = {}
        self.tok = {}
        self.vc = {e: {} for e in COMPUTE + ('sp',)}
        self.dma_n = 0
        self.pending_barrier = {}

    def barrier(self):
        deps = set()
        for s, ops in self.bystream.items():
            if ops:
                deps.add((s, len(ops) - 1))
        for e in self.lists:
            self.pending_barrier[e] = set(deps)
        self.tok = {}

    def add(self, eng, fn, r=(), w=()):
        if eng == 'sp':
            stream = 'd%d' % (self.dma_n % self.ndma)
            self.dma_n += 1
        else:
            stream = eng
        slist = self.bystream.setdefault(stream, [])
        sidx = len(slist)
        deps = set()
        if eng in self.pending_barrier:
            deps |= self.pending_barrier.pop(eng)
        for k in r:
            st = self.tok.get(k)
            if st is not None and st[0] is not None:
                deps.add(st[0])
        for k in w:
            st = self.tok.get(k)
            if st is not None:
                if st[0] is not None:
                    deps.add(st[0])
                deps.update(st[1])
        if eng == 'sp' and sidx > 0:
            deps.add((stream, sidx - 1))
        vc = self.vc[eng]
        waits = {}
        for (s, i) in deps:
            if s == 'pe' and eng == 'pe':
                continue
            if vc.get(s, -1) < i:
                if waits.get(s, -1) < i:
                    waits[s] = i
        for s, i in waits.items():
            dop = self.bystream[s][i]
            dop['sig'] = True
            for s2, i2 in dop['vc'].items():
                if vc.get(s2, -1) < i2:
                    vc[s2] = i2
            if vc.get(s, -1) < i:
                vc[s] = i
        ovc = dict(vc)
        ovc[stream] = sidx
        op = dict(eng=eng, fn=fn, stream=stream, sidx=sidx, waits=waits, sig=False, vc=ovc)
        slist.append(op)
        self.lists[eng].append(op)
        me = (stream, sidx)
        for k in r:
            st = self.tok.setdefault(k, [None, []])
            st[1].append(me)
        for k in w:
            self.tok[k] = [me, []]
        return op

    def emit(self, nc, block, sems):
        counts = {}
        for s in COMPUTE:
            c = 0
            arr = []
            for op in self.bystream.get(s, []):
                if op['sig']:
                    c += 1
                arr.append(c)
            counts[s] = arr

        def val(s, i):
            if s in COMPUTE:
                return counts[s][i]
            return 16 * (i + 1)

        def run(e, ename):
            for op in self.lists[ename]:
                for s, i in op['waits'].items():
                    e.wait_ge(sems[s], val(s, i))
                ins = op['fn'](e)
                if ename == 'sp':
                    ins.then_inc(sems[op['stream']], 16)
                elif op['sig']:
                    ins.then_inc(sems[ename], 1)
            if ename == 'sp':
                for s, ops in self.bystream.items():
                    if s not in COMPUTE and ops:
                        e.wait_ge(sems[s], 16 * len(ops))

        @block.sync
        def _(e):
            run(e, 'sp')

        @block.tensor
        def _(e):
            run(e, 'pe')

        @block.scalar
        def _(e):
            run(e, 'act')

        @block.vector
        def _(e):
            run(e, 'dve')

        @block.gpsimd
        def _(e):
            run(e, 'pool')


def bcast_free(ap, n):
    return bass.AP(ap.tensor, ap.offset, [list(x) for x in ap.ap] + [[0, n]])


def bcast_mid(ap, n):
    l = [list(x) for x in ap.ap]
    return bass.AP(ap.tensor, ap.offset, [l[0], [0, n]] + l[1:])


def sst(lo, n, d):
    return slice(lo, lo + (n - 1) * d + 1, d)


def pstok(bank, lo=0, hi=0):
    return [('ps', bank)]


DEBUG_OFFS = {}


def build(seq_lens, branches=BRANCHES, stop_after=None):
    nc = bass.Bass("TRN2", target_bir_lowering=False)
    NT = sum(seq_lens)
    SM = max(seq_lens)
    dt = nc.dram_tensor
    xs = dt("xs", [NT, D], F32, kind="ExternalInput").ap()
    ys = dt("ys", [NT, D], F32, kind="ExternalOutput").ap()
    w_in = dt("w_in", [D, INW], F32, kind="ExternalInput").ap()
    w_out = dt("w_out", [D, D], F32, kind="ExternalInput").ap()
    w_gate = dt("w_gate", [D, DFF], F32, kind="ExternalInput").ap()
    w_up = dt("w_up", [D, DFF], F32, kind="ExternalInput").ap()
    w_down = dt("w_down", [DFF, D], F32, kind="ExternalInput").ap()
    vec = {}
    for nm, n in (("norm_mix_pre", D), ("norm_mix_post", D), ("norm_ffn_pre", D), ("norm_ffn_post", D),
                  ("conv_b", DFF), ("hgrn_out_norm", 64)):
        vec[nm] = dt(nm, [n], F32, kind="ExternalInput")
    conv_w = dt("conv_w", [3, DFF], F32, kind="ExternalInput")
    lbf = dt("hgrn_lb_fwd", [2, 512], F32, kind="ExternalInput")
    lbb = dt("hgrn_lb_bwd", [2, 512], F32, kind="ExternalInput")
    rotc_d = dt("rot_c", [128, SM], F32, kind="ExternalInput").ap()
    rots_d = dt("rot_s", [128, SM], F32, kind="ExternalInput").ap()
    win_b = dt("win_b", [D, INW], BF16, kind=("ExternalOutput" if os.environ.get("KDEBUG") else "Internal")).ap()
    winsw_b = dt("winsw_b", [D, 1024], BF16, kind="Internal").ap()
    wout_b = dt("wout_b", [D, D], BF16, kind="Internal").ap()
    wg_b = dt("wg_b", [NFC, 128, KC, 128], BF16, kind="Internal").ap()
    wu_b = dt("wu_b", [NFC, 128, KC, 128], BF16, kind="Internal").ap()
    wd_b = dt("wd_b", [DFF, D], BF16, kind="Internal").ap()
    mix_s = dt("mix_s", [KC, 128, SM], BF16, kind=("ExternalOutput" if os.environ.get("KDEBUG") else "Internal")).ap()

    dbg_xt = dt("dbg_xt", [128, KC, SM + 2], BF16, kind="ExternalOutput").ap() if os.environ.get("KDEBUG") else None
    P = Prog()
    from contextlib import ExitStack
    es = ExitStack()
    ARF = 53200
    arena = es.enter_context(nc.sbuf_tensor("arena", [128, ARF], F32))
    arena_b = arena.bitcast(BF16)
    psf = [es.enter_context(nc.psum_tensor("ps%d" % i, [128, 512], F32)) for i in range(8)]
    psb = [p.bitcast(BF16) for p in psf]
    sems = {}
    for s in list(COMPUTE) + ['d%d' % i for i in range(P.ndma)]:
        sems[s] = es.enter_context(nc.semaphore("sem_" + s))

    state = {'off': 0, 'uid': 0}

    def alloc(shape, dtype, name):
        n = 1
        for s_ in shape:
            n *= s_
        esz = 4 if dtype == F32 else 2
        off = (state['off'] + 31) // 32 * 32
        state['off'] = off + n * esz
        assert state['off'] <= ARF * 4, ("SBUF arena overflow", name, state['off'])
        DEBUG_OFFS[name] = (off, list(shape), 'f32' if dtype == F32 else 'bf16')
        base = arena if dtype == F32 else arena_b
        o = off // esz
        v = base[:, o:o + n]
        if len(shape) == 2:
            v = v.rearrange("p (a b) -> p a b", a=shape[0])
        elif len(shape) == 3:
            v = v.rearrange("p (a b c) -> p a b c", a=shape[0], b=shape[1])
        return v

    XT = alloc([KC, SM + 2], BF16, "xnT")
    ident = alloc([128], BF16, "ident")
    maskA = alloc([256], BF16, "maskA")
    maskMB = alloc([256], BF16, "maskMB")
    maskF = alloc([64], BF16, "maskF")
    maskB = alloc([64], BF16, "maskB")
    onesbd = alloc([128], F32, "onesbd")
    esel = alloc([64], F32, "esel")
    cneg = alloc([1], F32, "cneg")
    eps256 = alloc([1], F32, "eps256")
    wpre = alloc([KC], F32, "wpre")
    wfpre = alloc([KC], F32, "wfpre")
    cw = alloc([3, NFC], F32, "cw")
    cb = alloc([NFC], F32, "cb")
    onw4 = alloc([1], F32, "onw4")
    lbt = alloc([2, 2, 4], F32, "lbt")
    ga = alloc([2, 4], F32, "ga")
    gb = alloc([2, 4], F32, "gb")
    gna = alloc([2, 4], F32, "gna")
    gnb = alloc([2, 4], F32, "gnb")
    state['off'] += int(os.environ.get('KPAD', '0'))
    PERSIST = state['off']

    def setup():
        P.add('pool', lambda e: e.memset(ident, 0.0), w=['ident'])
        P.add('pool', lambda e: e.affine_select(out=ident, in_=ident, pattern=[[-1, 128]], compare_op=ALU.not_equal,
                                                 fill=1.0, base=0, channel_multiplier=1), r=['ident'], w=['ident'])
        P.add('pool', lambda e: e.memset(maskA, 1.0), w=['maskA'])
        P.add('pool', lambda e: e.affine_select(out=maskA, in_=maskA, pattern=[[1, 256]], compare_op=ALU.is_ge,
                                                 fill=0.0, base=0, channel_multiplier=-1), r=['maskA'], w=['maskA'])
        P.add('pool', lambda e: e.affine_select(out=maskA, in_=maskA, pattern=[[-1, 256]], compare_op=ALU.is_ge,
                                                 fill=0.0, base=128, channel_multiplier=1), r=['maskA'], w=['maskA'])
        P.add('dve', lambda e: e.tensor_scalar(out=maskMB, in0=maskA, scalar1=-1.0, scalar2=30000.0, op0=ALU.add, op1=ALU.mult),
              r=['maskA'], w=['maskMB'])
        P.add('pool', lambda e: e.memset(maskF[0:64, :], 1.0), w=['maskF'])
        P.add('pool', lambda e: e.affine_select(out=maskF[0:64, :], in_=maskF[0:64, :], pattern=[[1, 64]], compare_op=ALU.is_ge,
                                                 fill=0.0, base=0, channel_multiplier=-1), r=['maskF'], w=['maskF'])
        P.add('pool', lambda e: e.memset(maskB[0:64, :], 1.0), w=['maskB'])
        P.add('pool', lambda e: e.affine_select(out=maskB[0:64, :], in_=maskB[0:64, :], pattern=[[-1, 64]], compare_op=ALU.is_ge,
                                                 fill=0.0, base=0, channel_multiplier=1), r=['maskB'], w=['maskB'])
        P.add('pool', lambda e: e.memset(onesbd, 0.0), w=['onesbd'])
        P.add('pool', lambda e: e.memset(onesbd[0:64, 0:64], 1.0), r=['onesbd'], w=['onesbd'])
        P.add('pool', lambda e: e.memset(onesbd[64:128, 64:128], 1.0), r=['onesbd'], w=['onesbd'])
        P.add('pool', lambda e: e.memset(esel[0:65, :], 0.0), w=['esel'])
        P.add('pool', lambda e: e.memset(esel[64:65, :], 1.0), r=['esel'], w=['esel'])
        P.add('pool', lambda e: e.memset(cneg, -0.5), w=['cneg'])
        P.add('pool', lambda e: e.memset(eps256, 256.0 * EPS), w=['eps256'])
        P.add('pool', lambda e: e.memset(XT[:, :, 0:1], 0.0), w=['xhalo'])
        with nc.allow_non_contiguous_dma(reason="tiny per-feature vectors"):
            P.add('sp', lambda e: e.dma_start(allow_slow_non_contiguous=True, out=wpre, in_=vec["norm_mix_pre"].ap().rearrange("(k p) -> p k", p=128)), w=['wpre'])
            P.add('sp', lambda e: e.dma_start(allow_slow_non_contiguous=True, out=wfpre, in_=vec["norm_ffn_pre"].ap().rearrange("(k p) -> p k", p=128)), w=['wfpre'])
            P.add('sp', lambda e: e.dma_start(allow_slow_non_contiguous=True, out=cw, in_=conv_w.ap().rearrange("w (f p) -> p w f", p=128)), w=['cw'])
            P.add('sp', lambda e: e.dma_start(allow_slow_non_contiguous=True, out=cb, in_=vec["conv_b"].ap().rearrange("(f p) -> p f", p=128)), w=['cb'])
            P.add('sp', lambda e: e.dma_start(allow_slow_non_contiguous=True, out=onw4[0:64, :], in_=vec["hgrn_out_norm"].ap().rearrange("(p o) -> p o", o=1)), w=['onw4a'])
            P.add('sp', lambda e: e.dma_start(allow_slow_non_contiguous=True, out=onw4[64:128, :], in_=vec["hgrn_out_norm"].ap().rearrange("(p o) -> p o", o=1)), w=['onw4b'])
            P.add('sp', lambda e: e.dma_start(allow_slow_non_contiguous=True, out=lbt[:, 0, :, :], in_=lbf.ap().rearrange("s (c p) -> p s c", p=128)), w=['lbt0'])
            P.add('sp', lambda e: e.dma_start(allow_slow_non_contiguous=True, out=lbt[:, 1, :, :], in_=lbb.ap().rearrange("s (c p) -> p s c", p=128)), w=['lbt1'])
        P.add('dve', lambda e: e.tensor_scalar(out=onw4, in0=onw4, scalar1=4.0, scalar2=None, op0=ALU.mult),
              r=['onw4a', 'onw4b'], w=['onw4'])
        P.add('dve', lambda e: e.tensor_tensor(out=ga, in0=lbt[:, :, 0, :], in1=lbt[:, :, 1, :], op=ALU.subtract),
              r=['lbt0', 'lbt1'], w=['ga'])
        P.add('act', lambda e: e.activation(out=gb, in_=ga, func=AF.Tanh, scale=0.5), r=['ga'], w=['gb'])
        P.add('dve', lambda e: e.tensor_scalar(out=ga, in0=gb, scalar1=0.25, scalar2=0.75, op0=ALU.mult, op1=ALU.add),
              r=['gb'], w=['ga'])
        P.add('dve', lambda e: e.tensor_scalar(out=gna, in0=gb, scalar1=-0.25, scalar2=0.25, op0=ALU.mult, op1=ALU.add),
              r=['gb'], w=['gna'])
        P.add('dve', lambda e: e.tensor_scalar(out=gnb, in0=gb, scalar1=0.25, scalar2=-0.25, op0=ALU.mult, op1=ALU.add),
              r=['gb'], w=['gnb'])
        P.add('dve', lambda e: e.tensor_scalar(out=gb, in0=gb, scalar1=-0.25, scalar2=0.25, op0=ALU.mult, op1=ALU.add),
              r=['gb', 'gna', 'gnb'], w=['gb'])

    def weight_prep():
        base = state['off']
        st = [alloc([4096], F32, "wst%d" % i) for i in range(2)]
        bt = [alloc([4096], BF16, "wbt%d" % i) for i in range(2)]
        sw2 = alloc([1024], BF16, "wsw")
        sw = sw2.rearrange("p (h d) -> p h d", h=16)
        P.add('pool', lambda e: e.memset(sw2, 0.0), w=['wsw'])
        it = [0]

        def cast_rows(src, dst, ncols, sw_dst=None, dst_rearr=None):
            r_ = it[0] % 2
            it[0] += 1
            s_, b_ = st[r_], bt[r_]
            P.add('sp', lambda e: e.dma_start(allow_slow_non_contiguous=True, out=s_[:, 0:ncols], in_=src), w=[('wst', r_)])
            h1 = ncols // 2
            P.add('act', lambda e: e.activation(out=b_[:, 0:h1], in_=s_[:, 0:h1], func=AF.Copy), r=[('wst', r_)], w=[('wbt', r_, 0)])
            P.add('dve', lambda e: e.tensor_copy(out=b_[:, h1:ncols], in_=s_[:, h1:ncols]), r=[('wst', r_)], w=[('wbt', r_, 1)])
            if sw_dst is not None:
                sv = s_[:, 0:1024].rearrange("p (h d) -> p h d", h=16)
                P.add('pool', lambda e: e.tensor_copy(out=sw[:, :, 0:8], in_=sv[:, :, 8:16]), r=[('wst', r_)], w=['wsw'])
                P.add('pool', lambda e: e.tensor_copy(out=sw[:, :, 8:16], in_=sv[:, :, 0:8]), r=[('wst', r_)], w=['wsw'])
                P.add('sp', lambda e: e.dma_start(allow_slow_non_contiguous=True, out=sw_dst, in_=sw2), r=['wsw'], w=[('winsw_b', it[0])])
            if dst_rearr is None:
                P.add('sp', lambda e: e.dma_start(allow_slow_non_contiguous=True, out=dst, in_=b_[:, 0:ncols]), r=[('wbt', r_, 0), ('wbt', r_, 1)], w=[('wscr', it[0])])
            else:
                P.add('sp', lambda e: e.dma_start(allow_slow_non_contiguous=True, out=dst, in_=b_[:, 0:ncols].rearrange("p (f j) -> p f j", j=128)),
                      r=[('wbt', r_, 0), ('wbt', r_, 1)], w=[('wscr', it[0])])

        for kc in range(KC):
            rs = slice(kc * 128, (kc + 1) * 128)
            cast_rows(w_in[rs, :], win_b[rs, :], INW, sw_dst=winsw_b[rs, :])
        for kc in range(KC):
            rs = slice(kc * 128, (kc + 1) * 128)
            cast_rows(w_out[rs, :], wout_b[rs, :], D)
        with nc.allow_non_contiguous_dma(reason="chunked weight scratch, 256B segments, one-time"):
            for kc in range(KC):
                rs = slice(kc * 128, (kc + 1) * 128)
                cast_rows(w_gate[rs, :], wg_b[:, :, kc, :].rearrange("f p j -> p f j"), DFF, dst_rearr=True)
                cast_rows(w_up[rs, :], wu_b[:, :, kc, :].rearrange("f p j -> p f j"), DFF, dst_rearr=True)
        for fc in range(NFC):
            rs = slice(fc * 128, (fc + 1) * 128)
            cast_rows(w_down[rs, :], wd_b[rs, :], D)
        state['off'] = base

    pr = {'i': 0}

    def rstd_from_ssq(ssq, out, n, eps, name):
        P.add('dve', lambda e: e.tensor_scalar(out=out, in0=ssq, scalar1=1.0 / n, scalar2=eps, op0=ALU.mult, op1=ALU.add),
              r=[name + 'ssq'], w=[name + 'rstd'])
        P.add('pool', lambda e: e.tensor_tensor(out=out, in0=out, in1=cneg, op=ALU.pow),
              r=[name + 'rstd', 'cneg'], w=[name + 'rstd'])

    def phase_A0(t_base, S, B):
        base = state['off']
        xt = [alloc([D], F32, "xt%d" % i) for i in range(4)]
        xb = [alloc([D], BF16, "xb%d" % i) for i in range(2)]
        junk = alloc([D], BF16, "junk")
        ss = [alloc([2], F32, "ss%d" % i) for i in range(4)]
        NJ = S // 128

        def a1(j):
            r_ = j % 4
            x_, s_ = xt[r_], ss[r_]
            nm = 'a0_%d' % r_
            P.add('sp', lambda e, j=j, x_=x_: e.dma_start(allow_slow_non_contiguous=True, out=x_, in_=xs[t_base + j * 128: t_base + (j + 1) * 128, :]), w=[('xt', r_)])
            P.add('act', lambda e, x_=x_, s_=s_: e.activation(out=junk, in_=x_, func=AF.Square, accum_out=s_[:, 0:1]),
                  r=[('xt', r_)], w=['junk', nm + 'ssq'])
            rstd_from_ssq(s_[:, 0:1], s_[:, 1:2], D, EPS, nm)

        def a2(j):
            r_ = j % 4
            x_, s_, b_ = xt[r_], ss[r_], xb[j % 2]
            nm = 'a0_%d' % r_
            P.add('act', lambda e, x_=x_, s_=s_, b_=b_: e.activation(out=b_, in_=x_, func=AF.Copy, scale=s_[:, 1:2]),
                  r=[('xt', r_), nm + 'rstd'], w=[('xb', j % 2)])

        def a3(j):
            b_ = xb[j % 2]
            bank = 6 + (j % 2)
            for kc in range(KC):
                P.add('pe', lambda e, kc=kc, b_=b_, bank=bank: e.transpose(out=psb[bank][:, kc * 128:(kc + 1) * 128],
                                                                          in_=b_[:, kc * 128:(kc + 1) * 128], identity=ident),
                      r=[('xb', j % 2), 'ident'], w=pstok(bank))
            P.add('dve', lambda e, j=j, bank=bank: e.tensor_tensor(
                out=XT[:, :, 1 + j * 128: 1 + (j + 1) * 128],
                in0=psb[bank][:, :].rearrange("p (k t) -> p k t", k=KC),
                in1=bcast_free(wpre, 128), op=ALU.mult),
                r=pstok(bank) + ['wpre'], w=[('XT', kc, j) for kc in range(KC)])

        for j in range(NJ + 2):
            if j < NJ:
                a1(j)
            if 0 <= j - 1 < NJ:
                a2(j - 1)
            if 0 <= j - 2 < NJ:
                a3(j - 2)
        state['off'] = base

    def load_wA(cols_main, cols_sw, wA):
        i = 0
        with nc.allow_non_contiguous_dma(reason="weight column chunk, 256B segments"):
            for c in cols_main:
                P.add('sp', lambda e, c=c, i=i: e.dma_start(allow_slow_non_contiguous=True, out=wA[i], in_=win_b[:, c:c + 128].rearrange("(k p) j -> p k j", p=128)),
                      r=['wscr'], w=[('wA', i)])
                i += 1
            for c in cols_sw:
                P.add('sp', lambda e, c=c, i=i: e.dma_start(allow_slow_non_contiguous=True, out=wA[i], in_=winsw_b[:, c:c + 128].rearrange("(k p) j -> p k j", p=128)),
                      r=['winsw_b'], w=[('wA', i)])
                i += 1

    def proj(wA_i, tb, bank):
        for kc in range(KC):
            P.add('pe', lambda e, kc=kc: e.matmul(psf[bank][:, :], lhsT=wA_i[1][:, kc, :], rhs=XT[:, kc, 1 + tb * 512: 1 + (tb + 1) * 512],
                                                   start=(kc == 0), stop=(kc == KC - 1)),
                  r=[('wA', wA_i[0])] + [('XT', kc, tb * 4 + q) for q in range(4)], w=pstok(bank, 0, 2048))

    def phase_attn(S, hp, B):
        base = state['off']
        wA = [alloc([KC, 128], BF16, "wA%d" % i) for i in range(5)]
        rotc = [alloc([512], F32, "rotc%d" % i) for i in range(2)]
        rots = [alloc([512], F32, "rots%d" % i) for i in range(2)]
        qT = alloc([S], BF16, "qT")
        kT = alloc([S], BF16, "kT")
        vT = alloc([S], BF16, "vT")
        NTL = S // 128
        vtok = [alloc([NTL, 2, 65], BF16, "vtok%d" % b) for b in range(len(B))]
        tA = [alloc([512], F32, "tA%d" % i) for i in range(2)]
        tB = [alloc([512], F32, "tB%d" % i) for i in range(2)]
        praw = [alloc([256], BF16, "praw%d" % i) for i in range(4)]
        pmk = [alloc([256], BF16, "pmk%d" % i) for i in range(4)]
        UT = alloc([S], F32, "UT")
        rrow = alloc([512], F32, "rrow")
        aout = [alloc([512], BF16, "aout%d" % i) for i in range(2)]
        load_wA([hp * 128, 512 + hp * 128, 1024 + hp * 128], [hp * 128, 512 + hp * 128], wA)
        for b in range(len(B)):
            P.add('pool', lambda e, b=b: e.memset(vtok[b][:, :, :, 64:65], 1.0), w=[('vones', b)])
        P.add('pool', lambda e: e.memset(rrow[0:65, :], 0.0), w=['rrow'])
        for tb in range(S // 512):
            sl = slice(tb * 512, (tb + 1) * 512)
            rr = tb % 2
            P.add('sp', lambda e, rr=rr, sl=sl: e.dma_start(out=rotc[rr], in_=rotc_d[:, sl]), w=[('rotc', rr)])
            P.add('sp', lambda e, rr=rr, sl=sl: e.dma_start(out=rots[rr], in_=rots_d[:, sl]), w=[('rots', rr)])
            for (dst, wi, swi, nm) in ((qT, 0, 3, 'qT'), (kT, 1, 4, 'kT')):
                b0 = pr['i'] % 4
                b1 = (pr['i'] + 1) % 4
                pr['i'] += 2
                r_ = (pr['i'] // 2) % 2
                proj((wi, wA[wi]), tb, b0)
                proj((swi, wA[swi]), tb, b1)
                P.add('dve', lambda e, b0=b0, r_=r_, rr=rr: e.tensor_tensor(out=tA[r_], in0=psf[b0][:, :], in1=rotc[rr], op=ALU.mult),
                      r=pstok(b0, 0, 2048) + [('rotc', rr)], w=[('tA', r_)])
                P.add('dve', lambda e, b1=b1, r_=r_, rr=rr: e.tensor_tensor(out=tB[r_], in0=psf[b1][:, :], in1=rots[rr], op=ALU.mult),
                      r=pstok(b1, 0, 2048) + [('rots', rr)], w=[('tB', r_)])
                P.add('dve', lambda e, dst=dst, r_=r_, sl=sl: e.tensor_tensor(out=dst[:, sl], in0=tA[r_], in1=tB[r_], op=ALU.add),
                      r=[('tA', r_), ('tB', r_)], w=[(nm, tb)])
            b0 = pr['i'] % 4
            pr['i'] += 1
            proj((2, wA[2]), tb, b0)
            P.add('act', lambda e, b0=b0, sl=sl: e.activation(out=vT[:, sl], in_=psf[b0][:, :], func=AF.Copy),
                  r=pstok(b0, 0, 2048), w=[('vT', tb)])
        for b, d in enumerate(B):
            L = S // d
            for r in range(d):
                for i0 in range(0, L // 128, 4):
                    n4 = min(4, L // 128 - i0)
                    bank = 6 + (pr['i'] % 2)
                    pr['i'] += 1
                    for ii in range(n4):
                        i = i0 + ii
                        lo = r + d * 128 * i
                        P.add('pe', lambda e, ii=ii, lo=lo, bank=bank, d=d: e.transpose(
                            out=psb[bank][:, ii * 128:(ii + 1) * 128], in_=vT[:, sst(lo, 128, d)], identity=ident),
                            r=[('vT', t) for t in range(lo // 512, (lo + 128 * d - d) // 512 + 1)] + ['ident'],
                            w=pstok(bank, ii * 256, ii * 256 + 256))
                    t0 = r * (L // 128) + i0
                    P.add('dve', lambda e, b=b, t0=t0, n4=n4, bank=bank: e.tensor_copy(
                        out=vtok[b][:, t0:t0 + n4, :, 0:64],
                        in_=psb[bank][:, 0:n4 * 128].rearrange("p (a h c) -> p a h c", a=n4, h=2)),
                        r=pstok(bank, 0, n4 * 256), w=[('vtok', b, t0 + q) for q in range(n4)])
        for h in range(2):
            hb = h * 64
            items = []
            for b, d in enumerate(B):
                L = S // d
                NCH = L // 128
                for r in range(d):
                    for i in range(NCH):
                        items.append((b, d, L, NCH, r, i))

            def stA(k, hb=hb):
                b, d, L, NCH, r, i = items[k]
                qlo = max(0, 128 * i - 64)
                qhi = min(L, 128 * i + 192)
                nq = qhi - qlo
                off = qlo - (128 * i - 64)
                slot = k % 4
                bank = (0, 1, 4, 5)[slot]
                klo = r + d * 128 * i
                qpl = r + d * qlo
                ktoks = [('kT', t) for t in range(klo // 512, (klo + 127 * d) // 512 + 1)]
                qtoks = [('qT', t) for t in range(qpl // 512, (qpl + (nq - 1) * d) // 512 + 1)]
                P.add('pe', lambda e: e.matmul(
                    psf[bank][:, 0:nq], lhsT=kT[hb:hb + 64, sst(klo, 128, d)],
                    rhs=qT[hb:hb + 64, sst(qpl, nq, d)], start=True, stop=False),
                    r=ktoks + qtoks, w=pstok(bank))
                P.add('pe', lambda e: e.matmul(
                    psf[bank][:, 0:nq], lhsT=ident, rhs=maskMB[:, off:off + nq], start=False, stop=True),
                    r=['ident', 'maskMB'], w=pstok(bank))
                P.add('act', lambda e: e.activation(
                    out=pmk[slot][:, 0:nq], in_=psf[bank][:, 0:nq], func=AF.Exp, scale=0.125),
                    r=pstok(bank), w=[('pmk', slot)])

            def stB(k, h=h):
                b, d, L, NCH, r, i = items[k]
                qlo = max(0, 128 * i - 64)
                slot = k % 4
                vt = vtok[b][:, r * NCH + i, h, :]
                for n in (i, i + 1):
                    jlo = max(0, 128 * n - 64)
                    jhi = min(L, 128 * n + 64)
                    nb = jhi - jlo
                    c0 = jlo - qlo
                    obk = (6, 7, 2, 3)[n % 4]
                    first = (n == i + 1) or (i == 0)
                    last = (n == i) or (i == NCH - 1)
                    P.add('pe', lambda e, nb=nb, c0=c0, first=first, last=last, obk=obk: e.matmul(
                        psf[obk][0:65, 0:nb], lhsT=vt, rhs=pmk[slot][:, c0:c0 + nb],
                        start=first, stop=last),
                        r=[('pmk', slot), ('vtok', b, r * NCH + i), ('vones', b)], w=pstok(obk))
                    if last:
                        plo = r + d * jlo
                        is_end = (k == len(items) - 1 or items[k + 1][0] != b) and n == i + 1 or \
                                 ((k == len(items) - 1 or items[k + 1][0] != b) and i == NCH - 1 and n == i and NCH - 1 == i and False)
                        wtok = [('UTop', h, b, k, n)]
                        if (k == len(items) - 1 or items[k + 1][0] != b) and n == i + 1:
                            wtok.append(('UTend', b))
                        if b == 0:
                            P.add('act', lambda e, nb=nb, plo=plo, obk=obk: e.activation(
                                out=UT[0:65, sst(plo, nb, d)], in_=psf[obk][0:65, 0:nb], func=AF.Copy),
                                r=pstok(obk) + ['UTnorm'], w=wtok)
                        else:
                            P.add('dve', lambda e, nb=nb, plo=plo, obk=obk: e.tensor_tensor(
                                out=UT[0:65, sst(plo, nb, d)], in0=psf[obk][0:65, 0:nb],
                                in1=UT[0:65, sst(plo, nb, d)], op=ALU.add),
                                r=pstok(obk) + [('UTend', b - 1)], w=wtok)

            LA = 2
            for k in range(min(LA, len(items))):
                stA(k)
            for k in range(len(items)):
                if k + LA < len(items):
                    stA(k + LA)
                stB(k)
            for tb in range(S // 512):
                sl = slice(tb * 512, (tb + 1) * 512)
                bank = pr['i'] % 4
                pr['i'] += 1
                P.add('dve', lambda e, sl=sl: e.reciprocal(out=rrow[64:65, :], in_=UT[64:65, sl]), r=[('UTend', len(B) - 1)], w=['rrow'])
                P.add('pe', lambda e, bank=bank: e.matmul(psf[bank][0:64, :], lhsT=esel[0:65, :], rhs=rrow[0:65, :], start=True, stop=True),
                      r=['rrow', 'esel'], w=pstok(bank, 0, 2048))
                ar_ = tb % 2
                P.add('dve', lambda e, sl=sl, bank=bank, ar_=ar_: e.tensor_tensor(out=aout[ar_][0:64, :], in0=psf[bank][0:64, :], in1=UT[0:64, sl], op=ALU.mult),
                      r=pstok(bank, 0, 2048) + [('UTend', len(B) - 1)], w=[('aout', ar_)] + (['UTnorm'] if tb == S // 512 - 1 else []))
                P.add('sp', lambda e, hb=hb, sl=sl, ar_=ar_: e.dma_start(allow_slow_non_contiguous=True, out=mix_s[hp, hb:hb + 64, sl], in_=aout[ar_][0:64, :]),
                      r=[('aout', ar_)], w=[('mix_s', hp, h, tb)])
        state['off'] = base

    def phase_hgrn(S, hp):
        base = state['off']
        NCk = S // CH
        wA = [alloc([KC, 128], BF16, "wA%d" % i) for i in range(5)]
        qf = [alloc([S], BF16, "qf%d" % dr) for dr in range(2)]
        kf = [alloc([S], BF16, "kf%d" % dr) for dr in range(2)]
        vtk = alloc([NCk, 128], BF16, "vtk")
        gate = alloc([S], BF16, "gate")
        dS = [alloc([NCk, 64], F32, "dS%d" % dr) for dr in range(2)]
        Dl = [alloc([NCk], F32, "Dl%d" % dr) for dr in range(2)]
        p1base = state['off']
        th = [alloc([512], F32, "th%d" % i) for i in range(2)]
        thd = [alloc([512], F32, "thd%d" % i) for i in range(2)]
        q2 = alloc([512], F32, "q2")
        Fms = [alloc([512], F32, "Fm%d" % i) for i in range(2)]
        D1s = [alloc([512], F32, "D1_%d" % i) for i in range(2)]
        Kks = [alloc([512], F32, "Kk%d" % i) for i in range(2)]
        Ics = [alloc([512], F32, "Ic%d" % i) for i in range(2)]
        RIs = [alloc([512], F32, "RI%d" % i) for i in range(2)]
        vTb = alloc([512], BF16, "vTb")
        ktk = [alloc([128], BF16, "ktk%d" % i) for i in range(4)]
        c0 = 1536 + hp * 128
        load_wA([c0, c0 + 512, c0 + 1024, c0 + 1536, c0 + 2048], [], wA)
        for i in range(2):
            P.add('pool', lambda e, i=i: e.memset(D1s[i], 0.0), w=[('D1', i)])
        pending = []
        for tb in range(S // 512):
            pending_new = []
            sl = slice(tb * 512, (tb + 1) * 512)
            b0 = pr['i'] % 4
            pr['i'] += 1
            proj((0, wA[0]), tb, b0)
            P.add('act', lambda e, b0=b0: e.activation(out=th[0], in_=psf[b0][:, :], func=AF.Tanh, scale=0.5),
                  r=pstok(b0, 0, 2048), w=[('th', 0)])
            P.add('dve', lambda e, b0=b0: e.scalar_tensor_tensor(out=q2, in0=th[0], scalar=1.0, in1=psf[b0][:, :], op0=ALU.add, op1=ALU.mult),
                  r=pstok(b0, 0, 2048) + [('th', 0)], w=['q2'])
            b0 = pr['i'] % 4
            pr['i'] += 1
            proj((3, wA[3]), tb, b0)
            P.add('act', lambda e, b0=b0: e.activation(out=vTb, in_=psf[b0][:, :], func=AF.Copy), r=pstok(b0, 0, 2048), w=['vTb'])
            for half in range(2):
                bank = 6 + (pr['i'] % 2)
                pr['i'] += 1
                for cc in range(4):
                    c = half * 4 + cc
                    P.add('pe', lambda e, c=c, cc=cc, bank=bank: e.transpose(out=psb[bank][0:64, cc * 128:(cc + 1) * 128],
                                                                               in_=vTb[:, c * 64:(c + 1) * 64], identity=ident),
                          r=['vTb', 'ident'], w=pstok(bank, cc * 256, cc * 256 + 256))
                cg = tb * 8 + half * 4
                P.add('dve', lambda e, cg=cg, bank=bank: e.tensor_copy(out=vtk[0:64, cg:cg + 4, :],
                                                                      in_=psb[bank][0:64, 0:512].rearrange("p (a c) -> p a c", a=4)),
                      r=pstok(bank, 0, 1024), w=[('vtk', cg + q) for q in range(4)])
            b0 = pr['i'] % 4
            pr['i'] += 1
            proj((4, wA[4]), tb, b0)
            P.add('act', lambda e, b0=b0: e.activation(out=th[1], in_=psf[b0][:, :], func=AF.Tanh, scale=0.5),
                  r=pstok(b0, 0, 2048), w=[('th', 1)])
            P.add('dve', lambda e, b0=b0, sl=sl: e.scalar_tensor_tensor(out=gate[:, sl], in0=th[1], scalar=1.0, in1=psf[b0][:, :],
                                                                        op0=ALU.add, op1=ALU.mult),
                  r=pstok(b0, 0, 2048) + [('th', 1)], w=[('gate', tb)])
            for dr in range(2):
                b0 = pr['i'] % 4
                pr['i'] += 1
                proj((1 + dr, wA[1 + dr]), tb, b0)
                Fm, Kk, Ic, RI, tdr = Fms[dr], Kks[dr], Ics[dr], RIs[dr], thd[dr]
                P.add('act', lambda e, b0=b0, tdr=tdr: e.activation(out=tdr, in_=psf[b0][:, :], func=AF.Tanh, scale=0.5),
                      r=pstok(b0, 0, 2048), w=[('thd', dr)])
                P.add('dve', lambda e, dr=dr, Fm=Fm, tdr=tdr: e.tensor_scalar(out=Fm, in0=tdr, scalar1=gb[:, dr, hp:hp + 1], scalar2=ga[:, dr, hp:hp + 1],
                                                             op0=ALU.mult, op1=ALU.add), r=[('thd', dr), 'ga', 'gb'], w=[('Fm', dr)])
                P.add('act', lambda e, dr=dr, Kk=Kk, tdr=tdr: e.activation(out=Kk, in_=tdr, func=AF.Identity, scale=gnb[:, dr, hp:hp + 1],
                                                           bias=gb[:, dr, hp:hp + 1]), r=[('thd', dr), 'gnb', 'gb'], w=[('Kk', dr)])
                D1 = D1s[dr]
                if dr == 0:
                    edge = slice(0, 512, 64)
                    Fv, D1v, Iv = Fm, D1, Ic
                else:
                    edge = slice(63, 512, 64)
                    Fv, D1v, Iv = Fm[:, ::-1], D1[:, ::-1], Ic[:, ::-1]
                P.add('dve', lambda e, edge=edge, D1=D1, Fm=Fm: e.tensor_copy(out=D1[:, edge], in_=Fm[:, edge]), r=[('Fm', dr)], w=[('D1', dr)])
                P.add('dve', lambda e, edge=edge, Fm=Fm: e.memset(Fm[:, edge], 0.0), r=[('D1', dr), ('Fm', dr)], w=[('Fm', dr)])
                P.add('dve', lambda e, Fv=Fv, D1v=D1v, Iv=Iv: e.tensor_tensor_scan(out=Iv, data0=Fv, data1=D1v, initial=0.0,
                                                                                   op0=ALU.mult, op1=ALU.add),
                      r=[('Fm', dr), ('D1', dr)], w=[('Ic', dr)])
                P.add('dve', lambda e, RI=RI, Ic=Ic: e.reciprocal(out=RI, in_=Ic), r=[('Ic', dr)], w=[('RI', dr)])
                P.add('dve', lambda e, dr=dr, sl=sl, Ic=Ic: e.tensor_tensor(out=qf[dr][:, sl], in0=q2, in1=Ic, op=ALU.mult),
                      r=['q2', ('Ic', dr)], w=[('qf', dr, tb)])
                P.add('dve', lambda e, dr=dr, sl=sl, Kk=Kk, RI=RI: e.tensor_tensor(out=kf[dr][:, sl], in0=Kk, in1=RI, op=ALU.mult),
                      r=[('Kk', dr), ('RI', dr)], w=[('kf', dr, tb)])
                ecol = slice(63, 512, 64) if dr == 0 else slice(0, 512, 64)
                P.add('pool', lambda e, dr=dr, ecol=ecol, tb=tb, Ic=Ic: e.tensor_copy(out=Dl[dr][:, tb * 8:(tb + 1) * 8], in_=Ic[:, ecol]),
                      r=[('Ic', dr)], w=[('Dl', dr, tb)])
                if os.environ.get('HSTOP') == 'p1a':
                    continue
                def dsA(c8, dr=dr, tb=tb):
                    c = tb * 8 + c8
                    kr = c8 % 4
                    bank = 6 + (c8 % 2)
                    P.add('pe', lambda e: e.transpose(out=psb[bank][0:64, 0:128], in_=kf[dr][:, c * 64:(c + 1) * 64], identity=ident),
                          r=[('kf', dr, tb), 'ident'], w=pstok(bank))
                    P.add('act', lambda e: e.activation(out=ktk[kr][0:64, :], in_=psb[bank][0:64, 0:128], func=AF.Copy),
                          r=pstok(bank), w=[('ktk', kr)])

                def dsB(c8, dr=dr, tb=tb):
                    c = tb * 8 + c8
                    kr = c8 % 4
                    bank = 4 + (c8 % 2)
                    P.add('pe', lambda e: e.matmul(psf[bank][:, 0:128], lhsT=ktk[kr][0:64, :], rhs=vtk[0:64, c, :], start=True, stop=True),
                          r=[('ktk', kr), ('vtk', c)], w=pstok(bank))
                    for hh in range(2):
                        ps_ = slice(hh * 64, hh * 64 + 64)
                        if hh == 0:
                            P.add('dve', lambda e, ps_=ps_, hh=hh: e.tensor_scalar(
                                out=dS[dr][ps_, c, :], in0=psf[bank][ps_, hh * 64: 64 + hh * 64],
                                scalar1=Dl[dr][ps_, c:c + 1], scalar2=None, op0=ALU.mult),
                                r=pstok(bank) + [('Dl', dr, tb)], w=[('dS', dr, c, hh)])
                        else:
                            P.add('act', lambda e, ps_=ps_, hh=hh: e.activation(
                                out=dS[dr][ps_, c, :], in_=psf[bank][ps_, hh * 64: 64 + hh * 64], func=AF.Copy,
                                scale=Dl[dr][ps_, c:c + 1]),
                                r=pstok(bank) + [('Dl', dr, tb)], w=[('dS', dr, c, hh)])
                    if dr == 0 and c > 0:
                        P.add('dve', lambda e: e.scalar_tensor_tensor(
                            out=dS[0][:, c, :], in0=dS[0][:, c - 1, :], scalar=Dl[0][:, c:c + 1], in1=dS[0][:, c, :],
                            op0=ALU.mult, op1=ALU.add),
                            r=[('dS', 0, c - 1, 0), ('dS', 0, c - 1, 1), ('dS', 0, c, 0), ('dS', 0, c, 1), ('Dl', 0, tb)],
                            w=[('dS', 0, c, 0), ('dS', 0, c, 1)])

                def run_ds(dsA=dsA, dsB=dsB):
                    dsA(0)
                    for c8 in range(8):
                        if c8 + 1 < 8:
                            dsA(c8 + 1)
                        dsB(c8)
                pending_new.append(run_ds)
            for f_ in pending:
                f_()
            pending = pending_new
        for f_ in pending:
            f_()
        if os.environ.get('HSTOP') in ('p1a', 'p1'):
            state['off'] = base
            return
        for n_ in range(1, NCk):
            for dr in (1,):
                c = n_ if dr == 0 else NCk - 1 - n_
                pc = c - 1 if dr == 0 else c + 1
                P.add('dve', lambda e, dr=dr, c=c, pc=pc: e.scalar_tensor_tensor(
                    out=dS[dr][:, c, :], in0=dS[dr][:, pc, :], scalar=Dl[dr][:, c:c + 1], in1=dS[dr][:, c, :],
                    op0=ALU.mult, op1=ALU.add),
                    r=[('dS', dr, pc, 0), ('dS', dr, pc, 1), ('dS', dr, c, 0), ('dS', dr, c, 1), ('Dl', dr, c // 8)],
                    w=[('dS', dr, c, 0), ('dS', dr, c, 1)])
        if os.environ.get('HSTOP') == 'chain':
            state['off'] = base
            return
        P.barrier()
        state['off'] = p1base
        att = [alloc([2, 64], BF16, "att%d" % i) for i in range(2)]
        Sbd4 = [alloc([8, 128], BF16, "Sbd%d" % i) for i in range(4)]
        osum = alloc([512], F32, "osum")
        osq = alloc([512], F32, "osq")
        rs8 = alloc([512], F32, "rs8")
        houts = [alloc([512], BF16, "hout%d" % i) for i in range(2)]
        for i in range(4):
            P.add('pool', lambda e, i=i: e.memset(Sbd4[i], 0.0), w=[('Sbd', i % 2, i // 2)])
        prev_norm = [None]
        for tb in range(S // 512):
            sl = slice(tb * 512, (tb + 1) * 512)
            obA = 4 + 2 * (tb % 2)
            obB = 5 + 2 * (tb % 2)
            Sbd = [Sbd4[0 + 2 * (tb % 2)], Sbd4[1 + 2 * (tb % 2)]]
            sbp = tb % 2
            for dr in range(2):
                for hh in range(2):
                    ps_ = slice(hh * 64, hh * 64 + 64)
                    cs = [tb * 8 + c8 + (-1 if dr == 0 else 1) for c8 in range(8)]
                    valid = [c8 for c8 in range(8) if 0 <= cs[c8] < NCk]
                    lo, hi = valid[0], valid[-1] + 1
                    P.add('pool', lambda e, dr=dr, ps_=ps_, lo=lo, hi=hi, cs=cs, hh=hh, Sbd=Sbd: e.tensor_copy(
                        out=Sbd[dr][ps_, lo:hi, hh * 64:hh * 64 + 64], in_=dS[dr][ps_, cs[lo]:cs[hi - 1] + 1, :]),
                        r=[('dS', dr, cs[c8], hh) for c8 in valid], w=[('Sbd', dr, sbp)])
            if os.environ.get('HSTOP') == 'p2a1':
                continue
            items2 = [(c8, dr) for c8 in range(8) for dr in range(2)]

            def p2A(k, tb=tb):
                c8, dr = items2[k]
                c = tb * 8 + c8
                cs_ = slice(c * 64, (c + 1) * 64)
                ar = k % 2
                mk = maskF if dr == 0 else maskB
                for hh in range(2):
                    ps_ = slice(hh * 64, hh * 64 + 64)
                    abank = ar * 2 + hh
                    P.add('pe', lambda e, ps_=ps_, abank=abank: e.matmul(
                        psf[abank][0:64, 0:64], lhsT=kf[dr][ps_, cs_], rhs=qf[dr][ps_, cs_], start=True, stop=True),
                        r=[('kf', dr, tb), ('qf', dr, tb)], w=pstok(abank))
                    P.add('dve', lambda e, abank=abank, hh=hh: e.tensor_tensor(
                        out=att[ar][0:64, hh, :], in0=psf[abank][0:64, 0:64], in1=mk[0:64, :], op=ALU.mult),
                        r=pstok(abank) + ['maskF', 'maskB'], w=[('att', ar, hh)])

            def p2B(k, tb=tb, obA=obA, obB=obB, Sbd=Sbd, sbp=sbp):
                c8, dr = items2[k]
                c = tb * 8 + c8
                cs_ = slice(c * 64, (c + 1) * 64)
                ar = k % 2
                first = (dr == 0)
                skip_inter = (dr == 0 and c == 0) or (dr == 1 and c == NCk - 1)
                for hh, ob in ((0, obA), (1, obB)):
                    P.add('pe', lambda e, hh=hh, ob=ob: e.matmul(
                        psf[ob][:, c8 * 64:(c8 + 1) * 64], lhsT=vtk[0:64, c, :], rhs=att[ar][0:64, hh, :],
                        start=first, stop=(dr == 1 and skip_inter)),
                        r=[('att', ar, hh), ('vtk', c)], w=pstok(ob))
                    if not skip_inter:
                        P.add('pe', lambda e, ob=ob: e.matmul(
                            psf[ob][:, c8 * 64:(c8 + 1) * 64], lhsT=Sbd[dr][:, c8, :], rhs=qf[dr][:, cs_],
                            start=False, stop=(dr == 1)),
                            r=[('Sbd', dr, sbp), ('qf', dr, tb)], w=pstok(ob))

            def norm_chain(tb=tb, sl=sl, obA=obA, obB=obB):
                P.add('act', lambda e: e.activation(out=osum[0:64, :], in_=psf[obA][0:64, :], func=AF.Copy), r=pstok(obA), w=['osumA'])
                P.add('act', lambda e: e.activation(out=osum[64:128, :], in_=psf[obB][64:128, :], func=AF.Copy), r=pstok(obB), w=['osumB'])
                P.add('act', lambda e: e.activation(out=osq, in_=osum, func=AF.Square), r=['osumA', 'osumB'], w=['osq'])
                nb_ = pr['i'] % 4
                pr['i'] += 1
                P.add('pe', lambda e: e.matmul(psf[nb_][:, :], lhsT=onesbd, rhs=osq, start=True, stop=True),
                      r=['osq', 'onesbd'], w=pstok(nb_))
                P.add('act', lambda e: e.activation(out=rs8, in_=psf[nb_][:, :], func=AF.Ln, bias=eps256[:, 0:1]),
                      r=pstok(nb_) + ['eps256'], w=['rs8a'])
                P.add('act', lambda e: e.activation(out=rs8, in_=rs8, func=AF.Exp, scale=-0.5), r=['rs8a'], w=['rs8'])
                P.add('dve', lambda e: e.tensor_tensor(out=osum, in0=osum, in1=rs8, op=ALU.mult), r=['osumA', 'osumB', 'rs8'], w=['osn'])
                hr = tb % 2
                P.add('dve', lambda e: e.scalar_tensor_tensor(out=houts[hr], in0=osum, scalar=onw4[:, 0:1], in1=gate[:, sl],
                                                              op0=ALU.mult, op1=ALU.mult),
                      r=['osn', 'onw4', ('gate', tb)], w=[('hout', hr)])
                P.add('sp', lambda e: e.dma_start(allow_slow_non_contiguous=True, out=mix_s[4 + hp, :, sl], in_=houts[hr]),
                      r=[('hout', hr)], w=[('mix_s', 4 + hp, tb)])

            p2A(0)
            for k in range(16):
                if k + 1 < 16:
                    p2A(k + 1)
                p2B(k)
                if k == 5 and prev_norm[0] is not None:
                    prev_norm[0]()
                    prev_norm[0] = None
            prev_norm[0] = norm_chain
        if prev_norm[0] is not None:
            prev_norm[0]()
        state['off'] = base

    def phase_B1(t_base, S):
        base = state['off']
        wo = alloc([KC, D], BF16, "wo")
        wpost = alloc([D], F32, "wpost")
        P.add('sp', lambda e: e.dma_start(allow_slow_non_contiguous=True, out=wpost, in_=bass.AP(vec["norm_mix_post"], 0, [[0, 128], [1, D]])), w=['wpost'])
        mt = [alloc([KC, 512], BF16, "mt%d" % i) for i in range(2)]
        xt = [alloc([D], F32, "xt%d" % i) for i in range(4)]
        ht = [alloc([D], F32, "ht%d" % i) for i in range(4)]
        tm = [alloc([D], F32, "tm%d" % i) for i in range(4)]
        hb_ = [alloc([D], BF16, "hb%d" % i) for i in range(2)]
        junk = alloc([D], BF16, "junk")
        ss = [alloc([4], F32, "ss%d" % i) for i in range(4)]
        P.add('sp', lambda e: e.dma_start(allow_slow_non_contiguous=True, out=wo, in_=wout_b.rearrange("(k p) c -> p k c", p=128)), r=['wscr'], w=['wo'])
        P.add('pool', lambda e: e.memset(XT[:, :, S + 1:S + 2], 0.0), w=['xhalo2'])
        NJ = S // 128

        ld = {'x': 0, 'm': 0}

        def b1_loads(j_upto, g_upto):
            while ld['m'] < min(g_upto, S // 512):
                g_ = ld['m']
                P.add('sp', lambda e, g_=g_: e.dma_start(allow_slow_non_contiguous=True, out=mt[g_ % 2], in_=mix_s[:, :, g_ * 512:(g_ + 1) * 512].rearrange("k p t -> p k t")),
                      r=[], w=[('mt', g_ % 2)])
                ld['m'] += 1
            while ld['x'] < min(j_upto, NJ):
                j_ = ld['x']
                P.add('sp', lambda e, j_=j_: e.dma_start(allow_slow_non_contiguous=True, out=xt[j_ % 4], in_=xs[t_base + j_ * 128: t_base + (j_ + 1) * 128, :]),
                      w=[('xt', j_ % 4)])
                ld['x'] += 1

        def b1a(j):
            g, tt = j // 4, j % 4
            m_ = mt[g % 2]
            b1_loads(j + 3, g + 2 if tt >= 2 else g + 1)
            r_ = j % 4
            x_, h_, t_, s_ = xt[r_], ht[r_], tm[r_], ss[r_]
            nm = 'b1_%d' % r_
            for half in range(2):
                bank = (j % 2) * 2 + half
                hs = slice(half * 512, (half + 1) * 512)
                for kc in range(KC):
                    P.add('pe', lambda e, kc=kc, hs=hs, bank=bank: e.matmul(
                        psf[bank][:, :], lhsT=m_[:, kc, tt * 128:(tt + 1) * 128], rhs=wo[:, kc, hs], start=(kc == 0), stop=(kc == KC - 1)),
                        r=[('mt', g % 2), 'wo'], w=pstok(bank))
                P.add('act', lambda e, bank=bank, half=half: e.activation(out=junk[:, 0:512], in_=psf[bank][:, :], func=AF.Square,
                                                                           accum_out=s_[:, half:half + 1]),
                      r=pstok(bank), w=['junk', (nm, 'p', half)])
                P.add('act', lambda e, bank=bank, hs=hs: e.activation(out=t_[:, hs], in_=psf[bank][:, :], func=AF.Copy),
                      r=pstok(bank), w=[('tm', r_, half)])
            P.add('dve', lambda e: e.tensor_tensor(out=s_[:, 2:3], in0=s_[:, 0:1], in1=s_[:, 1:2], op=ALU.add),
                  r=[(nm, 'p', 0), (nm, 'p', 1)], w=[nm + 'ssq'])
            rstd_from_ssq(s_[:, 2:3], s_[:, 3:4], D, EPS, nm)
            P.add('dve', lambda e: e.scalar_tensor_tensor(out=t_, in0=t_, scalar=s_[:, 3:4], in1=wpost, op0=ALU.mult, op1=ALU.mult),
                  r=[('tm', r_, 0), ('tm', r_, 1), nm + 'rstd', 'wpost'], w=[('tm', r_, 0), ('tm', r_, 1)])
            P.add('dve', lambda e: e.tensor_tensor(out=h_, in0=t_, in1=x_, op=ALU.add),
                  r=[('tm', r_, 0), ('tm', r_, 1), ('xt', r_)], w=[('ht', r_)])
            P.add('sp', lambda e: e.dma_start(allow_slow_non_contiguous=True, out=ys[t_base + j * 128: t_base + (j + 1) * 128, :], in_=h_),
                  r=[('ht', r_)], w=[('ys', j)])
            nm2 = 'b1n_%d' % r_
            P.add('act', lambda e: e.activation(out=junk, in_=h_, func=AF.Square, accum_out=s_[:, 0:1]),
                  r=[('ht', r_)], w=['junk', nm2 + 'ssq'])
            rstd_from_ssq(s_[:, 0:1], s_[:, 1:2], D, EPS, nm2)

        def b1b(j):
            r_ = j % 4
            h_, s_, b_ = ht[r_], ss[r_], hb_[j % 2]
            nm2 = 'b1n_%d' % r_
            P.add('act', lambda e: e.activation(out=b_, in_=h_, func=AF.Copy, scale=s_[:, 1:2]),
                  r=[('ht', r_), nm2 + 'rstd'], w=[('hb', j % 2)])

        def b1c(j):
            b_ = hb_[j % 2]
            bank = 6 + (j % 2)
            for kc in range(KC):
                P.add('pe', lambda e, kc=kc: e.transpose(out=psb[bank][:, kc * 128:(kc + 1) * 128],
                                                       in_=b_[:, kc * 128:(kc + 1) * 128], identity=ident),
                      r=[('hb', j % 2), 'ident'], w=pstok(bank))
            P.add('dve', lambda e: e.tensor_tensor(
                out=XT[:, :, 1 + j * 128: 1 + (j + 1) * 128], in0=psb[bank][:, :].rearrange("p (k t) -> p k t", k=KC),
                in1=bcast_free(wfpre, 128), op=ALU.mult),
                r=pstok(bank) + ['wfpre'], w=[('XT', kc, j) for kc in range(KC)])

        for j in range(NJ + 2):
            if j < NJ:
                b1a(j)
            if 0 <= j - 1 < NJ:
                b1b(j - 1)
            if 0 <= j - 2 < NJ:
                b1c(j - 2)
        state['off'] = base

    def phase_B2(t_base, S):
        base = state['off']
        wfpost = alloc([D], F32, "wfpost")
        P.add('sp', lambda e: e.dma_start(allow_slow_non_contiguous=True, out=wfpost, in_=bass.AP(vec["norm_ffn_post"], 0, [[0, 128], [1, D]])), w=['wfpost'])
        wg = [alloc([KC, 128], BF16, "wg%d" % i) for i in range(6)]
        wu = [alloc([KC, 128], BF16, "wu%d" % i) for i in range(6)]
        wd = [alloc([512], BF16, "wd%d" % i) for i in range(16)]
        hid = alloc([NFC, 512], BF16, "hid")
        Asb = [alloc([514], F32, "Asb%d" % i) for i in range(5)]
        cc = [alloc([512], F32, "cc%d" % i) for i in range(5)]
        c2 = [alloc([512], F32, "c2%d" % i) for i in range(5)]
        c3 = [alloc([512], F32, "c3%d" % i) for i in range(5)]
        fsb = alloc([4, D], F32, "fsb")
        ht = [alloc([D], F32, "ht%d" % i) for i in range(4)]
        junk = alloc([512], BF16, "junk")
        ss = alloc([4, 4], F32, "ssB")
        wi = {'g': 0, 'd': 0}
        NB = S // 512
        pf = {'g': 0, 'd': 0}

        def prefetch(g_upto, d_upto):
            while pf['g'] < min(g_upto, NB * NFC):
                g = pf['g']
                fc_, r3_ = g % NFC, g % 6
                P.add('sp', lambda e, fc_=fc_, r3_=r3_: e.dma_start(allow_slow_non_contiguous=True, out=wg[r3_], in_=wg_b[fc_]), r=['wscr'], w=[('wg', r3_)])
                P.add('sp', lambda e, fc_=fc_, r3_=r3_: e.dma_start(allow_slow_non_contiguous=True, out=wu[r3_], in_=wu_b[fc_]), r=['wscr'], w=[('wu', r3_)])
                pf['g'] += 1
            while pf['d'] < min(d_upto, NB * 2 * NFC):
                dd = pf['d']
                fc_, half_, r4_ = dd % NFC, (dd // NFC) % 2, dd % 16
                P.add('sp', lambda e, fc_=fc_, half_=half_, r4_=r4_: e.dma_start(
                    allow_slow_non_contiguous=True, out=wd[r4_], in_=wd_b[fc_ * 128:(fc_ + 1) * 128, half_ * 512:(half_ + 1) * 512]),
                    r=['wscr'], w=[('wd', r4_)])
                pf['d'] += 1

        for blk in range(NB):
            t0 = blk * 512
            xtoks = lambda kc: [('XT', kc, blk * 4 + q) for q in range(4)]
            def st1(fc, blk=blk, t0=t0):
                r3 = wi['g'] % 6
                wi['g'] += 1
                r2 = fc % 5
                prefetch(wi['g'] + 4, wi['d'] + (10 if fc >= NFC - 6 else 0))
                gb_ = (0, 1)[fc % 2]
                ub_ = (2, 3, 6, 7)[fc % 4]
                hbk = (4, 5)[fc % 2]
                xtoks = lambda kc: [('XT', kc, blk * 4 + q) for q in range(4)]
                for kc in range(KC):
                    P.add('pe', lambda e, kc=kc: e.matmul(psf[gb_][:, :], lhsT=wg[r3][:, kc, :],
                                                          rhs=XT[:, kc, 1 + t0: 1 + t0 + 512], start=(kc == 0), stop=(kc == KC - 1)),
                          r=[('wg', r3)] + xtoks(kc), w=pstok(gb_))
                halo_r = ['xhalo', 'xhalo2'] + [('XT', kc, q) for kc in range(KC) for q in (max(blk * 4 - 1, 0), min(blk * 4 + 4, S // 128 - 1))]
                for kc in range(KC):
                    P.add('pe', lambda e, kc=kc: e.matmul(psf[hbk][:, 0:2], lhsT=wg[r3][:, kc, :],
                                                          rhs=XT[:, kc, t0: t0 + 514: 513], start=(kc == 0), stop=(kc == KC - 1)),
                          r=[('wg', r3)] + halo_r, w=pstok(hbk))
                for kc in range(KC):
                    P.add('pe', lambda e, kc=kc: e.matmul(psf[ub_][:, :], lhsT=wu[r3][:, kc, :],
                                                          rhs=XT[:, kc, 1 + t0: 1 + t0 + 512], start=(kc == 0), stop=(kc == KC - 1)),
                          r=[('wu', r3)] + xtoks(kc), w=pstok(ub_))
                A_, c_ = Asb[r2], cc[r2]
                P.add('act', lambda e: e.activation(out=A_[:, 1:513], in_=psf[gb_][:, :], func=AF.Copy),
                      r=pstok(gb_), w=[('Asb', r2, 0)])
                P.add('act', lambda e: e.activation(out=A_[:, 0:514:513], in_=psf[hbk][:, 0:2], func=AF.Copy),
                      r=pstok(hbk), w=[('Asb', r2, 1)])
                P.add('act', lambda e: e.activation(out=c_, in_=A_[:, 1:513], func=AF.Identity, scale=cw[:, 1, fc:fc + 1],
                                                    bias=cb[:, fc:fc + 1]),
                      r=[('Asb', r2, 0), 'cw', 'cb'], w=[('cc', r2)])

            def st2(fc):
                r2 = fc % 5
                A_, c_, c2_ = Asb[r2], cc[r2], c2[r2]
                P.add('dve', lambda e: e.scalar_tensor_tensor(out=c_, in0=A_[:, 0:512], scalar=cw[:, 0, fc:fc + 1], in1=c_,
                                                              op0=ALU.mult, op1=ALU.add),
                      r=[('Asb', r2, 0), ('Asb', r2, 1), 'cw', ('cc', r2)], w=[('cc', r2)])
                P.add('dve', lambda e: e.scalar_tensor_tensor(out=c_, in0=A_[:, 2:514], scalar=cw[:, 2, fc:fc + 1], in1=c_,
                                                              op0=ALU.mult, op1=ALU.add),
                      r=[('Asb', r2, 0), ('Asb', r2, 1), 'cw', ('cc', r2)], w=[('cc', r2)])
                P.add('act', lambda e: e.activation(out=c2_, in_=c_, func=AF.Square, scale=0.21145921592590237),
                      r=[('cc', r2)], w=[('c2', r2)])

            def st3(fc):
                r2 = fc % 5
                c_, c2_, c3_ = cc[r2], c2[r2], c3[r2]
                P.add('dve', lambda e: e.scalar_tensor_tensor(out=c3_, in0=c2_, scalar=1.0, in1=c_, op0=ALU.add, op1=ALU.mult),
                      r=[('c2', r2), ('cc', r2)], w=[('c3', r2)])
                P.add('act', lambda e: e.activation(out=c3_, in_=c3_, func=AF.Tanh, scale=0.7978845608028654),
                      r=[('c3', r2)], w=[('c3', r2)])

            def st4(fc):
                r2 = fc % 5
                ub_ = (2, 3, 6, 7)[fc % 4]
                c_, c2_, c3_ = cc[r2], c2[r2], c3[r2]
                P.add('dve', lambda e: e.scalar_tensor_tensor(out=c2_, in0=c3_, scalar=1.0, in1=c_, op0=ALU.add, op1=ALU.mult),
                      r=[('c3', r2), ('cc', r2), ('c2', r2)], w=[('c2', r2)])
                P.add('dve', lambda e: e.tensor_tensor(out=hid[:, fc, :], in0=psf[ub_][:, :], in1=c2_, op=ALU.mult),
                      r=pstok(ub_) + [('c2', r2)], w=[('hid', fc)])

            for it in range(NFC + 3):
                if it < NFC:
                    st1(it)
                if 0 <= it - 1 < NFC:
                    st2(it - 1)
                if 0 <= it - 2 < NFC:
                    st3(it - 2)
                if 0 <= it - 3 < NFC:
                    st4(it - 3)
            for tt in range(4):
                j = blk * 4 + tt
                P.add('sp', lambda e, j=j: e.dma_start(allow_slow_non_contiguous=True, out=ht[j % 4], in_=ys[t_base + j * 128: t_base + (j + 1) * 128, :]),
                      r=[('ys', j)], w=[('ht', j % 4)])
            for half in range(2):
                hs = slice(half * 512, (half + 1) * 512)
                for fc in range(NFC):
                    r4 = wi['d'] % 16
                    wi['d'] += 1
                    prefetch(wi['g'] + (5 if (half == 1 and fc >= NFC - 8) else 0), wi['d'] + 11)
                    for tt in range(4):
                        P.add('pe', lambda e, fc=fc, r4=r4, tt=tt: e.matmul(psf[4 + tt][:, :], lhsT=hid[:, fc, tt * 128:(tt + 1) * 128], rhs=wd[r4],
                                                                          start=(fc == 0), stop=(fc == NFC - 1)),
                              r=[('hid', fc), ('wd', r4)], w=pstok(4 + tt))
                for tt in range(4):
                    P.add('act', lambda e, tt=tt, half=half: e.activation(out=junk, in_=psf[4 + tt][:, :], func=AF.Square,
                                                                        accum_out=ss[:, tt, half:half + 1]),
                          r=pstok(4 + tt), w=['junk', ('ssB', tt, half)])
                    P.add('act', lambda e, tt=tt, hs=hs: e.activation(out=fsb[:, tt, hs], in_=psf[4 + tt][:, :], func=AF.Copy),
                          r=pstok(4 + tt), w=[('fsb', tt, half)])
            for tt in range(4):
                j = blk * 4 + tt
                r_ = j % 4
                h_ = ht[r_]
                nm = 'b2_%d' % tt
                P.add('dve', lambda e, tt=tt: e.tensor_tensor(out=ss[:, tt, 2:3], in0=ss[:, tt, 0:1], in1=ss[:, tt, 1:2], op=ALU.add),
                      r=[('ssB', tt, 0), ('ssB', tt, 1)], w=[nm + 'ssq'])
                rstd_from_ssq(ss[:, tt, 2:3], ss[:, tt, 3:4], D, 4.0 * EPS, nm)
                P.add('dve', lambda e, tt=tt: e.scalar_tensor_tensor(out=fsb[:, tt, :], in0=fsb[:, tt, :], scalar=ss[:, tt, 3:4], in1=wfpost,
                                                                   op0=ALU.mult, op1=ALU.mult),
                      r=[('fsb', tt, 0), ('fsb', tt, 1), nm + 'rstd', 'wfpost'], w=[('fsb', tt, 0), ('fsb', tt, 1)])
                P.add('dve', lambda e, tt=tt, h_=h_: e.tensor_tensor(out=h_, in0=h_, in1=fsb[:, tt, :], op=ALU.add),
                      r=[('fsb', tt, 0), ('fsb', tt, 1), ('ht', r_)], w=[('ht', r_)])
                P.add('sp', lambda e, j=j, h_=h_: e.dma_start(allow_slow_non_contiguous=True, out=ys[t_base + j * 128: t_base + (j + 1) * 128, :], in_=h_),
                      r=[('ht', r_)], w=[('ys', j)])
        state['off'] = base

    setup()
    weight_prep()
    P.barrier()
    t_base = 0
    for S in seq_lens:
        phase_A0(t_base, S, branches)
        P.barrier()
        if dbg_xt is not None and t_base == 0:
            P.add('sp', lambda e, S=S: e.dma_start(out=dbg_xt[:, :, 0:S + 1], in_=XT[:, :, 0:S + 1]), w=['dbgxt'])
        for hp in range(4):
            phase_attn(S, hp, branches)
            P.barrier()
        if stop_after == 'attn':
            break
        for hp in range(4):
            phase_hgrn(S, hp)
            P.barrier()
            if stop_after == 'hgrn0':
                break
        if stop_after == 'hgrn0':
            break
        phase_B1(t_base, S)
        P.barrier()
        phase_B2(t_base, S)
        P.barrier()
        t_base += S

    with nc.Block() as block:
        P.emit(nc, block, sems)
    es.close()
    return nc


def rot_tables(SM):
    half = 8
    inv = ROPE_THETA ** (-np.arange(half, dtype=np.float32) * 2.0 / 16.0)
    ang = np.arange(SM, dtype=np.float32)[:, None] * inv[None, :]
    cos = np.cos(ang).astype(np.float32).T
    sin = np.sin(ang).astype(np.float32).T
    c = np.ones((128, SM), np.float32)
    s = np.zeros((128, SM), np.float32)
    for hb in (0, 64):
        c[hb:hb + 8] = cos
        c[hb + 8:hb + 16] = cos
        s[hb:hb + 8] = -sin
        s[hb + 8:hb + 16] = sin
    return c, s


_CACHE = {}


def kernel(x_prompt, x_sample, norm_mix_pre, w_in, hgrn_lb_fwd, hgrn_lb_bwd, hgrn_out_norm, w_out,
           norm_mix_post, norm_ffn_pre, w_gate, w_up, conv_w, conv_b, w_down, norm_ffn_post):
    n = 8
    x_prompt = np.asarray(x_prompt)
    x_sample = np.asarray(x_sample)
    Bp, Sp, _ = x_prompt.shape
    Bs, Ss, _ = x_sample.shape
    pp, sp_ = Bp // n, Bs // n
    seq_lens = tuple([Sp] * pp + [Ss] * sp_)
    if seq_lens not in _CACHE:
        _CACHE[seq_lens] = build(seq_lens)
    nc = _CACHE[seq_lens]
    rc, rs = rot_tables(max(seq_lens))
    f = lambda a: np.ascontiguousarray(np.asarray(a, dtype=np.float32))
    common = {
        "w_in": f(w_in)[0], "w_out": f(w_out)[0], "w_gate": f(w_gate)[0], "w_up": f(w_up)[0], "w_down": f(w_down)[0],
        "norm_mix_pre": f(norm_mix_pre)[0], "norm_mix_post": f(norm_mix_post)[0], "norm_ffn_pre": f(norm_ffn_pre)[0],
        "norm_ffn_post": f(norm_ffn_post)[0], "conv_b": f(conv_b)[0], "hgrn_out_norm": f(hgrn_out_norm)[0],
        "conv_w": f(conv_w)[0], "hgrn_lb_fwd": f(hgrn_lb_fwd), "hgrn_lb_bwd": f(hgrn_lb_bwd),
        "rot_c": rc, "rot_s": rs,
    }
    in_maps = []
    for c in range(n):
        xs = np.concatenate([x_prompt[c * pp:(c + 1) * pp].reshape(-1, D), x_sample[c * sp_:(c + 1) * sp_].reshape(-1, D)], axis=0)
        m = dict(common)
        m["xs"] = np.ascontiguousarray(xs, dtype=np.float32)
        in_maps.append(m)
    res = run_bass_kernel_spmd(nc, in_maps, core_ids=list(range(n)))
    yp = np.empty((Bp, Sp, D), np.float32)
    ysm = np.empty((Bs, Ss, D), np.float32)
    for c in range(n):
        y = res.results[c]["ys"]
        yp[c * pp:(c + 1) * pp] = y[:pp * Sp].reshape(pp, Sp, D)
        ysm[c * sp_:(c + 1) * sp_] = y[pp * Sp:].reshape(sp_, Ss, D)
    return (yp, ysm)
```

```python
import os
import numpy as np
import ml_dtypes
import concourse.bass as bass
import concourse.mybir as mybir
from concourse.bass_utils import run_bass_kernel_spmd

F32 = mybir.dt.float32
BF16 = mybir.dt.bfloat16
AF = mybir.ActivationFunctionType
ALU = mybir.AluOpType

D = 1024
KC = 8
INW = 4096
DFF = 2816
NFC = 22
EPS = 1e-6
ROPE_THETA = 500000.0
BRANCHES = (1, 4, 16)
CH = 64
COMPUTE = ('pe', 'act', 'dve', 'pool')


class Prog:
    def __init__(self, ndma=12):
        self.ndma = ndma
        self.lists = {e: [] for e in COMPUTE + ('sp',)}
        self.bystream = {}
        self.tok = {}
        self.vc = {e: {} for e in COMPUTE + ('sp',)}
        self.dma_n = 0
        self.pending_barrier = {}

    def barrier(self):
        deps = set()
        for s, ops in self.bystream.items():
            if ops:
                deps.add((s, len(ops) - 1))
        for e in self.lists:
            self.pending_barrier[e] = set(deps)
        self.tok = {}

    def add(self, eng, fn, r=(), w=()):
        if eng == 'sp':
            stream = 'd%d' % (self.dma_n % self.ndma)
            self.dma_n += 1
        else:
            stream = eng
        slist = self.bystream.setdefault(stream, [])
        sidx = len(slist)
        deps = set()
        if eng in self.pending_barrier:
            deps |= self.pending_barrier.pop(eng)
        for k in r:
            st = self.tok.get(k)
            if st is not None and st[0] is not None:
                deps.add(st[0])
        for k in w:
            st = self.tok.get(k)
            if st is not None:
                if st[0] is not None:
                    deps.add(st[0])
                deps.update(st[1])
        if eng == 'sp' and sidx > 0:
            deps.add((stream, sidx - 1))
        vc = self.vc[eng]
        waits = {}
        for (s, i) in deps:
            if s == 'pe' and eng == 'pe':
                continue
            if vc.get(s, -1) < i:
                if waits.get(s, -1) < i:
                    waits[s] = i
        for s, i in waits.items():
            dop = self.bystream[s][i]
            dop['sig'] = True
            for s2, i2 in dop['vc'].items():
                if vc.get(s2, -1) < i2:
                    vc[s2] = i2
            if vc.get(s, -1) < i:
                vc[s] = i
        ovc = dict(vc)
        ovc[stream] = sidx
        op = dict(eng=eng, fn=fn, stream=stream, sidx=sidx, waits=waits, sig=False, vc=ovc)
        slist.append(op)
        self.lists[eng].append(op)
        me = (stream, sidx)
        for k in r:
            st = self.tok.setdefault(k, [None, []])
            st[1].append(me)
        for k in w:
            self.tok[k] = [me, []]
        return op

    def emit(self, nc, block, sems):
        counts = {}
        for s in COMPUTE:
            c = 0
            arr = []
            for op in self.bystream.get(s, []):
                if op['sig']:
                    c += 1
                arr.append(c)
            counts[s] = arr

        def val(s, i):
            if s in COMPUTE:
                return counts[s][i]
            return 16 * (i + 1)

        def run(e, ename):
            for op in self.lists[ename]:
                for s, i in op['waits'].items():
                    e.wait_ge(sems[s], val(s, i))
                ins = op['fn'](e)
                if ename == 'sp':
                    ins.then_inc(sems[op['stream']], 16)
                elif op['sig']:
                    ins.then_inc(sems[ename], 1)
            if ename == 'sp':
                for s, ops in self.bystream.items():
                    if s not in COMPUTE and ops:
                        e.wait_ge(sems[s], 16 * len(ops))

        @block.sync
        def _(e):
            run(e, 'sp')

        @block.tensor
        def _(e):
            run(e, 'pe')

        @block.scalar
        def _(e):
            run(e, 'act')

        @block.vector
        def _(e):
            run(e, 'dve')

        @block.gpsimd
        def _(e):
            run(e, 'pool')


def bcast_free(ap, n):
    return bass.AP(ap.tensor, ap.offset, [list(x) for x in ap.ap] + [[0, n]])


def bcast_mid(ap, n):
    l = [list(x) for x in ap.ap]
    return bass.AP(ap.tensor, ap.offset, [l[0], [0, n]] + l[1:])


def sst(lo, n, d):
    return slice(lo, lo + (n - 1) * d + 1, d)


def pstok(bank, lo=0, hi=0):
    return [('ps', bank)]


DEBUG_OFFS = {}


def build(seq_lens, branches=BRANCHES, stop_after=None):
    nc = bass.Bass("TRN2", target_bir_lowering=False)
    NT = sum(seq_lens)
    SM = max(seq_lens)
    dt = nc.dram_tensor
    xs = dt("xs", [NT, D], F32, kind="ExternalInput").ap()
    ys = dt("ys", [NT, D], F32, kind="ExternalOutput").ap()
    w_in = dt("w_in", [D, INW], F32, kind="ExternalInput").ap()
    w_out = dt("w_out", [D, D], F32, kind="ExternalInput").ap()
    w_gate = dt("w_gate", [D, DFF], F32, kind="ExternalInput").ap()
    w_up = dt("w_up", [D, DFF], F32, kind="ExternalInput").ap()
    w_down = dt("w_down", [DFF, D], F32, kind="ExternalInput").ap()
    vec = {}
    for nm, n in (("norm_mix_pre", D), ("norm_mix_post", D), ("norm_ffn_pre", D), ("norm_ffn_post", D),
                  ("conv_b", DFF), ("hgrn_out_norm", 64)):
        vec[nm] = dt(nm, [n], F32, kind="ExternalInput")
    conv_w = dt("conv_w", [3, DFF], F32, kind="ExternalInput")
    lbf = dt("hgrn_lb_fwd", [2, 512], F32, kind="ExternalInput")
    lbb = dt("hgrn_lb_bwd", [2, 512], F32, kind="ExternalInput")
    rotc_d = dt("rot_c", [128, SM], F32, kind="ExternalInput").ap()
    rots_d = dt("rot_s", [128, SM], F32, kind="ExternalInput").ap()
    win_b = dt("win_b", [D, INW], BF16, kind=("ExternalOutput" if os.environ.get("KDEBUG") else "Internal")).ap()
    winsw_b = dt("winsw_b", [D, 1024], BF16, kind="Internal").ap()
    wout_b = dt("wout_b", [D, D], BF16, kind="Internal").ap()
    wg_b = dt("wg_b", [NFC, 128, KC, 128], BF16, kind="Internal").ap()
    wu_b = dt("wu_b", [NFC, 128, KC, 128], BF16, kind="Internal").ap()
    wd_b = dt("wd_b", [DFF, D], BF16, kind="Internal").ap()
    mix_s = dt("mix_s", [KC, 128, SM], BF16, kind=("ExternalOutput" if os.environ.get("KDEBUG") else "Internal")).ap()

    dbg_xt = dt("dbg_xt", [128, KC, SM + 2], BF16, kind="ExternalOutput").ap() if os.environ.get("KDEBUG") else None
    P = Prog()
    from contextlib import ExitStack
    es = ExitStack()
    ARF = 53200
    arena = es.enter_context(nc.sbuf_tensor("arena", [128, ARF], F32))
    arena_b = arena.bitcast(BF16)
    psf = [es.enter_context(nc.psum_tensor("ps%d" % i, [128, 512], F32)) for i in range(8)]
    psb = [p.bitcast(BF16) for p in psf]
    sems = {}
    for s in list(COMPUTE) + ['d%d' % i for i in range(P.ndma)]:
        sems[s] = es.enter_context(nc.semaphore("sem_" + s))

    state = {'off': 0, 'uid': 0}

    def alloc(shape, dtype, name):
        n = 1
        for s_ in shape:
            n *= s_
        esz = 4 if dtype == F32 else 2
        off = (state['off'] + 31) // 32 * 32
        state['off'] = off + n * esz
        assert state['off'] <= ARF * 4, ("SBUF arena overflow", name, state['off'])
        DEBUG_OFFS[name] = (off, list(shape), 'f32' if dtype == F32 else 'bf16')
        base = arena if dtype == F32 else arena_b
        o = off // esz
        v = base[:, o:o + n]
        if len(shape) == 2:
            v = v.rearrange("p (a b) -> p a b", a=shape[0])
        elif len(shape) == 3:
            v = v.rearrange("p (a b c) -> p a b c", a=shape[0], b=shape[1])
        return v

    XT = alloc([KC, SM + 2], BF16, "xnT")
    ident = alloc([128], BF16, "ident")
    maskA = alloc([256], BF16, "maskA")
    maskMB = alloc([256], BF16, "maskMB")
    maskF = alloc([64], BF16, "maskF")
    maskB = alloc([64], BF16, "maskB")
    onesbd = alloc([128], F32, "onesbd")
    esel = alloc([64], F32, "esel")
    cneg = alloc([1], F32, "cneg")
    eps256 = alloc([1], F32, "eps256")
    wpre = alloc([KC], F32, "wpre")
    wfpre = alloc([KC], F32, "wfpre")
    cw = alloc([3, NFC], F32, "cw")
    cb = alloc([NFC], F32, "cb")
    onw4 = alloc([1], F32, "onw4")
    lbt = alloc([2, 2, 4], F32, "lbt")
    ga = alloc([2, 4], F32, "ga")
    gb = alloc([2, 4], F32, "gb")
    gna = alloc([2, 4], F32, "gna")
    gnb = alloc([2, 4], F32, "gnb")
    state['off'] += int(os.environ.get('KPAD', '0'))
    PERSIST = state['off']

    def setup():
        P.add('pool', lambda e: e.memset(ident, 0.0), w=['ident'])
        P.add('pool', lambda e: e.affine_select(out=ident, in_=ident, pattern=[[-1, 128]], compare_op=ALU.not_equal,
                                                 fill=1.0, base=0, channel_multiplier=1), r=['ident'], w=['ident'])
        P.add('pool', lambda e: e.memset(maskA, 1.0), w=['maskA'])
        P.add('pool', lambda e: e.affine_select(out=maskA, in_=maskA, pattern=[[1, 256]], compare_op=ALU.is_ge,
                                                 fill=0.0, base=0, channel_multiplier=-1), r=['maskA'], w=['maskA'])
        P.add('pool', lambda e: e.affine_select(out=maskA, in_=maskA, pattern=[[-1, 256]], compare_op=ALU.is_ge,
                                                 fill=0.0, base=128, channel_multiplier=1), r=['maskA'], w=['maskA'])
        P.add('dve', lambda e: e.tensor_scalar(out=maskMB, in0=maskA, scalar1=-1.0, scalar2=30000.0, op0=ALU.add, op1=ALU.mult),
              r=['maskA'], w=['maskMB'])
        P.add('pool', lambda e: e.memset(maskF[0:64, :], 1.0), w=['maskF'])
        P.add('pool', lambda e: e.affine_select(out=maskF[0:64, :], in_=maskF[0:64, :], pattern=[[1, 64]], compare_op=ALU.is_ge,
                                                 fill=0.0, base=0, channel_multiplier=-1), r=['maskF'], w=['maskF'])
        P.add('pool', lambda e: e.memset(maskB[0:64, :], 1.0), w=['maskB'])
        P.add('pool', lambda e: e.affine_select(out=maskB[0:64, :], in_=maskB[0:64, :], pattern=[[-1, 64]], compare_op=ALU.is_ge,
                                                 fill=0.0, base=0, channel_multiplier=1), r=['maskB'], w=['maskB'])
        P.add('pool', lambda e: e.memset(onesbd, 0.0), w=['onesbd'])
        P.add('pool', lambda e: e.memset(onesbd[0:64, 0:64], 1.0), r=['onesbd'], w=['onesbd'])
        P.add('pool', lambda e: e.memset(onesbd[64:128, 64:128], 1.0), r=['onesbd'], w=['onesbd'])
        P.add('pool', lambda e: e.memset(esel[0:65, :], 0.0), w=['esel'])
        P.add('pool', lambda e: e.memset(esel[64:65, :], 1.0), r=['esel'], w=['esel'])
        P.add('pool', lambda e: e.memset(cneg, -0.5), w=['cneg'])
        P.add('pool', lambda e: e.memset(eps256, 256.0 * EPS), w=['eps256'])
        P.add('pool', lambda e: e.memset(XT[:, :, 0:1], 0.0), w=['xhalo'])
        with nc.allow_non_contiguous_dma(reason="tiny per-feature vectors"):
            P.add('sp', lambda e: e.dma_start(allow_slow_non_contiguous=True, out=wpre, in_=vec["norm_mix_pre"].ap().rearrange("(k p) -> p k", p=128)), w=['wpre'])
            P.add('sp', lambda e: e.dma_start(allow_slow_non_contiguous=True, out=wfpre, in_=vec["norm_ffn_pre"].ap().rearrange("(k p) -> p k", p=128)), w=['wfpre'])
            P.add('sp', lambda e: e.dma_start(allow_slow_non_contiguous=True, out=cw, in_=conv_w.ap().rearrange("w (f p) -> p w f", p=128)), w=['cw'])
            P.add('sp', lambda e: e.dma_start(allow_slow_non_contiguous=True, out=cb, in_=vec["conv_b"].ap().rearrange("(f p) -> p f", p=128)), w=['cb'])
            P.add('sp', lambda e: e.dma_start(allow_slow_non_contiguous=True, out=onw4[0:64, :], in_=vec["hgrn_out_norm"].ap().rearrange("(p o) -> p o", o=1)), w=['onw4a'])
            P.add('sp', lambda e: e.dma_start(allow_slow_non_contiguous=True, out=onw4[64:128, :], in_=vec["hgrn_out_norm"].ap().rearrange("(p o) -> p o", o=1)), w=['onw4b'])
            P.add('sp', lambda e: e.dma_start(allow_slow_non_contiguous=True, out=lbt[:, 0, :, :], in_=lbf.ap().rearrange("s (c p) -> p s c", p=128)), w=['lbt0'])
            P.add('sp', lambda e: e.dma_start(allow_slow_non_contiguous=True, out=lbt[:, 1, :, :], in_=lbb.ap().rearrange("s (c p) -> p s c", p=128)), w=['lbt1'])
        P.add('dve', lambda e: e.tensor_scalar(out=onw4, in0=onw4, scalar1=4.0, scalar2=None, op0=ALU.mult),
              r=['onw4a', 'onw4b'], w=['onw4'])
        P.add('dve', lambda e: e.tensor_tensor(out=ga, in0=lbt[:, :, 0, :], in1=lbt[:, :, 1, :], op=ALU.subtract),
              r=['lbt0', 'lbt1'], w=['ga'])
        P.add('act', lambda e: e.activation(out=gb, in_=ga, func=AF.Tanh, scale=0.5), r=['ga'], w=['gb'])
        P.add('dve', lambda e: e.tensor_scalar(out=ga, in0=gb, scalar1=0.25, scalar2=0.75, op0=ALU.mult, op1=ALU.add),
              r=['gb'], w=['ga'])
        P.add('dve', lambda e: e.tensor_scalar(out=gna, in0=gb, scalar1=-0.25, scalar2=0.25, op0=ALU.mult, op1=ALU.add),
              r=['gb'], w=['gna'])
        P.add('dve', lambda e: e.tensor_scalar(out=gnb, in0=gb, scalar1=0.25, scalar2=-0.25, op0=ALU.mult, op1=ALU.add),
              r=['gb'], w=['gnb'])
        P.add('dve', lambda e: e.tensor_scalar(out=gb, in0=gb, scalar1=-0.25, scalar2=0.25, op0=ALU.mult, op1=ALU.add),
              r=['gb', 'gna', 'gnb'], w=['gb'])

    def weight_prep():
        base = state['off']
        st = [alloc([4096], F32, "wst%d" % i) for i in range(2)]
        bt = [alloc([4096], BF16, "wbt%d" % i) for i in range(2)]
        sw2 = alloc([1024], BF16, "wsw")
        sw = sw2.rearrange("p (h d) -> p h d", h=16)
        P.add('pool', lambda e: e.memset(sw2, 0.0), w=['wsw'])
        it = [0]

        def cast_rows(src, dst, ncols, sw_dst=None, dst_rearr=None):
            r_ = it[0] % 2
            it[0] += 1
            s_, b_ = st[r_], bt[r_]
            P.add('sp', lambda e: e.dma_start(allow_slow_non_contiguous=True, out=s_[:, 0:ncols], in_=src), w=[('wst', r_)])
            h1 = ncols // 2
            P.add('act', lambda e: e.activation(out=b_[:, 0:h1], in_=s_[:, 0:h1], func=AF.Copy), r=[('wst', r_)], w=[('wbt', r_, 0)])
            P.add('dve', lambda e: e.tensor_copy(out=b_[:, h1:ncols], in_=s_[:, h1:ncols]), r=[('wst', r_)], w=[('wbt', r_, 1)])
            if sw_dst is not None:
                sv = s_[:, 0:1024].rearrange("p (h d) -> p h d", h=16)
                P.add('pool', lambda e: e.tensor_copy(out=sw[:, :, 0:8], in_=sv[:, :, 8:16]), r=[('wst', r_)], w=['wsw'])
                P.add('pool', lambda e: e.tensor_copy(out=sw[:, :, 8:16], in_=sv[:, :, 0:8]), r=[('wst', r_)], w=['wsw'])
                P.add('sp', lambda e: e.dma_start(allow_slow_non_contiguous=True, out=sw_dst, in_=sw2), r=['wsw'], w=[('winsw_b', it[0])])
            if dst_rearr is None:
                P.add('sp', lambda e: e.dma_start(allow_slow_non_contiguous=True, out=dst, in_=b_[:, 0:ncols]), r=[('wbt', r_, 0), ('wbt', r_, 1)], w=[('wscr', it[0])])
            else:
                P.add('sp', lambda e: e.dma_start(allow_slow_non_contiguous=True, out=dst, in_=b_[:, 0:ncols].rearrange("p (f j) -> p f j", j=128)),
                      r=[('wbt', r_, 0), ('wbt', r_, 1)], w=[('wscr', it[0])])

        for kc in range(KC):
            rs = slice(kc * 128, (kc + 1) * 128)
            cast_rows(w_in[rs, :], win_b[rs, :], INW, sw_dst=winsw_b[rs, :])
        for kc in range(KC):
            rs = slice(kc * 128, (kc + 1) * 128)
            cast_rows(w_out[rs, :], wout_b[rs, :], D)
        with nc.allow_non_contiguous_dma(reason="chunked weight scratch, 256B segments, one-time"):
            for kc in range(KC):
                rs = slice(kc * 128, (kc + 1) * 128)
                cast_rows(w_gate[rs, :], wg_b[:, :, kc, :].rearrange("f p j -> p f j"), DFF, dst_rearr=True)
                cast_rows(w_up[rs, :], wu_b[:, :, kc, :].rearrange("f p j -> p f j"), DFF, dst_rearr=True)
        for fc in range(NFC):
            rs = slice(fc * 128, (fc + 1) * 128)
            cast_rows(w_down[rs, :], wd_b[rs, :], D)
        state['off'] = base

    pr = {'i': 0}

    def rstd_from_ssq(ssq, out, n, eps, name):
        P.add('dve', lambda e: e.tensor_scalar(out=out, in0=ssq, scalar1=1.0 / n, scalar2=eps, op0=ALU.mult, op1=ALU.add),
              r=[name + 'ssq'], w=[name + 'rstd'])
        P.add('pool', lambda e: e.tensor_tensor(out=out, in0=out, in1=cneg, op=ALU.pow),
              r=[name + 'rstd', 'cneg'], w=[name + 'rstd'])

    def phase_A0(t_base, S, B):
        base = state['off']
        xt = [alloc([D], F32, "xt%d" % i) for i in range(4)]
        xb = [alloc([D], BF16, "xb%d" % i) for i in range(2)]
        junk = alloc([D], BF16, "junk")
        ss = [alloc([2], F32, "ss%d" % i) for i in range(4)]
        NJ = S // 128

        def a1(j):
            r_ = j % 4
            x_, s_ = xt[r_], ss[r_]
            nm = 'a0_%d' % r_
            P.add('sp', lambda e, j=j, x_=x_: e.dma_start(allow_slow_non_contiguous=True, out=x_, in_=xs[t_base + j * 128: t_base + (j + 1) * 128, :]), w=[('xt', r_)])
            P.add('act', lambda e, x_=x_, s_=s_: e.activation(out=junk, in_=x_, func=AF.Square, accum_out=s_[:, 0:1]),
                  r=[('xt', r_)], w=['junk', nm + 'ssq'])
            rstd_from_ssq(s_[:, 0:1], s_[:, 1:2], D, EPS, nm)

        def a2(j):
            r_ = j % 4
            x_, s_, b_ = xt[r_], ss[r_], xb[j % 2]
            nm = 'a0_%d' % r_
            P.add('act', lambda e, x_=x_, s_=s_, b_=b_: e.activation(out=b_, in_=x_, func=AF.Copy, scale=s_[:, 1:2]),
                  r=[('xt', r_), nm + 'rstd'], w=[('xb', j % 2)])

        def a3(j):
            b_ = xb[j % 2]
            bank = 6 + (j % 2)
            for kc in range(KC):
                P.add('pe', lambda e, kc=kc, b_=b_, bank=bank: e.transpose(out=psb[bank][:, kc * 128:(kc + 1) * 128],
                                                                          in_=b_[:, kc * 128:(kc + 1) * 128], identity=ident),
                      r=[('xb', j % 2), 'ident'], w=pstok(bank))
            P.add('dve', lambda e, j=j, bank=bank: e.tensor_tensor(
                out=XT[:, :, 1 + j * 128: 1 + (j + 1) * 128],
                in0=psb[bank][:, :].rearrange("p (k t) -> p k t", k=KC),
                in1=bcast_free(wpre, 128), op=ALU.mult),
                r=pstok(bank) + ['wpre'], w=[('XT', kc, j) for kc in range(KC)])

        for j in range(NJ + 2):
            if j < NJ:
                a1(j)
            if 0 <= j - 1 < NJ:
                a2(j - 1)
            if 0 <= j - 2 < NJ:
                a3(j - 2)
        state['off'] = base

    def load_wA(cols_main, cols_sw, wA):
        i = 0
        with nc.allow_non_contiguous_dma(reason="weight column chunk, 256B segments"):
            for c in cols_main:
                P.add('sp', lambda e, c=c, i=i: e.dma_start(allow_slow_non_contiguous=True, out=wA[i], in_=win_b[:, c:c + 128].rearrange("(k p) j -> p k j", p=128)),
                      r=['wscr'], w=[('wA', i)])
                i += 1
            for c in cols_sw:
                P.add('sp', lambda e, c=c, i=i: e.dma_start(allow_slow_non_contiguous=True, out=wA[i], in_=winsw_b[:, c:c + 128].rearrange("(k p) j -> p k j", p=128)),
                      r=['winsw_b'], w=[('wA', i)])
                i += 1

    def proj(wA_i, tb, bank):
        for kc in range(KC):
            P.add('pe', lambda e, kc=kc: e.matmul(psf[bank][:, :], lhsT=wA_i[1][:, kc, :], rhs=XT[:, kc, 1 + tb * 512: 1 + (tb + 1) * 512],
                                                   start=(kc == 0), stop=(kc == KC - 1)),
                  r=[('wA', wA_i[0])] + [('XT', kc, tb * 4 + q) for q in range(4)], w=pstok(bank, 0, 2048))

    def phase_attn(S, hp, B):
        base = state['off']
        wA = [alloc([KC, 128], BF16, "wA%d" % i) for i in range(5)]
        rotc = [alloc([512], F32, "rotc%d" % i) for i in range(2)]
        rots = [alloc([512], F32, "rots%d" % i) for i in range(2)]
        qT = alloc([S], BF16, "qT")
        kT = alloc([S], BF16, "kT")
        vT = alloc([S], BF16, "vT")
        NTL = S // 128
        vtok = [alloc([NTL, 2, 65], BF16, "vtok%d" % b) for b in range(len(B))]
        tA = [alloc([512], F32, "tA%d" % i) for i in range(2)]
        tB = [alloc([512], F32, "tB%d" % i) for i in range(2)]
        praw = [alloc([256], BF16, "praw%d" % i) for i in range(4)]
        pmk = [alloc([256], BF16, "pmk%d" % i) for i in range(4)]
        UTs = [alloc([S], F32, "UT%d" % i) for i in range(2)]
        pending_norm = [None]
        rrow = alloc([512], F32, "rrow")
        aout = [alloc([512], BF16, "aout%d" % i) for i in range(2)]
        load_wA([hp * 128, 512 + hp * 128, 1024 + hp * 128], [hp * 128, 512 + hp * 128], wA)
        for b in range(len(B)):
            P.add('pool', lambda e, b=b: e.memset(vtok[b][:, :, :, 64:65], 1.0), w=[('vones', b)])
        P.add('pool', lambda e: e.memset(rrow[0:65, :], 0.0), w=['rrow'])
        for tb in range(S // 512):
            sl = slice(tb * 512, (tb + 1) * 512)
            rr = tb % 2
            P.add('sp', lambda e, rr=rr, sl=sl: e.dma_start(out=rotc[rr], in_=rotc_d[:, sl]), w=[('rotc', rr)])
            P.add('sp', lambda e, rr=rr, sl=sl: e.dma_start(out=rots[rr], in_=rots_d[:, sl]), w=[('rots', rr)])
            for (dst, wi, swi, nm) in ((qT, 0, 3, 'qT'), (kT, 1, 4, 'kT')):
                b0 = pr['i'] % 4
                b1 = (pr['i'] + 1) % 4
                pr['i'] += 2
                r_ = (pr['i'] // 2) % 2
                proj((wi, wA[wi]), tb, b0)
                proj((swi, wA[swi]), tb, b1)
                P.add('dve', lambda e, b0=b0, r_=r_, rr=rr: e.tensor_tensor(out=tA[r_], in0=psf[b0][:, :], in1=rotc[rr], op=ALU.mult),
                      r=pstok(b0, 0, 2048) + [('rotc', rr)], w=[('tA', r_)])
                P.add('dve', lambda e, b1=b1, r_=r_, rr=rr: e.tensor_tensor(out=tB[r_], in0=psf[b1][:, :], in1=rots[rr], op=ALU.mult),
                      r=pstok(b1, 0, 2048) + [('rots', rr)], w=[('tB', r_)])
                P.add('dve', lambda e, dst=dst, r_=r_, sl=sl: e.tensor_tensor(out=dst[:, sl], in0=tA[r_], in1=tB[r_], op=ALU.add),
                      r=[('tA', r_), ('tB', r_)], w=[(nm, tb)])
            b0 = pr['i'] % 4
            pr['i'] += 1
            proj((2, wA[2]), tb, b0)
            P.add('act', lambda e, b0=b0, sl=sl: e.activation(out=vT[:, sl], in_=psf[b0][:, :], func=AF.Copy),
                  r=pstok(b0, 0, 2048), w=[('vT', tb)])
        for b, d in enumerate(B):
            L = S // d
            for r in range(d):
                for i0 in range(0, L // 128, 4):
                    n4 = min(4, L // 128 - i0)
                    bank = 6 + (pr['i'] % 2)
                    pr['i'] += 1
                    for ii in range(n4):
                        i = i0 + ii
                        lo = r + d * 128 * i
                        P.add('pe', lambda e, ii=ii, lo=lo, bank=bank, d=d: e.transpose(
                            out=psb[bank][:, ii * 128:(ii + 1) * 128], in_=vT[:, sst(lo, 128, d)], identity=ident),
                            r=[('vT', t) for t in range(lo // 512, (lo + 128 * d - d) // 512 + 1)] + ['ident'],
                            w=pstok(bank, ii * 256, ii * 256 + 256))
                    t0 = r * (L // 128) + i0
                    P.add('dve', lambda e, b=b, t0=t0, n4=n4, bank=bank: e.tensor_copy(
                        out=vtok[b][:, t0:t0 + n4, :, 0:64],
                        in_=psb[bank][:, 0:n4 * 128].rearrange("p (a h c) -> p a h c", a=n4, h=2)),
                        r=pstok(bank, 0, n4 * 256), w=[('vtok', b, t0 + q) for q in range(n4)])
        for h in range(2):
            hb = h * 64
            UT = UTs[h]
            items = []
            for b, d in enumerate(B):
                L = S // d
                NCH = L // 128
                for r in range(d):
                    for i in range(NCH):
                        items.append((b, d, L, NCH, r, i))

            def stA(k, hb=hb):
                b, d, L, NCH, r, i = items[k]
                qlo = max(0, 128 * i - 64)
                qhi = min(L, 128 * i + 192)
                nq = qhi - qlo
                off = qlo - (128 * i - 64)
                slot = k % 4
                bank = (0, 1, 4, 5)[slot]
                klo = r + d * 128 * i
                qpl = r + d * qlo
                ktoks = [('kT', t) for t in range(klo // 512, (klo + 127 * d) // 512 + 1)]
                qtoks = [('qT', t) for t in range(qpl // 512, (qpl + (nq - 1) * d) // 512 + 1)]
                P.add('pe', lambda e: e.matmul(
                    psf[bank][:, 0:nq], lhsT=kT[hb:hb + 64, sst(klo, 128, d)],
                    rhs=qT[hb:hb + 64, sst(qpl, nq, d)], start=True, stop=False),
                    r=ktoks + qtoks, w=pstok(bank))
                P.add('pe', lambda e: e.matmul(
                    psf[bank][:, 0:nq], lhsT=ident, rhs=maskMB[:, off:off + nq], start=False, stop=True),
                    r=['ident', 'maskMB'], w=pstok(bank))
                P.add('act', lambda e: e.activation(
                    out=pmk[slot][:, 0:nq], in_=psf[bank][:, 0:nq], func=AF.Exp, scale=0.125),
                    r=pstok(bank), w=[('pmk', slot)])

            def stB(k, h=h, UT=UT):
                b, d, L, NCH, r, i = items[k]
                qlo = max(0, 128 * i - 64)
                slot = k % 4
                vt = vtok[b][:, r * NCH + i, h, :]
                for n in (i, i + 1):
                    jlo = max(0, 128 * n - 64)
                    jhi = min(L, 128 * n + 64)
                    nb = jhi - jlo
                    c0 = jlo - qlo
                    obk = (6, 7, 2, 3)[n % 4]
                    first = (n == i + 1) or (i == 0)
                    last = (n == i) or (i == NCH - 1)
                    P.add('pe', lambda e, nb=nb, c0=c0, first=first, last=last, obk=obk: e.matmul(
                        psf[obk][0:65, 0:nb], lhsT=vt, rhs=pmk[slot][:, c0:c0 + nb],
                        start=first, stop=last),
                        r=[('pmk', slot), ('vtok', b, r * NCH + i), ('vones', b)], w=pstok(obk))
                    if last:
                        plo = r + d * jlo
                        is_end = (k == len(items) - 1 or items[k + 1][0] != b) and n == i + 1 or \
                                 ((k == len(items) - 1 or items[k + 1][0] != b) and i == NCH - 1 and n == i and NCH - 1 == i and False)
                        wtok = [('UTop', h, b, k, n)]
                        if (k == len(items) - 1 or items[k + 1][0] != b) and n == i + 1:
                            wtok.append(('UTend', h, b))
                        if b == 0:
                            P.add('act', lambda e, nb=nb, plo=plo, obk=obk: e.activation(
                                out=UT[0:65, sst(plo, nb, d)], in_=psf[obk][0:65, 0:nb], func=AF.Copy),
                                r=pstok(obk) + [('UTnorm', h)], w=wtok)
                        else:
                            P.add('dve', lambda e, nb=nb, plo=plo, obk=obk: e.tensor_tensor(
                                out=UT[0:65, sst(plo, nb, d)], in0=psf[obk][0:65, 0:nb],
                                in1=UT[0:65, sst(plo, nb, d)], op=ALU.add),
                                r=pstok(obk) + [('UTend', h, b - 1)], w=wtok)

            LA = 2
            for k in range(min(LA, len(items))):
                stA(k)
            for k in range(len(items)):
                if k + LA < len(items):
                    stA(k + LA)
                stB(k)
                if k == 8 and pending_norm[0] is not None:
                    pending_norm[0]()
                    pending_norm[0] = None

            def norm(h=h, hb=hb, UT=UT):
                for tb in range(S // 512):
                    sl = slice(tb * 512, (tb + 1) * 512)
                    bank = pr['i'] % 4
                    pr['i'] += 1
                    P.add('dve', lambda e, sl=sl: e.reciprocal(out=rrow[64:65, :], in_=UT[64:65, sl]), r=[('UTend', h, len(B) - 1)], w=['rrow'])
                    P.add('pe', lambda e, bank=bank: e.matmul(psf[bank][0:64, :], lhsT=esel[0:65, :], rhs=rrow[0:65, :], start=True, stop=True),
                          r=['rrow', 'esel'], w=pstok(bank))
                    ar_ = tb % 2
                    P.add('dve', lambda e, sl=sl, bank=bank, ar_=ar_: e.tensor_tensor(out=aout[ar_][0:64, :], in0=psf[bank][0:64, :], in1=UT[0:64, sl], op=ALU.mult),
                          r=pstok(bank) + [('UTend', h, len(B) - 1)], w=[('aout', ar_)] + ([('UTnorm', h)] if tb == S // 512 - 1 else []))
                    P.add('sp', lambda e, sl=sl, ar_=ar_: e.dma_start(allow_slow_non_contiguous=True, out=mix_s[hp, hb:hb + 64, sl], in_=aout[ar_][0:64, :]),
                          r=[('aout', ar_)], w=[('mix_s', hp, h, tb)])
            if pending_norm[0] is not None:
                pending_norm[0]()
            pending_norm[0] = norm
        if pending_norm[0] is not None:
            pending_norm[0]()
        state['off'] = base

    def phase_hgrn(S, hp):
        base = state['off']
        NCk = S // CH
        wA = [alloc([KC, 128], BF16, "wA%d" % i) for i in range(5)]
        qf = [alloc([S], BF16, "qf%d" % dr) for dr in range(2)]
        kf = [alloc([S], BF16, "kf%d" % dr) for dr in range(2)]
        vtk = alloc([NCk, 128], BF16, "vtk")
        gate = alloc([S], BF16, "gate")
        dS = [alloc([NCk, 64], F32, "dS%d" % dr) for dr in range(2)]
        Dl = [alloc([NCk], F32, "Dl%d" % dr) for dr in range(2)]
        p1base = state['off']
        th = [alloc([512], F32, "th%d" % i) for i in range(2)]
        thd = [alloc([512], F32, "thd%d" % i) for i in range(2)]
        q2 = alloc([512], F32, "q2")
        Fms = [alloc([512], F32, "Fm%d" % i) for i in range(2)]
        D1s = [alloc([512], F32, "D1_%d" % i) for i in range(2)]
        Kks = [alloc([512], F32, "Kk%d" % i) for i in range(2)]
        Ics = [alloc([512], F32, "Ic%d" % i) for i in range(2)]
        RIs = [alloc([512], F32, "RI%d" % i) for i in range(2)]
        vTb = alloc([512], BF16, "vTb")
        ktk = [alloc([128], BF16, "ktk%d" % i) for i in range(4)]
        c0 = 1536 + hp * 128
        load_wA([c0, c0 + 512, c0 + 1024, c0 + 1536, c0 + 2048], [], wA)
        for i in range(2):
            P.add('pool', lambda e, i=i: e.memset(D1s[i], 0.0), w=[('D1', i)])
        pending = []
        for tb in range(S // 512):
            pending_new = []
            sl = slice(tb * 512, (tb + 1) * 512)
            b0 = pr['i'] % 4
            pr['i'] += 1
            proj((0, wA[0]), tb, b0)
            P.add('act', lambda e, b0=b0: e.activation(out=th[0], in_=psf[b0][:, :], func=AF.Tanh, scale=0.5),
                  r=pstok(b0, 0, 2048), w=[('th', 0)])
            P.add('dve', lambda e, b0=b0: e.scalar_tensor_tensor(out=q2, in0=th[0], scalar=1.0, in1=psf[b0][:, :], op0=ALU.add, op1=ALU.mult),
                  r=pstok(b0, 0, 2048) + [('th', 0)], w=['q2'])
            b0 = pr['i'] % 4
            pr['i'] += 1
            proj((3, wA[3]), tb, b0)
            P.add('act', lambda e, b0=b0: e.activation(out=vTb, in_=psf[b0][:, :], func=AF.Copy), r=pstok(b0, 0, 2048), w=['vTb'])
            for half in range(2):
                bank = 6 + (pr['i'] % 2)
                pr['i'] += 1
                for cc in range(4):
                    c = half * 4 + cc
                    P.add('pe', lambda e, c=c, cc=cc, bank=bank: e.transpose(out=psb[bank][0:64, cc * 128:(cc + 1) * 128],
                                                                               in_=vTb[:, c * 64:(c + 1) * 64], identity=ident),
                          r=['vTb', 'ident'], w=pstok(bank, cc * 256, cc * 256 + 256))
                cg = tb * 8 + half * 4
                P.add('dve', lambda e, cg=cg, bank=bank: e.tensor_copy(out=vtk[0:64, cg:cg + 4, :],
                                                                      in_=psb[bank][0:64, 0:512].rearrange("p (a c) -> p a c", a=4)),
                      r=pstok(bank, 0, 1024), w=[('vtk', cg + q) for q in range(4)])
            b0 = pr['i'] % 4
            pr['i'] += 1
            proj((4, wA[4]), tb, b0)
            P.add('act', lambda e, b0=b0: e.activation(out=th[1], in_=psf[b0][:, :], func=AF.Tanh, scale=0.5),
                  r=pstok(b0, 0, 2048), w=[('th', 1)])
            P.add('dve', lambda e, b0=b0, sl=sl: e.scalar_tensor_tensor(out=gate[:, sl], in0=th[1], scalar=1.0, in1=psf[b0][:, :],
                                                                        op0=ALU.add, op1=ALU.mult),
                  r=pstok(b0, 0, 2048) + [('th', 1)], w=[('gate', tb)])
            for dr in range(2):
                b0 = pr['i'] % 4
                pr['i'] += 1
                proj((1 + dr, wA[1 + dr]), tb, b0)
                Fm, Kk, Ic, RI, tdr = Fms[dr], Kks[dr], Ics[dr], RIs[dr], thd[dr]
                P.add('act', lambda e, b0=b0, tdr=tdr: e.activation(out=tdr, in_=psf[b0][:, :], func=AF.Tanh, scale=0.5),
                      r=pstok(b0, 0, 2048), w=[('thd', dr)])
                P.add('dve', lambda e, dr=dr, Fm=Fm, tdr=tdr: e.tensor_scalar(out=Fm, in0=tdr, scalar1=gb[:, dr, hp:hp + 1], scalar2=ga[:, dr, hp:hp + 1],
                                                             op0=ALU.mult, op1=ALU.add), r=[('thd', dr), 'ga', 'gb'], w=[('Fm', dr)])
                P.add('act', lambda e, dr=dr, Kk=Kk, tdr=tdr: e.activation(out=Kk, in_=tdr, func=AF.Identity, scale=gnb[:, dr, hp:hp + 1],
                                                           bias=gb[:, dr, hp:hp + 1]), r=[('thd', dr), 'gnb', 'gb'], w=[('Kk', dr)])
                D1 = D1s[dr]
                if dr == 0:
                    edge = slice(0, 512, 64)
                    Fv, D1v, Iv = Fm, D1, Ic
                else:
                    edge = slice(63, 512, 64)
                    Fv, D1v, Iv = Fm[:, ::-1], D1[:, ::-1], Ic[:, ::-1]
                P.add('dve', lambda e, edge=edge, D1=D1, Fm=Fm: e.tensor_copy(out=D1[:, edge], in_=Fm[:, edge]), r=[('Fm', dr)], w=[('D1', dr)])
                P.add('dve', lambda e, edge=edge, Fm=Fm: e.memset(Fm[:, edge], 0.0), r=[('D1', dr), ('Fm', dr)], w=[('Fm', dr)])
                P.add('dve', lambda e, Fv=Fv, D1v=D1v, Iv=Iv: e.tensor_tensor_scan(out=Iv, data0=Fv, data1=D1v, initial=0.0,
                                                                                   op0=ALU.mult, op1=ALU.add),
                      r=[('Fm', dr), ('D1', dr)], w=[('Ic', dr)])
                P.add('dve', lambda e, RI=RI, Ic=Ic: e.reciprocal(out=RI, in_=Ic), r=[('Ic', dr)], w=[('RI', dr)])
                P.add('dve', lambda e, dr=dr, sl=sl, Ic=Ic: e.tensor_tensor(out=qf[dr][:, sl], in0=q2, in1=Ic, op=ALU.mult),
                      r=['q2', ('Ic', dr)], w=[('qf', dr, tb)])
                P.add('dve', lambda e, dr=dr, sl=sl, Kk=Kk, RI=RI: e.tensor_tensor(out=kf[dr][:, sl], in0=Kk, in1=RI, op=ALU.mult),
                      r=[('Kk', dr), ('RI', dr)], w=[('kf', dr, tb)])
                ecol = slice(63, 512, 64) if dr == 0 else slice(0, 512, 64)
                P.add('pool', lambda e, dr=dr, ecol=ecol, tb=tb, Ic=Ic: e.tensor_copy(out=Dl[dr][:, tb * 8:(tb + 1) * 8], in_=Ic[:, ecol]),
                      r=[('Ic', dr)], w=[('Dl', dr, tb)])
                if os.environ.get('HSTOP') == 'p1a':
                    continue
                def dsA(c8, dr=dr, tb=tb):
                    c = tb * 8 + c8
                    kr = c8 % 4
                    bank = 6 + (c8 % 2)
                    P.add('pe', lambda e: e.transpose(out=psb[bank][0:64, 0:128], in_=kf[dr][:, c * 64:(c + 1) * 64], identity=ident),
                          r=[('kf', dr, tb), 'ident'], w=pstok(bank))
                    P.add('act', lambda e: e.activation(out=ktk[kr][0:64, :], in_=psb[bank][0:64, 0:128], func=AF.Copy),
                          r=pstok(bank), w=[('ktk', kr)])

                def dsB(c8, dr=dr, tb=tb):
                    c = tb * 8 + c8
                    kr = c8 % 4
                    bank = 4 + (c8 % 2)
                    P.add('pe', lambda e: e.matmul(psf[bank][:, 0:128], lhsT=ktk[kr][0:64, :], rhs=vtk[0:64, c, :], start=True, stop=True),
                          r=[('ktk', kr), ('vtk', c)], w=pstok(bank))
                    for hh in range(2):
                        ps_ = slice(hh * 64, hh * 64 + 64)
                        if hh == 0:
                            P.add('dve', lambda e, ps_=ps_, hh=hh: e.tensor_scalar(
                                out=dS[dr][ps_, c, :], in0=psf[bank][ps_, hh * 64: 64 + hh * 64],
                                scalar1=Dl[dr][ps_, c:c + 1], scalar2=None, op0=ALU.mult),
                                r=pstok(bank) + [('Dl', dr, tb)], w=[('dS', dr, c, hh)])
                        else:
                            P.add('act', lambda e, ps_=ps_, hh=hh: e.activation(
                                out=dS[dr][ps_, c, :], in_=psf[bank][ps_, hh * 64: 64 + hh * 64], func=AF.Copy,
                                scale=Dl[dr][ps_, c:c + 1]),
                                r=pstok(bank) + [('Dl', dr, tb)], w=[('dS', dr, c, hh)])
                    if dr == 0 and c > 0:
                        P.add('dve', lambda e: e.scalar_tensor_tensor(
                            out=dS[0][:, c, :], in0=dS[0][:, c - 1, :], scalar=Dl[0][:, c:c + 1], in1=dS[0][:, c, :],
                            op0=ALU.mult, op1=ALU.add),
                            r=[('dS', 0, c - 1, 0), ('dS', 0, c - 1, 1), ('dS', 0, c, 0), ('dS', 0, c, 1), ('Dl', 0, tb)],
                            w=[('dS', 0, c, 0), ('dS', 0, c, 1)])

                def run_ds(dsA=dsA, dsB=dsB):
                    dsA(0)
                    for c8 in range(8):
                        if c8 + 1 < 8:
                            dsA(c8 + 1)
                        dsB(c8)
                pending_new.append(run_ds)
            for f_ in pending:
                f_()
            pending = pending_new
        for f_ in pending:
            f_()
        if os.environ.get('HSTOP') in ('p1a', 'p1'):
            state['off'] = base
            return
        for n_ in range(1, NCk):
            for dr in (1,):
                c = n_ if dr == 0 else NCk - 1 - n_
                pc = c - 1 if dr == 0 else c + 1
                P.add('dve', lambda e, dr=dr, c=c, pc=pc: e.scalar_tensor_tensor(
                    out=dS[dr][:, c, :], in0=dS[dr][:, pc, :], scalar=Dl[dr][:, c:c + 1], in1=dS[dr][:, c, :],
                    op0=ALU.mult, op1=ALU.add),
                    r=[('dS', dr, pc, 0), ('dS', dr, pc, 1), ('dS', dr, c, 0), ('dS', dr, c, 1), ('Dl', dr, c // 8)],
                    w=[('dS', dr, c, 0), ('dS', dr, c, 1)])
        if os.environ.get('HSTOP') == 'chain':
            state['off'] = base
            return
        P.barrier()
        state['off'] = p1base
        att = [alloc([2, 64], BF16, "att%d" % i) for i in range(2)]
        Sbd4 = [alloc([8, 128], BF16, "Sbd%d" % i) for i in range(4)]
        osum = alloc([512], F32, "osum")
        osq = alloc([512], F32, "osq")
        rs8 = alloc([512], F32, "rs8")
        houts = [alloc([512], BF16, "hout%d" % i) for i in range(2)]
        for i in range(4):
            P.add('pool', lambda e, i=i: e.memset(Sbd4[i], 0.0), w=[('Sbd', i % 2, i // 2)])
        prev_norm = [None]
        for tb in range(S // 512):
            sl = slice(tb * 512, (tb + 1) * 512)
            obA = 4 + 2 * (tb % 2)
            obB = 5 + 2 * (tb % 2)
            Sbd = [Sbd4[0 + 2 * (tb % 2)], Sbd4[1 + 2 * (tb % 2)]]
            sbp = tb % 2
            for dr in range(2):
                for hh in range(2):
                    ps_ = slice(hh * 64, hh * 64 + 64)
                    cs = [tb * 8 + c8 + (-1 if dr == 0 else 1) for c8 in range(8)]
                    valid = [c8 for c8 in range(8) if 0 <= cs[c8] < NCk]
                    lo, hi = valid[0], valid[-1] + 1
                    P.add('pool', lambda e, dr=dr, ps_=ps_, lo=lo, hi=hi, cs=cs, hh=hh, Sbd=Sbd: e.tensor_copy(
                        out=Sbd[dr][ps_, lo:hi, hh * 64:hh * 64 + 64], in_=dS[dr][ps_, cs[lo]:cs[hi - 1] + 1, :]),
                        r=[('dS', dr, cs[c8], hh) for c8 in valid], w=[('Sbd', dr, sbp)])
            if os.environ.get('HSTOP') == 'p2a1':
                continue
            items2 = [(c8, dr) for c8 in range(8) for dr in range(2)]

            def p2A(k, tb=tb):
                c8, dr = items2[k]
                c = tb * 8 + c8
                cs_ = slice(c * 64, (c + 1) * 64)
                ar = k % 2
                mk = maskF if dr == 0 else maskB
                for hh in range(2):
                    ps_ = slice(hh * 64, hh * 64 + 64)
                    abank = ar * 2 + hh
                    P.add('pe', lambda e, ps_=ps_, abank=abank: e.matmul(
                        psf[abank][0:64, 0:64], lhsT=kf[dr][ps_, cs_], rhs=qf[dr][ps_, cs_], start=True, stop=True),
                        r=[('kf', dr, tb), ('qf', dr, tb)], w=pstok(abank))
                    P.add('dve', lambda e, abank=abank, hh=hh: e.tensor_tensor(
                        out=att[ar][0:64, hh, :], in0=psf[abank][0:64, 0:64], in1=mk[0:64, :], op=ALU.mult),
                        r=pstok(abank) + ['maskF', 'maskB'], w=[('att', ar, hh)])

            def p2B(k, tb=tb, obA=obA, obB=obB, Sbd=Sbd, sbp=sbp):
                c8, dr = items2[k]
                c = tb * 8 + c8
                cs_ = slice(c * 64, (c + 1) * 64)
                ar = k % 2
                first = (dr == 0)
                skip_inter = (dr == 0 and c == 0) or (dr == 1 and c == NCk - 1)
                for hh, ob in ((0, obA), (1, obB)):
                    P.add('pe', lambda e, hh=hh, ob=ob: e.matmul(
                        psf[ob][:, c8 * 64:(c8 + 1) * 64], lhsT=vtk[0:64, c, :], rhs=att[ar][0:64, hh, :],
                        start=first, stop=(dr == 1 and skip_inter)),
                        r=[('att', ar, hh), ('vtk', c)], w=pstok(ob))
                    if not skip_inter:
                        P.add('pe', lambda e, ob=ob: e.matmul(
                            psf[ob][:, c8 * 64:(c8 + 1) * 64], lhsT=Sbd[dr][:, c8, :], rhs=qf[dr][:, cs_],
                            start=False, stop=(dr == 1)),
                            r=[('Sbd', dr, sbp), ('qf', dr, tb)], w=pstok(ob))

            def norm_chain(tb=tb, sl=sl, obA=obA, obB=obB):
                P.add('act', lambda e: e.activation(out=osum[0:64, :], in_=psf[obA][0:64, :], func=AF.Copy), r=pstok(obA), w=['osumA'])
                P.add('act', lambda e: e.activation(out=osum[64:128, :], in_=psf[obB][64:128, :], func=AF.Copy), r=pstok(obB), w=['osumB'])
                P.add('act', lambda e: e.activation(out=osq, in_=osum, func=AF.Square), r=['osumA', 'osumB'], w=['osq'])
                nb_ = pr['i'] % 4
                pr['i'] += 1
                P.add('pe', lambda e: e.matmul(psf[nb_][:, :], lhsT=onesbd, rhs=osq, start=True, stop=True),
                      r=['osq', 'onesbd'], w=pstok(nb_))
                P.add('act', lambda e: e.activation(out=rs8, in_=psf[nb_][:, :], func=AF.Ln, bias=eps256[:, 0:1]),
                      r=pstok(nb_) + ['eps256'], w=['rs8a'])
                P.add('act', lambda e: e.activation(out=rs8, in_=rs8, func=AF.Exp, scale=-0.5), r=['rs8a'], w=['rs8'])
                P.add('dve', lambda e: e.tensor_tensor(out=osum, in0=osum, in1=rs8, op=ALU.mult), r=['osumA', 'osumB', 'rs8'], w=['osn'])
                hr = tb % 2
                P.add('dve', lambda e: e.scalar_tensor_tensor(out=houts[hr], in0=osum, scalar=onw4[:, 0:1], in1=gate[:, sl],
                                                              op0=ALU.mult, op1=ALU.mult),
                      r=['osn', 'onw4', ('gate', tb)], w=[('hout', hr)])
                P.add('sp', lambda e: e.dma_start(allow_slow_non_contiguous=True, out=mix_s[4 + hp, :, sl], in_=houts[hr]),
                      r=[('hout', hr)], w=[('mix_s', 4 + hp, tb)])

            p2A(0)
            for k in range(16):
                if k + 1 < 16:
                    p2A(k + 1)
                p2B(k)
                if k == 5 and prev_norm[0] is not None:
                    prev_norm[0]()
                    prev_norm[0] = None
            prev_norm[0] = norm_chain
        if prev_norm[0] is not None:
            prev_norm[0]()
        state['off'] = base

    def phase_B1(t_base, S):
        base = state['off']
        wo = alloc([KC, D], BF16, "wo")
        wpost = alloc([D], F32, "wpost")
        P.add('sp', lambda e: e.dma_start(allow_slow_non_contiguous=True, out=wpost, in_=bass.AP(vec["norm_mix_post"], 0, [[0, 128], [1, D]])), w=['wpost'])
        mt = [alloc([KC, 512], BF16, "mt%d" % i) for i in range(2)]
        xt = [alloc([D], F32, "xt%d" % i) for i in range(4)]
        ht = [alloc([D], F32, "ht%d" % i) for i in range(4)]
        tm = [alloc([D], F32, "tm%d" % i) for i in range(4)]
        hb_ = [alloc([D], BF16, "hb%d" % i) for i in range(2)]
        junk = alloc([D], BF16, "junk")
        ss = [alloc([4], F32, "ss%d" % i) for i in range(4)]
        P.add('sp', lambda e: e.dma_start(allow_slow_non_contiguous=True, out=wo, in_=wout_b.rearrange("(k p) c -> p k c", p=128)), r=['wscr'], w=['wo'])
        P.add('pool', lambda e: e.memset(XT[:, :, S + 1:S + 2], 0.0), w=['xhalo2'])
        NJ = S // 128

        ld = {'x': 0, 'm': 0}

        def b1_loads(j_upto, g_upto):
            while ld['m'] < min(g_upto, S // 512):
                g_ = ld['m']
                P.add('sp', lambda e, g_=g_: e.dma_start(allow_slow_non_contiguous=True, out=mt[g_ % 2], in_=mix_s[:, :, g_ * 512:(g_ + 1) * 512].rearrange("k p t -> p k t")),
                      r=[], w=[('mt', g_ % 2)])
                ld['m'] += 1
            while ld['x'] < min(j_upto, NJ):
                j_ = ld['x']
                P.add('sp', lambda e, j_=j_: e.dma_start(allow_slow_non_contiguous=True, out=xt[j_ % 4], in_=xs[t_base + j_ * 128: t_base + (j_ + 1) * 128, :]),
                      w=[('xt', j_ % 4)])
                ld['x'] += 1

        def b1a(j):
            g, tt = j // 4, j % 4
            m_ = mt[g % 2]
            b1_loads(j + 3, g + 2 if tt >= 2 else g + 1)
            r_ = j % 4
            x_, h_, t_, s_ = xt[r_], ht[r_], tm[r_], ss[r_]
            nm = 'b1_%d' % r_
            for half in range(2):
                bank = (j % 2) * 2 + half
                hs = slice(half * 512, (half + 1) * 512)
                for kc in range(KC):
                    P.add('pe', lambda e, kc=kc, hs=hs, bank=bank: e.matmul(
                        psf[bank][:, :], lhsT=m_[:, kc, tt * 128:(tt + 1) * 128], rhs=wo[:, kc, hs], start=(kc == 0), stop=(kc == KC - 1)),
                        r=[('mt', g % 2), 'wo'], w=pstok(bank))
                P.add('act', lambda e, bank=bank, half=half: e.activation(out=junk[:, 0:512], in_=psf[bank][:, :], func=AF.Square,
                                                                           accum_out=s_[:, half:half + 1]),
                      r=pstok(bank), w=['junk', (nm, 'p', half)])
                P.add('act', lambda e, bank=bank, hs=hs: e.activation(out=t_[:, hs], in_=psf[bank][:, :], func=AF.Copy),
                      r=pstok(bank), w=[('tm', r_, half)])
            P.add('dve', lambda e: e.tensor_tensor(out=s_[:, 2:3], in0=s_[:, 0:1], in1=s_[:, 1:2], op=ALU.add),
                  r=[(nm, 'p', 0), (nm, 'p', 1)], w=[nm + 'ssq'])
            rstd_from_ssq(s_[:, 2:3], s_[:, 3:4], D, EPS, nm)
            P.add('dve', lambda e: e.scalar_tensor_tensor(out=t_, in0=t_, scalar=s_[:, 3:4], in1=wpost, op0=ALU.mult, op1=ALU.mult),
                  r=[('tm', r_, 0), ('tm', r_, 1), nm + 'rstd', 'wpost'], w=[('tm', r_, 0), ('tm', r_, 1)])
            P.add('dve', lambda e: e.tensor_tensor(out=h_, in0=t_, in1=x_, op=ALU.add),
                  r=[('tm', r_, 0), ('tm', r_, 1), ('xt', r_)], w=[('ht', r_)])
            P.add('sp', lambda e: e.dma_start(allow_slow_non_contiguous=True, out=ys[t_base + j * 128: t_base + (j + 1) * 128, :], in_=h_),
                  r=[('ht', r_)], w=[('ys', j)])
            nm2 = 'b1n_%d' % r_
            P.add('act', lambda e: e.activation(out=junk, in_=h_, func=AF.Square, accum_out=s_[:, 0:1]),
                  r=[('ht', r_)], w=['junk', nm2 + 'ssq'])
            rstd_from_ssq(s_[:, 0:1], s_[:, 1:2], D, EPS, nm2)

        def b1b(j):
            r_ = j % 4
            h_, s_, b_ = ht[r_], ss[r_], hb_[j % 2]
            nm2 = 'b1n_%d' % r_
            P.add('act', lambda e: e.activation(out=b_, in_=h_, func=AF.Copy, scale=s_[:, 1:2]),
                  r=[('ht', r_), nm2 + 'rstd'], w=[('hb', j % 2)])

        def b1c(j):
            b_ = hb_[j % 2]
            bank = 6 + (j % 2)
            for kc in range(KC):
                P.add('pe', lambda e, kc=kc: e.transpose(out=psb[bank][:, kc * 128:(kc + 1) * 128],
                                                       in_=b_[:, kc * 128:(kc + 1) * 128], identity=ident),
                      r=[('hb', j % 2), 'ident'], w=pstok(bank))
            P.add('dve', lambda e: e.tensor_tensor(
                out=XT[:, :, 1 + j * 128: 1 + (j + 1) * 128], in0=psb[bank][:, :].rearrange("p (k t) -> p k t", k=KC),
                in1=bcast_free(wfpre, 128), op=ALU.mult),
                r=pstok(bank) + ['wfpre'], w=[('XT', kc, j) for kc in range(KC)])

        for j in range(NJ + 2):
            if j < NJ:
                b1a(j)
            if 0 <= j - 1 < NJ:
                b1b(j - 1)
            if 0 <= j - 2 < NJ:
                b1c(j - 2)
        state['off'] = base

    def phase_B2(t_base, S):
        base = state['off']
        wfpost = alloc([D], F32, "wfpost")
        P.add('sp', lambda e: e.dma_start(allow_slow_non_contiguous=True, out=wfpost, in_=bass.AP(vec["norm_ffn_post"], 0, [[0, 128], [1, D]])), w=['wfpost'])
        wg = [alloc([KC, 128], BF16, "wg%d" % i) for i in range(6)]
        wu = [alloc([KC, 128], BF16, "wu%d" % i) for i in range(6)]
        wd = [alloc([512], BF16, "wd%d" % i) for i in range(16)]
        hid = alloc([NFC, 512], BF16, "hid")
        Asb = [alloc([514], F32, "Asb%d" % i) for i in range(5)]
        cc = [alloc([512], F32, "cc%d" % i) for i in range(5)]
        c2 = [alloc([512], F32, "c2%d" % i) for i in range(5)]
        c3 = [alloc([512], F32, "c3%d" % i) for i in range(5)]
        fsb = alloc([4, D], F32, "fsb")
        ht = [alloc([D], F32, "ht%d" % i) for i in range(4)]
        junk = alloc([512], BF16, "junk")
        ss = alloc([4, 4], F32, "ssB")
        wi = {'g': 0, 'd': 0}
        NB = S // 512
        pf = {'g': 0, 'd': 0}

        def prefetch(g_upto, d_upto):
            while pf['g'] < min(g_upto, NB * NFC):
                g = pf['g']
                fc_, r3_ = g % NFC, g % 6
                P.add('sp', lambda e, fc_=fc_, r3_=r3_: e.dma_start(allow_slow_non_contiguous=True, out=wg[r3_], in_=wg_b[fc_]), r=['wscr'], w=[('wg', r3_)])
                P.add('sp', lambda e, fc_=fc_, r3_=r3_: e.dma_start(allow_slow_non_contiguous=True, out=wu[r3_], in_=wu_b[fc_]), r=['wscr'], w=[('wu', r3_)])
                pf['g'] += 1
            while pf['d'] < min(d_upto, NB * 2 * NFC):
                dd = pf['d']
                fc_, half_, r4_ = dd % NFC, (dd // NFC) % 2, dd % 16
                P.add('sp', lambda e, fc_=fc_, half_=half_, r4_=r4_: e.dma_start(
                    allow_slow_non_contiguous=True, out=wd[r4_], in_=wd_b[fc_ * 128:(fc_ + 1) * 128, half_ * 512:(half_ + 1) * 512]),
                    r=['wscr'], w=[('wd', r4_)])
                pf['d'] += 1

        for blk in range(NB):
            t0 = blk * 512
            xtoks = lambda kc: [('XT', kc, blk * 4 + q) for q in range(4)]
            def st1(fc, blk=blk, t0=t0):
                r3 = wi['g'] % 6
                wi['g'] += 1
                r2 = fc % 5
                prefetch(wi['g'] + 4, wi['d'] + (10 if fc >= NFC - 6 else 0))
                gb_ = (0, 1)[fc % 2]
                ub_ = (2, 3, 6, 7)[fc % 4]
                hbk = (4, 5)[fc % 2]
                xtoks = lambda kc: [('XT', kc, blk * 4 + q) for q in range(4)]
                for kc in range(KC):
                    P.add('pe', lambda e, kc=kc: e.matmul(psf[gb_][:, :], lhsT=wg[r3][:, kc, :],
                                                          rhs=XT[:, kc, 1 + t0: 1 + t0 + 512], start=(kc == 0), stop=(kc == KC - 1)),
                          r=[('wg', r3)] + xtoks(kc), w=pstok(gb_))
                halo_r = ['xhalo', 'xhalo2'] + [('XT', kc, q) for kc in range(KC) for q in (max(blk * 4 - 1, 0), min(blk * 4 + 4, S // 128 - 1))]
                for kc in range(KC):
                    P.add('pe', lambda e, kc=kc: e.matmul(psf[hbk][:, 0:2], lhsT=wg[r3][:, kc, :],
                                                          rhs=XT[:, kc, t0: t0 + 514: 513], start=(kc == 0), stop=(kc == KC - 1)),
                          r=[('wg', r3)] + halo_r, w=pstok(hbk))
                for kc in range(KC):
                    P.add('pe', lambda e, kc=kc: e.matmul(psf[ub_][:, :], lhsT=wu[r3][:, kc, :],
                                                          rhs=XT[:, kc, 1 + t0: 1 + t0 + 512], start=(kc == 0), stop=(kc == KC - 1)),
                          r=[('wu', r3)] + xtoks(kc), w=pstok(ub_))
                A_, c_ = Asb[r2], cc[r2]
                P.add('act', lambda e: e.activation(out=A_[:, 1:513], in_=psf[gb_][:, :], func=AF.Copy),
                      r=pstok(gb_), w=[('Asb', r2, 0)])
                P.add('act', lambda e: e.activation(out=A_[:, 0:514:513], in_=psf[hbk][:, 0:2], func=AF.Copy),
                      r=pstok(hbk), w=[('Asb', r2, 1)])
                P.add('act', lambda e: e.activation(out=c_, in_=A_[:, 1:513], func=AF.Identity, scale=cw[:, 1, fc:fc + 1],
                                                    bias=cb[:, fc:fc + 1]),
                      r=[('Asb', r2, 0), 'cw', 'cb'], w=[('cc', r2)])

            def st2(fc):
                r2 = fc % 5
                A_, c_, c2_ = Asb[r2], cc[r2], c2[r2]
                P.add('dve', lambda e: e.scalar_tensor_tensor(out=c_, in0=A_[:, 0:512], scalar=cw[:, 0, fc:fc + 1], in1=c_,
                                                              op0=ALU.mult, op1=ALU.add),
                      r=[('Asb', r2, 0), ('Asb', r2, 1), 'cw', ('cc', r2)], w=[('cc', r2)])
                P.add('dve', lambda e: e.scalar_tensor_tensor(out=c_, in0=A_[:, 2:514], scalar=cw[:, 2, fc:fc + 1], in1=c_,
                                                              op0=ALU.mult, op1=ALU.add),
                      r=[('Asb', r2, 0), ('Asb', r2, 1), 'cw', ('cc', r2)], w=[('cc', r2)])
                P.add('act', lambda e: e.activation(out=c2_, in_=c_, func=AF.Square, scale=0.21145921592590237),
                      r=[('cc', r2)], w=[('c2', r2)])

            def st3(fc):
                r2 = fc % 5
                c_, c2_, c3_ = cc[r2], c2[r2], c3[r2]
                P.add('dve', lambda e: e.scalar_tensor_tensor(out=c3_, in0=c2_, scalar=1.0, in1=c_, op0=ALU.add, op1=ALU.mult),
                      r=[('c2', r2), ('cc', r2)], w=[('c3', r2)])
                P.add('act', lambda e: e.activation(out=c3_, in_=c3_, func=AF.Tanh, scale=0.7978845608028654),
                      r=[('c3', r2)], w=[('c3', r2)])

            def st4(fc):
                r2 = fc % 5
                ub_ = (2, 3, 6, 7)[fc % 4]
                c_, c2_, c3_ = cc[r2], c2[r2], c3[r2]
                P.add('dve', lambda e: e.scalar_tensor_tensor(out=c2_, in0=c3_, scalar=1.0, in1=c_, op0=ALU.add, op1=ALU.mult),
                      r=[('c3', r2), ('cc', r2), ('c2', r2)], w=[('c2', r2)])
                P.add('dve', lambda e: e.tensor_tensor(out=hid[:, fc, :], in0=psf[ub_][:, :], in1=c2_, op=ALU.mult),
                      r=pstok(ub_) + [('c2', r2)], w=[('hid', fc)])

            for it in range(NFC + 3):
                if it < NFC:
                    st1(it)
                if 0 <= it - 1 < NFC:
                    st2(it - 1)
                if 0 <= it - 2 < NFC:
                    st3(it - 2)
                if 0 <= it - 3 < NFC:
                    st4(it - 3)
            for tt in range(4):
                j = blk * 4 + tt
                P.add('sp', lambda e, j=j: e.dma_start(allow_slow_non_contiguous=True, out=ht[j % 4], in_=ys[t_base + j * 128: t_base + (j + 1) * 128, :]),
                      r=[('ys', j)], w=[('ht', j % 4)])
            for half in range(2):
                hs = slice(half * 512, (half + 1) * 512)
                for fc in range(NFC):
                    r4 = wi['d'] % 16
                    wi['d'] += 1
                    prefetch(wi['g'] + (5 if (half == 1 and fc >= NFC - 8) else 0), wi['d'] + 11)
                    for tt in range(4):
                        P.add('pe', lambda e, fc=fc, r4=r4, tt=tt: e.matmul(psf[4 + tt][:, :], lhsT=hid[:, fc, tt * 128:(tt + 1) * 128], rhs=wd[r4],
                                                                          start=(fc == 0), stop=(fc == NFC - 1)),
                              r=[('hid', fc), ('wd', r4)], w=pstok(4 + tt))
                for tt in range(4):
                    P.add('act', lambda e, tt=tt, half=half: e.activation(out=junk, in_=psf[4 + tt][:, :], func=AF.Square,
                                                                        accum_out=ss[:, tt, half:half + 1]),
                          r=pstok(4 + tt), w=['junk', ('ssB', tt, half)])
                    P.add('act', lambda e, tt=tt, hs=hs: e.activation(out=fsb[:, tt, hs], in_=psf[4 + tt][:, :], func=AF.Copy),
                          r=pstok(4 + tt), w=[('fsb', tt, half)])
            for tt in range(4):
                j = blk * 4 + tt
                r_ = j % 4
                h_ = ht[r_]
                nm = 'b2_%d' % tt
                P.add('dve', lambda e, tt=tt: e.tensor_tensor(out=ss[:, tt, 2:3], in0=ss[:, tt, 0:1], in1=ss[:, tt, 1:2], op=ALU.add),
                      r=[('ssB', tt, 0), ('ssB', tt, 1)], w=[nm + 'ssq'])
                rstd_from_ssq(ss[:, tt, 2:3], ss[:, tt, 3:4], D, 4.0 * EPS, nm)
                P.add('dve', lambda e, tt=tt: e.scalar_tensor_tensor(out=fsb[:, tt, :], in0=fsb[:, tt, :], scalar=ss[:, tt, 3:4], in1=wfpost,
                                                                   op0=ALU.mult, op1=ALU.mult),
                      r=[('fsb', tt, 0), ('fsb', tt, 1), nm + 'rstd', 'wfpost'], w=[('fsb', tt, 0), ('fsb', tt, 1)])
                P.add('dve', lambda e, tt=tt, h_=h_: e.tensor_tensor(out=h_, in0=h_, in1=fsb[:, tt, :], op=ALU.add),
                      r=[('fsb', tt, 0), ('fsb', tt, 1), ('ht', r_)], w=[('ht', r_)])
                P.add('sp', lambda e, j=j, h_=h_: e.dma_start(allow_slow_non_contiguous=True, out=ys[t_base + j * 128: t_base + (j + 1) * 128, :], in_=h_),
                      r=[('ht', r_)], w=[('ys', j)])
        state['off'] = base

    setup()
    weight_prep()
    P.barrier()
    t_base = 0
    for S in seq_lens:
        phase_A0(t_base, S, branches)
        P.barrier()
        if dbg_xt is not None and t_base == 0:
            P.add('sp', lambda e, S=S: e.dma_start(out=dbg_xt[:, :, 0:S + 1], in_=XT[:, :, 0:S + 1]), w=['dbgxt'])
        for hp in range(4):
            phase_attn(S, hp, branches)
        P.barrier()
        if stop_after == 'attn':
            break
        for hp in range(4):
            phase_hgrn(S, hp)
            P.barrier()
            if stop_after == 'hgrn0':
                break
        if stop_after == 'hgrn0':
            break
        phase_B1(t_base, S)
        P.barrier()
        phase_B2(t_base, S)
        P.barrier()
        t_base += S

    with nc.Block() as block:
        P.emit(nc, block, sems)
    es.close()
    return nc


def rot_tables(SM):
    half = 8
    inv = ROPE_THETA ** (-np.arange(half, dtype=np.float32) * 2.0 / 16.0)
    ang = np.arange(SM, dtype=np.float32)[:, None] * inv[None, :]
    cos = np.cos(ang).astype(np.float32).T
    sin = np.sin(ang).astype(np.float32).T
    c = np.ones((128, SM), np.float32)
    s = np.zeros((128, SM), np.float32)
    for hb in (0, 64):
        c[hb:hb + 8] = cos
        c[hb + 8:hb + 16] = cos
        s[hb:hb + 8] = -sin
        s[hb + 8:hb + 16] = sin
    return c, s


_CACHE = {}


def kernel(x_prompt, x_sample, norm_mix_pre, w_in, hgrn_lb_fwd, hgrn_lb_bwd, hgrn_out_norm, w_out,
           norm_mix_post, norm_ffn_pre, w_gate, w_up, conv_w, conv_b, w_down, norm_ffn_post):
    n = 8
    x_prompt = np.asarray(x_prompt)
    x_sample = np.asarray(x_sample)
    Bp, Sp, _ = x_prompt.shape
    Bs, Ss, _ = x_sample.shape
    pp, sp_ = Bp // n, Bs // n
    seq_lens = tuple([Sp] * pp + [Ss] * sp_)
    if seq_lens not in _CACHE:
        _CACHE[seq_lens] = build(seq_lens)
    nc = _CACHE[seq_lens]
    rc, rs = rot_tables(max(seq_lens))
    f = lambda a: np.ascontiguousarray(np.asarray(a, dtype=np.float32))
    common = {
        "w_in": f(w_in)[0], "w_out": f(w_out)[0], "w_gate": f(w_gate)[0], "w_up": f(w_up)[0], "w_down": f(w_down)[0],
        "norm_mix_pre": f(norm_mix_pre)[0], "norm_mix_post": f(norm_mix_post)[0], "norm_ffn_pre": f(norm_ffn_pre)[0],
        "norm_ffn_post": f(norm_ffn_post)[0], "conv_b": f(conv_b)[0], "hgrn_out_norm": f(hgrn_out_norm)[0],
        "conv_w": f(conv_w)[0], "hgrn_lb_fwd": f(hgrn_lb_fwd), "hgrn_lb_bwd": f(hgrn_lb_bwd),
        "rot_c": rc, "rot_s": rs,
    }
    in_maps = []
    for c in range(n):
        xs = np.concatenate([x_prompt[c * pp:(c + 1) * pp].reshape(-1, D), x_sample[c * sp_:(c + 1) * sp_].reshape(-1, D)], axis=0)
        m = dict(common)
        m["xs"] = np.ascontiguousarray(xs, dtype=np.float32)
        in_maps.append(m)
    res = run_bass_kernel_spmd(nc, in_maps, core_ids=list(range(n)))
    yp = np.empty((Bp, Sp, D), np.float32)
    ysm = np.empty((Bs, Ss, D), np.float32)
    for c in range(n):
        y = res.results[c]["ys"]
        yp[c * pp:(c + 1) * pp] = y[:pp * Sp].reshape(pp, Sp, D)
        ysm[c * sp_:(c + 1) * sp_] = y[pp * Sp:].reshape(sp_, Ss, D)
    return (yp, ysm)
```

```python
import os
import numpy as np
import ml_dtypes
import concourse.bass as bass
import concourse.mybir as mybir
from concourse.bass_utils import run_bass_kernel_spmd

F32 = mybir.dt.float32
BF16 = mybir.dt.bfloat16
AF = mybir.ActivationFunctionType
ALU = mybir.AluOpType

D = 1024
KC = 8
INW = 4096
DFF = 2816
NFC = 22
EPS = 1e-6
ROPE_THETA = 500000.0
BRANCHES = (1, 4, 16)
CH = 64
COMPUTE = ('pe', 'act', 'dve', 'pool')


class Prog:
    def __init__(self, ndma=12):
        self.ndma = ndma
        self.lists = {e: [] for e in COMPUTE + ('sp',)}
        self.bystream = {}
        self.tok = {}
        self.vc = {e: {} for e in COMPUTE + ('sp',)}
        self.dma_n = 0
        self.pending_barrier = {}

    def barrier(self):
        deps = set()
        for s, ops in self.bystream.items():
            if ops:
                deps.add((s, len(ops) - 1))
        for e in self.lists:
            self.pending_barrier[e] = set(deps)
        self.tok = {}

    def add(self, eng, fn, r=(), w=()):
        if eng == 'sp':
            stream = 'd%d' % (self.dma_n % self.ndma)
            self.dma_n += 1
        else:
            stream = eng
        slist = self.bystream.setdefault(stream, [])
        sidx = len(slist)
        deps = set()
        if eng in self.pending_barrier:
            deps |= self.pending_barrier.pop(eng)
        for k in r:
            st = self.tok.get(k)
            if st is not None and st[0] is not None:
                deps.add(st[0])
        for k in w:
            st = self.tok.get(k)
            if st is not None:
                if st[0] is not None:
                    deps.add(st[0])
                deps.update(st[1])
        if eng == 'sp' and sidx > 0:
            deps.add((stream, sidx - 1))
        vc = self.vc[eng]
        waits = {}
        for (s, i) in deps:
            if s == 'pe' and eng == 'pe':
                continue
            if vc.get(s, -1) < i:
                if waits.get(s, -1) < i:
                    waits[s] = i
        for s, i in waits.items():
            dop = self.bystream[s][i]
            dop['sig'] = True
            for s2, i2 in dop['vc'].items():
                if vc.get(s2, -1) < i2:
                    vc[s2] = i2
            if vc.get(s, -1) < i:
                vc[s] = i
        ovc = dict(vc)
        ovc[stream] = sidx
        op = dict(eng=eng, fn=fn, stream=stream, sidx=sidx, waits=waits, sig=False, vc=ovc)
        slist.append(op)
        self.lists[eng].append(op)
        me = (stream, sidx)
        for k in r:
            st = self.tok.setdefault(k, [None, []])
            st[1].append(me)
        for k in w:
            self.tok[k] = [me, []]
        return op

    def emit(self, nc, block, sems):
        counts = {}
        for s in COMPUTE:
            c = 0
            arr = []
            for op in self.bystream.get(s, []):
                if op['sig']:
                    c += 1
                arr.append(c)
            counts[s] = arr

        def val(s, i):
            if s in COMPUTE:
                return counts[s][i]
            return 16 * (i + 1)

        def run(e, ename):
            for op in self.lists[ename]:
                for s, i in op['waits'].items():
                    e.wait_ge(sems[s], val(s, i))
                ins = op['fn'](e)
                if ename == 'sp':
                    ins.then_inc(sems[op['stream']], 16)
                elif op['sig']:
                    ins.then_inc(sems[ename], 1)
            if ename == 'sp':
                for s, ops in self.bystream.items():
                    if s not in COMPUTE and ops:
                        e.wait_ge(sems[s], 16 * len(ops))

        @block.sync
        def _(e):
            run(e, 'sp')

        @block.tensor
        def _(e):
            run(e, 'pe')

        @block.scalar
        def _(e):
            run(e, 'act')

        @block.vector
        def _(e):
            run(e, 'dve')

        @block.gpsimd
        def _(e):
            run(e, 'pool')


def bcast_free(ap, n):
    return bass.AP(ap.tensor, ap.offset, [list(x) for x in ap.ap] + [[0, n]])


def bcast_mid(ap, n):
    l = [list(x) for x in ap.ap]
    return bass.AP(ap.tensor, ap.offset, [l[0], [0, n]] + l[1:])


def sst(lo, n, d):
    return slice(lo, lo + (n - 1) * d + 1, d)


def pstok(bank, lo=0, hi=0):
    return [('ps', bank)]


DEBUG_OFFS = {}


def build(seq_lens, branches=BRANCHES, stop_after=None):
    nc = bass.Bass("TRN2", target_bir_lowering=False)
    NT = sum(seq_lens)
    SM = max(seq_lens)
    dt = nc.dram_tensor
    xs = dt("xs", [NT, D], F32, kind="ExternalInput").ap()
    ys = dt("ys", [NT, D], F32, kind="ExternalOutput").ap()
    w_in = dt("w_in", [D, INW], F32, kind="ExternalInput").ap()
    w_out = dt("w_out", [D, D], F32, kind="ExternalInput").ap()
    w_gate = dt("w_gate", [D, DFF], F32, kind="ExternalInput").ap()
    w_up = dt("w_up", [D, DFF], F32, kind="ExternalInput").ap()
    w_down = dt("w_down", [DFF, D], F32, kind="ExternalInput").ap()
    vec = {}
    for nm, n in (("norm_mix_pre", D), ("norm_mix_post", D), ("norm_ffn_pre", D), ("norm_ffn_post", D),
                  ("conv_b", DFF), ("hgrn_out_norm", 64)):
        vec[nm] = dt(nm, [n], F32, kind="ExternalInput")
    conv_w = dt("conv_w", [3, DFF], F32, kind="ExternalInput")
    lbf = dt("hgrn_lb_fwd", [2, 512], F32, kind="ExternalInput")
    lbb = dt("hgrn_lb_bwd", [2, 512], F32, kind="ExternalInput")
    rotc_d = dt("rot_c", [128, SM], F32, kind="ExternalInput").ap()
    rots_d = dt("rot_s", [128, SM], F32, kind="ExternalInput").ap()
    win_b = dt("win_b", [D, INW], BF16, kind=("ExternalOutput" if os.environ.get("KDEBUG") else "Internal")).ap()
    winsw_b = dt("winsw_b", [D, 1024], BF16, kind="Internal").ap()
    wout_b = dt("wout_b", [D, D], BF16, kind="Internal").ap()
    wg_b = dt("wg_b", [NFC, 128, KC, 128], BF16, kind="Internal").ap()
    wu_b = dt("wu_b", [NFC, 128, KC, 128], BF16, kind="Internal").ap()
    wd_b = dt("wd_b", [DFF, D], BF16, kind="Internal").ap()
    mix_s = dt("mix_s", [KC, 128, SM], BF16, kind=("ExternalOutput" if os.environ.get("KDEBUG") else "Internal")).ap()

    dbg_xt = dt("dbg_xt", [128, KC, SM + 2], BF16, kind="ExternalOutput").ap() if os.environ.get("KDEBUG") else None
    P = Prog()
    from contextlib import ExitStack
    es = ExitStack()
    ARF = 53200
    arena = es.enter_context(nc.sbuf_tensor("arena", [128, ARF], F32))
    arena_b = arena.bitcast(BF16)
    psf = [es.enter_context(nc.psum_tensor("ps%d" % i, [128, 512], F32)) for i in range(8)]
    psb = [p.bitcast(BF16) for p in psf]
    sems = {}
    for s in list(COMPUTE) + ['d%d' % i for i in range(P.ndma)]:
        sems[s] = es.enter_context(nc.semaphore("sem_" + s))

    state = {'off': 0, 'uid': 0}

    def alloc(shape, dtype, name):
        n = 1
        for s_ in shape:
            n *= s_
        esz = 4 if dtype == F32 else 2
        off = (state['off'] + 31) // 32 * 32
        state['off'] = off + n * esz
        assert state['off'] <= ARF * 4, ("SBUF arena overflow", name, state['off'])
        DEBUG_OFFS[name] = (off, list(shape), 'f32' if dtype == F32 else 'bf16')
        base = arena if dtype == F32 else arena_b
        o = off // esz
        v = base[:, o:o + n]
        if len(shape) == 2:
            v = v.rearrange("p (a b) -> p a b", a=shape[0])
        elif len(shape) == 3:
            v = v.rearrange("p (a b c) -> p a b c", a=shape[0], b=shape[1])
        return v

    XT = alloc([KC, SM + 2], BF16, "xnT")
    ident = alloc([128], BF16, "ident")
    maskA = alloc([256], BF16, "maskA")
    maskMB = alloc([256], BF16, "maskMB")
    maskF = alloc([64], BF16, "maskF")
    maskB = alloc([64], BF16, "maskB")
    onesbd = alloc([128], F32, "onesbd")
    esel = alloc([64], F32, "esel")
    cneg = alloc([1], F32, "cneg")
    eps256 = alloc([1], F32, "eps256")
    wpre = alloc([KC], F32, "wpre")
    wfpre = alloc([KC], F32, "wfpre")
    cw = alloc([3, NFC], F32, "cw")
    cb = alloc([NFC], F32, "cb")
    onw4 = alloc([1], F32, "onw4")
    lbt = alloc([2, 2, 4], F32, "lbt")
    ga = alloc([2, 4], F32, "ga")
    gb = alloc([2, 4], F32, "gb")
    gna = alloc([2, 4], F32, "gna")
    gnb = alloc([2, 4], F32, "gnb")
    state['off'] += int(os.environ.get('KPAD', '0'))
    PERSIST = state['off']

    def setup():
        P.add('pool', lambda e: e.memset(ident, 0.0), w=['ident'])
        P.add('pool', lambda e: e.affine_select(out=ident, in_=ident, pattern=[[-1, 128]], compare_op=ALU.not_equal,
                                                 fill=1.0, base=0, channel_multiplier=1), r=['ident'], w=['ident'])
        P.add('pool', lambda e: e.memset(maskA, 1.0), w=['maskA'])
        P.add('pool', lambda e: e.affine_select(out=maskA, in_=maskA, pattern=[[1, 256]], compare_op=ALU.is_ge,
                                                 fill=0.0, base=0, channel_multiplier=-1), r=['maskA'], w=['maskA'])
        P.add('pool', lambda e: e.affine_select(out=maskA, in_=maskA, pattern=[[-1, 256]], compare_op=ALU.is_ge,
                                                 fill=0.0, base=128, channel_multiplier=1), r=['maskA'], w=['maskA'])
        P.add('dve', lambda e: e.tensor_scalar(out=maskMB, in0=maskA, scalar1=-1.0, scalar2=30000.0, op0=ALU.add, op1=ALU.mult),
              r=['maskA'], w=['maskMB'])
        P.add('pool', lambda e: e.memset(maskF[0:64, :], 1.0), w=['maskF'])
        P.add('pool', lambda e: e.affine_select(out=maskF[0:64, :], in_=maskF[0:64, :], pattern=[[1, 64]], compare_op=ALU.is_ge,
                                                 fill=0.0, base=0, channel_multiplier=-1), r=['maskF'], w=['maskF'])
        P.add('pool', lambda e: e.memset(maskB[0:64, :], 1.0), w=['maskB'])
        P.add('pool', lambda e: e.affine_select(out=maskB[0:64, :], in_=maskB[0:64, :], pattern=[[-1, 64]], compare_op=ALU.is_ge,
                                                 fill=0.0, base=0, channel_multiplier=1), r=['maskB'], w=['maskB'])
        P.add('pool', lambda e: e.memset(onesbd, 0.0), w=['onesbd'])
        P.add('pool', lambda e: e.memset(onesbd[0:64, 0:64], 1.0), r=['onesbd'], w=['onesbd'])
        P.add('pool', lambda e: e.memset(onesbd[64:128, 64:128], 1.0), r=['onesbd'], w=['onesbd'])
        P.add('pool', lambda e: e.memset(esel[0:65, :], 0.0), w=['esel'])
        P.add('pool', lambda e: e.memset(esel[64:65, :], 1.0), r=['esel'], w=['esel'])
        P.add('pool', lambda e: e.memset(cneg, -0.5), w=['cneg'])
        P.add('pool', lambda e: e.memset(eps256, 256.0 * EPS), w=['eps256'])
        P.add('pool', lambda e: e.memset(XT[:, :, 0:1], 0.0), w=['xhalo'])
        with nc.allow_non_contiguous_dma(reason="tiny per-feature vectors"):
            P.add('sp', lambda e: e.dma_start(allow_slow_non_contiguous=True, out=wpre, in_=vec["norm_mix_pre"].ap().rearrange("(k p) -> p k", p=128)), w=['wpre'])
            P.add('sp', lambda e: e.dma_start(allow_slow_non_contiguous=True, out=wfpre, in_=vec["norm_ffn_pre"].ap().rearrange("(k p) -> p k", p=128)), w=['wfpre'])
            P.add('sp', lambda e: e.dma_start(allow_slow_non_contiguous=True, out=cw, in_=conv_w.ap().rearrange("w (f p) -> p w f", p=128)), w=['cw'])
            P.add('sp', lambda e: e.dma_start(allow_slow_non_contiguous=True, out=cb, in_=vec["conv_b"].ap().rearrange("(f p) -> p f", p=128)), w=['cb'])
            P.add('sp', lambda e: e.dma_start(allow_slow_non_contiguous=True, out=onw4[0:64, :], in_=vec["hgrn_out_norm"].ap().rearrange("(p o) -> p o", o=1)), w=['onw4a'])
            P.add('sp', lambda e: e.dma_start(allow_slow_non_contiguous=True, out=onw4[64:128, :], in_=vec["hgrn_out_norm"].ap().rearrange("(p o) -> p o", o=1)), w=['onw4b'])
            P.add('sp', lambda e: e.dma_start(allow_slow_non_contiguous=True, out=lbt[:, 0, :, :], in_=lbf.ap().rearrange("s (c p) -> p s c", p=128)), w=['lbt0'])
            P.add('sp', lambda e: e.dma_start(allow_slow_non_contiguous=True, out=lbt[:, 1, :, :], in_=lbb.ap().rearrange("s (c p) -> p s c", p=128)), w=['lbt1'])
        P.add('dve', lambda e: e.tensor_scalar(out=onw4, in0=onw4, scalar1=4.0, scalar2=None, op0=ALU.mult),
              r=['onw4a', 'onw4b'], w=['onw4'])
        P.add('dve', lambda e: e.tensor_tensor(out=ga, in0=lbt[:, :, 0, :], in1=lbt[:, :, 1, :], op=ALU.subtract),
              r=['lbt0', 'lbt1'], w=['ga'])
        P.add('act', lambda e: e.activation(out=gb, in_=ga, func=AF.Tanh, scale=0.5), r=['ga'], w=['gb'])
        P.add('dve', lambda e: e.tensor_scalar(out=ga, in0=gb, scalar1=0.25, scalar2=0.75, op0=ALU.mult, op1=ALU.add),
              r=['gb'], w=['ga'])
        P.add('dve', lambda e: e.tensor_scalar(out=gna, in0=gb, scalar1=-0.25, scalar2=0.25, op0=ALU.mult, op1=ALU.add),
              r=['gb'], w=['gna'])
        P.add('dve', lambda e: e.tensor_scalar(out=gnb, in0=gb, scalar1=0.25, scalar2=-0.25, op0=ALU.mult, op1=ALU.add),
              r=['gb'], w=['gnb'])
        P.add('dve', lambda e: e.tensor_scalar(out=gb, in0=gb, scalar1=-0.25, scalar2=0.25, op0=ALU.mult, op1=ALU.add),
              r=['gb', 'gna', 'gnb'], w=['gb'])

    def weight_prep():
        base = state['off']
        st = [alloc([4096], F32, "wst%d" % i) for i in range(2)]
        bt = [alloc([4096], BF16, "wbt%d" % i) for i in range(2)]
        sw2 = alloc([1024], BF16, "wsw")
        sw = sw2.rearrange("p (h d) -> p h d", h=16)
        P.add('pool', lambda e: e.memset(sw2, 0.0), w=['wsw'])
        it = [0]

        def cast_rows(src, dst, ncols, sw_dst=None, dst_rearr=None):
            r_ = it[0] % 2
            it[0] += 1
            s_, b_ = st[r_], bt[r_]
            P.add('sp', lambda e: e.dma_start(allow_slow_non_contiguous=True, out=s_[:, 0:ncols], in_=src), w=[('wst', r_)])
            h1 = ncols // 2
            P.add('act', lambda e: e.activation(out=b_[:, 0:h1], in_=s_[:, 0:h1], func=AF.Copy), r=[('wst', r_)], w=[('wbt', r_, 0)])
            P.add('dve', lambda e: e.tensor_copy(out=b_[:, h1:ncols], in_=s_[:, h1:ncols]), r=[('wst', r_)], w=[('wbt', r_, 1)])
            if sw_dst is not None:
                sv = s_[:, 0:1024].rearrange("p (h d) -> p h d", h=16)
                P.add('pool', lambda e: e.tensor_copy(out=sw[:, :, 0:8], in_=sv[:, :, 8:16]), r=[('wst', r_)], w=['wsw'])
                P.add('pool', lambda e: e.tensor_copy(out=sw[:, :, 8:16], in_=sv[:, :, 0:8]), r=[('wst', r_)], w=['wsw'])
                P.add('sp', lambda e: e.dma_start(allow_slow_non_contiguous=True, out=sw_dst, in_=sw2), r=['wsw'], w=[('winsw_b', it[0])])
            if dst_rearr is None:
                P.add('sp', lambda e: e.dma_start(allow_slow_non_contiguous=True, out=dst, in_=b_[:, 0:ncols]), r=[('wbt', r_, 0), ('wbt', r_, 1)], w=[('wscr', it[0])])
            else:
                P.add('sp', lambda e: e.dma_start(allow_slow_non_contiguous=True, out=dst, in_=b_[:, 0:ncols].rearrange("p (f j) -> p f j", j=128)),
                      r=[('wbt', r_, 0), ('wbt', r_, 1)], w=[('wscr', it[0])])

        for kc in range(KC):
            rs = slice(kc * 128, (kc + 1) * 128)
            cast_rows(w_in[rs, :], win_b[rs, :], INW, sw_dst=winsw_b[rs, :])
        for kc in range(KC):
            rs = slice(kc * 128, (kc + 1) * 128)
            cast_rows(w_out[rs, :], wout_b[rs, :], D)
        with nc.allow_non_contiguous_dma(reason="chunked weight scratch, 256B segments, one-time"):
            for kc in range(KC):
                rs = slice(kc * 128, (kc + 1) * 128)
                cast_rows(w_gate[rs, :], wg_b[:, :, kc, :].rearrange("f p j -> p f j"), DFF, dst_rearr=True)
                cast_rows(w_up[rs, :], wu_b[:, :, kc, :].rearrange("f p j -> p f j"), DFF, dst_rearr=True)
        for fc in range(NFC):
            rs = slice(fc * 128, (fc + 1) * 128)
            cast_rows(w_down[rs, :], wd_b[rs, :], D)
        state['off'] = base

    pr = {'i': 0}

    def rstd_from_ssq(ssq, out, n, eps, name):
        P.add('dve', lambda e: e.tensor_scalar(out=out, in0=ssq, scalar1=1.0 / n, scalar2=eps, op0=ALU.mult, op1=ALU.add),
              r=[name + 'ssq'], w=[name + 'rstd'])
        P.add('pool', lambda e: e.tensor_tensor(out=out, in0=out, in1=cneg, op=ALU.pow),
              r=[name + 'rstd', 'cneg'], w=[name + 'rstd'])

    def phase_A0(t_base, S, B):
        base = state['off']
        xt = [alloc([D], F32, "xt%d" % i) for i in range(4)]
        xb = [alloc([D], BF16, "xb%d" % i) for i in range(2)]
        junk = alloc([D], BF16, "junk")
        ss = [alloc([2], F32, "ss%d" % i) for i in range(4)]
        NJ = S // 128

        def a1(j):
            r_ = j % 4
            x_, s_ = xt[r_], ss[r_]
            nm = 'a0_%d' % r_
            P.add('sp', lambda e, j=j, x_=x_: e.dma_start(allow_slow_non_contiguous=True, out=x_, in_=xs[t_base + j * 128: t_base + (j + 1) * 128, :]), w=[('xt', r_)])
            P.add('act', lambda e, x_=x_, s_=s_: e.activation(out=junk, in_=x_, func=AF.Square, accum_out=s_[:, 0:1]),
                  r=[('xt', r_)], w=['junk', nm + 'ssq'])
            rstd_from_ssq(s_[:, 0:1], s_[:, 1:2], D, EPS, nm)

        def a2(j):
            r_ = j % 4
            x_, s_, b_ = xt[r_], ss[r_], xb[j % 2]
            nm = 'a0_%d' % r_
            P.add('act', lambda e, x_=x_, s_=s_, b_=b_: e.activation(out=b_, in_=x_, func=AF.Copy, scale=s_[:, 1:2]),
                  r=[('xt', r_), nm + 'rstd'], w=[('xb', j % 2)])

        def a3(j):
            b_ = xb[j % 2]
            bank = 6 + (j % 2)
            for kc in range(KC):
                P.add('pe', lambda e, kc=kc, b_=b_, bank=bank: e.transpose(out=psb[bank][:, kc * 128:(kc + 1) * 128],
                                                                          in_=b_[:, kc * 128:(kc + 1) * 128], identity=ident),
                      r=[('xb', j % 2), 'ident'], w=pstok(bank))
            P.add('dve', lambda e, j=j, bank=bank: e.tensor_tensor(
                out=XT[:, :, 1 + j * 128: 1 + (j + 1) * 128],
                in0=psb[bank][:, :].rearrange("p (k t) -> p k t", k=KC),
                in1=bcast_free(wpre, 128), op=ALU.mult),
                r=pstok(bank) + ['wpre'], w=[('XT', kc, j) for kc in range(KC)])

        for j in range(NJ + 2):
            if j < NJ:
                a1(j)
            if 0 <= j - 1 < NJ:
                a2(j - 1)
            if 0 <= j - 2 < NJ:
                a3(j - 2)
        state['off'] = base

    def load_wA(cols_main, cols_sw, wA):
        i = 0
        with nc.allow_non_contiguous_dma(reason="weight column chunk, 256B segments"):
            for c in cols_main:
                P.add('sp', lambda e, c=c, i=i: e.dma_start(allow_slow_non_contiguous=True, out=wA[i], in_=win_b[:, c:c + 128].rearrange("(k p) j -> p k j", p=128)),
                      r=['wscr'], w=[('wA', i)])
                i += 1
            for c in cols_sw:
                P.add('sp', lambda e, c=c, i=i: e.dma_start(allow_slow_non_contiguous=True, out=wA[i], in_=winsw_b[:, c:c + 128].rearrange("(k p) j -> p k j", p=128)),
                      r=['winsw_b'], w=[('wA', i)])
                i += 1

    def proj(wA_i, tb, bank):
        for kc in range(KC):
            P.add('pe', lambda e, kc=kc: e.matmul(psf[bank][:, :], lhsT=wA_i[1][:, kc, :], rhs=XT[:, kc, 1 + tb * 512: 1 + (tb + 1) * 512],
                                                   start=(kc == 0), stop=(kc == KC - 1)),
                  r=[('wA', wA_i[0])] + [('XT', kc, tb * 4 + q) for q in range(4)], w=pstok(bank, 0, 2048))

    def phase_attn(S, hp, B):
        base = state['off']
        wA = [alloc([KC, 128], BF16, "wA%d" % i) for i in range(5)]
        rotc = [alloc([512], F32, "rotc%d" % i) for i in range(2)]
        rots = [alloc([512], F32, "rots%d" % i) for i in range(2)]
        qT = alloc([S], BF16, "qT")
        kT = alloc([S], BF16, "kT")
        vT = alloc([S], BF16, "vT")
        NTL = S // 128
        vtok = [alloc([NTL, 2, 65], BF16, "vtok%d" % b) for b in range(len(B))]
        tA = [alloc([512], F32, "tA%d" % i) for i in range(2)]
        tB = [alloc([512], F32, "tB%d" % i) for i in range(2)]
        praw = [alloc([256], BF16, "praw%d" % i) for i in range(4)]
        pmk = [alloc([256], BF16, "pmk%d" % i) for i in range(4)]
        UTs = [alloc([S], F32, "UT%d" % i) for i in range(2)]
        pending_norm = [None]
        rrow = alloc([512], F32, "rrow")
        aout = [alloc([512], BF16, "aout%d" % i) for i in range(2)]
        load_wA([hp * 128, 512 + hp * 128, 1024 + hp * 128], [hp * 128, 512 + hp * 128], wA)
        for b in range(len(B)):
            P.add('pool', lambda e, b=b: e.memset(vtok[b][:, :, :, 64:65], 1.0), w=[('vones', b)])
        P.add('pool', lambda e: e.memset(rrow[0:65, :], 0.0), w=['rrow'])
        for tb in range(S // 512):
            sl = slice(tb * 512, (tb + 1) * 512)
            rr = tb % 2
            P.add('sp', lambda e, rr=rr, sl=sl: e.dma_start(out=rotc[rr], in_=rotc_d[:, sl]), w=[('rotc', rr)])
            P.add('sp', lambda e, rr=rr, sl=sl: e.dma_start(out=rots[rr], in_=rots_d[:, sl]), w=[('rots', rr)])
            for (dst, wi, swi, nm) in ((qT, 0, 3, 'qT'), (kT, 1, 4, 'kT')):
                b0 = pr['i'] % 4
                b1 = (pr['i'] + 1) % 4
                pr['i'] += 2
                r_ = (pr['i'] // 2) % 2
                proj((wi, wA[wi]), tb, b0)
                proj((swi, wA[swi]), tb, b1)
                P.add('dve', lambda e, b0=b0, r_=r_, rr=rr: e.tensor_tensor(out=tA[r_], in0=psf[b0][:, :], in1=rotc[rr], op=ALU.mult),
                      r=pstok(b0, 0, 2048) + [('rotc', rr)], w=[('tA', r_)])
                P.add('dve', lambda e, b1=b1, r_=r_, rr=rr: e.tensor_tensor(out=tB[r_], in0=psf[b1][:, :], in1=rots[rr], op=ALU.mult),
                      r=pstok(b1, 0, 2048) + [('rots', rr)], w=[('tB', r_)])
                P.add('dve', lambda e, dst=dst, r_=r_, sl=sl: e.tensor_tensor(out=dst[:, sl], in0=tA[r_], in1=tB[r_], op=ALU.add),
                      r=[('tA', r_), ('tB', r_)], w=[(nm, tb)])
            b0 = pr['i'] % 4
            pr['i'] += 1
            proj((2, wA[2]), tb, b0)
            P.add('act', lambda e, b0=b0, sl=sl: e.activation(out=vT[:, sl], in_=psf[b0][:, :], func=AF.Copy),
                  r=pstok(b0, 0, 2048), w=[('vT', tb)])
        for b, d in enumerate(B):
            L = S // d
            for r in range(d):
                for i0 in range(0, L // 128, 4):
                    n4 = min(4, L // 128 - i0)
                    bank = 6 + (pr['i'] % 2)
                    pr['i'] += 1
                    for ii in range(n4):
                        i = i0 + ii
                        lo = r + d * 128 * i
                        P.add('pe', lambda e, ii=ii, lo=lo, bank=bank, d=d: e.transpose(
                            out=psb[bank][:, ii * 128:(ii + 1) * 128], in_=vT[:, sst(lo, 128, d)], identity=ident),
                            r=[('vT', t) for t in range(lo // 512, (lo + 128 * d - d) // 512 + 1)] + ['ident'],
                            w=pstok(bank, ii * 256, ii * 256 + 256))
                    t0 = r * (L // 128) + i0
                    P.add('dve', lambda e, b=b, t0=t0, n4=n4, bank=bank: e.tensor_copy(
                        out=vtok[b][:, t0:t0 + n4, :, 0:64],
                        in_=psb[bank][:, 0:n4 * 128].rearrange("p (a h c) -> p a h c", a=n4, h=2)),
                        r=pstok(bank, 0, n4 * 256), w=[('vtok', b, t0 + q) for q in range(n4)])
        for h in range(2):
            hb = h * 64
            UT = UTs[h]
            items = []
            for b, d in enumerate(B):
                L = S // d
                NCH = L // 128
                for r in range(d):
                    for i in range(NCH):
                        items.append((b, d, L, NCH, r, i))

            def stA(k, hb=hb):
                b, d, L, NCH, r, i = items[k]
                qlo = max(0, 128 * i - 64)
                qhi = min(L, 128 * i + 192)
                nq = qhi - qlo
                off = qlo - (128 * i - 64)
                slot = k % 4
                bank = (0, 1, 4, 5)[slot]
                klo = r + d * 128 * i
                qpl = r + d * qlo
                ktoks = [('kT', t) for t in range(klo // 512, (klo + 127 * d) // 512 + 1)]
                qtoks = [('qT', t) for t in range(qpl // 512, (qpl + (nq - 1) * d) // 512 + 1)]
                P.add('pe', lambda e: e.matmul(
                    psf[bank][:, 0:nq], lhsT=kT[hb:hb + 64, sst(klo, 128, d)],
                    rhs=qT[hb:hb + 64, sst(qpl, nq, d)], start=True, stop=True),
                    r=ktoks + qtoks, w=pstok(bank))
                P.add('act', lambda e: e.activation(
                    out=praw[slot][:, 0:nq], in_=psf[bank][:, 0:nq], func=AF.Exp, scale=0.125),
                    r=pstok(bank), w=[('praw', slot)])
                P.add('dve', lambda e: e.tensor_tensor(
                    out=pmk[slot][:, 0:nq], in0=praw[slot][:, 0:nq], in1=maskA[:, off:off + nq], op=ALU.mult),
                    r=[('praw', slot), 'maskA'], w=[('pmk', slot)])

            def stB(k, h=h, UT=UT):
                b, d, L, NCH, r, i = items[k]
                qlo = max(0, 128 * i - 64)
                slot = k % 4
                vt = vtok[b][:, r * NCH + i, h, :]
                for n in (i, i + 1):
                    jlo = max(0, 128 * n - 64)
                    jhi = min(L, 128 * n + 64)
                    nb = jhi - jlo
                    c0 = jlo - qlo
                    obk = (6, 7, 2, 3)[n % 4]
                    first = (n == i + 1) or (i == 0)
                    last = (n == i) or (i == NCH - 1)
                    P.add('pe', lambda e, nb=nb, c0=c0, first=first, last=last, obk=obk: e.matmul(
                        psf[obk][0:65, 0:nb], lhsT=vt, rhs=pmk[slot][:, c0:c0 + nb],
                        start=first, stop=last),
                        r=[('pmk', slot), ('vtok', b, r * NCH + i), ('vones', b)], w=pstok(obk))
                    if last:
                        plo = r + d * jlo
                        is_end = (k == len(items) - 1 or items[k + 1][0] != b) and n == i + 1 or \
                                 ((k == len(items) - 1 or items[k + 1][0] != b) and i == NCH - 1 and n == i and NCH - 1 == i and False)
                        wtok = [('UTop', h, b, k, n)]
                        if (k == len(items) - 1 or items[k + 1][0] != b) and n == i + 1:
                            wtok.append(('UTend', h, b))
                        if b == 0:
                            P.add('act', lambda e, nb=nb, plo=plo, obk=obk: e.activation(
                                out=UT[0:65, sst(plo, nb, d)], in_=psf[obk][0:65, 0:nb], func=AF.Copy),
                                r=pstok(obk) + [('UTnorm', h)], w=wtok)
                        else:
                            P.add('dve', lambda e, nb=nb, plo=plo, obk=obk: e.tensor_tensor(
                                out=UT[0:65, sst(plo, nb, d)], in0=psf[obk][0:65, 0:nb],
                                in1=UT[0:65, sst(plo, nb, d)], op=ALU.add),
                                r=pstok(obk) + [('UTend', h, b - 1)], w=wtok)

            LA = 2
            for k in range(min(LA, len(items))):
                stA(k)
            for k in range(len(items)):
                if k + LA < len(items):
                    stA(k + LA)
                stB(k)
                if k == 8 and pending_norm[0] is not None:
                    pending_norm[0]()
                    pending_norm[0] = None

            def norm(h=h, hb=hb, UT=UT):
                for tb in range(S // 512):
                    sl = slice(tb * 512, (tb + 1) * 512)
                    bank = pr['i'] % 4
                    pr['i'] += 1
                    P.add('dve', lambda e, sl=sl: e.reciprocal(out=rrow[64:65, :], in_=UT[64:65, sl]), r=[('UTend', h, len(B) - 1)], w=['rrow'])
                    P.add('pe', lambda e, bank=bank: e.matmul(psf[bank][0:64, :], lhsT=esel[0:65, :], rhs=rrow[0:65, :], start=True, stop=True),
                          r=['rrow', 'esel'], w=pstok(bank))
                    ar_ = tb % 2
                    P.add('dve', lambda e, sl=sl, bank=bank, ar_=ar_: e.tensor_tensor(out=aout[ar_][0:64, :], in0=psf[bank][0:64, :], in1=UT[0:64, sl], op=ALU.mult),
                          r=pstok(bank) + [('UTend', h, len(B) - 1)], w=[('aout', ar_)] + ([('UTnorm', h)] if tb == S // 512 - 1 else []))
                    P.add('sp', lambda e, sl=sl, ar_=ar_: e.dma_start(allow_slow_non_contiguous=True, out=mix_s[hp, hb:hb + 64, sl], in_=aout[ar_][0:64, :]),
                          r=[('aout', ar_)], w=[('mix_s', hp, h, tb)])
            if pending_norm[0] is not None:
                pending_norm[0]()
            pending_norm[0] = norm
        if pending_norm[0] is not None:
            pending_norm[0]()
        state['off'] = base

    def phase_hgrn(S, hp):
        base = state['off']
        NCk = S // CH
        wA = [alloc([KC, 128], BF16, "wA%d" % i) for i in range(5)]
        qf = [alloc([S], BF16, "qf%d" % dr) for dr in range(2)]
        kf = [alloc([S], BF16, "kf%d" % dr) for dr in range(2)]
        vtk = alloc([NCk, 128], BF16, "vtk")
        gate = alloc([S], BF16, "gate")
        dS = [alloc([NCk, 64], F32, "dS%d" % dr) for dr in range(2)]
        Dl = [alloc([NCk], F32, "Dl%d" % dr) for dr in range(2)]
        p1base = state['off']
        th = [alloc([512], F32, "th%d" % i) for i in range(2)]
        thd = [alloc([512], F32, "thd%d" % i) for i in range(2)]
        q2 = alloc([512], F32, "q2")
        Fms = [alloc([512], F32, "Fm%d" % i) for i in range(2)]
        D1s = [alloc([512], F32, "D1_%d" % i) for i in range(2)]
        Kks = [alloc([512], F32, "Kk%d" % i) for i in range(2)]
        Ics = [alloc([512], F32, "Ic%d" % i) for i in range(2)]
        RIs = [alloc([512], F32, "RI%d" % i) for i in range(2)]
        vTb = alloc([512], BF16, "vTb")
        ktk = [alloc([128], BF16, "ktk%d" % i) for i in range(4)]
        c0 = 1536 + hp * 128
        load_wA([c0, c0 + 512, c0 + 1024, c0 + 1536, c0 + 2048], [], wA)
        for i in range(2):
            P.add('pool', lambda e, i=i: e.memset(D1s[i], 0.0), w=[('D1', i)])
        pending = []
        for tb in range(S // 512):
            pending_new = []
            sl = slice(tb * 512, (tb + 1) * 512)
            b0 = pr['i'] % 4
            pr['i'] += 1
            proj((0, wA[0]), tb, b0)
            P.add('act', lambda e, b0=b0: e.activation(out=th[0], in_=psf[b0][:, :], func=AF.Tanh, scale=0.5),
                  r=pstok(b0, 0, 2048), w=[('th', 0)])
            P.add('dve', lambda e, b0=b0: e.scalar_tensor_tensor(out=q2, in0=th[0], scalar=1.0, in1=psf[b0][:, :], op0=ALU.add, op1=ALU.mult),
                  r=pstok(b0, 0, 2048) + [('th', 0)], w=['q2'])
            b0 = pr['i'] % 4
            pr['i'] += 1
            proj((3, wA[3]), tb, b0)
            P.add('act', lambda e, b0=b0: e.activation(out=vTb, in_=psf[b0][:, :], func=AF.Copy), r=pstok(b0, 0, 2048), w=['vTb'])
            for half in range(2):
                bank = 6 + (pr['i'] % 2)
                pr['i'] += 1
                for cc in range(4):
                    c = half * 4 + cc
                    P.add('pe', lambda e, c=c, cc=cc, bank=bank: e.transpose(out=psb[bank][0:64, cc * 128:(cc + 1) * 128],
                                                                               in_=vTb[:, c * 64:(c + 1) * 64], identity=ident),
                          r=['vTb', 'ident'], w=pstok(bank, cc * 256, cc * 256 + 256))
                cg = tb * 8 + half * 4
                P.add('dve', lambda e, cg=cg, bank=bank: e.tensor_copy(out=vtk[0:64, cg:cg + 4, :],
                                                                      in_=psb[bank][0:64, 0:512].rearrange("p (a c) -> p a c", a=4)),
                      r=pstok(bank, 0, 1024), w=[('vtk', cg + q) for q in range(4)])
            b0 = pr['i'] % 4
            pr['i'] += 1
            proj((4, wA[4]), tb, b0)
            P.add('act', lambda e, b0=b0: e.activation(out=th[1], in_=psf[b0][:, :], func=AF.Tanh, scale=0.5),
                  r=pstok(b0, 0, 2048), w=[('th', 1)])
            P.add('dve', lambda e, b0=b0, sl=sl: e.scalar_tensor_tensor(out=gate[:, sl], in0=th[1], scalar=1.0, in1=psf[b0][:, :],
                                                                        op0=ALU.add, op1=ALU.mult),
                  r=pstok(b0, 0, 2048) + [('th', 1)], w=[('gate', tb)])
            for dr in range(2):
                b0 = pr['i'] % 4
                pr['i'] += 1
                proj((1 + dr, wA[1 + dr]), tb, b0)
                Fm, Kk, Ic, RI, tdr = Fms[dr], Kks[dr], Ics[dr], RIs[dr], thd[dr]
                P.add('act', lambda e, b0=b0, tdr=tdr: e.activation(out=tdr, in_=psf[b0][:, :], func=AF.Tanh, scale=0.5),
                      r=pstok(b0, 0, 2048), w=[('thd', dr)])
                P.add('dve', lambda e, dr=dr, Fm=Fm, tdr=tdr: e.tensor_scalar(out=Fm, in0=tdr, scalar1=gb[:, dr, hp:hp + 1], scalar2=ga[:, dr, hp:hp + 1],
                                                             op0=ALU.mult, op1=ALU.add), r=[('thd', dr), 'ga', 'gb'], w=[('Fm', dr)])
                P.add('act', lambda e, dr=dr, Kk=Kk, tdr=tdr: e.activation(out=Kk, in_=tdr, func=AF.Identity, scale=gnb[:, dr, hp:hp + 1],
                                                           bias=gb[:, dr, hp:hp + 1]), r=[('thd', dr), 'gnb', 'gb'], w=[('Kk', dr)])
                D1 = D1s[dr]
                if dr == 0:
                    edge = slice(0, 512, 64)
                    Fv, D1v, Iv = Fm, D1, Ic
                else:
                    edge = slice(63, 512, 64)
                    Fv, D1v, Iv = Fm[:, ::-1], D1[:, ::-1], Ic[:, ::-1]
                P.add('dve', lambda e, edge=edge, D1=D1, Fm=Fm: e.tensor_copy(out=D1[:, edge], in_=Fm[:, edge]), r=[('Fm', dr)], w=[('D1', dr)])
                P.add('dve', lambda e, edge=edge, Fm=Fm: e.memset(Fm[:, edge], 0.0), r=[('D1', dr), ('Fm', dr)], w=[('Fm', dr)])
                P.add('dve', lambda e, Fv=Fv, D1v=D1v, Iv=Iv: e.tensor_tensor_scan(out=Iv, data0=Fv, data1=D1v, initial=0.0,
                                                                                   op0=ALU.mult, op1=ALU.add),
                      r=[('Fm', dr), ('D1', dr)], w=[('Ic', dr)])
                P.add('dve', lambda e, RI=RI, Ic=Ic: e.reciprocal(out=RI, in_=Ic), r=[('Ic', dr)], w=[('RI', dr)])
                P.add('dve', lambda e, dr=dr, sl=sl, Ic=Ic: e.tensor_tensor(out=qf[dr][:, sl], in0=q2, in1=Ic, op=ALU.mult),
                      r=['q2', ('Ic', dr)], w=[('qf', dr, tb)])
                P.add('dve', lambda e, dr=dr, sl=sl, Kk=Kk, RI=RI: e.tensor_tensor(out=kf[dr][:, sl], in0=Kk, in1=RI, op=ALU.mult),
                      r=[('Kk', dr), ('RI', dr)], w=[('kf', dr, tb)])
                ecol = slice(63, 512, 64) if dr == 0 else slice(0, 512, 64)
                P.add('pool', lambda e, dr=dr, ecol=ecol, tb=tb, Ic=Ic: e.tensor_copy(out=Dl[dr][:, tb * 8:(tb + 1) * 8], in_=Ic[:, ecol]),
                      r=[('Ic', dr)], w=[('Dl', dr, tb)])
                if os.environ.get('HSTOP') == 'p1a':
                    continue
                def dsA(c8, dr=dr, tb=tb):
                    c = tb * 8 + c8
                    kr = c8 % 4
                    bank = 6 + (c8 % 2)
                    P.add('pe', lambda e: e.transpose(out=psb[bank][0:64, 0:128], in_=kf[dr][:, c * 64:(c + 1) * 64], identity=ident),
                          r=[('kf', dr, tb), 'ident'], w=pstok(bank))
                    P.add('act', lambda e: e.activation(out=ktk[kr][0:64, :], in_=psb[bank][0:64, 0:128], func=AF.Copy),
                          r=pstok(bank), w=[('ktk', kr)])

                def dsB(c8, dr=dr, tb=tb):
                    c = tb * 8 + c8
                    kr = c8 % 4
                    bank = 4 + (c8 % 2)
                    P.add('pe', lambda e: e.matmul(psf[bank][:, 0:128], lhsT=ktk[kr][0:64, :], rhs=vtk[0:64, c, :], start=True, stop=True),
                          r=[('ktk', kr), ('vtk', c)], w=pstok(bank))
                    for hh in range(2):
                        ps_ = slice(hh * 64, hh * 64 + 64)
                        if hh == 0:
                            P.add('dve', lambda e, ps_=ps_, hh=hh: e.tensor_scalar(
                                out=dS[dr][ps_, c, :], in0=psf[bank][ps_, hh * 64: 64 + hh * 64],
                                scalar1=Dl[dr][ps_, c:c + 1], scalar2=None, op0=ALU.mult),
                                r=pstok(bank) + [('Dl', dr, tb)], w=[('dS', dr, c, hh)])
                        else:
                            P.add('act', lambda e, ps_=ps_, hh=hh: e.activation(
                                out=dS[dr][ps_, c, :], in_=psf[bank][ps_, hh * 64: 64 + hh * 64], func=AF.Copy,
                                scale=Dl[dr][ps_, c:c + 1]),
                                r=pstok(bank) + [('Dl', dr, tb)], w=[('dS', dr, c, hh)])
                    if dr == 0 and c > 0:
                        P.add('dve', lambda e: e.scalar_tensor_tensor(
                            out=dS[0][:, c, :], in0=dS[0][:, c - 1, :], scalar=Dl[0][:, c:c + 1], in1=dS[0][:, c, :],
                            op0=ALU.mult, op1=ALU.add),
                            r=[('dS', 0, c - 1, 0), ('dS', 0, c - 1, 1), ('dS', 0, c, 0), ('dS', 0, c, 1), ('Dl', 0, tb)],
                            w=[('dS', 0, c, 0), ('dS', 0, c, 1)])

                def run_ds(dsA=dsA, dsB=dsB):
                    dsA(0)
                    for c8 in range(8):
                        if c8 + 1 < 8:
                            dsA(c8 + 1)
                        dsB(c8)
                pending_new.append(run_ds)
            for f_ in pending:
                f_()
            pending = pending_new
        for f_ in pending:
            f_()
        if os.environ.get('HSTOP') in ('p1a', 'p1'):
            state['off'] = base
            return
        for n_ in range(1, NCk):
            for dr in (1,):
                c = n_ if dr == 0 else NCk - 1 - n_
                pc = c - 1 if dr == 0 else c + 1
                P.add('dve', lambda e, dr=dr, c=c, pc=pc: e.scalar_tensor_tensor(
                    out=dS[dr][:, c, :], in0=dS[dr][:, pc, :], scalar=Dl[dr][:, c:c + 1], in1=dS[dr][:, c, :],
                    op0=ALU.mult, op1=ALU.add),
                    r=[('dS', dr, pc, 0), ('dS', dr, pc, 1), ('dS', dr, c, 0), ('dS', dr, c, 1), ('Dl', dr, c // 8)],
                    w=[('dS', dr, c, 0), ('dS', dr, c, 1)])
        if os.environ.get('HSTOP') == 'chain':
            state['off'] = base
            return
        P.barrier()
        state['off'] = p1base
        att = [alloc([2, 64], BF16, "att%d" % i) for i in range(2)]
        Sbd4 = [alloc([8, 128], BF16, "Sbd%d" % i) for i in range(4)]
        osum = alloc([512], F32, "osum")
        osq = alloc([512], F32, "osq")
        rs8 = alloc([512], F32, "rs8")
        houts = [alloc([512], BF16, "hout%d" % i) for i in range(2)]
        for i in range(4):
            P.add('pool', lambda e, i=i: e.memset(Sbd4[i], 0.0), w=[('Sbd', i % 2, i // 2)])
        prev_norm = [None]
        for tb in range(S // 512):
            sl = slice(tb * 512, (tb + 1) * 512)
            obA = 4 + 2 * (tb % 2)
            obB = 5 + 2 * (tb % 2)
            Sbd = [Sbd4[0 + 2 * (tb % 2)], Sbd4[1 + 2 * (tb % 2)]]
            sbp = tb % 2
            for dr in range(2):
                for hh in range(2):
                    ps_ = slice(hh * 64, hh * 64 + 64)
                    cs = [tb * 8 + c8 + (-1 if dr == 0 else 1) for c8 in range(8)]
                    valid = [c8 for c8 in range(8) if 0 <= cs[c8] < NCk]
                    lo, hi = valid[0], valid[-1] + 1
                    P.add('pool', lambda e, dr=dr, ps_=ps_, lo=lo, hi=hi, cs=cs, hh=hh, Sbd=Sbd: e.tensor_copy(
                        out=Sbd[dr][ps_, lo:hi, hh * 64:hh * 64 + 64], in_=dS[dr][ps_, cs[lo]:cs[hi - 1] + 1, :]),
                        r=[('dS', dr, cs[c8], hh) for c8 in valid], w=[('Sbd', dr, sbp)])
            if os.environ.get('HSTOP') == 'p2a1':
                continue
            items2 = [(c8, dr) for c8 in range(8) for dr in range(2)]

            def p2A(k, tb=tb):
                c8, dr = items2[k]
                c = tb * 8 + c8
                cs_ = slice(c * 64, (c + 1) * 64)
                ar = k % 2
                mk = maskF if dr == 0 else maskB
                for hh in range(2):
                    ps_ = slice(hh * 64, hh * 64 + 64)
                    abank = ar * 2 + hh
                    P.add('pe', lambda e, ps_=ps_, abank=abank: e.matmul(
                        psf[abank][0:64, 0:64], lhsT=kf[dr][ps_, cs_], rhs=qf[dr][ps_, cs_], start=True, stop=True),
                        r=[('kf', dr, tb), ('qf', dr, tb)], w=pstok(abank))
                    P.add('dve', lambda e, abank=abank, hh=hh: e.tensor_tensor(
                        out=att[ar][0:64, hh, :], in0=psf[abank][0:64, 0:64], in1=mk[0:64, :], op=ALU.mult),
                        r=pstok(abank) + ['maskF', 'maskB'], w=[('att', ar, hh)])

            def p2B(k, tb=tb, obA=obA, obB=obB, Sbd=Sbd, sbp=sbp):
                c8, dr = items2[k]
                c = tb * 8 + c8
                cs_ = slice(c * 64, (c + 1) * 64)
                ar = k % 2
                first = (dr == 0)
                skip_inter = (dr == 0 and c == 0) or (dr == 1 and c == NCk - 1)
                for hh, ob in ((0, obA), (1, obB)):
                    P.add('pe', lambda e, hh=hh, ob=ob: e.matmul(
                        psf[ob][:, c8 * 64:(c8 + 1) * 64], lhsT=vtk[0:64, c, :], rhs=att[ar][0:64, hh, :],
                        start=first, stop=(dr == 1 and skip_inter)),
                        r=[('att', ar, hh), ('vtk', c)], w=pstok(ob))
                    if not skip_inter:
                        P.add('pe', lambda e, ob=ob: e.matmul(
                            psf[ob][:, c8 * 64:(c8 + 1) * 64], lhsT=Sbd[dr][:, c8, :], rhs=qf[dr][:, cs_],
                            start=False, stop=(dr == 1)),
                            r=[('Sbd', dr, sbp), ('qf', dr, tb)], w=pstok(ob))

            def norm_chain(tb=tb, sl=sl, obA=obA, obB=obB):
                P.add('act', lambda e: e.activation(out=osum[0:64, :], in_=psf[obA][0:64, :], func=AF.Copy), r=pstok(obA), w=['osumA'])
                P.add('act', lambda e: e.activation(out=osum[64:128, :], in_=psf[obB][64:128, :], func=AF.Copy), r=pstok(obB), w=['osumB'])
                P.add('act', lambda e: e.activation(out=osq, in_=osum, func=AF.Square), r=['osumA', 'osumB'], w=['osq'])
                nb_ = pr['i'] % 4
                pr['i'] += 1
                P.add('pe', lambda e: e.matmul(psf[nb_][:, :], lhsT=onesbd, rhs=osq, start=True, stop=True),
                      r=['osq', 'onesbd'], w=pstok(nb_))
                P.add('act', lambda e: e.activation(out=rs8, in_=psf[nb_][:, :], func=AF.Ln, bias=eps256[:, 0:1]),
                      r=pstok(nb_) + ['eps256'], w=['rs8a'])
                P.add('act', lambda e: e.activation(out=rs8, in_=rs8, func=AF.Exp, scale=-0.5), r=['rs8a'], w=['rs8'])
                P.add('dve', lambda e: e.tensor_tensor(out=osum, in0=osum, in1=rs8, op=ALU.mult), r=['osumA', 'osumB', 'rs8'], w=['osn'])
                hr = tb % 2
                P.add('dve', lambda e: e.scalar_tensor_tensor(out=houts[hr], in0=osum, scalar=onw4[:, 0:1], in1=gate[:, sl],
                                                              op0=ALU.mult, op1=ALU.mult),
                      r=['osn', 'onw4', ('gate', tb)], w=[('hout', hr)])
                P.add('sp', lambda e: e.dma_start(allow_slow_non_contiguous=True, out=mix_s[4 + hp, :, sl], in_=houts[hr]),
                      r=[('hout', hr)], w=[('mix_s', 4 + hp, tb)])

            p2A(0)
            for k in range(16):
                if k + 1 < 16:
                    p2A(k + 1)
                p2B(k)
                if k == 5 and prev_norm[0] is not None:
                    prev_norm[0]()
                    prev_norm[0] = None
            prev_norm[0] = norm_chain
        if prev_norm[0] is not None:
            prev_norm[0]()
        state['off'] = base

    def phase_B1(t_base, S):
        base = state['off']
        wo = alloc([KC, D], BF16, "wo")
        wpost = alloc([D], F32, "wpost")
        P.add('sp', lambda e: e.dma_start(allow_slow_non_contiguous=True, out=wpost, in_=bass.AP(vec["norm_mix_post"], 0, [[0, 128], [1, D]])), w=['wpost'])
        mt = [alloc([KC, 512], BF16, "mt%d" % i) for i in range(2)]
        xt = [alloc([D], F32, "xt%d" % i) for i in range(4)]
        ht = [alloc([D], F32, "ht%d" % i) for i in range(4)]
        tm = [alloc([D], F32, "tm%d" % i) for i in range(4)]
        hb_ = [alloc([D], BF16, "hb%d" % i) for i in range(2)]
        junk = alloc([D], BF16, "junk")
        ss = [alloc([4], F32, "ss%d" % i) for i in range(4)]
        P.add('sp', lambda e: e.dma_start(allow_slow_non_contiguous=True, out=wo, in_=wout_b.rearrange("(k p) c -> p k c", p=128)), r=['wscr'], w=['wo'])
        P.add('pool', lambda e: e.memset(XT[:, :, S + 1:S + 2], 0.0), w=['xhalo2'])
        NJ = S // 128

        ld = {'x': 0, 'm': 0}

        def b1_loads(j_upto, g_upto):
            while ld['m'] < min(g_upto, S // 512):
                g_ = ld['m']
                P.add('sp', lambda e, g_=g_: e.dma_start(allow_slow_non_contiguous=True, out=mt[g_ % 2], in_=mix_s[:, :, g_ * 512:(g_ + 1) * 512].rearrange("k p t -> p k t")),
                      r=[], w=[('mt', g_ % 2)])
                ld['m'] += 1
            while ld['x'] < min(j_upto, NJ):
                j_ = ld['x']
                P.add('sp', lambda e, j_=j_: e.dma_start(allow_slow_non_contiguous=True, out=xt[j_ % 4], in_=xs[t_base + j_ * 128: t_base + (j_ + 1) * 128, :]),
                      w=[('xt', j_ % 4)])
                ld['x'] += 1

        def b1a(j):
            g, tt = j // 4, j % 4
            m_ = mt[g % 2]
            b1_loads(j + 3, g + 2 if tt >= 2 else g + 1)
            r_ = j % 4
            x_, h_, t_, s_ = xt[r_], ht[r_], tm[r_], ss[r_]
            nm = 'b1_%d' % r_
            for half in range(2):
                bank = (j % 2) * 2 + half
                hs = slice(half * 512, (half + 1) * 512)
                for kc in range(KC):
                    P.add('pe', lambda e, kc=kc, hs=hs, bank=bank: e.matmul(
                        psf[bank][:, :], lhsT=m_[:, kc, tt * 128:(tt + 1) * 128], rhs=wo[:, kc, hs], start=(kc == 0), stop=(kc == KC - 1)),
                        r=[('mt', g % 2), 'wo'], w=pstok(bank))
                P.add('act', lambda e, bank=bank, half=half: e.activation(out=junk[:, 0:512], in_=psf[bank][:, :], func=AF.Square,
                                                                           accum_out=s_[:, half:half + 1]),
                      r=pstok(bank), w=['junk', (nm, 'p', half)])
                P.add('act', lambda e, bank=bank, hs=hs: e.activation(out=t_[:, hs], in_=psf[bank][:, :], func=AF.Copy),
                      r=pstok(bank), w=[('tm', r_, half)])
            P.add('dve', lambda e: e.tensor_tensor(out=s_[:, 2:3], in0=s_[:, 0:1], in1=s_[:, 1:2], op=ALU.add),
                  r=[(nm, 'p', 0), (nm, 'p', 1)], w=[nm + 'ssq'])
            rstd_from_ssq(s_[:, 2:3], s_[:, 3:4], D, EPS, nm)
            P.add('dve', lambda e: e.scalar_tensor_tensor(out=t_, in0=t_, scalar=s_[:, 3:4], in1=wpost, op0=ALU.mult, op1=ALU.mult),
                  r=[('tm', r_, 0), ('tm', r_, 1), nm + 'rstd', 'wpost'], w=[('tm', r_, 0), ('tm', r_, 1)])
            P.add('dve', lambda e: e.tensor_tensor(out=h_, in0=t_, in1=x_, op=ALU.add),
                  r=[('tm', r_, 0), ('tm', r_, 1), ('xt', r_)], w=[('ht', r_)])
            P.add('sp', lambda e: e.dma_start(allow_slow_non_contiguous=True, out=ys[t_base + j * 128: t_base + (j + 1) * 128, :], in_=h_),
                  r=[('ht', r_)], w=[('ys', j)])
            nm2 = 'b1n_%d' % r_
            P.add('act', lambda e: e.activation(out=junk, in_=h_, func=AF.Square, accum_out=s_[:, 0:1]),
                  r=[('ht', r_)], w=['junk', nm2 + 'ssq'])
            rstd_from_ssq(s_[:, 0:1], s_[:, 1:2], D, EPS, nm2)

        def b1b(j):
            r_ = j % 4
            h_, s_, b_ = ht[r_], ss[r_], hb_[j % 2]
            nm2 = 'b1n_%d' % r_
            P.add('act', lambda e: e.activation(out=b_, in_=h_, func=AF.Copy, scale=s_[:, 1:2]),
                  r=[('ht', r_), nm2 + 'rstd'], w=[('hb', j % 2)])

        def b1c(j):
            b_ = hb_[j % 2]
            bank = 6 + (j % 2)
            for kc in range(KC):
                P.add('pe', lambda e, kc=kc: e.transpose(out=psb[bank][:, kc * 128:(kc + 1) * 128],
                                                       in_=b_[:, kc * 128:(kc + 1) * 128], identity=ident),
                      r=[('hb', j % 2), 'ident'], w=pstok(bank))
            P.add('dve', lambda e: e.tensor_tensor(
                out=XT[:, :, 1 + j * 128: 1 + (j + 1) * 128], in0=psb[bank][:, :].rearrange("p (k t) -> p k t", k=KC),
                in1=bcast_free(wfpre, 128), op=ALU.mult),
                r=pstok(bank) + ['wfpre'], w=[('XT', kc, j) for kc in range(KC)])

        for j in range(NJ + 2):
            if j < NJ:
                b1a(j)
            if 0 <= j - 1 < NJ:
                b1b(j - 1)
            if 0 <= j - 2 < NJ:
                b1c(j - 2)
        state['off'] = base

    def phase_B2(t_base, S):
        base = state['off']
        wfpost = alloc([D], F32, "wfpost")
        P.add('sp', lambda e: e.dma_start(allow_slow_non_contiguous=True, out=wfpost, in_=bass.AP(vec["norm_ffn_post"], 0, [[0, 128], [1, D]])), w=['wfpost'])
        wg = [alloc([KC, 128], BF16, "wg%d" % i) for i in range(6)]
        wu = [alloc([KC, 128], BF16, "wu%d" % i) for i in range(6)]
        wd = [alloc([512], BF16, "wd%d" % i) for i in range(16)]
        hid = alloc([NFC, 512], BF16, "hid")
        Asb = [alloc([514], F32, "Asb%d" % i) for i in range(5)]
        cc = [alloc([512], F32, "cc%d" % i) for i in range(5)]
        c2 = [alloc([512], F32, "c2%d" % i) for i in range(5)]
        c3 = [alloc([512], F32, "c3%d" % i) for i in range(5)]
        fsb = alloc([4, D], F32, "fsb")
        ht = [alloc([D], F32, "ht%d" % i) for i in range(4)]
        junk = alloc([512], BF16, "junk")
        ss = alloc([4, 4], F32, "ssB")
        wi = {'g': 0, 'd': 0}
        NB = S // 512
        pf = {'g': 0, 'd': 0}

        def prefetch(g_upto, d_upto):
            while pf['g'] < min(g_upto, NB * NFC):
                g = pf['g']
                fc_, r3_ = g % NFC, g % 6
                P.add('sp', lambda e, fc_=fc_, r3_=r3_: e.dma_start(allow_slow_non_contiguous=True, out=wg[r3_], in_=wg_b[fc_]), r=['wscr'], w=[('wg', r3_)])
                P.add('sp', lambda e, fc_=fc_, r3_=r3_: e.dma_start(allow_slow_non_contiguous=True, out=wu[r3_], in_=wu_b[fc_]), r=['wscr'], w=[('wu', r3_)])
                pf['g'] += 1
            while pf['d'] < min(d_upto, NB * 2 * NFC):
                dd = pf['d']
                fc_, half_, r4_ = dd % NFC, (dd // NFC) % 2, dd % 16
                P.add('sp', lambda e, fc_=fc_, half_=half_, r4_=r4_: e.dma_start(
                    allow_slow_non_contiguous=True, out=wd[r4_], in_=wd_b[fc_ * 128:(fc_ + 1) * 128, half_ * 512:(half_ + 1) * 512]),
                    r=['wscr'], w=[('wd', r4_)])
                pf['d'] += 1

        for blk in range(NB):
            t0 = blk * 512
            xtoks = lambda kc: [('XT', kc, blk * 4 + q) for q in range(4)]
            def st1(fc, blk=blk, t0=t0):
                r3 = wi['g'] % 6
                wi['g'] += 1
                r2 = fc % 5
                prefetch(wi['g'] + 4, wi['d'] + (10 if fc >= NFC - 6 else 0))
                gb_ = (0, 1)[fc % 2]
                ub_ = (2, 3, 6, 7)[fc % 4]
                hbk = (4, 5)[fc % 2]
                xtoks = lambda kc: [('XT', kc, blk * 4 + q) for q in range(4)]
                for kc in range(KC):
                    P.add('pe', lambda e, kc=kc: e.matmul(psf[gb_][:, :], lhsT=wg[r3][:, kc, :],
                                                          rhs=XT[:, kc, 1 + t0: 1 + t0 + 512], start=(kc == 0), stop=(kc == KC - 1)),
                          r=[('wg', r3)] + xtoks(kc), w=pstok(gb_))
                halo_r = ['xhalo', 'xhalo2'] + [('XT', kc, q) for kc in range(KC) for q in (max(blk * 4 - 1, 0), min(blk * 4 + 4, S // 128 - 1))]
                for kc in range(KC):
                    P.add('pe', lambda e, kc=kc: e.matmul(psf[hbk][:, 0:2], lhsT=wg[r3][:, kc, :],
                                                          rhs=XT[:, kc, t0: t0 + 514: 513], start=(kc == 0), stop=(kc == KC - 1)),
                          r=[('wg', r3)] + halo_r, w=pstok(hbk))
                for kc in range(KC):
                    P.add('pe', lambda e, kc=kc: e.matmul(psf[ub_][:, :], lhsT=wu[r3][:, kc, :],
                                                          rhs=XT[:, kc, 1 + t0: 1 + t0 + 512], start=(kc == 0), stop=(kc == KC - 1)),
                          r=[('wu', r3)] + xtoks(kc), w=pstok(ub_))
                A_, c_ = Asb[r2], cc[r2]
                P.add('act', lambda e: e.activation(out=A_[:, 1:513], in_=psf[gb_][:, :], func=AF.Copy),
                      r=pstok(gb_), w=[('Asb', r2, 0)])
                P.add('act', lambda e: e.activation(out=A_[:, 0:514:513], in_=psf[hbk][:, 0:2], func=AF.Copy),
                      r=pstok(hbk), w=[('Asb', r2, 1)])
                P.add('act', lambda e: e.activation(out=c_, in_=A_[:, 1:513], func=AF.Identity, scale=cw[:, 1, fc:fc + 1],
                                                    bias=cb[:, fc:fc + 1]),
                      r=[('Asb', r2, 0), 'cw', 'cb'], w=[('cc', r2)])

            def st2(fc):
                r2 = fc % 5
                A_, c_, c2_ = Asb[r2], cc[r2], c2[r2]
                P.add('dve', lambda e: e.scalar_tensor_tensor(out=c_, in0=A_[:, 0:512], scalar=cw[:, 0, fc:fc + 1], in1=c_,
                                                              op0=ALU.mult, op1=ALU.add),
                      r=[('Asb', r2, 0), ('Asb', r2, 1), 'cw', ('cc', r2)], w=[('cc', r2)])
                P.add('dve', lambda e: e.scalar_tensor_tensor(out=c_, in0=A_[:, 2:514], scalar=cw[:, 2, fc:fc + 1], in1=c_,
                                                              op0=ALU.mult, op1=ALU.add),
                      r=[('Asb', r2, 0), ('Asb', r2, 1), 'cw', ('cc', r2)], w=[('cc', r2)])
                P.add('act', lambda e: e.activation(out=c2_, in_=c_, func=AF.Square, scale=0.21145921592590237),
                      r=[('cc', r2)], w=[('c2', r2)])

            def st3(fc):
                r2 = fc % 5
                c_, c2_, c3_ = cc[r2], c2[r2], c3[r2]
                P.add('dve', lambda e: e.scalar_tensor_tensor(out=c3_, in0=c2_, scalar=1.0, in1=c_, op0=ALU.add, op1=ALU.mult),
                      r=[('c2', r2), ('cc', r2)], w=[('c3', r2)])
                P.add('act', lambda e: e.activation(out=c3_, in_=c3_, func=AF.Tanh, scale=0.7978845608028654),
                      r=[('c3', r2)], w=[('c3', r2)])

            def st4(fc):
                r2 = fc % 5
                ub_ = (2, 3, 6, 7)[fc % 4]
                c_, c2_, c3_ = cc[r2], c2[r2], c3[r2]
                P.add('dve', lambda e: e.scalar_tensor_tensor(out=c2_, in0=c3_, scalar=1.0, in1=c_, op0=ALU.add, op1=ALU.mult),
                      r=[('c3', r2), ('cc', r2), ('c2', r2)], w=[('c2', r2)])
                P.add('dve', lambda e: e.tensor_tensor(out=hid[:, fc, :], in0=psf[ub_][:, :], in1=c2_, op=ALU.mult),
                      r=pstok(ub_) + [('c2', r2)], w=[('hid', fc)])

            for it in range(NFC + 3):
                if it < NFC:
                    st1(it)
                if 0 <= it - 1 < NFC:
                    st2(it - 1)
                if 0 <= it - 2 < NFC:
                    st3(it - 2)
                if 0 <= it - 3 < NFC:
                    st4(it - 3)
            for tt in range(4):
                j = blk * 4 + tt
                P.add('sp', lambda e, j=j: e.dma_start(allow_slow_non_contiguous=True, out=ht[j % 4], in_=ys[t_base + j * 128: t_base + (j + 1) * 128, :]),
                      r=[('ys', j)], w=[('ht', j % 4)])
            for half in range(2):
                hs = slice(half * 512, (half + 1) * 512)
                for fc in range(NFC):
                    r4 = wi['d'] % 16
                    wi['d'] += 1
                    prefetch(wi['g'] + (5 if (half == 1 and fc >= NFC - 8) else 0), wi['d'] + 11)
                    for tt in range(4):
                        P.add('pe', lambda e, fc=fc, r4=r4, tt=tt: e.matmul(psf[4 + tt][:, :], lhsT=hid[:, fc, tt * 128:(tt + 1) * 128], rhs=wd[r4],
                                                                          start=(fc == 0), stop=(fc == NFC - 1)),
                              r=[('hid', fc), ('wd', r4)], w=pstok(4 + tt))
                for tt in range(4):
                    P.add('act', lambda e, tt=tt, half=half: e.activation(out=junk, in_=psf[4 + tt][:, :], func=AF.Square,
                                                                        accum_out=ss[:, tt, half:half + 1]),
                          r=pstok(4 + tt), w=['junk', ('ssB', tt, half)])
                    P.add('act', lambda e, tt=tt, hs=hs: e.activation(out=fsb[:, tt, hs], in_=psf[4 + tt][:, :], func=AF.Copy),
                          r=pstok(4 + tt), w=[('fsb', tt, half)])
            for tt in range(4):
                j = blk * 4 + tt
                r_ = j % 4
                h_ = ht[r_]
                nm = 'b2_%d' % tt
                P.add('dve', lambda e, tt=tt: e.tensor_tensor(out=ss[:, tt, 2:3], in0=ss[:, tt, 0:1], in1=ss[:, tt, 1:2], op=ALU.add),
                      r=[('ssB', tt, 0), ('ssB', tt, 1)], w=[nm + 'ssq'])
                rstd_from_ssq(ss[:, tt, 2:3], ss[:, tt, 3:4], D, 4.0 * EPS, nm)
                P.add('dve', lambda e, tt=tt: e.scalar_tensor_tensor(out=fsb[:, tt, :], in0=fsb[:, tt, :], scalar=ss[:, tt, 3:4], in1=wfpost,
                                                                   op0=ALU.mult, op1=ALU.mult),
                      r=[('fsb', tt, 0), ('fsb', tt, 1), nm + 'rstd', 'wfpost'], w=[('fsb', tt, 0), ('fsb', tt, 1)])
                P.add('dve', lambda e, tt=tt, h_=h_: e.tensor_tensor(out=h_, in0=h_, in1=fsb[:, tt, :], op=ALU.add),
                      r=[('fsb', tt, 0), ('fsb', tt, 1), ('ht', r_)], w=[('ht', r_)])
                P.add('sp', lambda e, j=j, h_=h_: e.dma_start(allow_slow_non_contiguous=True, out=ys[t_base + j * 128: t_base + (j + 1) * 128, :], in_=h_),
                      r=[('ht', r_)], w=[('ys', j)])
        state['off'] = base

    setup()
    weight_prep()
    P.barrier()
    t_base = 0
    for S in seq_lens:
        phase_A0(t_base, S, branches)
        P.barrier()
        if dbg_xt is not None and t_base == 0:
            P.add('sp', lambda e, S=S: e.dma_start(out=dbg_xt[:, :, 0:S + 1], in_=XT[:, :, 0:S + 1]), w=['dbgxt'])
        for hp in range(4):
            phase_attn(S, hp, branches)
        P.barrier()
        if stop_after == 'attn':
            break
        for hp in range(4):
            phase_hgrn(S, hp)
            P.barrier()
            if stop_after == 'hgrn0':
                break
        if stop_after == 'hgrn0':
            break
        phase_B1(t_base, S)
        P.barrier()
        phase_B2(t_base, S)
        P.barrier()
        t_base += S

    with nc.Block() as block:
        P.emit(nc, block, sems)
    es.close()
    return nc


def rot_tables(SM):
    half = 8
    inv = ROPE_THETA ** (-np.arange(half, dtype=np.float32) * 2.0 / 16.0)
    ang = np.arange(SM, dtype=np.float32)[:, None] * inv[None, :]
    cos = np.cos(ang).astype(np.float32).T
    sin = np.sin(ang).astype(np.float32).T
    c = np.ones((128, SM), np.float32)
    s = np.zeros((128, SM), np.float32)
    for hb in (0, 64):
        c[hb:hb + 8] = cos
        c[hb + 8:hb + 16] = cos
        s[hb:hb + 8] = -sin
        s[hb + 8:hb + 16] = sin
    return c, s


_CACHE = {}


def kernel(x_prompt, x_sample, norm_mix_pre, w_in, hgrn_lb_fwd, hgrn_lb_bwd, hgrn_out_norm, w_out,
           norm_mix_post, norm_ffn_pre, w_gate, w_up, conv_w, conv_b, w_down, norm_ffn_post):
    n = 8
    x_prompt = np.asarray(x_prompt)
    x_sample = np.asarray(x_sample)
    Bp, Sp, _ = x_prompt.shape
    Bs, Ss, _ = x_sample.shape
    pp, sp_ = Bp // n, Bs // n
    seq_lens = tuple([Sp] * pp + [Ss] * sp_)
    if seq_lens not in _CACHE:
        _CACHE[seq_lens] = build(seq_lens)
    nc = _CACHE[seq_lens]
    rc, rs = rot_tables(max(seq_lens))
    f = lambda a: np.ascontiguousarray(np.asarray(a, dtype=np.float32))
    common = {
        "w_in": f(w_in)[0], "w_out": f(w_out)[0], "w_gate": f(w_gate)[0], "w_up": f(w_up)[0], "w_down": f(w_down)[0],
        "norm_mix_pre": f(norm_mix_pre)[0], "norm_mix_post": f(norm_mix_post)[0], "norm_ffn_pre": f(norm_ffn_pre)[0],
        "norm_ffn_post": f(norm_ffn_post)[0], "conv_b": f(conv_b)[0], "hgrn_out_norm": f(hgrn_out_norm)[0],
        "conv_w": f(conv_w)[0], "hgrn_lb_fwd": f(hgrn_lb_fwd), "hgrn_lb_bwd": f(hgrn_lb_bwd),
        "rot_c": rc, "rot_s": rs,
    }
    in_maps = []
    for c in range(n):
        xs = np.concatenate([x_prompt[c * pp:(c + 1) * pp].reshape(-1, D), x_sample[c * sp_:(c + 1) * sp_].reshape(-1, D)], axis=0)
        m = dict(common)
        m["xs"] = np.ascontiguousarray(xs, dtype=np.float32)
        in_maps.append(m)
    res = run_bass_kernel_spmd(nc, in_maps, core_ids=list(range(n)))
    yp = np.empty((Bp, Sp, D), np.float32)
    ysm = np.empty((Bs, Ss, D), np.float32)
    for c in range(n):
        y = res.results[c]["ys"]
        yp[c * pp:(c + 1) * pp] = y[:pp * Sp].reshape(pp, Sp, D)
        ysm[c * sp_:(c + 1) * sp_] = y[pp * Sp:].reshape(sp_, Ss, D)
    return (yp, ysm)
```

```python
import os
import numpy as np
import ml_dtypes
import concourse.bass as bass
import concourse.mybir as mybir
from concourse.bass_utils import run_bass_kernel_spmd

F32 = mybir.dt.float32
BF16 = mybir.dt.bfloat16
AF = mybir.ActivationFunctionType
ALU = mybir.AluOpType

D = 1024
KC = 8
INW = 4096
DFF = 2816
NFC = 22
EPS = 1e-6
ROPE_THETA = 500000.0
BRANCHES = (1, 4, 16)
CH = 64
COMPUTE = ('pe', 'act', 'dve', 'pool')


class Prog:
    def __init__(self, ndma=12):
        self.ndma = ndma
        self.lists = {e: [] for e in COMPUTE + ('sp',)}
        self.bystream = {}
        self.tok = {}
        self.vc = {e: {} for e in COMPUTE + ('sp',)}
        self.dma_n = 0
        self.pending_barrier = {}

    def barrier(self):
        deps = set()
        for s, ops in self.bystream.items():
            if ops:
                deps.add((s, len(ops) - 1))
        for e in self.lists:
            self.pending_barrier[e] = set(deps)
        self.tok = {}

    def add(self, eng, fn, r=(), w=()):
        if eng == 'sp':
            stream = 'd%d' % (self.dma_n % self.ndma)
            self.dma_n += 1
        else:
            stream = eng
        slist = self.bystream.setdefault(stream, [])
        sidx = len(slist)
        deps = set()
        if eng in self.pending_barrier:
            deps |= self.pending_barrier.pop(eng)
        for k in r:
            st = self.tok.get(k)
            if st is not None and st[0] is not None:
                deps.add(st[0])
        for k in w:
            st = self.tok.get(k)
            if st is not None:
                if st[0] is not None:
                    deps.add(st[0])
                deps.update(st[1])
        if eng == 'sp' and sidx > 0:
            deps.add((stream, sidx - 1))
        vc = self.vc[eng]
        waits = {}
        for (s, i) in deps:
            if s == 'pe' and eng == 'pe':
                continue
            if vc.get(s, -1) < i:
                if waits.get(s, -1) < i:
                    waits[s] = i
        for s, i in waits.items():
            dop = self.bystream[s][i]
            dop['sig'] = True
            for s2, i2 in dop['vc'].items():
                if vc.get(s2, -1) < i2:
                    vc[s2] = i2
            if vc.get(s, -1) < i:
                vc[s] = i
        ovc = dict(vc)
        ovc[stream] = sidx
        op = dict(eng=eng, fn=fn, stream=stream, sidx=sidx, waits=waits, sig=False, vc=ovc)
        slist.append(op)
        self.lists[eng].append(op)
        me = (stream, sidx)
        for k in r:
            st = self.tok.setdefault(k, [None, []])
            st[1].append(me)
        for k in w:
            self.tok[k] = [me, []]
        return op

    def emit(self, nc, block, sems):
        counts = {}
        for s in COMPUTE:
            c = 0
            arr = []
            for op in self.bystream.get(s, []):
                if op['sig']:
                    c += 1
                arr.append(c)
            counts[s] = arr

        def val(s, i):
            if s in COMPUTE:
                return counts[s][i]
            return 16 * (i + 1)

        def run(e, ename):
            for op in self.lists[ename]:
                for s, i in op['waits'].items():
                    e.wait_ge(sems[s], val(s, i))
                ins = op['fn'](e)
                if ename == 'sp':
                    ins.then_inc(sems[op['stream']], 16)
                elif op['sig']:
                    ins.then_inc(sems[ename], 1)
            if ename == 'sp':
                for s, ops in self.bystream.items():
                    if s not in COMPUTE and ops:
                        e.wait_ge(sems[s], 16 * len(ops))

        @block.sync
        def _(e):
            run(e, 'sp')

        @block.tensor
        def _(e):
            run(e, 'pe')

        @block.scalar
        def _(e):
            run(e, 'act')

        @block.vector
        def _(e):
            run(e, 'dve')

        @block.gpsimd
        def _(e):
            run(e, 'pool')


def bcast_free(ap, n):
    return bass.AP(ap.tensor, ap.offset, [list(x) for x in ap.ap] + [[0, n]])


def bcast_mid(ap, n):
    l = [list(x) for x in ap.ap]
    return bass.AP(ap.tensor, ap.offset, [l[0], [0, n]] + l[1:])


def sst(lo, n, d):
    return slice(lo, lo + (n - 1) * d + 1, d)


def pstok(bank, lo=0, hi=0):
    return [('ps', bank)]


DEBUG_OFFS = {}


def build(seq_lens, branches=BRANCHES, stop_after=None):
    nc = bass.Bass("TRN2", target_bir_lowering=False)
    NT = sum(seq_lens)
    SM = max(seq_lens)
    dt = nc.dram_tensor
    xs = dt("xs", [NT, D], F32, kind="ExternalInput").ap()
    ys = dt("ys", [NT, D], F32, kind="ExternalOutput").ap()
    w_in = dt("w_in", [D, INW], F32, kind="ExternalInput").ap()
    w_out = dt("w_out", [D, D], F32, kind="ExternalInput").ap()
    w_gate = dt("w_gate", [D, DFF], F32, kind="ExternalInput").ap()
    w_up = dt("w_up", [D, DFF], F32, kind="ExternalInput").ap()
    w_down = dt("w_down", [DFF, D], F32, kind="ExternalInput").ap()
    vec = {}
    for nm, n in (("norm_mix_pre", D), ("norm_mix_post", D), ("norm_ffn_pre", D), ("norm_ffn_post", D),
                  ("conv_b", DFF), ("hgrn_out_norm", 64)):
        vec[nm] = dt(nm, [n], F32, kind="ExternalInput")
    conv_w = dt("conv_w", [3, DFF], F32, kind="ExternalInput")
    lbf = dt("hgrn_lb_fwd", [2, 512], F32, kind="ExternalInput")
    lbb = dt("hgrn_lb_bwd", [2, 512], F32, kind="ExternalInput")
    rotc_d = dt("rot_c", [128, SM], F32, kind="ExternalInput").ap()
    rots_d = dt("rot_s", [128, SM], F32, kind="ExternalInput").ap()
    win_b = dt("win_b", [D, INW], BF16, kind=("ExternalOutput" if os.environ.get("KDEBUG") else "Internal")).ap()
    winsw_b = dt("winsw_b", [D, 1024], BF16, kind="Internal").ap()
    wout_b = dt("wout_b", [D, D], BF16, kind="Internal").ap()
    wg_b = dt("wg_b", [NFC, 128, KC, 128], BF16, kind="Internal").ap()
    wu_b = dt("wu_b", [NFC, 128, KC, 128], BF16, kind="Internal").ap()
    wd_b = dt("wd_b", [DFF, D], BF16, kind="Internal").ap()
    mix_s = dt("mix_s", [KC, 128, SM], BF16, kind=("ExternalOutput" if os.environ.get("KDEBUG") else "Internal")).ap()

    dbg_xt = dt("dbg_xt", [128, KC, SM + 2], BF16, kind="ExternalOutput").ap() if os.environ.get("KDEBUG") else None
    P = Prog()
    from contextlib import ExitStack
    es = ExitStack()
    ARF = 53200
    arena = es.enter_context(nc.sbuf_tensor("arena", [128, ARF], F32))
    arena_b = arena.bitcast(BF16)
    psf = [es.enter_context(nc.psum_tensor("ps%d" % i, [128, 512], F32)) for i in range(8)]
    psb = [p.bitcast(BF16) for p in psf]
    sems = {}
    for s in list(COMPUTE) + ['d%d' % i for i in range(P.ndma)]:
        sems[s] = es.enter_context(nc.semaphore("sem_" + s))

    state = {'off': 0, 'uid': 0}

    def alloc(shape, dtype, name):
        n = 1
        for s_ in shape:
            n *= s_
        esz = 4 if dtype == F32 else 2
        off = (state['off'] + 31) // 32 * 32
        state['off'] = off + n * esz
        assert state['off'] <= ARF * 4, ("SBUF arena overflow", name, state['off'])
        DEBUG_OFFS[name] = (off, list(shape), 'f32' if dtype == F32 else 'bf16')
        base = arena if dtype == F32 else arena_b
        o = off // esz
        v = base[:, o:o + n]
        if len(shape) == 2:
            v = v.rearrange("p (a b) -> p a b", a=shape[0])
        elif len(shape) == 3:
            v = v.rearrange("p (a b c) -> p a b c", a=shape[0], b=shape[1])
        return v

    XT = alloc([KC, SM + 2], BF16, "xnT")
    ident = alloc([128], BF16, "ident")
    maskA = alloc([256], BF16, "maskA")
    maskMB = alloc([256], BF16, "maskMB")
    maskF = alloc([64], BF16, "maskF")
    maskB = alloc([64], BF16, "maskB")
    onesbd = alloc([128], F32, "onesbd")
    esel = alloc([64], F32, "esel")
    cneg = alloc([1], F32, "cneg")
    eps256 = alloc([1], F32, "eps256")
    wpre = alloc([KC], F32, "wpre")
    wfpre = alloc([KC], F32, "wfpre")
    cw = alloc([3, NFC], F32, "cw")
    cb = alloc([NFC], F32, "cb")
    onw4 = alloc([1], F32, "onw4")
    lbt = alloc([2, 2, 4], F32, "lbt")
    ga = alloc([2, 4], F32, "ga")
    gb = alloc([2, 4], F32, "gb")
    gna = alloc([2, 4], F32, "gna")
    gnb = alloc([2, 4], F32, "gnb")
    state['off'] += int(os.environ.get('KPAD', '0'))
    PERSIST = state['off']

    def setup():
        P.add('pool', lambda e: e.memset(ident, 0.0), w=['ident'])
        P.add('pool', lambda e: e.affine_select(out=ident, in_=ident, pattern=[[-1, 128]], compare_op=ALU.not_equal,
                                                 fill=1.0, base=0, channel_multiplier=1), r=['ident'], w=['ident'])
        P.add('pool', lambda e: e.memset(maskA, 1.0), w=['maskA'])
        P.add('pool', lambda e: e.affine_select(out=maskA, in_=maskA, pattern=[[1, 256]], compare_op=ALU.is_ge,
                                                 fill=0.0, base=0, channel_multiplier=-1), r=['maskA'], w=['maskA'])
        P.add('pool', lambda e: e.affine_select(out=maskA, in_=maskA, pattern=[[-1, 256]], compare_op=ALU.is_ge,
                                                 fill=0.0, base=128, channel_multiplier=1), r=['maskA'], w=['maskA'])
        P.add('dve', lambda e: e.tensor_scalar(out=maskMB, in0=maskA, scalar1=-1.0, scalar2=30000.0, op0=ALU.add, op1=ALU.mult),
              r=['maskA'], w=['maskMB'])
        P.add('pool', lambda e: e.memset(maskF[0:64, :], 1.0), w=['maskF'])
        P.add('pool', lambda e: e.affine_select(out=maskF[0:64, :], in_=maskF[0:64, :], pattern=[[1, 64]], compare_op=ALU.is_ge,
                                                 fill=0.0, base=0, channel_multiplier=-1), r=['maskF'], w=['maskF'])
        P.add('pool', lambda e: e.memset(maskB[0:64, :], 1.0), w=['maskB'])
        P.add('pool', lambda e: e.affine_select(out=maskB[0:64, :], in_=maskB[0:64, :], pattern=[[-1, 64]], compare_op=ALU.is_ge,
                                                 fill=0.0, base=0, channel_multiplier=1), r=['maskB'], w=['maskB'])
        P.add('pool', lambda e: e.memset(onesbd, 0.0), w=['onesbd'])
        P.add('pool', lambda e: e.memset(onesbd[0:64, 0:64], 1.0), r=['onesbd'], w=['onesbd'])
        P.add('pool', lambda e: e.memset(onesbd[64:128, 64:128], 1.0), r=['onesbd'], w=['onesbd'])
        P.add('pool', lambda e: e.memset(esel[0:65, :], 0.0), w=['esel'])
        P.add('pool', lambda e: e.memset(esel[64:65, :], 1.0), r=['esel'], w=['esel'])
        P.add('pool', lambda e: e.memset(cneg, -0.5), w=['cneg'])
        P.add('pool', lambda e: e.memset(eps256, 256.0 * EPS), w=['eps256'])
        P.add('pool', lambda e: e.memset(XT[:, :, 0:1], 0.0), w=['xhalo'])
        with nc.allow_non_contiguous_dma(reason="tiny per-feature vectors"):
            P.add('sp', lambda e: e.dma_start(allow_slow_non_contiguous=True, out=wpre, in_=vec["norm_mix_pre"].ap().rearrange("(k p) -> p k", p=128)), w=['wpre'])
            P.add('sp', lambda e: e.dma_start(allow_slow_non_contiguous=True, out=wfpre, in_=vec["norm_ffn_pre"].ap().rearrange("(k p) -> p k", p=128)), w=['wfpre'])
            P.add('sp', lambda e: e.dma_start(allow_slow_non_contiguous=True, out=cw, in_=conv_w.ap().rearrange("w (f p) -> p w f", p=128)), w=['cw'])
            P.add('sp', lambda e: e.dma_start(allow_slow_non_contiguous=True, out=cb, in_=vec["conv_b"].ap().rearrange("(f p) -> p f", p=128)), w=['cb'])
            P.add('sp', lambda e: e.dma_start(allow_slow_non_contiguous=True, out=onw4[0:64, :], in_=vec["hgrn_out_norm"].ap().rearrange("(p o) -> p o", o=1)), w=['onw4a'])
            P.add('sp', lambda e: e.dma_start(allow_slow_non_contiguous=True, out=onw4[64:128, :], in_=vec["hgrn_out_norm"].ap().rearrange("(p o) -> p o", o=1)), w=['onw4b'])
            P.add('sp', lambda e: e.dma_start(allow_slow_non_contiguous=True, out=lbt[:, 0, :, :], in_=lbf.ap().rearrange("s (c p) -> p s c", p=128)), w=['lbt0'])
            P.add('sp', lambda e: e.dma_start(allow_slow_non_contiguous=True, out=lbt[:, 1, :, :], in_=lbb.ap().rearrange("s (c p) -> p s c", p=128)), w=['lbt1'])
        P.add('dve', lambda e: e.tensor_scalar(out=onw4, in0=onw4, scalar1=4.0, scalar2=None, op0=ALU.mult),
              r=['onw4a', 'onw4b'], w=['onw4'])
        P.add('dve', lambda e: e.tensor_tensor(out=ga, in0=lbt[:, :, 0, :], in1=lbt[:, :, 1, :], op=ALU.subtract),
              r=['lbt0', 'lbt1'], w=['ga'])
        P.add('act', lambda e: e.activation(out=gb, in_=ga, func=AF.Tanh, scale=0.5), r=['ga'], w=['gb'])
        P.add('dve', lambda e: e.tensor_scalar(out=ga, in0=gb, scalar1=0.25, scalar2=0.75, op0=ALU.mult, op1=ALU.add),
              r=['gb'], w=['ga'])
        P.add('dve', lambda e: e.tensor_scalar(out=gna, in0=gb, scalar1=-0.25, scalar2=0.25, op0=ALU.mult, op1=ALU.add),
              r=['gb'], w=['gna'])
        P.add('dve', lambda e: e.tensor_scalar(out=gnb, in0=gb, scalar1=0.25, scalar2=-0.25, op0=ALU.mult, op1=ALU.add),
              r=['gb'], w=['gnb'])
        P.add('dve', lambda e: e.tensor_scalar(out=gb, in0=gb, scalar1=-0.25, scalar2=0.25, op0=ALU.mult, op1=ALU.add),
              r=['gb', 'gna', 'gnb'], w=['gb'])

    def weight_prep():
        base = state['off']
        st = [alloc([4096], F32, "wst%d" % i) for i in range(2)]
        bt = [alloc([4096], BF16, "wbt%d" % i) for i in range(2)]
        sw2 = alloc([1024], BF16, "wsw")
        sw = sw2.rearrange("p (h d) -> p h d", h=16)
        P.add('pool', lambda e: e.memset(sw2, 0.0), w=['wsw'])
        it = [0]

        def cast_rows(src, dst, ncols, sw_dst=None, dst_rearr=None):
            r_ = it[0] % 2
            it[0] += 1
            s_, b_ = st[r_], bt[r_]
            P.add('sp', lambda e: e.dma_start(allow_slow_non_contiguous=True, out=s_[:, 0:ncols], in_=src), w=[('wst', r_)])
            h1 = ncols // 2
            P.add('act', lambda e: e.activation(out=b_[:, 0:h1], in_=s_[:, 0:h1], func=AF.Copy), r=[('wst', r_)], w=[('wbt', r_, 0)])
            P.add('dve', lambda e: e.tensor_copy(out=b_[:, h1:ncols], in_=s_[:, h1:ncols]), r=[('wst', r_)], w=[('wbt', r_, 1)])
            if sw_dst is not None:
                sv = s_[:, 0:1024].rearrange("p (h d) -> p h d", h=16)
                P.add('pool', lambda e: e.tensor_copy(out=sw[:, :, 0:8], in_=sv[:, :, 8:16]), r=[('wst', r_)], w=['wsw'])
                P.add('pool', lambda e: e.tensor_copy(out=sw[:, :, 8:16], in_=sv[:, :, 0:8]), r=[('wst', r_)], w=['wsw'])
                P.add('sp', lambda e: e.dma_start(allow_slow_non_contiguous=True, out=sw_dst, in_=sw2), r=['wsw'], w=[('winsw_b', it[0])])
            if dst_rearr is None:
                P.add('sp', lambda e: e.dma_start(allow_slow_non_contiguous=True, out=dst, in_=b_[:, 0:ncols]), r=[('wbt', r_, 0), ('wbt', r_, 1)], w=[('wscr', it[0])])
            else:
                P.add('sp', lambda e: e.dma_start(allow_slow_non_contiguous=True, out=dst, in_=b_[:, 0:ncols].rearrange("p (f j) -> p f j", j=128)),
                      r=[('wbt', r_, 0), ('wbt', r_, 1)], w=[('wscr', it[0])])

        for kc in range(KC):
            rs = slice(kc * 128, (kc + 1) * 128)
            cast_rows(w_in[rs, :], win_b[rs, :], INW, sw_dst=winsw_b[rs, :])
        for kc in range(KC):
            rs = slice(kc * 128, (kc + 1) * 128)
            cast_rows(w_out[rs, :], wout_b[rs, :], D)
        with nc.allow_non_contiguous_dma(reason="chunked weight scratch, 256B segments, one-time"):
            for kc in range(KC):
                rs = slice(kc * 128, (kc + 1) * 128)
                cast_rows(w_gate[rs, :], wg_b[:, :, kc, :].rearrange("f p j -> p f j"), DFF, dst_rearr=True)
                cast_rows(w_up[rs, :], wu_b[:, :, kc, :].rearrange("f p j -> p f j"), DFF, dst_rearr=True)
        for fc in range(NFC):
            rs = slice(fc * 128, (fc + 1) * 128)
            cast_rows(w_down[rs, :], wd_b[rs, :], D)
        state['off'] = base

    pr = {'i': 0}

    def rstd_from_ssq(ssq, out, n, eps, name):
        P.add('dve', lambda e: e.tensor_scalar(out=out, in0=ssq, scalar1=1.0 / n, scalar2=eps, op0=ALU.mult, op1=ALU.add),
              r=[name + 'ssq'], w=[name + 'rstd'])
        P.add('pool', lambda e: e.tensor_tensor(out=out, in0=out, in1=cneg, op=ALU.pow),
              r=[name + 'rstd', 'cneg'], w=[name + 'rstd'])

    def phase_A0(t_base, S, B):
        base = state['off']
        xt = [alloc([D], F32, "xt%d" % i) for i in range(4)]
        xb = [alloc([D], BF16, "xb%d" % i) for i in range(2)]
        junk = alloc([D], BF16, "junk")
        ss = [alloc([2], F32, "ss%d" % i) for i in range(4)]
        NJ = S // 128

        def a1(j):
            r_ = j % 4
            x_, s_ = xt[r_], ss[r_]
            nm = 'a0_%d' % r_
            P.add('sp', lambda e, j=j, x_=x_: e.dma_start(allow_slow_non_contiguous=True, out=x_, in_=xs[t_base + j * 128: t_base + (j + 1) * 128, :]), w=[('xt', r_)])
            P.add('act', lambda e, x_=x_, s_=s_: e.activation(out=junk, in_=x_, func=AF.Square, accum_out=s_[:, 0:1]),
                  r=[('xt', r_)], w=['junk', nm + 'ssq'])
            rstd_from_ssq(s_[:, 0:1], s_[:, 1:2], D, EPS, nm)

        def a2(j):
            r_ = j % 4
            x_, s_, b_ = xt[r_], ss[r_], xb[j % 2]
            nm = 'a0_%d' % r_
            P.add('act', lambda e, x_=x_, s_=s_, b_=b_: e.activation(out=b_, in_=x_, func=AF.Copy, scale=s_[:, 1:2]),
                  r=[('xt', r_), nm + 'rstd'], w=[('xb', j % 2)])

        def a3(j):
            b_ = xb[j % 2]
            bank = 6 + (j % 2)
            for kc in range(KC):
                P.add('pe', lambda e, kc=kc, b_=b_, bank=bank: e.transpose(out=psb[bank][:, kc * 128:(kc + 1) * 128],
                                                                          in_=b_[:, kc * 128:(kc + 1) * 128], identity=ident),
                      r=[('xb', j % 2), 'ident'], w=pstok(bank))
            P.add('dve', lambda e, j=j, bank=bank: e.tensor_tensor(
                out=XT[:, :, 1 + j * 128: 1 + (j + 1) * 128],
                in0=psb[bank][:, :].rearrange("p (k t) -> p k t", k=KC),
                in1=bcast_free(wpre, 128), op=ALU.mult),
                r=pstok(bank) + ['wpre'], w=[('XT', kc, j) for kc in range(KC)])

        for j in range(NJ + 2):
            if j < NJ:
                a1(j)
            if 0 <= j - 1 < NJ:
                a2(j - 1)
            if 0 <= j - 2 < NJ:
                a3(j - 2)
        state['off'] = base

    def load_wA(cols_main, cols_sw, wA):
        i = 0
        with nc.allow_non_contiguous_dma(reason="weight column chunk, 256B segments"):
            for c in cols_main:
                P.add('sp', lambda e, c=c, i=i: e.dma_start(allow_slow_non_contiguous=True, out=wA[i], in_=win_b[:, c:c + 128].rearrange("(k p) j -> p k j", p=128)),
                      r=['wscr'], w=[('wA', i)])
                i += 1
            for c in cols_sw:
                P.add('sp', lambda e, c=c, i=i: e.dma_start(allow_slow_non_contiguous=True, out=wA[i], in_=winsw_b[:, c:c + 128].rearrange("(k p) j -> p k j", p=128)),
                      r=['winsw_b'], w=[('wA', i)])
                i += 1

    def proj(wA_i, tb, bank):
        for kc in range(KC):
            P.add('pe', lambda e, kc=kc: e.matmul(psf[bank][:, :], lhsT=wA_i[1][:, kc, :], rhs=XT[:, kc, 1 + tb * 512: 1 + (tb + 1) * 512],
                                                   start=(kc == 0), stop=(kc == KC - 1)),
                  r=[('wA', wA_i[0])] + [('XT', kc, tb * 4 + q) for q in range(4)], w=pstok(bank, 0, 2048))

    def phase_attn(S, hp, B):
        base = state['off']
        wA = [alloc([KC, 128], BF16, "wA%d" % i) for i in range(5)]
        rotc = [alloc([512], F32, "rotc%d" % i) for i in range(2)]
        rots = [alloc([512], F32, "rots%d" % i) for i in range(2)]
        qT = alloc([S], BF16, "qT")
        kT = alloc([S], BF16, "kT")
        vT = alloc([S], BF16, "vT")
        NTL = S // 128
        vtok = [alloc([NTL, 2, 65], BF16, "vtok%d" % b) for b in range(len(B))]
        tA = [alloc([512], F32, "tA%d" % i) for i in range(2)]
        tB = [alloc([512], F32, "tB%d" % i) for i in range(2)]
        praw = [alloc([256], BF16, "praw%d" % i) for i in range(4)]
        pmk = [alloc([256], BF16, "pmk%d" % i) for i in range(4)]
        UTs = [alloc([S], F32, "UT%d" % i) for i in range(2)]
        pending_norm = [None]
        rrow = alloc([512], F32, "rrow")
        aout = [alloc([512], BF16, "aout%d" % i) for i in range(2)]
        load_wA([hp * 128, 512 + hp * 128, 1024 + hp * 128], [hp * 128, 512 + hp * 128], wA)
        for b in range(len(B)):
            P.add('pool', lambda e, b=b: e.memset(vtok[b][:, :, :, 64:65], 1.0), w=[('vones', b)])
        P.add('pool', lambda e: e.memset(rrow[0:65, :], 0.0), w=['rrow'])
        for tb in range(S // 512):
            sl = slice(tb * 512, (tb + 1) * 512)
            rr = tb % 2
            P.add('sp', lambda e, rr=rr, sl=sl: e.dma_start(out=rotc[rr], in_=rotc_d[:, sl]), w=[('rotc', rr)])
            P.add('sp', lambda e, rr=rr, sl=sl: e.dma_start(out=rots[rr], in_=rots_d[:, sl]), w=[('rots', rr)])
            for (dst, wi, swi, nm) in ((qT, 0, 3, 'qT'), (kT, 1, 4, 'kT')):
                b0 = pr['i'] % 4
                b1 = (pr['i'] + 1) % 4
                pr['i'] += 2
                r_ = (pr['i'] // 2) % 2
                proj((wi, wA[wi]), tb, b0)
                proj((swi, wA[swi]), tb, b1)
                P.add('dve', lambda e, b0=b0, r_=r_, rr=rr: e.tensor_tensor(out=tA[r_], in0=psf[b0][:, :], in1=rotc[rr], op=ALU.mult),
                      r=pstok(b0, 0, 2048) + [('rotc', rr)], w=[('tA', r_)])
                P.add('dve', lambda e, b1=b1, r_=r_, rr=rr: e.tensor_tensor(out=tB[r_], in0=psf[b1][:, :], in1=rots[rr], op=ALU.mult),
                      r=pstok(b1, 0, 2048) + [('rots', rr)], w=[('tB', r_)])
                P.add('dve', lambda e, dst=dst, r_=r_, sl=sl: e.tensor_tensor(out=dst[:, sl], in0=tA[r_], in1=tB[r_], op=ALU.add),
                      r=[('tA', r_), ('tB', r_)], w=[(nm, tb)])
            b0 = pr['i'] % 4
            pr['i'] += 1
            proj((2, wA[2]), tb, b0)
            P.add('act', lambda e, b0=b0, sl=sl: e.activation(out=vT[:, sl], in_=psf[b0][:, :], func=AF.Copy),
                  r=pstok(b0, 0, 2048), w=[('vT', tb)])
        for b, d in enumerate(B):
            L = S // d
            for r in range(d):
                for i0 in range(0, L // 128, 4):
                    n4 = min(4, L // 128 - i0)
                    bank = 6 + (pr['i'] % 2)
                    pr['i'] += 1
                    for ii in range(n4):
                        i = i0 + ii
                        lo = r + d * 128 * i
                        P.add('pe', lambda e, ii=ii, lo=lo, bank=bank, d=d: e.transpose(
                            out=psb[bank][:, ii * 128:(ii + 1) * 128], in_=vT[:, sst(lo, 128, d)], identity=ident),
                            r=[('vT', t) for t in range(lo // 512, (lo + 128 * d - d) // 512 + 1)] + ['ident'],
                            w=pstok(bank, ii * 256, ii * 256 + 256))
                    t0 = r * (L // 128) + i0
                    P.add('dve', lambda e, b=b, t0=t0, n4=n4, bank=bank: e.tensor_copy(
                        out=vtok[b][:, t0:t0 + n4, :, 0:64],
                        in_=psb[bank][:, 0:n4 * 128].rearrange("p (a h c) -> p a h c", a=n4, h=2)),
                        r=pstok(bank, 0, n4 * 256), w=[('vtok', b, t0 + q) for q in range(n4)])
        for h in range(2):
            hb = h * 64
            UT = UTs[h]
            items = []
            for b, d in enumerate(B):
                L = S // d
                NCH = L // 128
                for r in range(d):
                    for i in range(NCH):
                        items.append((b, d, L, NCH, r, i))

            def stA(k, hb=hb):
                b, d, L, NCH, r, i = items[k]
                qlo = max(0, 128 * i - 64)
                qhi = min(L, 128 * i + 192)
                nq = qhi - qlo
                off = qlo - (128 * i - 64)
                slot = k % 4
                bank = (0, 1, 4, 5)[slot]
                klo = r + d * 128 * i
                qpl = r + d * qlo
                ktoks = [('kT', t) for t in range(klo // 512, (klo + 127 * d) // 512 + 1)]
                qtoks = [('qT', t) for t in range(qpl // 512, (qpl + (nq - 1) * d) // 512 + 1)]
                P.add('pe', lambda e: e.matmul(
                    psf[bank][:, 0:nq], lhsT=kT[hb:hb + 64, sst(klo, 128, d)],
                    rhs=qT[hb:hb + 64, sst(qpl, nq, d)], start=True, stop=True),
                    r=ktoks + qtoks, w=pstok(bank))
                P.add('act', lambda e: e.activation(
                    out=praw[slot][:, 0:nq], in_=psf[bank][:, 0:nq], func=AF.Exp, scale=0.125),
                    r=pstok(bank), w=[('praw', slot)])
                P.add('dve', lambda e: e.tensor_tensor(
                    out=pmk[slot][:, 0:nq], in0=praw[slot][:, 0:nq], in1=maskA[:, off:off + nq], op=ALU.mult),
                    r=[('praw', slot), 'maskA'], w=[('pmk', slot)])

            def stB(k, h=h, UT=UT):
                b, d, L, NCH, r, i = items[k]
                qlo = max(0, 128 * i - 64)
                slot = k % 4
                vt = vtok[b][:, r * NCH + i, h, :]
                for n in (i, i + 1):
                    jlo = max(0, 128 * n - 64)
                    jhi = min(L, 128 * n + 64)
                    nb = jhi - jlo
                    c0 = jlo - qlo
                    obk = (6, 7, 2, 3)[n % 4]
                    first = (n == i + 1) or (i == 0)
                    last = (n == i) or (i == NCH - 1)
                    P.add('pe', lambda e, nb=nb, c0=c0, first=first, last=last, obk=obk: e.matmul(
                        psf[obk][0:65, 0:nb], lhsT=vt, rhs=pmk[slot][:, c0:c0 + nb],
                        start=first, stop=last),
                        r=[('pmk', slot), ('vtok', b, r * NCH + i), ('vones', b)], w=pstok(obk))
                    if last:
                        plo = r + d * jlo
                        is_end = (k == len(items) - 1 or items[k + 1][0] != b) and n == i + 1 or \
                                 ((k == len(items) - 1 or items[k + 1][0] != b) and i == NCH - 1 and n == i and NCH - 1 == i and False)
                        wtok = [('UTop', h, b, k, n)]
                        if (k == len(items) - 1 or items[k + 1][0] != b) and n == i + 1:
                            wtok.append(('UTend', h, b))
                        if b == 0:
                            P.add('act', lambda e, nb=nb, plo=plo, obk=obk: e.activation(
                                out=UT[0:65, sst(plo, nb, d)], in_=psf[obk][0:65, 0:nb], func=AF.Copy),
                                r=pstok(obk) + [('UTnorm', h)], w=wtok)
                        else:
                            P.add('dve', lambda e, nb=nb, plo=plo, obk=obk: e.tensor_tensor(
                                out=UT[0:65, sst(plo, nb, d)], in0=psf[obk][0:65, 0:nb],
                                in1=UT[0:65, sst(plo, nb, d)], op=ALU.add),
                                r=pstok(obk) + [('UTend', h, b - 1)], w=wtok)

            LA = 3
            for k in range(min(LA, len(items))):
                stA(k)
            for k in range(len(items)):
                if k + LA < len(items):
                    stA(k + LA)
                stB(k)
                if k == 8 and pending_norm[0] is not None:
                    pending_norm[0]()
                    pending_norm[0] = None

            def norm(h=h, hb=hb, UT=UT):
                for tb in range(S // 512):
                    sl = slice(tb * 512, (tb + 1) * 512)
                    bank = pr['i'] % 4
                    pr['i'] += 1
                    P.add('dve', lambda e, sl=sl: e.reciprocal(out=rrow[64:65, :], in_=UT[64:65, sl]), r=[('UTend', h, len(B) - 1)], w=['rrow'])
                    P.add('pe', lambda e, bank=bank: e.matmul(psf[bank][0:64, :], lhsT=esel[0:65, :], rhs=rrow[0:65, :], start=True, stop=True),
                          r=['rrow', 'esel'], w=pstok(bank))
                    ar_ = tb % 2
                    P.add('dve', lambda e, sl=sl, bank=bank, ar_=ar_: e.tensor_tensor(out=aout[ar_][0:64, :], in0=psf[bank][0:64, :], in1=UT[0:64, sl], op=ALU.mult),
                          r=pstok(bank) + [('UTend', h, len(B) - 1)], w=[('aout', ar_)] + ([('UTnorm', h)] if tb == S // 512 - 1 else []))
                    P.add('sp', lambda e, sl=sl, ar_=ar_: e.dma_start(allow_slow_non_contiguous=True, out=mix_s[hp, hb:hb + 64, sl], in_=aout[ar_][0:64, :]),
                          r=[('aout', ar_)], w=[('mix_s', hp, h, tb)])
            if pending_norm[0] is not None:
                pending_norm[0]()
            pending_norm[0] = norm
        if pending_norm[0] is not None:
            pending_norm[0]()
        state['off'] = base

    def phase_hgrn(S, hp):
        base = state['off']
        NCk = S // CH
        wA = [alloc([KC, 128], BF16, "wA%d" % i) for i in range(5)]
        qf = [alloc([S], BF16, "qf%d" % dr) for dr in range(2)]
        kf = [alloc([S], BF16, "kf%d" % dr) for dr in range(2)]
        vtk = alloc([NCk, 128], BF16, "vtk")
        gate = alloc([S], BF16, "gate")
        dS = [alloc([NCk, 64], F32, "dS%d" % dr) for dr in range(2)]
        Dl = [alloc([NCk], F32, "Dl%d" % dr) for dr in range(2)]
        p1base = state['off']
        th = [alloc([512], F32, "th%d" % i) for i in range(2)]
        thd = [alloc([512], F32, "thd%d" % i) for i in range(2)]
        q2 = alloc([512], F32, "q2")
        Fms = [alloc([512], F32, "Fm%d" % i) for i in range(2)]
        D1s = [alloc([512], F32, "D1_%d" % i) for i in range(2)]
        Kks = [alloc([512], F32, "Kk%d" % i) for i in range(2)]
        Ics = [alloc([512], F32, "Ic%d" % i) for i in range(2)]
        RIs = [alloc([512], F32, "RI%d" % i) for i in range(2)]
        vTb = alloc([512], BF16, "vTb")
        ktk = [alloc([128], BF16, "ktk%d" % i) for i in range(4)]
        c0 = 1536 + hp * 128
        load_wA([c0, c0 + 512, c0 + 1024, c0 + 1536, c0 + 2048], [], wA)
        for i in range(2):
            P.add('pool', lambda e, i=i: e.memset(D1s[i], 0.0), w=[('D1', i)])
        pending = []
        for tb in range(S // 512):
            pending_new = []
            sl = slice(tb * 512, (tb + 1) * 512)
            b0 = pr['i'] % 4
            pr['i'] += 1
            proj((0, wA[0]), tb, b0)
            P.add('act', lambda e, b0=b0: e.activation(out=th[0], in_=psf[b0][:, :], func=AF.Tanh, scale=0.5),
                  r=pstok(b0, 0, 2048), w=[('th', 0)])
            P.add('dve', lambda e, b0=b0: e.scalar_tensor_tensor(out=q2, in0=th[0], scalar=1.0, in1=psf[b0][:, :], op0=ALU.add, op1=ALU.mult),
                  r=pstok(b0, 0, 2048) + [('th', 0)], w=['q2'])
            b0 = pr['i'] % 4
            pr['i'] += 1
            proj((3, wA[3]), tb, b0)
            P.add('act', lambda e, b0=b0: e.activation(out=vTb, in_=psf[b0][:, :], func=AF.Copy), r=pstok(b0, 0, 2048), w=['vTb'])
            for half in range(2):
                bank = 6 + (pr['i'] % 2)
                pr['i'] += 1
                for cc in range(4):
                    c = half * 4 + cc
                    P.add('pe', lambda e, c=c, cc=cc, bank=bank: e.transpose(out=psb[bank][0:64, cc * 128:(cc + 1) * 128],
                                                                               in_=vTb[:, c * 64:(c + 1) * 64], identity=ident),
                          r=['vTb', 'ident'], w=pstok(bank, cc * 256, cc * 256 + 256))
                cg = tb * 8 + half * 4
                P.add('dve', lambda e, cg=cg, bank=bank: e.tensor_copy(out=vtk[0:64, cg:cg + 4, :],
                                                                      in_=psb[bank][0:64, 0:512].rearrange("p (a c) -> p a c", a=4)),
                      r=pstok(bank, 0, 1024), w=[('vtk', cg + q) for q in range(4)])
            b0 = pr['i'] % 4
            pr['i'] += 1
            proj((4, wA[4]), tb, b0)
            P.add('act', lambda e, b0=b0: e.activation(out=th[1], in_=psf[b0][:, :], func=AF.Tanh, scale=0.5),
                  r=pstok(b0, 0, 2048), w=[('th', 1)])
            P.add('dve', lambda e, b0=b0, sl=sl: e.scalar_tensor_tensor(out=gate[:, sl], in0=th[1], scalar=1.0, in1=psf[b0][:, :],
                                                                        op0=ALU.add, op1=ALU.mult),
                  r=pstok(b0, 0, 2048) + [('th', 1)], w=[('gate', tb)])
            for dr in range(2):
                b0 = pr['i'] % 4
                pr['i'] += 1
                proj((1 + dr, wA[1 + dr]), tb, b0)
                Fm, Kk, Ic, RI, tdr = Fms[dr], Kks[dr], Ics[dr], RIs[dr], thd[dr]
                P.add('act', lambda e, b0=b0, tdr=tdr: e.activation(out=tdr, in_=psf[b0][:, :], func=AF.Tanh, scale=0.5),
                      r=pstok(b0, 0, 2048), w=[('thd', dr)])
                P.add('dve', lambda e, dr=dr, Fm=Fm, tdr=tdr: e.tensor_scalar(out=Fm, in0=tdr, scalar1=gb[:, dr, hp:hp + 1], scalar2=ga[:, dr, hp:hp + 1],
                                                             op0=ALU.mult, op1=ALU.add), r=[('thd', dr), 'ga', 'gb'], w=[('Fm', dr)])
                P.add('act', lambda e, dr=dr, Kk=Kk, tdr=tdr: e.activation(out=Kk, in_=tdr, func=AF.Identity, scale=gnb[:, dr, hp:hp + 1],
                                                           bias=gb[:, dr, hp:hp + 1]), r=[('thd', dr), 'gnb', 'gb'], w=[('Kk', dr)])
                D1 = D1s[dr]
                if dr == 0:
                    edge = slice(0, 512, 64)
                    Fv, D1v, Iv = Fm, D1, Ic
                else:
                    edge = slice(63, 512, 64)
                    Fv, D1v, Iv = Fm[:, ::-1], D1[:, ::-1], Ic[:, ::-1]
                P.add('dve', lambda e, edge=edge, D1=D1, Fm=Fm: e.tensor_copy(out=D1[:, edge], in_=Fm[:, edge]), r=[('Fm', dr)], w=[('D1', dr)])
                P.add('dve', lambda e, edge=edge, Fm=Fm: e.memset(Fm[:, edge], 0.0), r=[('D1', dr), ('Fm', dr)], w=[('Fm', dr)])
                P.add('dve', lambda e, Fv=Fv, D1v=D1v, Iv=Iv: e.tensor_tensor_scan(out=Iv, data0=Fv, data1=D1v, initial=0.0,
                                                                                   op0=ALU.mult, op1=ALU.add),
                      r=[('Fm', dr), ('D1', dr)], w=[('Ic', dr)])
                P.add('dve', lambda e, RI=RI, Ic=Ic: e.reciprocal(out=RI, in_=Ic), r=[('Ic', dr)], w=[('RI', dr)])
                P.add('dve', lambda e, dr=dr, sl=sl, Ic=Ic: e.tensor_tensor(out=qf[dr][:, sl], in0=q2, in1=Ic, op=ALU.mult),
                      r=['q2', ('Ic', dr)], w=[('qf', dr, tb)])
                P.add('dve', lambda e, dr=dr, sl=sl, Kk=Kk, RI=RI: e.tensor_tensor(out=kf[dr][:, sl], in0=Kk, in1=RI, op=ALU.mult),
                      r=[('Kk', dr), ('RI', dr)], w=[('kf', dr, tb)])
                ecol = slice(63, 512, 64) if dr == 0 else slice(0, 512, 64)
                P.add('pool', lambda e, dr=dr, ecol=ecol, tb=tb, Ic=Ic: e.tensor_copy(out=Dl[dr][:, tb * 8:(tb + 1) * 8], in_=Ic[:, ecol]),
                      r=[('Ic', dr)], w=[('Dl', dr, tb)])
                if os.environ.get('HSTOP') == 'p1a':
                    continue
                def dsA(c8, dr=dr, tb=tb):
                    c = tb * 8 + c8
                    kr = c8 % 4
                    bank = 6 + (c8 % 2)
                    P.add('pe', lambda e: e.transpose(out=psb[bank][0:64, 0:128], in_=kf[dr][:, c * 64:(c + 1) * 64], identity=ident),
                          r=[('kf', dr, tb), 'ident'], w=pstok(bank))
                    P.add('act', lambda e: e.activation(out=ktk[kr][0:64, :], in_=psb[bank][0:64, 0:128], func=AF.Copy),
                          r=pstok(bank), w=[('ktk', kr)])

                def dsB(c8, dr=dr, tb=tb):
                    c = tb * 8 + c8
                    kr = c8 % 4
                    bank = 4 + (c8 % 2)
                    P.add('pe', lambda e: e.matmul(psf[bank][:, 0:128], lhsT=ktk[kr][0:64, :], rhs=vtk[0:64, c, :], start=True, stop=True),
                          r=[('ktk', kr), ('vtk', c)], w=pstok(bank))
                    for hh in range(2):
                        ps_ = slice(hh * 64, hh * 64 + 64)
                        if hh == 0:
                            P.add('dve', lambda e, ps_=ps_, hh=hh: e.tensor_scalar(
                                out=dS[dr][ps_, c, :], in0=psf[bank][ps_, hh * 64: 64 + hh * 64],
                                scalar1=Dl[dr][ps_, c:c + 1], scalar2=None, op0=ALU.mult),
                                r=pstok(bank) + [('Dl', dr, tb)], w=[('dS', dr, c, hh)])
                        else:
                            P.add('act', lambda e, ps_=ps_, hh=hh: e.activation(
                                out=dS[dr][ps_, c, :], in_=psf[bank][ps_, hh * 64: 64 + hh * 64], func=AF.Copy,
                                scale=Dl[dr][ps_, c:c + 1]),
                                r=pstok(bank) + [('Dl', dr, tb)], w=[('dS', dr, c, hh)])
                    if dr == 0 and c > 0:
                        P.add('dve', lambda e: e.scalar_tensor_tensor(
                            out=dS[0][:, c, :], in0=dS[0][:, c - 1, :], scalar=Dl[0][:, c:c + 1], in1=dS[0][:, c, :],
                            op0=ALU.mult, op1=ALU.add),
                            r=[('dS', 0, c - 1, 0), ('dS', 0, c - 1, 1), ('dS', 0, c, 0), ('dS', 0, c, 1), ('Dl', 0, tb)],
                            w=[('dS', 0, c, 0), ('dS', 0, c, 1)])

                def run_ds(dsA=dsA, dsB=dsB):
                    dsA(0)
                    for c8 in range(8):
                        if c8 + 1 < 8:
                            dsA(c8 + 1)
                        dsB(c8)
                pending_new.append(run_ds)
            for f_ in pending:
                f_()
            pending = pending_new
        for f_ in pending:
            f_()
        if os.environ.get('HSTOP') in ('p1a', 'p1'):
            state['off'] = base
            return
        for n_ in range(1, NCk):
            for dr in (1,):
                c = n_ if dr == 0 else NCk - 1 - n_
                pc = c - 1 if dr == 0 else c + 1
                P.add('dve', lambda e, dr=dr, c=c, pc=pc: e.scalar_tensor_tensor(
                    out=dS[dr][:, c, :], in0=dS[dr][:, pc, :], scalar=Dl[dr][:, c:c + 1], in1=dS[dr][:, c, :],
                    op0=ALU.mult, op1=ALU.add),
                    r=[('dS', dr, pc, 0), ('dS', dr, pc, 1), ('dS', dr, c, 0), ('dS', dr, c, 1), ('Dl', dr, c // 8)],
                    w=[('dS', dr, c, 0), ('dS', dr, c, 1)])
        if os.environ.get('HSTOP') == 'chain':
            state['off'] = base
            return
        P.barrier()
        state['off'] = p1base
        att = [alloc([2, 64], BF16, "att%d" % i) for i in range(2)]
        Sbd4 = [alloc([8, 128], BF16, "Sbd%d" % i) for i in range(4)]
        osum = alloc([512], F32, "osum")
        osq = alloc([512], F32, "osq")
        rs8 = alloc([512], F32, "rs8")
        houts = [alloc([512], BF16, "hout%d" % i) for i in range(2)]
        for i in range(4):
            P.add('dve', lambda e, i=i: e.memset(Sbd4[i], 0.0), w=[('Sbd', i % 2, i // 2)])
        prev_norm = [None]
        for tb in range(S // 512):
            sl = slice(tb * 512, (tb + 1) * 512)
            obA = 4 + 2 * (tb % 2)
            obB = 5 + 2 * (tb % 2)
            Sbd = [Sbd4[0 + 2 * (tb % 2)], Sbd4[1 + 2 * (tb % 2)]]
            sbp = tb % 2
            for dr in range(2):
                for hh in range(2):
                    ps_ = slice(hh * 64, hh * 64 + 64)
                    cs = [tb * 8 + c8 + (-1 if dr == 0 else 1) for c8 in range(8)]
                    valid = [c8 for c8 in range(8) if 0 <= cs[c8] < NCk]
                    lo, hi = valid[0], valid[-1] + 1
                    P.add('pool', lambda e, dr=dr, ps_=ps_, lo=lo, hi=hi, cs=cs, hh=hh, Sbd=Sbd: e.tensor_copy(
                        out=Sbd[dr][ps_, lo:hi, hh * 64:hh * 64 + 64], in_=dS[dr][ps_, cs[lo]:cs[hi - 1] + 1, :]),
                        r=[('dS', dr, cs[c8], hh) for c8 in valid], w=[('Sbd', dr, sbp)])
            if os.environ.get('HSTOP') == 'p2a1':
                continue
            items2 = [(c8, dr) for c8 in range(8) for dr in range(2)]

            def p2A(k, tb=tb):
                c8, dr = items2[k]
                c = tb * 8 + c8
                cs_ = slice(c * 64, (c + 1) * 64)
                ar = k % 2
                mk = maskF if dr == 0 else maskB
                for hh in range(2):
                    ps_ = slice(hh * 64, hh * 64 + 64)
                    abank = ar * 2 + hh
                    P.add('pe', lambda e, ps_=ps_, abank=abank: e.matmul(
                        psf[abank][0:64, 0:64], lhsT=kf[dr][ps_, cs_], rhs=qf[dr][ps_, cs_], start=True, stop=True),
                        r=[('kf', dr, tb), ('qf', dr, tb)], w=pstok(abank))
                    P.add('dve', lambda e, abank=abank, hh=hh: e.tensor_tensor(
                        out=att[ar][0:64, hh, :], in0=psf[abank][0:64, 0:64], in1=mk[0:64, :], op=ALU.mult),
                        r=pstok(abank) + ['maskF', 'maskB'], w=[('att', ar, hh)])

            def p2B(k, tb=tb, obA=obA, obB=obB, Sbd=Sbd, sbp=sbp):
                c8, dr = items2[k]
                c = tb * 8 + c8
                cs_ = slice(c * 64, (c + 1) * 64)
                ar = k % 2
                first = (dr == 0)
                skip_inter = (dr == 0 and c == 0) or (dr == 1 and c == NCk - 1)
                for hh, ob in ((0, obA), (1, obB)):
                    P.add('pe', lambda e, hh=hh, ob=ob: e.matmul(
                        psf[ob][:, c8 * 64:(c8 + 1) * 64], lhsT=vtk[0:64, c, :], rhs=att[ar][0:64, hh, :],
                        start=first, stop=(dr == 1 and skip_inter)),
                        r=[('att', ar, hh), ('vtk', c)], w=pstok(ob))
                    if not skip_inter:
                        P.add('pe', lambda e, ob=ob: e.matmul(
                            psf[ob][:, c8 * 64:(c8 + 1) * 64], lhsT=Sbd[dr][:, c8, :], rhs=qf[dr][:, cs_],
                            start=False, stop=(dr == 1)),
                            r=[('Sbd', dr, sbp), ('qf', dr, tb)], w=pstok(ob))

            def norm_chain(tb=tb, sl=sl, obA=obA, obB=obB):
                P.add('act', lambda e: e.activation(out=osum[0:64, :], in_=psf[obA][0:64, :], func=AF.Copy), r=pstok(obA), w=['osumA'])
                P.add('act', lambda e: e.activation(out=osum[64:128, :], in_=psf[obB][64:128, :], func=AF.Copy), r=pstok(obB), w=['osumB'])
                P.add('act', lambda e: e.activation(out=osq, in_=osum, func=AF.Square), r=['osumA', 'osumB'], w=['osq'])
                nb_ = pr['i'] % 4
                pr['i'] += 1
                P.add('pe', lambda e: e.matmul(psf[nb_][:, :], lhsT=onesbd, rhs=osq, start=True, stop=True),
                      r=['osq', 'onesbd'], w=pstok(nb_))
                P.add('act', lambda e: e.activation(out=rs8, in_=psf[nb_][:, :], func=AF.Ln, bias=eps256[:, 0:1]),
                      r=pstok(nb_) + ['eps256'], w=['rs8a'])
                P.add('act', lambda e: e.activation(out=rs8, in_=rs8, func=AF.Exp, scale=-0.5), r=['rs8a'], w=['rs8'])
                P.add('dve', lambda e: e.tensor_tensor(out=osum, in0=osum, in1=rs8, op=ALU.mult), r=['osumA', 'osumB', 'rs8'], w=['osn'])
                hr = tb % 2
                P.add('dve', lambda e: e.scalar_tensor_tensor(out=houts[hr], in0=osum, scalar=onw4[:, 0:1], in1=gate[:, sl],
                                                              op0=ALU.mult, op1=ALU.mult),
                      r=['osn', 'onw4', ('gate', tb)], w=[('hout', hr)])
                P.add('sp', lambda e: e.dma_start(allow_slow_non_contiguous=True, out=mix_s[4 + hp, :, sl], in_=houts[hr]),
                      r=[('hout', hr)], w=[('mix_s', 4 + hp, tb)])

            p2A(0)
            for k in range(16):
                if k + 1 < 16:
                    p2A(k + 1)
                p2B(k)
                if k == 5 and prev_norm[0] is not None:
                    prev_norm[0]()
                    prev_norm[0] = None
            prev_norm[0] = norm_chain
        if prev_norm[0] is not None:
            prev_norm[0]()
        state['off'] = base

    def phase_B1(t_base, S):
        base = state['off']
        wo = alloc([KC, D], BF16, "wo")
        wpost = alloc([D], F32, "wpost")
        P.add('sp', lambda e: e.dma_start(allow_slow_non_contiguous=True, out=wpost, in_=bass.AP(vec["norm_mix_post"], 0, [[0, 128], [1, D]])), w=['wpost'])
        mt = [alloc([KC, 512], BF16, "mt%d" % i) for i in range(2)]
        xt = [alloc([D], F32, "xt%d" % i) for i in range(4)]
        ht = [alloc([D], F32, "ht%d" % i) for i in range(4)]
        tm = [alloc([D], F32, "tm%d" % i) for i in range(4)]
        hb_ = [alloc([D], BF16, "hb%d" % i) for i in range(2)]
        junk = alloc([D], BF16, "junk")
        ss = [alloc([4], F32, "ss%d" % i) for i in range(4)]
        P.add('sp', lambda e: e.dma_start(allow_slow_non_contiguous=True, out=wo, in_=wout_b.rearrange("(k p) c -> p k c", p=128)), r=['wscr'], w=['wo'])
        P.add('pool', lambda e: e.memset(XT[:, :, S + 1:S + 2], 0.0), w=['xhalo2'])
        NJ = S // 128

        ld = {'x': 0, 'm': 0}

        def b1_loads(j_upto, g_upto):
            while ld['m'] < min(g_upto, S // 512):
                g_ = ld['m']
                P.add('sp', lambda e, g_=g_: e.dma_start(allow_slow_non_contiguous=True, out=mt[g_ % 2], in_=mix_s[:, :, g_ * 512:(g_ + 1) * 512].rearrange("k p t -> p k t")),
                      r=[], w=[('mt', g_ % 2)])
                ld['m'] += 1
            while ld['x'] < min(j_upto, NJ):
                j_ = ld['x']
                P.add('sp', lambda e, j_=j_: e.dma_start(allow_slow_non_contiguous=True, out=xt[j_ % 4], in_=xs[t_base + j_ * 128: t_base + (j_ + 1) * 128, :]),
                      w=[('xt', j_ % 4)])
                ld['x'] += 1

        def b1a(j):
            g, tt = j // 4, j % 4
            m_ = mt[g % 2]
            b1_loads(j + 3, g + 2 if tt >= 2 else g + 1)
            r_ = j % 4
            x_, h_, t_, s_ = xt[r_], ht[r_], tm[r_], ss[r_]
            nm = 'b1_%d' % r_
            for half in range(2):
                bank = (j % 2) * 2 + half
                hs = slice(half * 512, (half + 1) * 512)
                for kc in range(KC):
                    P.add('pe', lambda e, kc=kc, hs=hs, bank=bank: e.matmul(
                        psf[bank][:, :], lhsT=m_[:, kc, tt * 128:(tt + 1) * 128], rhs=wo[:, kc, hs], start=(kc == 0), stop=(kc == KC - 1)),
                        r=[('mt', g % 2), 'wo'], w=pstok(bank))
                P.add('act', lambda e, bank=bank, half=half: e.activation(out=junk[:, 0:512], in_=psf[bank][:, :], func=AF.Square,
                                                                           accum_out=s_[:, half:half + 1]),
                      r=pstok(bank), w=['junk', (nm, 'p', half)])
                P.add('act', lambda e, bank=bank, hs=hs: e.activation(out=t_[:, hs], in_=psf[bank][:, :], func=AF.Copy),
                      r=pstok(bank), w=[('tm', r_, half)])
            P.add('dve', lambda e: e.tensor_tensor(out=s_[:, 2:3], in0=s_[:, 0:1], in1=s_[:, 1:2], op=ALU.add),
                  r=[(nm, 'p', 0), (nm, 'p', 1)], w=[nm + 'ssq'])
            rstd_from_ssq(s_[:, 2:3], s_[:, 3:4], D, EPS, nm)
            P.add('dve', lambda e: e.scalar_tensor_tensor(out=t_, in0=t_, scalar=s_[:, 3:4], in1=wpost, op0=ALU.mult, op1=ALU.mult),
                  r=[('tm', r_, 0), ('tm', r_, 1), nm + 'rstd', 'wpost'], w=[('tm', r_, 0), ('tm', r_, 1)])
            P.add('dve', lambda e: e.tensor_tensor(out=h_, in0=t_, in1=x_, op=ALU.add),
                  r=[('tm', r_, 0), ('tm', r_, 1), ('xt', r_)], w=[('ht', r_)])
            P.add('sp', lambda e: e.dma_start(allow_slow_non_contiguous=True, out=ys[t_base + j * 128: t_base + (j + 1) * 128, :], in_=h_),
                  r=[('ht', r_)], w=[('ys', j)])
            nm2 = 'b1n_%d' % r_
            P.add('act', lambda e: e.activation(out=junk, in_=h_, func=AF.Square, accum_out=s_[:, 0:1]),
                  r=[('ht', r_)], w=['junk', nm2 + 'ssq'])
            rstd_from_ssq(s_[:, 0:1], s_[:, 1:2], D, EPS, nm2)

        def b1b(j):
            r_ = j % 4
            h_, s_, b_ = ht[r_], ss[r_], hb_[j % 2]
            nm2 = 'b1n_%d' % r_
            P.add('act', lambda e: e.activation(out=b_, in_=h_, func=AF.Copy, scale=s_[:, 1:2]),
                  r=[('ht', r_), nm2 + 'rstd'], w=[('hb', j % 2)])

        def b1c(j):
            b_ = hb_[j % 2]
            bank = 6 + (j % 2)
            for kc in range(KC):
                P.add('pe', lambda e, kc=kc: e.transpose(out=psb[bank][:, kc * 128:(kc + 1) * 128],
                                                       in_=b_[:, kc * 128:(kc + 1) * 128], identity=ident),
                      r=[('hb', j % 2), 'ident'], w=pstok(bank))
            P.add('dve', lambda e: e.tensor_tensor(
                out=XT[:, :, 1 + j * 128: 1 + (j + 1) * 128], in0=psb[bank][:, :].rearrange("p (k t) -> p k t", k=KC),
                in1=bcast_free(wfpre, 128), op=ALU.mult),
                r=pstok(bank) + ['wfpre'], w=[('XT', kc, j) for kc in range(KC)])

        for j in range(NJ + 2):
            if j < NJ:
                b1a(j)
            if 0 <= j - 1 < NJ:
                b1b(j - 1)
            if 0 <= j - 2 < NJ:
                b1c(j - 2)
        state['off'] = base

    def phase_B2(t_base, S):
        base = state['off']
        wfpost = alloc([D], F32, "wfpost")
        P.add('sp', lambda e: e.dma_start(allow_slow_non_contiguous=True, out=wfpost, in_=bass.AP(vec["norm_ffn_post"], 0, [[0, 128], [1, D]])), w=['wfpost'])
        wg = [alloc([KC, 128], BF16, "wg%d" % i) for i in range(6)]
        wu = [alloc([KC, 128], BF16, "wu%d" % i) for i in range(6)]
        wd = [alloc([512], BF16, "wd%d" % i) for i in range(16)]
        hid = alloc([NFC, 512], BF16, "hid")
        Asb = [alloc([514], F32, "Asb%d" % i) for i in range(5)]
        cc = [alloc([512], F32, "cc%d" % i) for i in range(5)]
        c2 = [alloc([512], F32, "c2%d" % i) for i in range(5)]
        c3 = [alloc([512], F32, "c3%d" % i) for i in range(5)]
        fsb = alloc([4, D], F32, "fsb")
        ht = [alloc([D], F32, "ht%d" % i) for i in range(4)]
        junk = alloc([512], BF16, "junk")
        ss = alloc([4, 4], F32, "ssB")
        wi = {'g': 0, 'd': 0}
        NB = S // 512
        pf = {'g': 0, 'd': 0}

        def prefetch(g_upto, d_upto):
            while pf['g'] < min(g_upto, NB * NFC):
                g = pf['g']
                fc_, r3_ = g % NFC, g % 6
                P.add('sp', lambda e, fc_=fc_, r3_=r3_: e.dma_start(allow_slow_non_contiguous=True, out=wg[r3_], in_=wg_b[fc_]), r=['wscr'], w=[('wg', r3_)])
                P.add('sp', lambda e, fc_=fc_, r3_=r3_: e.dma_start(allow_slow_non_contiguous=True, out=wu[r3_], in_=wu_b[fc_]), r=['wscr'], w=[('wu', r3_)])
                pf['g'] += 1
            while pf['d'] < min(d_upto, NB * 2 * NFC):
                dd = pf['d']
                fc_, half_, r4_ = dd % NFC, (dd // NFC) % 2, dd % 16
                P.add('sp', lambda e, fc_=fc_, half_=half_, r4_=r4_: e.dma_start(
                    allow_slow_non_contiguous=True, out=wd[r4_], in_=wd_b[fc_ * 128:(fc_ + 1) * 128, half_ * 512:(half_ + 1) * 512]),
                    r=['wscr'], w=[('wd', r4_)])
                pf['d'] += 1

        for blk in range(NB):
            t0 = blk * 512
            xtoks = lambda kc: [('XT', kc, blk * 4 + q) for q in range(4)]
            def st1(fc, blk=blk, t0=t0):
                r3 = wi['g'] % 6
                wi['g'] += 1
                r2 = fc % 5
                prefetch(wi['g'] + 4, wi['d'] + (10 if fc >= NFC - 6 else 0))
                gb_ = (0, 1)[fc % 2]
                ub_ = (2, 3, 6, 7)[fc % 4]
                hbk = (4, 5)[fc % 2]
                xtoks = lambda kc: [('XT', kc, blk * 4 + q) for q in range(4)]
                for kc in range(KC):
                    P.add('pe', lambda e, kc=kc: e.matmul(psf[gb_][:, :], lhsT=wg[r3][:, kc, :],
                                                          rhs=XT[:, kc, 1 + t0: 1 + t0 + 512], start=(kc == 0), stop=(kc == KC - 1)),
                          r=[('wg', r3)] + xtoks(kc), w=pstok(gb_))
                halo_r = ['xhalo', 'xhalo2'] + [('XT', kc, q) for kc in range(KC) for q in (max(blk * 4 - 1, 0), min(blk * 4 + 4, S // 128 - 1))]
                for kc in range(KC):
                    P.add('pe', lambda e, kc=kc: e.matmul(psf[hbk][:, 0:2], lhsT=wg[r3][:, kc, :],
                                                          rhs=XT[:, kc, t0: t0 + 514: 513], start=(kc == 0), stop=(kc == KC - 1)),
                          r=[('wg', r3)] + halo_r, w=pstok(hbk))
                for kc in range(KC):
                    P.add('pe', lambda e, kc=kc: e.matmul(psf[ub_][:, :], lhsT=wu[r3][:, kc, :],
                                                          rhs=XT[:, kc, 1 + t0: 1 + t0 + 512], start=(kc == 0), stop=(kc == KC - 1)),
                          r=[('wu', r3)] + xtoks(kc), w=pstok(ub_))
                A_, c_ = Asb[r2], cc[r2]
                P.add('act', lambda e: e.activation(out=A_[:, 1:513], in_=psf[gb_][:, :], func=AF.Copy),
                      r=pstok(gb_), w=[('Asb', r2, 0)])
                P.add('act', lambda e: e.activation(out=A_[:, 0:514:513], in_=psf[hbk][:, 0:2], func=AF.Copy),
                      r=pstok(hbk), w=[('Asb', r2, 1)])
                P.add('act', lambda e: e.activation(out=c_, in_=A_[:, 1:513], func=AF.Identity, scale=cw[:, 1, fc:fc + 1],
                                                    bias=cb[:, fc:fc + 1]),
                      r=[('Asb', r2, 0), 'cw', 'cb'], w=[('cc', r2)])

            def st2(fc):
                r2 = fc % 5
                A_, c_, c2_ = Asb[r2], cc[r2], c2[r2]
                P.add('dve', lambda e: e.scalar_tensor_tensor(out=c_, in0=A_[:, 0:512], scalar=cw[:, 0, fc:fc + 1], in1=c_,
                                                              op0=ALU.mult, op1=ALU.add),
                      r=[('Asb', r2, 0), ('Asb', r2, 1), 'cw', ('cc', r2)], w=[('cc', r2)])
                P.add('dve', lambda e: e.scalar_tensor_tensor(out=c_, in0=A_[:, 2:514], scalar=cw[:, 2, fc:fc + 1], in1=c_,
                                                              op0=ALU.mult, op1=ALU.add),
                      r=[('Asb', r2, 0), ('Asb', r2, 1), 'cw', ('cc', r2)], w=[('cc', r2)])
                P.add('act', lambda e: e.activation(out=c2_, in_=c_, func=AF.Square, scale=0.21145921592590237),
                      r=[('cc', r2)], w=[('c2', r2)])

            def st3(fc):
                r2 = fc % 5
                c_, c2_, c3_ = cc[r2], c2[r2], c3[r2]
                P.add('dve', lambda e: e.scalar_tensor_tensor(out=c3_, in0=c2_, scalar=1.0, in1=c_, op0=ALU.add, op1=ALU.mult),
                      r=[('c2', r2), ('cc', r2)], w=[('c3', r2)])
                P.add('act', lambda e: e.activation(out=c3_, in_=c3_, func=AF.Tanh, scale=0.7978845608028654),
                      r=[('c3', r2)], w=[('c3', r2)])

            def st4(fc):
                r2 = fc % 5
                ub_ = (2, 3, 6, 7)[fc % 4]
                c_, c2_, c3_ = cc[r2], c2[r2], c3[r2]
                P.add('dve', lambda e: e.scalar_tensor_tensor(out=c2_, in0=c3_, scalar=1.0, in1=c_, op0=ALU.add, op1=ALU.mult),
                      r=[('c3', r2), ('cc', r2), ('c2', r2)], w=[('c2', r2)])
                P.add('dve', lambda e: e.tensor_tensor(out=hid[:, fc, :], in0=psf[ub_][:, :], in1=c2_, op=ALU.mult),
                      r=pstok(ub_) + [('c2', r2)], w=[('hid', fc)])

            for it in range(NFC + 3):
                if it < NFC:
                    st1(it)
                if 0 <= it - 1 < NFC:
                    st2(it - 1)
                if 0 <= it - 2 < NFC:
                    st3(it - 2)
                if 0 <= it - 3 < NFC:
                    st4(it - 3)
            for tt in range(4):
                j = blk * 4 + tt
                P.add('sp', lambda e, j=j: e.dma_start(allow_slow_non_contiguous=True, out=ht[j % 4], in_=ys[t_base + j * 128: t_base + (j + 1) * 128, :]),
                      r=[('ys', j)], w=[('ht', j % 4)])
            for half in range(2):
                hs = slice(half * 512, (half + 1) * 512)
                for fc in range(NFC):
                    r4 = wi['d'] % 16
                    wi['d'] += 1
                    prefetch(wi['g'] + (5 if (half == 1 and fc >= NFC - 8) else 0), wi['d'] + 11)
                    for tt in range(4):
                        P.add('pe', lambda e, fc=fc, r4=r4, tt=tt: e.matmul(psf[4 + tt][:, :], lhsT=hid[:, fc, tt * 128:(tt + 1) * 128], rhs=wd[r4],
                                                                          start=(fc == 0), stop=(fc == NFC - 1)),
                              r=[('hid', fc), ('wd', r4)], w=pstok(4 + tt))
                for tt in range(4):
                    P.add('act', lambda e, tt=tt, half=half: e.activation(out=junk, in_=psf[4 + tt][:, :], func=AF.Square,
                                                                        accum_out=ss[:, tt, half:half + 1]),
                          r=pstok(4 + tt), w=['junk', ('ssB', tt, half)])
                    P.add('act', lambda e, tt=tt, hs=hs: e.activation(out=fsb[:, tt, hs], in_=psf[4 + tt][:, :], func=AF.Copy),
                          r=pstok(4 + tt), w=[('fsb', tt, half)])
            for tt in range(4):
                j = blk * 4 + tt
                r_ = j % 4
                h_ = ht[r_]
                nm = 'b2_%d' % tt
                P.add('dve', lambda e, tt=tt: e.tensor_tensor(out=ss[:, tt, 2:3], in0=ss[:, tt, 0:1], in1=ss[:, tt, 1:2], op=ALU.add),
                      r=[('ssB', tt, 0), ('ssB', tt, 1)], w=[nm + 'ssq'])
                rstd_from_ssq(ss[:, tt, 2:3], ss[:, tt, 3:4], D, 4.0 * EPS, nm)
                P.add('dve', lambda e, tt=tt: e.scalar_tensor_tensor(out=fsb[:, tt, :], in0=fsb[:, tt, :], scalar=ss[:, tt, 3:4], in1=wfpost,
                                                                   op0=ALU.mult, op1=ALU.mult),
                      r=[('fsb', tt, 0), ('fsb', tt, 1), nm + 'rstd', 'wfpost'], w=[('fsb', tt, 0), ('fsb', tt, 1)])
                P.add('dve', lambda e, tt=tt, h_=h_: e.tensor_tensor(out=h_, in0=h_, in1=fsb[:, tt, :], op=ALU.add),
                      r=[('fsb', tt, 0), ('fsb', tt, 1), ('ht', r_)], w=[('ht', r_)])
                P.add('sp', lambda e, j=j, h_=h_: e.dma_start(allow_slow_non_contiguous=True, out=ys[t_base + j * 128: t_base + (j + 1) * 128, :], in_=h_),
                      r=[('ht', r_)], w=[('ys', j)])
        state['off'] = base

    setup()
    weight_prep()
    P.barrier()
    t_base = 0
    for S in seq_lens:
        phase_A0(t_base, S, branches)
        P.barrier()
        if dbg_xt is not None and t_base == 0:
            P.add('sp', lambda e, S=S: e.dma_start(out=dbg_xt[:, :, 0:S + 1], in_=XT[:, :, 0:S + 1]), w=['dbgxt'])
        for hp in range(4):
            phase_attn(S, hp, branches)
        P.barrier()
        if stop_after == 'attn':
            break
        for hp in range(4):
            phase_hgrn(S, hp)
            P.barrier()
            if stop_after == 'hgrn0':
                break
        if stop_after == 'hgrn0':
            break
        phase_B1(t_base, S)
        P.barrier()
        phase_B2(t_base, S)
        P.barrier()
        t_base += S

    with nc.Block() as block:
        P.emit(nc, block, sems)
    es.close()
    return nc


def rot_tables(SM):
    half = 8
    inv = ROPE_THETA ** (-np.arange(half, dtype=np.float32) * 2.0 / 16.0)
    ang = np.arange(SM, dtype=np.float32)[:, None] * inv[None, :]
    cos = np.cos(ang).astype(np.float32).T
    sin = np.sin(ang).astype(np.float32).T
    c = np.ones((128, SM), np.float32)
    s = np.zeros((128, SM), np.float32)
    for hb in (0, 64):
        c[hb:hb + 8] = cos
        c[hb + 8:hb + 16] = cos
        s[hb:hb + 8] = -sin
        s[hb + 8:hb + 16] = sin
    return c, s


_CACHE = {}


def kernel(x_prompt, x_sample, norm_mix_pre, w_in, hgrn_lb_fwd, hgrn_lb_bwd, hgrn_out_norm, w_out,
           norm_mix_post, norm_ffn_pre, w_gate, w_up, conv_w, conv_b, w_down, norm_ffn_post):
    n = 8
    x_prompt = np.asarray(x_prompt)
    x_sample = np.asarray(x_sample)
    Bp, Sp, _ = x_prompt.shape
    Bs, Ss, _ = x_sample.shape
    pp, sp_ = Bp // n, Bs // n
    seq_lens = tuple([Sp] * pp + [Ss] * sp_)
    if seq_lens not in _CACHE:
        _CACHE[seq_lens] = build(seq_lens)
    nc = _CACHE[seq_lens]
    rc, rs = rot_tables(max(seq_lens))
    f = lambda a: np.ascontiguousarray(np.asarray(a, dtype=np.float32))
    common = {
        "w_in": f(w_in)[0], "w_out": f(w_out)[0], "w_gate": f(w_gate)[0], "w_up": f(w_up)[0], "w_down": f(w_down)[0],
        "norm_mix_pre": f(norm_mix_pre)[0], "norm_mix_post": f(norm_mix_post)[0], "norm_ffn_pre": f(norm_ffn_pre)[0],
        "norm_ffn_post": f(norm_ffn_post)[0], "conv_b": f(conv_b)[0], "hgrn_out_norm": f(hgrn_out_norm)[0],
        "conv_w": f(conv_w)[0], "hgrn_lb_fwd": f(hgrn_lb_fwd), "hgrn_lb_bwd": f(hgrn_lb_bwd),
        "rot_c": rc, "rot_s": rs,
    }
    in_maps = []
    for c in range(n):
        xs = np.concatenate([x_prompt[c * pp:(c + 1) * pp].reshape(-1, D), x_sample[c * sp_:(c + 1) * sp_].reshape(-1, D)], axis=0)
        m = dict(common)
        m["xs"] = np.ascontiguousarray(xs, dtype=np.float32)
        in_maps.append(m)
    res = run_bass_kernel_spmd(nc, in_maps, core_ids=list(range(n)))
    yp = np.empty((Bp, Sp, D), np.float32)
    ysm = np.empty((Bs, Ss, D), np.float32)
    for c in range(n):
        y = res.results[c]["ys"]
        yp[c * pp:(c + 1) * pp] = y[:pp * Sp].reshape(pp, Sp, D)
        ysm[c * sp_:(c + 1) * sp_] = y[pp * Sp:].reshape(sp_, Ss, D)
    return (yp, ysm)
```
